# Optimizing a Trainium2 kernel written in Bass

```python
import jax, jax.numpy as jnp
from jax import lax
import numpy as np

D_MODEL = 2048
BATCH = 32
SEQ = 256
DEPTH = 1
DEC_BATCH = 2
DEC_SEQ = 2048
PAST_LEN = 512

GRID_W = 64
MIX_DIM = D_MODEL
RET_DIM = MIX_DIM // 2
MLSTM_DIM = MIX_DIM - RET_DIM
HEAD_DIM = 256
RET_HEADS = RET_DIM // HEAD_DIM
MLSTM_HEADS = MLSTM_DIM // HEAD_DIM
D_FF = ((8 * D_MODEL // 3 + 127) // 128) * 128
CONV_W = 3
CHUNK = 128
ROPE_BASE = 10000.0
N_IN = 4 * RET_DIM + 4 * MLSTM_DIM + 4 * MLSTM_HEADS
ALPHA = (2.0 * DEPTH) ** 0.25
BETA = (8.0 * DEPTH) ** -0.25

kernel_name = 'hybrid_retention_mlstm_flow_step'

F32 = jnp.float32


def _ln(x, eps=1e-6):
    xf = x.astype(F32)
    mu = jnp.mean(xf, axis=-1, keepdims=True)
    var = jnp.mean(jnp.square(xf - mu), axis=-1, keepdims=True)
    return ((xf - mu) * lax.rsqrt(var + eps)).astype(x.dtype)


def _seq_conv(u, w, grid):
    B, L, C = u.shape
    if grid:
        rows = L // GRID_W
        u = u.reshape(B * rows, GRID_W, C)
    n = u.shape[1]
    p = CONV_W // 2
    up = jnp.pad(u, ((0, 0), (p, p), (0, 0)))
    y = up[:, 0:n] * w[0]
    for t in range(1, CONV_W):
        y = y + up[:, t:t + n] * w[t]
    return y.reshape(B, L, C)


def _rope_2d(x):
    L, hd = x.shape[1], x.shape[-1]
    quarter = hd // 4
    half = hd // 2
    t = jnp.arange(L)
    row = (t // GRID_W).astype(F32)
    col = (t % GRID_W).astype(F32)
    inv = ROPE_BASE ** (-jnp.arange(quarter, dtype=F32) / quarter)

    def rot(xh, pos):
        ang = pos[:, None] * inv
        cos = jnp.cos(ang)[None, :, None, :].astype(x.dtype)
        sin = jnp.sin(ang)[None, :, None, :].astype(x.dtype)
        x1, x2 = xh[..., :quarter], xh[..., quarter:]
        return jnp.concatenate([x1 * cos - x2 * sin, x1 * sin + x2 * cos], axis=-1)

    return jnp.concatenate([rot(x[..., :half], row), rot(x[..., half:], col)], axis=-1)


def _to_chunks(a):
    B, L, H, d = a.shape
    return a.astype(F32).reshape(B, L // CHUNK, CHUNK, H, d).transpose(1, 0, 3, 2, 4)


def _gate_chunks(a):
    B, L, H = a.shape
    return a.astype(F32).reshape(B, L // CHUNK, CHUNK, H).transpose(1, 0, 3, 2)


def _from_chunks(o, B, L):
    H, d = o.shape[2], o.shape[-1]
    return o.transpose(1, 0, 3, 2, 4).reshape(B, L, H, d)


def _retention_dir(q, k, v, log_gamma, s0):
    B, L = q.shape[0], q.shape[1]
    pos = jnp.arange(CHUNK, dtype=F32)
    diff = pos[:, None] - pos[None, :]
    lg = log_gamma.astype(F32)
    dmask = jnp.where(diff >= 0, jnp.exp(lg[:, None, None] * jnp.maximum(diff, 0.0)), 0.0)
    q_dec = jnp.exp(lg[:, None] * (pos + 1.0))[..., None]
    k_dec = jnp.exp(lg[:, None] * (CHUNK - 1.0 - pos))[..., None]
    c_dec = jnp.exp(lg * CHUNK)[:, None, None]

    def step(s, xs):
        qb, kb, vb = xs
        att = jnp.einsum('bhik,bhjk->bhij', qb, kb) * dmask
        o = jnp.einsum('bhij,bhjv->bhiv', att, vb) + jnp.einsum('bhik,bhkv->bhiv', qb, s) * q_dec
        s = s * c_dec + jnp.einsum('bhjk,bhjv->bhkv', kb * k_dec, vb)
        return s, o

    s_fin, o = lax.scan(step, s0.astype(F32), (_to_chunks(q), _to_chunks(k), _to_chunks(v)))
    return _from_chunks(o, B, L).astype(q.dtype), s_fin


def _mlstm_dir(q, k, v, i_pre, log_f, c0, n0, m0):
    B, L = q.shape[0], q.shape[1]
    pos = jnp.arange(CHUNK)
    causal = pos[:, None] >= pos[None, :]

    def step(carry, xs):
        C, nv, m = carry
        qb, kb, vb, ib, fb = xs
        b = jnp.cumsum(fb, axis=-1)
        dm = jnp.where(causal, b[..., :, None] - b[..., None, :] + ib[..., None, :], -jnp.inf)
        inter = b + m[..., None]
        m_t = jnp.maximum(inter, jnp.max(dm, axis=-1))
        w = jnp.exp(dm - m_t[..., None])
        sp = jnp.exp(inter - m_t)
        s = jnp.einsum('bhik,bhjk->bhij', qb, kb) * w
        num = jnp.einsum('bhij,bhjv->bhiv', s, vb) + jnp.einsum('bhik,bhkv->bhiv', qb, C) * sp[..., None]
        den = jnp.sum(s, axis=-1) + jnp.einsum('bhik,bhk->bhi', qb, nv) * sp
        h = num / jnp.maximum(jnp.abs(den), jnp.exp(-m_t))[..., None]
        b_last = b[..., -1]
        g = b_last[..., None] - b + ib
        m_new = jnp.maximum(b_last + m, jnp.max(g, axis=-1))
        wk = jnp.exp(g - m_new[..., None])
        sc = jnp.exp(b_last + m - m_new)
        C = C * sc[..., None, None] + jnp.einsum('bhjk,bhjv->bhkv', kb * wk[..., None], vb)
        nv = nv * sc[..., None] + jnp.einsum('bhjk,bhj->bhk', kb, wk)
        return (C, nv, m_new), h

    (c_fin, n_fin, m_fin), h = lax.scan(
        step, (c0.astype(F32), n0.astype(F32), m0.astype(F32)),
        (_to_chunks(q), _to_chunks(k), _to_chunks(v), _gate_chunks(i_pre), _gate_chunks(log_f)))
    return _from_chunks(h, B, L).astype(q.dtype), c_fin, n_fin, m_fin


def _mixer(h, grid, s_ret0, c0, n0, m0, w_in, b_gate, conv_qk, ret_theta, gn_ret, gn_mlstm, w_out):
    B, L, _ = h.shape
    R, M, HM = RET_DIM, MLSTM_DIM, MLSTM_HEADS
    proj = h @ w_in
    rq, rk, rv, rg = (proj[..., i * R:(i + 1) * R] for i in range(4))
    off = 4 * R
    mq, mk, mv, mo = (proj[..., off + i * M:off + (i + 1) * M] for i in range(4))
    gates = (proj[..., off + 4 * M:] + b_gate).astype(F32).reshape(B, L, 4, HM)

    rq = rq.reshape(B, L, RET_HEADS, HEAD_DIM)
    rk = rk.reshape(B, L, RET_HEADS, HEAD_DIM) * (HEAD_DIM ** -0.5)
    rv = rv.reshape(B, L, RET_HEADS, HEAD_DIM)
    if grid:
        rq, rk = _rope_2d(rq), _rope_2d(rk)
    log_gamma = jax.nn.log_sigmoid(ret_theta.astype(F32))
    o_f, s_f = _retention_dir(rq, rk, rv, log_gamma[0], s_ret0[:, 0])
    o_b, s_b = _retention_dir(rq[:, ::-1], rk[:, ::-1], rv[:, ::-1], log_gamma[1], s_ret0[:, 1])
    ret_out = (_ln(o_f + o_b[:, ::-1]) * gn_ret).reshape(B, L, R) * jax.nn.silu(rg)

    qk = jax.nn.silu(_seq_conv(jnp.concatenate([mq, mk], axis=-1), conv_qk, grid))
    mq = qk[..., :M].reshape(B, L, HM, HEAD_DIM)
    mk = qk[..., M:].reshape(B, L, HM, HEAD_DIM) * (HEAD_DIM ** -0.5)
    mv = mv.reshape(B, L, HM, HEAD_DIM)
    i_f, lf_f = gates[:, :, 0], jax.nn.log_sigmoid(gates[:, :, 1])
    i_b, lf_b = gates[:, :, 2], jax.nn.log_sigmoid(gates[:, :, 3])
    h_f, cf, nf, mf = _mlstm_dir(mq, mk, mv, i_f, lf_f, c0[:, 0], n0[:, 0], m0[:, 0])
    h_b, cb, nb, mb = _mlstm_dir(mq[:, ::-1], mk[:, ::-1], mv[:, ::-1], i_b[:, ::-1], lf_b[:, ::-1],
                                 c0[:, 1], n0[:, 1], m0[:, 1])
    mlstm_out = jax.nn.sigmoid(mo) * (_ln(h_f + h_b[:, ::-1]) * gn_mlstm).reshape(B, L, M)

    out = jnp.concatenate([ret_out, mlstm_out], axis=-1) @ w_out
    states = (jnp.stack([s_f, s_b], axis=1), jnp.stack([cf, cb], axis=1),
              jnp.stack([nf, nb], axis=1), jnp.stack([mf, mb], axis=1))
    return out, states


def _ffn(h, grid, w_up, conv_ff, w_down):
    ug = h @ w_up
    u, g = ug[..., :D_FF], ug[..., D_FF:]
    return (jax.nn.silu(_seq_conv(u, conv_ff, grid)) * g) @ w_down


def _layer(x, mod, grid, s_ret0, c0, n0, m0, lp):
    (w_in, b_gate, conv_qk, ret_theta, gn_ret, gn_mlstm, w_out,
     ln1_g, ln1_b, w_up, conv_ff, w_down, ln2_g, ln2_b) = lp
    sh1, sc1, g1, sh2, sc2, g2 = jnp.split(mod, 6, axis=-1)
    h = _ln(x) * (1.0 + sc1) + sh1
    mix, states = _mixer(h, grid, s_ret0, c0, n0, m0, w_in, b_gate, conv_qk, ret_theta,
                         gn_ret, gn_mlstm, w_out)
    x = _ln(ALPHA * x + g1 * mix) * ln1_g + ln1_b
    h = _ln(x) * (1.0 + sc2) + sh2
    x = _ln(ALPHA * x + g2 * _ffn(h, grid, w_up, conv_ff, w_down)) * ln2_g + ln2_b
    return x, states


def setup_inputs(seed: int = 0) -> dict:
    key = jax.random.key(seed)
    ks = jax.random.split(key, 32)

    def nrm(k, shape, s):
        return s * jax.random.normal(k, shape, F32)

    D, HR, HM, HD = D_MODEL, RET_HEADS, MLSTM_HEADS, HEAD_DIM
    base_gamma = 1.0 - 2.0 ** (-5.0 - np.arange(HR, dtype=np.float32))
    theta0 = jnp.asarray(np.log(base_gamma / (1.0 - base_gamma)), F32)
    i_bias = nrm(ks[20], (DEPTH, 2, HM), 0.1)
    f_bias = jnp.asarray(np.linspace(3.0, 6.0, HM), F32) + nrm(ks[21], (DEPTH, 2, HM), 0.1)
    b_gate = jnp.stack([i_bias[:, 0], f_bias[:, 0], i_bias[:, 1], f_bias[:, 1]], axis=1).reshape(DEPTH, 4 * HM)
    return {
        'x_prompt': nrm(ks[0], (BATCH, SEQ, D), 1.0),
        'x_sample': nrm(ks[1], (DEC_BATCH, DEC_SEQ, D), 1.0),
        'state_ret': nrm(ks[2], (DEC_BATCH, DEPTH, 2, HR, HD, HD), 0.3),
        'state_mlstm_C': nrm(ks[3], (DEC_BATCH, DEPTH, 2, HM, HD, HD), 0.1),
        'state_mlstm_n': nrm(ks[4], (DEC_BATCH, DEPTH, 2, HM, HD), 0.1),
        'state_mlstm_m': 1.0 + nrm(ks[5], (DEC_BATCH, DEPTH, 2, HM), 0.5),
        'c': nrm(ks[6], (DEC_BATCH, D), 1.0),
        'c_ctx': nrm(ks[7], (D,), 1.0),
        'w_mod': nrm(ks[8], (DEPTH, D, 6 * D), 0.5 * D ** -0.5),
        'b_mod': nrm(ks[9], (DEPTH, 6 * D), 0.02),
        'w_in': nrm(ks[10], (DEPTH, D, N_IN), D ** -0.5),
        'b_gate': b_gate,
        'conv_qk': nrm(ks[11], (DEPTH, CONV_W, 2 * MLSTM_DIM), CONV_W ** -0.5),
        'ret_theta': theta0[None, None, :] + nrm(ks[12], (DEPTH, 2, HR), 0.05),
        'gn_ret': 1.0 + nrm(ks[13], (DEPTH, HR, HD), 0.02),
        'gn_mlstm': 1.0 + nrm(ks[14], (DEPTH, HM, HD), 0.02),
        'w_out': nrm(ks[15], (DEPTH, MIX_DIM, D), BETA * MIX_DIM ** -0.5),
        'ln1_g': 1.0 + nrm(ks[16], (DEPTH, D), 0.02),
        'ln1_b': nrm(ks[17], (DEPTH, D), 0.02),
        'w_up': nrm(ks[18], (DEPTH, D, 2 * D_FF), D ** -0.5),
        'conv_ff': nrm(ks[19], (DEPTH, CONV_W, D_FF), CONV_W ** -0.5),
        'w_down': nrm(ks[22], (DEPTH, D_FF, D), BETA * D_FF ** -0.5),
        'ln2_g': 1.0 + nrm(ks[23], (DEPTH, D), 0.02),
        'ln2_b': nrm(ks[24], (DEPTH, D), 0.02),
    }


def reference(x_prompt, x_sample, state_ret, state_mlstm_C, state_mlstm_n, state_mlstm_m, c, c_ctx,
              w_mod, b_mod, w_in, b_gate, conv_qk, ret_theta, gn_ret, gn_mlstm, w_out,
              ln1_g, ln1_b, w_up, conv_ff, w_down, ln2_g, ln2_b):
    Bp = x_prompt.shape[0]
    zero_ret = jnp.zeros((Bp, 2, RET_HEADS, HEAD_DIM, HEAD_DIM), F32)
    zero_C = jnp.zeros((Bp, 2, MLSTM_HEADS, HEAD_DIM, HEAD_DIM), F32)
    zero_n = jnp.zeros((Bp, 2, MLSTM_HEADS, HEAD_DIM), F32)
    zero_m = jnp.zeros((Bp, 2, MLSTM_HEADS), F32)
    y_prompt, y_sample = x_prompt, x_sample
    ret_list, C_list, n_list, m_list = [], [], [], []
    for l in range(DEPTH):
        lp = (w_in[l], b_gate[l], conv_qk[l], ret_theta[l], gn_ret[l], gn_mlstm[l], w_out[l],
              ln1_g[l], ln1_b[l], w_up[l], conv_ff[l], w_down[l], ln2_g[l], ln2_b[l])
        mod_ctx = (jax.nn.silu(c_ctx) @ w_mod[l] + b_mod[l])[None, None, :]
        mod_lat = (jax.nn.silu(c) @ w_mod[l] + b_mod[l])[:, None, :]
        y_prompt, st = _layer(y_prompt, mod_ctx, False, zero_ret, zero_C, zero_n, zero_m, lp)
        ret_list.append(st[0])
        C_list.append(st[1])
        n_list.append(st[2])
        m_list.append(st[3])
        y_sample, _ = _layer(y_sample, mod_lat, True, state_ret[:, l], state_mlstm_C[:, l],
                             state_mlstm_n[:, l], state_mlstm_m[:, l], lp)
    new_state_ret = jnp.stack(ret_list, axis=1)
    new_state_mlstm_C = jnp.stack(C_list, axis=1)
    new_state_mlstm_n = jnp.stack(n_list, axis=1)
    new_state_mlstm_m = jnp.stack(m_list, axis=1)
    return (y_prompt, y_sample, new_state_ret, new_state_mlstm_C, new_state_mlstm_n, new_state_mlstm_m)
```

```python
import contextlib
import numpy as np
import concourse.bass as bass
import concourse.mybir as mybir
from concourse.bass_utils import run_bass_kernel_spmd

F32 = mybir.dt.float32
BF16 = mybir.dt.bfloat16
ALU = mybir.AluOpType
AF = mybir.ActivationFunctionType
AX = mybir.AxisListType

D = 2048
KC = 16
HD = 256
NH = 4
DFF = 5504
NFF = 43
NIN = 8208
ALPHA = 2.0 ** 0.25
BIG = 1.0e9
SAME_ENGINE_SYNC = True
DEBUG_IDS = set()
GRAN = 256


class Rg:
    __slots__ = ("space", "lo", "hi")

    def __init__(self, space, lo, hi):
        self.space, self.lo, self.hi = space, lo, hi


class Tl:
    def __init__(self, ap, space, lo, hi):
        self.ap, self.space, self.lo, self.hi = ap, space, lo, hi

    def __getitem__(self, k):
        return self.ap[k]

    @property
    def r(self):
        return Rg(self.space, self.lo, self.hi)

    def part(self, i, n, cnt=1):
        sz = (self.hi - self.lo) // n
        return Rg(self.space, self.lo + i * sz, self.lo + (i + cnt) * sz)


def _rg(x):
    return x.r if isinstance(x, Tl) else x


class _Rec:
    def __getattr__(self, name):
        def f(*a, **k):
            self.call = (name, a, k)
            return self
        return f


class Prog:
    ENGS = ("pe", "act", "dve", "pool", "sp")

    def __init__(self, nc):
        self.nc = nc
        self.ops = []
        self.st = {}

    def _cells(self, rg):
        if rg.space == "sb":
            return [("sb", g) for g in range(rg.lo // GRAN, (rg.hi + GRAN - 1) // GRAN)]
        return [(rg.space, g) for g in range(rg.lo, rg.hi)]

    def op(self, eng, fn, r=(), w=(), dma=None):
        oid = len(self.ops)
        deps = set()
        rc = [c for x in r for c in self._cells(_rg(x))]
        wc = [c for x in w for c in self._cells(_rg(x))]
        st = self.st
        for c in rc:
            s = st.get(c)
            if s is not None:
                if s[0] is not None:
                    deps.add(s[0])
                if c[0] == "ps":
                    for r_ in s[1]:
                        if self.ops[r_]["eng"] != eng:
                            deps.add(r_)
        for c in wc:
            s = st.get(c)
            if s is not None:
                if s[0] is not None:
                    deps.add(s[0])
                deps.update(s[1])
        for c in rc:
            s = st.get(c)
            if s is None:
                st[c] = [None, [oid]]
            else:
                s[1].append(oid)
        for c in wc:
            st[c] = [oid, []]
        deps.discard(oid)
        rec = _Rec()
        fn(rec)
        self.ops.append(dict(eng=eng, call=rec.call, deps=deps, dma=dma, sig=False, sigidx=None))
        return oid

    def emit(self):
        nc = self.nc
        ops = self.ops

        def dom(o):
            return ("dma", o["dma"]) if o["dma"] is not None else ("eng", o["eng"])

        def skip(p, engname):
            return p["dma"] is None and p["eng"] == engname and (engname == "pe" or not SAME_ENGINE_SYNC)

        for o in ops:
            for d in o["deps"]:
                p = ops[d]
                if skip(p, o["eng"]):
                    continue
                p["sig"] = True
            if o["dma"] is not None:
                o["sig"] = True
        counters = {}
        for o in ops:
            if o["sig"]:
                dm = dom(o)
                counters[dm] = counters.get(dm, 0) + 1
                o["sigidx"] = counters[dm]
        with contextlib.ExitStack() as es:
            sems = {}
            for i, dm in enumerate(counters.keys()):
                sems[dm] = es.enter_context(nc.semaphore("s%d" % i))
            block = es.enter_context(nc.Block())
            streams = {e: [] for e in self.ENGS}
            for o in ops:
                streams[o["eng"]].append(o)

            def run(engname, eng):
                waited = {}
                for o in streams[engname]:
                    need = {}
                    for d in o["deps"]:
                        p = ops[d]
                        if not p["sig"] or skip(p, engname):
                            continue
                        dm = dom(p)
                        if p["sigidx"] > need.get(dm, 0):
                            need[dm] = p["sigidx"]
                    for dm, v in need.items():
                        if waited.get(dm, 0) >= v:
                            continue
                        waited[dm] = v
                        eng.wait_ge(sems[dm], v * (16 if dm[0] == "dma" else 1))
                    nm, a_, k_ = o["call"]
                    ins = getattr(eng, nm)(*a_, **k_)
                    if DEBUG_IDS:
                        try:
                            iname = str(ins.ins.name) if hasattr(ins, "ins") else str(getattr(ins, "name", ""))
                        except Exception:
                            iname = "?"
                        if iname in DEBUG_IDS:
                            print("DEBUGID", iname, engname, nm, {kk: (getattr(vv, "ap", None), getattr(vv, "dtype", None)) if hasattr(vv, "ap") else vv for kk, vv in k_.items()})
                    if o["sig"]:
                        ins.then_inc(sems[dom(o)], 16 if o["dma"] is not None else 1)
                if engname == "sp":
                    for dm, cnt in counters.items():
                        if dm[0] == "dma":
                            eng.wait_ge(sems[dm], cnt * 16)

            block.tensor(lambda e: run("pe", e))
            block.scalar(lambda e: run("act", e))
            block.vector(lambda e: run("dve", e))
            block.gpsimd(lambda e: run("pool", e))
            block.sync(lambda e: run("sp", e))


def make_consts():
    p = np.arange(128)
    j = p[:, None].astype(np.float64)
    i = p[None, :].astype(np.float64)
    D1 = np.maximum(i - j, 0)
    U1 = (i >= j).astype(np.float64)
    D2 = np.maximum(j - i, 0)
    U2 = (j >= i).astype(np.float64)
    BIGF = BIG * (1 - U1) + np.log(16.0)
    BIGB = BIG * (1 - U2) + np.log(16.0)
    NEGF = -BIG * (1 - U2)
    NEGB = -BIG * (1 - U1)
    vec = np.stack([p + 1.0, 128.0 - p, 127.0 - p, p * 1.0], axis=1)
    inv = 10000.0 ** (-np.arange(64, dtype=np.float32) / 64.0)
    f = p % 64
    sgn = np.where(p < 64, 1.0, -1.0)
    rows = np.arange(32, dtype=np.float32)
    cols = np.arange(64, dtype=np.float32)
    angR = (rows[None, :] * inv[f][:, None]).astype(np.float32)
    angC = (cols[None, :] * inv[f][:, None]).astype(np.float32)
    cosR, sinR = np.cos(angR), np.sin(angR) * sgn[:, None]
    cosC, sinC = np.cos(angC), np.sin(angC) * sgn[:, None]
    rope = np.concatenate([cosR, sinR, cosC, sinC], axis=1)
    ropeK = rope / 16.0
    cst = np.concatenate([D1, U1, D2, U2, BIGF, BIGB, NEGF, NEGB, vec, rope, ropeK, np.eye(128), np.ones((128, 128))], axis=1)
    return np.ascontiguousarray(cst.astype(np.float32))


C_D1, C_U1, C_D2, C_U2, C_BIGF, C_BIGB, C_NEGF, C_NEGB = [k * 128 for k in range(8)]
C_VEC = 1024
C_ROPE = 1028
C_ROPEK = 1028 + 192
C_ID = 1028 + 384
C_ONES = C_ID + 128
NCST = C_ONES + 128

O_CONST = 0
SZ_CONST = 13312
O_HT = O_CONST + SZ_CONST
SZ_HT = 65536
O_W = O_HT + SZ_HT
SZ_W = 16384
O_PROJ = O_W + SZ_W
SZ_PROJ = 43008
O_MIXO = O_PROJ + SZ_PROJ
SZ_MIXO = 32768
O_MIX = O_MIXO + SZ_MIXO
SZ_MIX = 41728
ARENA = O_MIX + SZ_MIX


def build_program():
    nc = bass.Bass("TRN2", target_bir_lowering=False)

    def din(name, shape):
        return nc.dram_tensor(name, list(shape), F32, kind="ExternalInput").ap()

    def dout(name, shape):
        return nc.dram_tensor(name, list(shape), F32, kind="ExternalOutput").ap()

    xp = din("xp", [1024, D]); xs = din("xs", [2048, D]); xo = din("xo", [512, D])
    flags = din("flags", [1, 4]); cT = din("cT", [128, 32])
    st_ret = din("st_ret", [2, 4, 256, 256]); st_C = din("st_C", [2, 4, 256, 256])
    st_n = din("st_n", [2, 4, 128, 2]); st_m = din("st_m", [1, 8])
    w_mod = din("w_mod", [D, 6 * D]); b_mod = din("b_mod", [1, 6 * D]); w_in = din("w_in", [D, NIN])
    b_gate = din("b_gate", [1, 16]); conv_qkT = din("conv_qkT", [128, 48]); theta = din("theta", [1, 8])
    gn_ret = din("gn_ret", [1, 1024]); gn_ml = din("gn_ml", [1, 1024]); w_out = din("w_out", [D, D])
    ln1_g = din("ln1_g", [1, D]); ln1_b = din("ln1_b", [1, D]); w_up = din("w_up", [D, 2 * DFF])
    conv_ffT = din("conv_ffT", [128, NFF * 3]); w_down = din("w_down", [DFF, D])
    ln2_g = din("ln2_g", [1, D]); ln2_b = din("ln2_b", [1, D]); cst_d = din("cst", [128, NCST])
    yp = dout("yp", [1024, D]); ys = dout("ys", [512, D])
    o_ret = dout("o_ret", [4, 2, 4, 256, 256]); o_C = dout("o_C", [4, 2, 4, 256, 256])
    o_n = dout("o_n", [4, 2, 4, 128, 2]); o_m = dout("o_m", [4, 8])
    mod_d = nc.dram_tensor("mod_d", [2, 6 * D], F32).ap()
    x1_d = nc.dram_tensor("x1_d", [1536, D], F32).ap()
    y2_d = nc.dram_tensor("y2_d", [1536, D], F32).ap()

    P = Prog(nc)
    es = contextlib.ExitStack()
    A = es.enter_context(nc.sbuf_tensor("arena", [128, ARENA // 2], BF16))
    psf = [es.enter_context(nc.psum_tensor("psf%d" % i, [128, 512], F32)) for i in range(6)]
    psb = [es.enter_context(nc.psum_tensor("psb%d" % i, [128, 1024], BF16)) for i in range(2)]
    PSF = [Tl(psf[i][:], "ps", i, i + 1) for i in range(6)]
    PSB = [Tl(psb[i][:], "ps", 6 + i, 7 + i) for i in range(2)]
    rot = {"f": 0, "b": 0}

    held = set()

    def psn():
        for _ in range(6):
            rot["f"] = (rot["f"] + 1) % 6
            if rot["f"] not in held:
                return PSF[rot["f"]]
        raise AssertionError("no free psum bank")

    def psh():
        ps = psn()
        held.add(ps.lo)
        return ps

    def prel(ps):
        held.discard(ps.lo)

    def navail():
        return 6 - len(held)

    heldb = set()

    def psbn():
        for _ in range(2):
            rot["b"] = (rot["b"] + 1) % 2
            if rot["b"] not in heldb:
                return PSB[rot["b"]]
        raise AssertionError("no free bf16 psum bank")

    def run_pipe(facts, width=2):
        free = list(range(width)); active = []; i = 0
        while i < len(facts) or active:
            while free and i < len(facts):
                k = free.pop(0)
                active.append((facts[i](k), k))
                i += 1
            nxt = []
            for g, k in active:
                try:
                    next(g)
                    nxt.append((g, k))
                except StopIteration:
                    free.append(k)
            active = nxt

    def sbt(off, shape, dt, parts=128):
        esz = 4 if dt == F32 else 2
        n = 1
        for s in shape:
            n *= s
        nb = n * esz
        assert off % 4 == 0
        ap = A[0:parts, off // 2:(off + nb) // 2]
        if dt != BF16:
            ap = ap.bitcast(dt)
        if len(shape) == 2:
            ap = ap.rearrange("p (a b) -> p a b", a=shape[0])
        elif len(shape) == 3:
            ap = ap.rearrange("p (a b c) -> p a b c", a=shape[0], b=shape[1])
        return Tl(ap, "sb", off, off + nb)

    class Bump:
        def __init__(self, off, size):
            self.off, self.end = off, off + size

        def __call__(self, shape, dt, parts=128, align=GRAN):
            esz = 4 if dt == F32 else 2
            n = 1
            for s in shape:
                n *= s
            self.off = (self.off + align - 1) // align * align
            t = sbt(self.off, shape, dt, parts)
            self.off += n * esz
            assert self.off <= self.end, ("arena overflow", self.off, self.end)
            return t

    def DR(name, lo=0, hi=1):
        return Rg("dr_" + name, lo, hi)

    cb = Bump(O_CONST, SZ_CONST)
    CST = cb([NCST], F32)
    IDB = cb([128], BF16)
    LG = cb([8], F32)
    FLG = cb([4], F32)
    BG = cb([16], F32)
    GN = cb([256], F32)
    DEC = cb([8], F32)
    RM = cb([128], F32)
    RMT = cb([128], F32)
    SILC = cb([16, 2], BF16)
    WGT = cb([16, 16], BF16)
    CQK = cb([48], F32)
    CFF = cb([NFF * 3], F32)
    ST_C = cb([4, 6], F32); MV_C = cb([2], F32, align=4); RSTD_C = cb([1], F32, align=4); NMR_C = cb([1], F32, align=4)
    IDF = CST.ap[:, C_ID:C_ID + 128]
    ONESF = CST.ap[:, C_ONES:C_ONES + 128]

    def cstm(c0):
        return CST.ap[:, c0:c0 + 128]

    P.op("sp", lambda e: e.dma_start(out=CST[:], in_=cst_d[:, :]), w=[CST], dma="cst")
    P.op("sp", lambda e: e.dma_start(out=LG[:], in_=theta[0:1, :].broadcast_to([128, 8])), w=[LG], dma="lg")
    P.op("sp", lambda e: e.dma_start(out=FLG[:], in_=flags[0:1, :].broadcast_to([128, 4])), w=[FLG], dma="flg")
    P.op("sp", lambda e: e.dma_start(out=BG[:], in_=b_gate[0:1, :].broadcast_to([128, 16])), w=[BG], dma="bg")
    P.op("sp", lambda e: e.dma_start(out=CQK[:], in_=conv_qkT[:, :]), w=[CQK], dma="cqk")
    P.op("sp", lambda e: e.dma_start(out=CFF[:], in_=conv_ffT[:, :]), w=[CFF], dma="cff")
    P.op("pool", lambda e: e.dma_start(out=WGT[:], in_=w_in[:, 8192:8208].rearrange("(kc p) n -> p kc n", p=128)), w=[WGT], dma="wgt")
    P.op("dve", lambda e: e.tensor_copy(out=IDB[:], in_=IDF), r=[CST], w=[IDB])
    P.op("act", lambda e: e.activation(out=LG[:], in_=LG[:], func=AF.Exp, scale=-1.0), r=[LG], w=[LG])
    P.op("act", lambda e: e.activation(out=LG[:], in_=LG[:], func=AF.Ln, bias=1.0), r=[LG], w=[LG])
    P.op("dve", lambda e: e.tensor_scalar(out=LG[:], in0=LG[:], scalar1=-1.0, scalar2=None, op0=ALU.mult), r=[LG], w=[LG])

    mod_state = {"init": False}

    def stage_mod(blks, stage_off, small_off):
        ct = sbt(small_off, [32], F32)
        brow = [sbt(small_off + 256 + k * 1024, [256], F32, parts=2) for k in range(2)]
        mrow = [sbt(small_off + 256 + 2048 + k * 1024, [256], F32, parts=2) for k in range(2)]
        if not mod_state["init"]:
            mod_state["init"] = True
            P.op("sp", lambda e: e.dma_start(out=ct[:], in_=cT[:, :]), w=[ct], dma="ct")
            P.op("act", lambda e: e.activation(out=SILC[:].rearrange("p a b -> p (a b)"), in_=ct[:], func=AF.Silu), r=[ct], w=[SILC])
        wslots = [sbt(stage_off + k * 8192, [16, 256], BF16) for k in range(2)]
        for i, blk in enumerate(blks):
            ws = wslots[i % 2]; br = brow[i % 2]; mr = mrow[i % 2]
            c0 = blk * 256
            P.op("pool", lambda e, ws=ws, c0=c0: e.dma_start(out=ws[:], in_=w_mod[:, c0:c0 + 256].rearrange("(kc p) n -> p kc n", p=128)), w=[ws], dma="wm%d_%d" % (stage_off, i % 2))
            P.op("sp", lambda e, br=br, c0=c0: e.dma_start(out=br[:], in_=b_mod[0:1, c0:c0 + 256].broadcast_to([2, 256])), w=[br], dma="brow%d_%d" % (small_off, i % 2))
            ps = psn()
            for kc in range(KC):
                P.op("pe", lambda e, ps=ps, ws=ws, kc=kc: e.matmul(ps[0:2, 0:256], lhsT=SILC[:, kc, :], rhs=ws[:, kc, :], start=(kc == 0), stop=(kc == KC - 1)), r=[SILC, ws], w=[ps])
            P.op("dve", lambda e, ps=ps, mr=mr, br=br: e.tensor_tensor(out=mr[:], in0=ps[0:2, 0:256], in1=br[:], op=ALU.add), r=[ps, br], w=[mr])
            P.op("sp", lambda e, mr=mr, c0=c0: e.dma_start(out=mod_d[:, c0:c0 + 256], in_=mr[:]), r=[mr], w=[DR("mod", blk // 8, blk // 8 + 1)], dma="mrow%d_%d" % (small_off, i % 2))

    def load_modrow(dst, q, g, plus1=False):
        P.op("sp", lambda e: e.dma_start(out=dst[:], in_=mod_d[g:g + 1, q * D:(q + 1) * D].broadcast_to([128, D])), r=[DR("mod", q, q + 1)], w=[dst], dma="mr_%d" % (dst.lo))
        if plus1:
            P.op("dve", lambda e: e.tensor_scalar(out=dst[:], in0=dst[:], scalar1=1.0, scalar2=None, op0=ALU.add), r=[dst], w=[dst])

    def load_row(dst, src):
        P.op("sp", lambda e: e.dma_start(out=dst[:], in_=src[0:1, :].broadcast_to([128, src.shape[1]])), w=[dst], dma="lr_%d" % (dst.lo))

    def ln_stats(x_ap, xr, st, mv, rstd, nmr, n):
        nch = max(1, n // 512)
        w_ = n // nch
        for c in range(nch):
            P.op("dve", lambda e, c=c: e.bn_stats(out=st[:, c, :], in_=x_ap[:, c * w_:(c + 1) * w_]), r=[xr], w=[st])
        P.op("dve", lambda e: e.bn_aggr(out=mv[:], in_=st[:, 0:nch, :].rearrange("p a b -> p (a b)")), r=[st], w=[mv])
        P.op("act", lambda e: e.activation(out=rstd[:], in_=mv[:, 1:2], func=AF.Ln, bias=1e-6), r=[mv], w=[rstd])
        P.op("act", lambda e: e.activation(out=rstd[:], in_=rstd[:], func=AF.Exp, scale=-0.5), r=[rstd], w=[rstd])
        if nmr is not None:
            P.op("dve", lambda e: e.scalar_tensor_tensor(out=nmr[:], in0=mv[:, 0:1], scalar=-1.0, in1=rstd[:], op0=ALU.mult, op1=ALU.mult), r=[mv, rstd], w=[nmr])

    def transpose_tile(src, src_r, dstT, dst_r, tcol):
        for _ in transpose_tile_g(src, src_r, dstT, dst_r, tcol):
            pass

    def transpose_tile_g(src, src_r, dstT, dst_r_, tcol):
        for half in range(2):
            while len(heldb) >= 2:
                yield
            pb = psbn()
            heldb.add(pb.lo - 6)
            dst_r = dstT.part(half, 2)
            for k in range(8):
                kc = half * 8 + k
                P.op("pe", lambda e, pb=pb, k=k, kc=kc: e.transpose(out=pb[:, k * 128:(k + 1) * 128], in_=src[:, kc * 128:(kc + 1) * 128], identity=IDB[:]), r=[src_r, IDB], w=[pb])
            eng = "act" if half == 0 else "dve"
            if eng == "act":
                P.op("act", lambda e, pb=pb, half=half: e.activation(out=dstT[:, half * 8:half * 8 + 8, tcol:tcol + 128], in_=pb[:].rearrange("p (a b) -> p a b", a=8), func=AF.Copy), r=[pb], w=[dst_r])
            else:
                P.op("dve", lambda e, pb=pb, half=half: e.tensor_copy(out=dstT[:, half * 8:half * 8 + 8, tcol:tcol + 128], in_=pb[:].rearrange("p (a b) -> p a b", a=8)), r=[pb], w=[dst_r])
            heldb.discard(pb.lo - 6)
            yield

    def process_group(gi):
        sample = gi == 1
        T = 2048 if sample else 1024
        nseq = 1 if sample else 4
        n = 16 if sample else 2
        NT = T // 128
        Town = 512 if sample else 1024
        NTO = Town // 128
        x_all = xs if sample else xp
        x_own = xo if sample else xp
        y_out = ys if sample else yp
        x1off = 1024 if sample else 0
        mg = 1 if sample else 0
        Lc = 64 if sample else 256
        HT = sbt(O_HT, [16, T], BF16)

        mb = Bump(O_MIX, SZ_MIX)
        r_sc = mb([D], F32); r_sh = mb([D], F32)
        sm_ = [mb([64], F32) for _ in range(2)]
        load_modrow(r_sh, 0, mg)
        load_modrow(r_sc, 1, mg, plus1=True)
        pbm = Bump(O_PROJ, SZ_PROJ)
        xts = [pbm([D], F32) for _ in range(2)]
        hts = [pbm([D], BF16) for _ in range(2)]

        def small4(tl):
            mk = lambda off, shape: Tl(sbt(tl.lo + off * 4, shape, F32).ap, "sb", tl.lo, tl.hi)
            return mk(0, [4, 6]), mk(24, [2]), mk(26, [1]), mk(27, [1])

        def a1_tile(t):
            def g(k):
                xt = xts[k]; ht = hts[k]
                st, mv, rstd, nmr = small4(sm_[k])
                P.op("sp", lambda e: e.dma_start(out=xt[:], in_=x_all[t * 128:(t + 1) * 128, :]), w=[xt], dma="xt%d" % k)
                yield
                ln_stats(xt.ap, xt, st, mv, rstd, None, D)
                yield
                P.op("dve", lambda e: e.scalar_tensor_tensor(out=xt[:], in0=xt[:], scalar=mv[:, 0:1], in1=r_sc[:], op0=ALU.subtract, op1=ALU.mult), r=[xt, mv, r_sc], w=[xt])
                yield
                P.op("dve", lambda e: e.scalar_tensor_tensor(out=ht[:], in0=xt[:], scalar=rstd[:], in1=r_sh[:], op0=ALU.mult, op1=ALU.add), r=[xt, rstd, r_sh], w=[ht])
                yield
                yield from transpose_tile_g(ht.ap, ht, HT, HT, t * 128)
            return g
        run_pipe([a1_tile(t) for t in range(NT)], 2)

        pbm = Bump(O_PROJ, SZ_PROJ)
        QT = pbm([2, T], BF16)
        KT = pbm([2, T], BF16)
        VG = pbm([NT, 520], BF16)
        KTM = pbm([NT, 256], BF16)
        mb = Bump(O_MIX, SZ_MIX)
        NSL = 1 if sample else 2
        MST = [[mb([2, 257], F32) for d in range(2)] for sl in range(NSL)]
        KX = [[mb([256], BF16) for d in range(2)] for sl in range(NSL)]
        SALL = []
        for sl in range(NSL):
            per_d = []
            for d in range(2):
                if sample and d == 0:
                    first = mb([2, 257], BF16)
                    rest = sbt(O_MIXO + 16384, [n - 1, 2, 257], BF16)
                    per_d.append([first] + [Tl(rest[:, c], "sb", rest.lo + c * 1028, rest.lo + (c + 1) * 1028) for c in range(n - 1)])
                else:
                    arr = mb([n, 2, 257], BF16)
                    per_d.append([Tl(arr[:, c], "sb", arr.lo + c * 1028, arr.lo + (c + 1) * 1028) for c in range(n)])
            SALL.append(per_d)
        Dm = mb([4, 128], F32)
        Am = mb([4, 128], F32)
        tmpc = [sbt(Dm.lo, [512], F32), sbt(Am.lo, [512], F32)]
        ATT = [sbt(Am.lo + k * 512, [2, 128], BF16) for k in range(2)]
        WTS = [sbt(Am.lo + 1024 + k * 512, [128], F32) for k in range(2)]
        DMS = [sbt(Dm.lo + k * 1024, [2, 128], F32) for k in range(2)]
        TOTT = [mb([2, 257], F32) for k in range(2)]
        TOT = [[Tl(TOTT[k][:, d, :], "sb", TOTT[k].lo + d * 1028, TOTT[k].lo + (d + 1) * 1028) for d in range(2)] for k in range(2)]
        XN = [mb([256], F32) for k in range(2)]
        SMALL = [mb([64], F32) for k in range(2)]
        GTS = mb([NT, 16], F32)
        LFN = mb([NT, 16], F32)
        GA = mb([NT, 8], F32); GNB = mb([NT, 8], F32); GMU = mb([NT, 8], F32); GSP = mb([NT, 8], F32)
        GWK = mb([NT, 8], F32); GSC = mb([NT, 8], F32); GEM = mb([NT, 8], F32); GBL = mb([NT, 8], F32)
        mprev = mb([8], F32); mnew = mb([8], F32, align=4); v8 = [mb([8], F32, align=4) for _ in range(4)]
        MIXO = sbt(O_MIXO, [NTO, D], BF16)
        wsl = [sbt(O_W + k * 8192, [16, 256], BF16) for k in range(2)]
        wcount = [0]

        def load_w(c0):
            ws = wsl[wcount[0] % 2]
            k = wcount[0] % 2
            wcount[0] += 1
            P.op("pool", lambda e: e.dma_start(out=ws[:], in_=w_in[:, c0:c0 + 256].rearrange("(kc p) n -> p kc n", p=128)), w=[ws], dma="wsl%d" % k)
            return ws

        P.op("dve", lambda e: e.memset(VG[:, :, 256:257], 1.0), w=[VG])

        def proj_fm(ws, dst, kind, chan0):
            for dc in range(2):
                for tb in range(T // 512):
                    ps = psn()
                    for kc in range(KC):
                        P.op("pe", lambda e, ps=ps, kc=kc, dc=dc, tb=tb: e.matmul(ps[:], lhsT=ws[:, kc, dc * 128:(dc + 1) * 128], rhs=HT[:, kc, tb * 512:(tb + 1) * 512], start=(kc == 0), stop=(kc == KC - 1)), r=[ws, HT], w=[ps])
                    dsl = dst[:, dc, tb * 512:(tb + 1) * 512]
                    if kind in ("rq", "rk"):
                        if not sample:
                            sc_ = 1.0 if kind == "rq" else 1.0 / 16.0
                            P.op("act", lambda e, ps=ps, dsl=dsl, sc_=sc_: e.activation(out=dsl, in_=ps[:], func=AF.Identity, scale=sc_), r=[ps], w=[dst])
                        else:
                            base = C_ROPE if kind == "rq" else C_ROPEK
                            if dc == 0:
                                cosb = CST.ap[:, base + 8 * tb: base + 8 * tb + 8].unsqueeze(2).to_broadcast([128, 8, 64])
                                sinb = CST.ap[:, base + 32 + 8 * tb: base + 32 + 8 * tb + 8].unsqueeze(2).to_broadcast([128, 8, 64])
                            else:
                                cosb = CST.ap[:, base + 64: base + 128].unsqueeze(1).to_broadcast([128, 8, 64])
                                sinb = CST.ap[:, base + 128: base + 192].unsqueeze(1).to_broadcast([128, 8, 64])
                            t1 = tmpc[0]; t2 = tmpc[1]
                            v3 = lambda ap: ap.rearrange("p (a b) -> p a b", a=8)
                            P.op("dve", lambda e, ps=ps, cosb=cosb: e.tensor_tensor(out=v3(t1[:]), in0=v3(ps[:]), in1=cosb, op=ALU.mult), r=[ps, CST], w=[t1])
                            P.op("dve", lambda e, ps=ps, sinb=sinb: e.tensor_tensor(out=v3(t2[:])[0:64], in0=v3(ps[:])[64:128], in1=sinb[64:128], op=ALU.mult), r=[ps, CST], w=[t2])
                            P.op("dve", lambda e, ps=ps, sinb=sinb: e.tensor_tensor(out=v3(t2[:])[64:128], in0=v3(ps[:])[0:64], in1=sinb[0:64], op=ALU.mult), r=[ps, CST], w=[t2])
                            P.op("dve", lambda e, dsl=dsl: e.tensor_tensor(out=dsl, in0=t1[:], in1=t2[:], op=ALU.add), r=[t1, t2], w=[dst])
                    else:
                        ch = chan0 // 128 + dc
                        w0 = CQK.ap[:, ch * 3 + 0: ch * 3 + 1]; w1 = CQK.ap[:, ch * 3 + 1: ch * 3 + 2]; w2 = CQK.ap[:, ch * 3 + 2: ch * 3 + 3]
                        z = tmpc[0]
                        nb_ = 512 // Lc
                        v3 = lambda ap: ap.rearrange("p (a b) -> p a b", a=nb_)
                        P.op("dve", lambda e, ps=ps, w1=w1: e.tensor_scalar(out=z[:], in0=ps[:], scalar1=w1, scalar2=None, op0=ALU.mult), r=[ps, CQK], w=[z])
                        P.op("dve", lambda e, ps=ps, w0=w0: e.scalar_tensor_tensor(out=v3(z[:])[:, :, 1:Lc], in0=v3(ps[:])[:, :, 0:Lc - 1], scalar=w0, in1=v3(z[:])[:, :, 1:Lc], op0=ALU.mult, op1=ALU.add), r=[ps, CQK, z], w=[z])
                        P.op("dve", lambda e, ps=ps, w2=w2: e.scalar_tensor_tensor(out=v3(z[:])[:, :, 0:Lc - 1], in0=v3(ps[:])[:, :, 1:Lc], scalar=w2, in1=v3(z[:])[:, :, 0:Lc - 1], op0=ALU.mult, op1=ALU.add), r=[ps, CQK, z], w=[z])
                        P.op("act", lambda e, dsl=dsl: e.activation(out=dsl, in_=z[:], func=AF.Silu), r=[z], w=[dst])

        def proj_tm(wv, wg, gfunc):
            for t in range(NT):
                ps = psn()
                for j, ws in enumerate((wv, wg)):
                    for kc in range(KC):
                        P.op("pe", lambda e, ps=ps, kc=kc, j=j, ws=ws, t=t: e.matmul(ps[:, j * 256:(j + 1) * 256], lhsT=HT[:, kc, t * 128:(t + 1) * 128], rhs=ws[:, kc, :], start=(kc == 0), stop=(kc == KC - 1)), r=[ws, HT], w=[ps])
                P.op("dve", lambda e, ps=ps, t=t: e.tensor_copy(out=VG[:, t, 0:256], in_=ps[:, 0:256]), r=[ps], w=[VG.part(t, NT)])
                P.op("act", lambda e, ps=ps, t=t: e.activation(out=VG[:, t, 264:520], in_=ps[:, 256:512], func=gfunc), r=[ps], w=[VG.part(t, NT)])

        def make_ktm(kscale=1.0):
            for t in range(NT):
                pb = psbn()
                for dc in range(2):
                    P.op("pe", lambda e, pb=pb, dc=dc, t=t: e.transpose(out=pb[:, dc * 128:(dc + 1) * 128], in_=KT[:, dc, t * 128:(t + 1) * 128], identity=IDB[:]), r=[KT, IDB], w=[pb])
                P.op("dve", lambda e, pb=pb, t=t: e.tensor_scalar(out=KTM[:, t, :], in0=pb[:, 0:256], scalar1=kscale, scalar2=None, op0=ALU.mult), r=[pb], w=[KTM.part(t, NT)])

        def smalls(k):
            o = SMALL[k].lo
            mk = lambda off, shape: Tl(sbt(o + off * 4, shape, F32).ap, "sb", SMALL[k].lo, SMALL[k].hi)
            return mk(0, [4, 6]), mk(24, [2]), mk(26, [1]), mk(27, [1]), mk(28, [2])

        def out_tail(k, t, hh):
            xn = XN[k]
            st, mv, rstd, nmr, _ = smalls(k)
            ln_stats(xn.ap, xn, st, mv, rstd, None, 256)
            yield
            P.op("dve", lambda e: e.scalar_tensor_tensor(out=xn[:], in0=xn[:], scalar=mv[:, 0:1], in1=GN[:], op0=ALU.subtract, op1=ALU.mult), r=[xn, mv, GN], w=[xn])
            yield
            col = hh * 256
            gate = VG[:, t, 264:520]
            gr = VG.part(t, NT)
            if not sample:
                P.op("dve", lambda e: e.scalar_tensor_tensor(out=MIXO[:, t, col:col + 256], in0=xn[:], scalar=rstd[:], in1=gate, op0=ALU.mult, op1=ALU.mult), r=[xn, rstd, gr], w=[MIXO.part(t, NTO)])
            else:
                pp, tp = t // 4, t % 4
                P.op("dve", lambda e: e.scalar_tensor_tensor(out=xn[:], in0=xn[:], scalar=rstd[:], in1=gate, op0=ALU.mult, op1=ALU.mult), r=[xn, rstd, gr], w=[xn])
                yield
                if pp == 0:
                    P.op("dve", lambda e: e.tensor_scalar(out=MIXO[:, tp, col:col + 256], in0=xn[:], scalar1=FLG[:, 0:1], scalar2=None, op0=ALU.mult), r=[xn, FLG], w=[MIXO.part(tp, NTO)])
                else:
                    P.op("dve", lambda e: e.scalar_tensor_tensor(out=MIXO[:, tp, col:col + 256], in0=xn[:], scalar=FLG[:, pp:pp + 1], in1=MIXO[:, tp, col:col + 256], op0=ALU.mult, op1=ALU.add), r=[xn, FLG, MIXO.part(tp, NTO)], w=[MIXO.part(tp, NTO)])
            yield

        def run_rr(gens):
            gens = list(gens)
            while gens:
                nxt = []
                for g in gens:
                    try:
                        next(g)
                        nxt.append(g)
                    except StopIteration:
                        pass
                gens = nxt

        def state_chain(sl, s, d, h, kind):
            ret = kind == "ret"
            ncol = 256 if ret else 257
            SM = MST[sl][d]; Kx = KX[sl][d]
            src4 = st_ret if ret else st_C
            srcn = None if ret else st_n
            if not sample:
                P.op("dve", lambda e: e.memset(SM[:], 0.0), w=[SM])
            else:
                P.op("sp", lambda e: e.dma_start(out=SM[:, :, 0:256], in_=src4[d, h].rearrange("(kc p) v -> p kc v", p=128)), w=[SM], dma="sm%d" % SM.lo)
                if srcn is not None:
                    P.op("sp", lambda e: e.dma_start(out=SM[:, :, 256:257], in_=srcn[d, h].unsqueeze(2), allow_slow_non_contiguous=True), w=[SM], dma="sm%d" % SM.lo)
            yield
            order = range(n) if d == 0 else range(n - 1, -1, -1)
            for c in order:
                t = s * n + c
                sa = SALL[sl][d][c]
                P.op("act", lambda e, sa=sa: e.activation(out=sa[:, :, 0:ncol], in_=SM[:, :, 0:ncol], func=AF.Copy), r=[SM], w=[sa])
                if ret:
                    sc_ap, sc_r = DEC.ap[:, 2 + d:3 + d], DEC
                    dk_ap, dk_r = DEC.ap[:, 4 + d:5 + d], DEC
                else:
                    sc_ap, sc_r = GWK.ap[:, t, d * 4 + h:d * 4 + h + 1], GWK
                    dk_ap, dk_r = GSC.ap[:, t, d * 4 + h:d * 4 + h + 1], GSC
                P.op("act", lambda e, t=t, sc_ap=sc_ap: e.activation(out=Kx[:], in_=KTM[:, t, :], func=AF.Identity, scale=sc_ap), r=[KTM.part(t, NT), sc_r], w=[Kx])
                yield
                if ret:
                    while navail() < 1:
                        yield
                    ps = psh()
                    for kc in range(2):
                        P.op("pe", lambda e, kc=kc, ps=ps, t=t: e.matmul(ps[:, kc * 256:(kc + 1) * 256], lhsT=Kx[:, kc * 128:(kc + 1) * 128], rhs=VG[:, t, 0:256], start=True, stop=True), r=[Kx, VG.part(t, NT)], w=[ps])
                    yield
                    P.op("dve", lambda e, ps=ps, dk_ap=dk_ap: e.scalar_tensor_tensor(out=SM[:, :, 0:256], in0=SM[:, :, 0:256], scalar=dk_ap, in1=ps[:].rearrange("p (a b) -> p a b", a=2), op0=ALU.mult, op1=ALU.add), r=[SM, ps, dk_r], w=[SM])
                    prel(ps)
                    yield
                else:
                    while navail() < 2:
                        yield
                    pss = [psh(), psh()]
                    for kc in range(2):
                        P.op("pe", lambda e, kc=kc, ps=pss[kc], t=t: e.matmul(ps[:, 0:257], lhsT=Kx[:, kc * 128:(kc + 1) * 128], rhs=VG[:, t, 0:257], start=True, stop=True), r=[Kx, VG.part(t, NT)], w=[pss[kc]])
                    yield
                    for kc in range(2):
                        P.op("dve", lambda e, ps=pss[kc], kc=kc, dk_ap=dk_ap: e.scalar_tensor_tensor(out=SM[:, kc, :], in0=SM[:, kc, :], scalar=dk_ap, in1=ps[:, 0:257], op0=ALU.mult, op1=ALU.add), r=[SM, pss[kc], dk_r], w=[SM])
                        prel(pss[kc])
                    yield
            if not sample:
                dst4 = o_ret if ret else o_C
                P.op("sp", lambda e: e.dma_start(out=dst4[s, d, h].rearrange("(kc p) v -> p kc v", p=128), in_=SM[:, :, 0:256]), r=[SM], dma="smo%d" % SM.lo)
                if not ret:
                    P.op("sp", lambda e: e.dma_start(out=o_n[s, d, h].unsqueeze(2), in_=SM[:, :, 256:257], allow_slow_non_contiguous=True), r=[SM], dma="smo%d" % SM.lo)
            yield

        def out_chunk_ret(k, sl, s, c, h):
            t = s * n + c
            tk = slice(t * 128, (t + 1) * 128)
            attm = ATT[k]; xn = XN[k]
            while navail() < 2:
                yield
            psA = psh()
            for kc in range(2):
                P.op("pe", lambda e, kc=kc: e.matmul(psA[:, 0:128], lhsT=KT[:, kc, tk], rhs=QT[:, kc, tk], start=(kc == 0), stop=(kc == 1)), r=[KT, QT], w=[psA])
            psS = psh()
            for d in range(2):
                sa = SALL[sl][d][c]
                for kc in range(2):
                    P.op("pe", lambda e, kc=kc, d=d, sa=sa: e.matmul(psS[:, d * 256:(d + 1) * 256], lhsT=QT[:, kc, tk], rhs=sa[:, kc, 0:256], start=(kc == 0), stop=(kc == 1)), r=[QT, sa], w=[psS])
            yield
            P.op("dve", lambda e: e.tensor_tensor(out=attm[:, 0, :], in0=psA[:, 0:128], in1=RM[:], op=ALU.mult), r=[psA, RM], w=[attm])
            prel(psA)
            yield
            while navail() < 1:
                yield
            psO = psh()
            P.op("pe", lambda e: e.matmul(psO[:, 0:256], lhsT=attm[:, 0, :], rhs=VG[:, t, 0:256], start=True, stop=True), r=[attm, VG.part(t, NT)], w=[psO])
            P.op("dve", lambda e: e.tensor_scalar(out=xn[:], in0=psS[:, 0:256], scalar1=DEC[:, 0:1], scalar2=None, op0=ALU.mult), r=[psS, DEC], w=[xn])
            yield
            P.op("dve", lambda e: e.scalar_tensor_tensor(out=xn[:], in0=psS[:, 256:512], scalar=DEC[:, 1:2], in1=xn[:], op0=ALU.mult, op1=ALU.add), r=[psS, DEC, xn], w=[xn])
            yield
            P.op("dve", lambda e: e.tensor_tensor(out=xn[:], in0=psO[:, 0:256], in1=xn[:], op=ALU.add), r=[psO, xn], w=[xn])
            prel(psS); prel(psO)
            yield
            yield from out_tail(k, t, h)

        def out_chunk_ml(k, sl, s, c, h):
            t = s * n + c
            tk = slice(t * 128, (t + 1) * 128)
            attm = ATT[k]; xn = XN[k]; Dk = DMS[k]; WT = WTS[k]
            _, _, _, _, dd = smalls(k)
            bigm = CST.ap[:, C_BIGF:C_BIGF + 256]
            while navail() < 1:
                yield
            psA = psh()
            for kc in range(2):
                P.op("pe", lambda e, kc=kc: e.matmul(psA[:, 0:128], lhsT=KT[:, kc, tk], rhs=QT[:, kc, tk], start=(kc == 0), stop=(kc == 1)), r=[KT, QT], w=[psA])
            mu2 = GMU.ap[:, t, h:h + 5:4]
            P.op("dve", lambda e: e.tensor_tensor(out=Dk[:], in0=IDF.unsqueeze(1).to_broadcast([128, 2, 128]), in1=mu2.unsqueeze(2).to_broadcast([128, 2, 128]), op=ALU.mult), r=[CST, GMU], w=[Dk])
            yield
            P.op("pe", lambda e: e.matmul(psA[:, 128:384], lhsT=ONESF, rhs=Dk[:].rearrange("p a b -> p (a b)"), start=True, stop=False), r=[CST, Dk], w=[psA])
            P.op("pe", lambda e: e.matmul(psA[:, 128:384], lhsT=IDF, rhs=bigm, start=False, stop=True), r=[CST], w=[psA])
            yield
            for d in range(2):
                col = d * 4 + h
                P.op("act", lambda e, d=d, col=col: e.activation(out=WT[:], in_=psA[:, 128 + d * 128:256 + d * 128], func=AF.Exp, bias=GA[:, t, col:col + 1], scale=-1.0), r=[psA, GA], w=[WT])
                yield
                P.op("dve", lambda e, d=d: e.tensor_tensor(out=attm[:, d, :], in0=psA[:, 0:128], in1=WT[:], op=ALU.mult), r=[psA, WT], w=[attm])
                yield
            prel(psA)
            for d in range(2):
                col = d * 4 + h
                sa = SALL[sl][d][c]
                td = TOT[k][d]
                while navail() < 2:
                    yield
                psN = psh()
                P.op("pe", lambda e, d=d, psN=psN: e.matmul(psN[:, 0:257], lhsT=attm[:, d, :], rhs=VG[:, t, 0:257], start=True, stop=True), r=[attm, VG.part(t, NT)], w=[psN])
                psI = psh()
                for kc in range(2):
                    P.op("pe", lambda e, kc=kc, psI=psI, sa=sa: e.matmul(psI[:, 0:257], lhsT=QT[:, kc, tk], rhs=sa[:, kc, :], start=(kc == 0), stop=(kc == 1)), r=[QT, sa], w=[psI])
                yield
                P.op("dve", lambda e, psI=psI, td=td, col=col: e.tensor_scalar(out=td[:], in0=psI[:, 0:257], scalar1=GSP[:, t, col:col + 1], scalar2=None, op0=ALU.mult), r=[psI, GSP], w=[td])
                prel(psI)
                yield
                P.op("dve", lambda e, psN=psN, td=td: e.tensor_tensor(out=td[:], in0=psN[:, 0:257], in1=td[:], op=ALU.add), r=[psN, td], w=[td])
                prel(psN)
                yield
            den2 = TOTT[k][:, :, 256:257]
            P.op("dve", lambda e: e.scalar_tensor_tensor(out=dd[:, 0:2].unsqueeze(2), in0=den2, scalar=-1.0, in1=den2, op0=ALU.mult, op1=ALU.max), r=[TOTT[k]], w=[dd])
            yield
            P.op("dve", lambda e: e.tensor_tensor(out=dd[:, 0:2], in0=dd[:, 0:2], in1=GEM.ap[:, t, h:h + 5:4], op=ALU.max), r=[dd, GEM], w=[dd])
            yield
            P.op("dve", lambda e: e.reciprocal(out=dd[:], in_=dd[:]), r=[dd], w=[dd])
            yield
            P.op("dve", lambda e: e.tensor_scalar(out=xn[:], in0=TOT[k][0][:, 0:256], scalar1=dd[:, 0:1], scalar2=None, op0=ALU.mult), r=[TOT[k][0], dd], w=[xn])
            yield
            P.op("dve", lambda e: e.scalar_tensor_tensor(out=xn[:], in0=TOT[k][1][:, 0:256], scalar=dd[:, 1:2], in1=xn[:], op0=ALU.mult, op1=ALU.add), r=[TOT[k][1], dd, xn], w=[xn])
            yield
            yield from out_tail(k, t, 4 + h)

        def mixer(h, kind):
            ret = kind == "ret"
            if ret:
                lgf = LG.ap[:, h:h + 1]; lgb = LG.ap[:, 4 + h:5 + h]
                vec = lambda k_: CST.ap[:, C_VEC + k_:C_VEC + k_ + 1]
                for col, (src, lg) in enumerate([(vec(0), lgf), (vec(1), lgb), (vec(2), lgf), (vec(3), lgb)]):
                    P.op("act", lambda e, col=col, src=src, lg=lg: e.activation(out=DEC[:, col:col + 1], in_=src, func=AF.Exp, scale=lg), r=[CST, LG], w=[DEC])
                P.op("act", lambda e: e.activation(out=DEC[:, 4:5], in_=lgf, func=AF.Exp, scale=128.0), r=[LG], w=[DEC])
                P.op("act", lambda e: e.activation(out=DEC[:, 5:6], in_=lgb, func=AF.Exp, scale=128.0), r=[LG], w=[DEC])
                P.op("act", lambda e: e.activation(out=RM[:], in_=cstm(C_D1), func=AF.Exp, scale=lgf), r=[CST, LG], w=[RM])
                P.op("dve", lambda e: e.tensor_tensor(out=RM[:], in0=RM[:], in1=cstm(C_U1), op=ALU.mult), r=[RM, CST], w=[RM])
                P.op("act", lambda e: e.activation(out=RMT[:], in_=cstm(C_D2), func=AF.Exp, scale=lgb), r=[CST, LG], w=[RMT])
                P.op("dve", lambda e: e.tensor_tensor(out=RMT[:], in0=RMT[:], in1=cstm(C_U2), op=ALU.mult), r=[RMT, CST], w=[RMT])
                P.op("dve", lambda e: e.tensor_tensor(out=RM[:], in0=RM[:], in1=RMT[:], op=ALU.add), r=[RM, RMT], w=[RM])
            gsrc = gn_ret if ret else gn_ml
            P.op("sp", lambda e: e.dma_start(out=GN[:], in_=gsrc[0:1, h * 256:(h + 1) * 256].broadcast_to([128, 256])), w=[GN], dma="gn")
            ocf = out_chunk_ret if ret else out_chunk_ml
            for s0 in range(0, nseq, NSL):
                run_rr([state_chain(sl, s0 + sl, d, h, kind) for sl in range(NSL) for d in range(2)])
                jobs = [(sl, s0 + sl, c) for sl in range(NSL) for c in range(n)]
                for j0 in range(0, len(jobs), 2):
                    run_rr([ocf(k, jobs[j0 + k][0], jobs[j0 + k][1], jobs[j0 + k][2], h) for k in range(min(2, len(jobs) - j0))])

        def gates_pre():
            for t in range(NT):
                ps = psn()
                for kc in range(KC):
                    P.op("pe", lambda e, ps=ps, kc=kc, t=t: e.matmul(ps[:, 0:16], lhsT=HT[:, kc, t * 128:(t + 1) * 128], rhs=WGT[:, kc, :], start=(kc == 0), stop=(kc == KC - 1)), r=[HT, WGT], w=[ps])
                P.op("dve", lambda e, ps=ps, t=t: e.tensor_tensor(out=GTS[:, t, :], in0=ps[:, 0:16], in1=BG[:], op=ALU.add), r=[ps, BG], w=[GTS])
            P.op("act", lambda e: e.activation(out=LFN[:], in_=GTS[:], func=AF.Exp, scale=-1.0), r=[GTS], w=[LFN])
            P.op("act", lambda e: e.activation(out=LFN[:], in_=LFN[:], func=AF.Ln, bias=1.0), r=[LFN], w=[LFN])
            for t in range(NT):
                ps = psn()
                P.op("pe", lambda e, ps=ps, t=t: e.matmul(ps[:, 0:4], lhsT=cstm(C_U1), rhs=LFN[:, t, 4:8], start=True, stop=True), r=[CST, LFN], w=[ps])
                P.op("pe", lambda e, ps=ps, t=t: e.matmul(ps[:, 4:8], lhsT=cstm(C_U2), rhs=LFN[:, t, 12:16], start=True, stop=True), r=[CST, LFN], w=[ps])
                P.op("pe", lambda e, ps=ps, t=t: e.matmul(ps[:, 8:12], lhsT=ONESF, rhs=LFN[:, t, 4:8], start=True, stop=True), r=[CST, LFN], w=[ps])
                P.op("pe", lambda e, ps=ps, t=t: e.matmul(ps[:, 12:16], lhsT=ONESF, rhs=LFN[:, t, 12:16], start=True, stop=True), r=[CST, LFN], w=[ps])
                P.op("dve", lambda e, ps=ps, t=t: e.tensor_copy(out=GNB[:, t, :], in_=ps[:, 0:8]), r=[ps], w=[GNB])
                P.op("dve", lambda e, ps=ps, t=t: e.tensor_copy(out=GBL[:, t, :], in_=ps[:, 8:16]), r=[ps], w=[GBL])
                P.op("dve", lambda e, t=t: e.tensor_tensor(out=GA[:, t, 0:4], in0=GTS[:, t, 0:4], in1=GNB[:, t, 0:4], op=ALU.add), r=[GTS, GNB], w=[GA])
                P.op("dve", lambda e, t=t: e.tensor_tensor(out=GA[:, t, 4:8], in0=GTS[:, t, 8:12], in1=GNB[:, t, 4:8], op=ALU.add), r=[GTS, GNB], w=[GA])
            for s in range(nseq):
                if not sample:
                    P.op("dve", lambda e: e.memset(mprev[:], 0.0), w=[mprev])
                else:
                    P.op("sp", lambda e: e.dma_start(out=mprev[:], in_=st_m[0:1, :].broadcast_to([128, 8])), w=[mprev], dma="mprev")
                for d in range(2):
                    ds = slice(d * 4, d * 4 + 4)
                    order = range(n) if d == 0 else range(n - 1, -1, -1)
                    neg = cstm(C_NEGF if d == 0 else C_NEGB)
                    for c in order:
                        t = s * n + c
                        P.op("dve", lambda e, t=t, ds=ds: e.tensor_tensor(out=Dm[:], in0=IDF.unsqueeze(1).to_broadcast([128, 4, 128]), in1=GA[:, t, ds].unsqueeze(2).to_broadcast([128, 4, 128]), op=ALU.mult), r=[CST, GA], w=[Dm])
                        ps = psn()
                        P.op("pe", lambda e, ps=ps: e.matmul(ps[:], lhsT=ONESF, rhs=Dm[:].rearrange("p a b -> p (a b)"), start=True, stop=True), r=[CST, Dm], w=[ps])
                        P.op("dve", lambda e, ps=ps, neg=neg: e.tensor_tensor(out=Am[:], in0=ps[:].rearrange("p (a b) -> p a b", a=4), in1=neg.unsqueeze(1).to_broadcast([128, 4, 128]), op=ALU.add), r=[ps, CST], w=[Am])
                        P.op("dve", lambda e: e.tensor_reduce(out=v8[0][:, 0:4], in_=Am[:], axis=AX.X, op=ALU.max), r=[Am], w=[v8[0]])
                        P.op("dve", lambda e, ps=ps: e.tensor_reduce(out=v8[1][:, 0:4], in_=ps[:].rearrange("p (a b) -> p a b", a=4), axis=AX.X, op=ALU.max), r=[ps], w=[v8[1]])
                        P.op("dve", lambda e, t=t, ds=ds: e.tensor_tensor(out=GMU[:, t, ds], in0=v8[0][:, 0:4], in1=mprev[:, ds], op=ALU.max), r=[v8[0], mprev], w=[GMU])
                        P.op("dve", lambda e, ds=ds: e.tensor_tensor(out=mnew[:, ds], in0=v8[1][:, 0:4], in1=mprev[:, ds], op=ALU.max), r=[v8[1], mprev], w=[mnew])
                        P.op("dve", lambda e, t=t, ds=ds: e.tensor_tensor(out=v8[2][:, 0:4], in0=mprev[:, ds], in1=GMU[:, t, ds], op=ALU.subtract), r=[mprev, GMU], w=[v8[2]])
                        P.op("act", lambda e, t=t, ds=ds: e.activation(out=GSP[:, t, ds], in_=v8[2][:, 0:4], func=AF.Exp), r=[v8[2]], w=[GSP])
                        P.op("dve", lambda e, t=t, ds=ds: e.tensor_tensor(out=v8[3][:, 0:4], in0=GA[:, t, ds], in1=mnew[:, ds], op=ALU.subtract), r=[GA, mnew], w=[v8[3]])
                        P.op("act", lambda e, t=t, ds=ds: e.activation(out=GWK[:, t, ds], in_=v8[3][:, 0:4], func=AF.Exp), r=[v8[3]], w=[GWK])
                        P.op("dve", lambda e, ds=ds: e.tensor_tensor(out=v8[2][:, 4:8], in0=mprev[:, ds], in1=mnew[:, ds], op=ALU.subtract), r=[mprev, mnew], w=[v8[2]])
                        P.op("act", lambda e, t=t, ds=ds: e.activation(out=GSC[:, t, ds], in_=v8[2][:, 4:8], func=AF.Exp), r=[v8[2]], w=[GSC])
                        P.op("dve", lambda e, t=t, ds=ds: e.tensor_tensor(out=v8[3][:, 4:8], in0=GNB[:, t, ds], in1=GMU[:, t, ds], op=ALU.subtract), r=[GNB, GMU], w=[v8[3]])
                        P.op("act", lambda e, t=t, ds=ds: e.activation(out=GEM[:, t, ds], in_=v8[3][:, 4:8], func=AF.Exp), r=[v8[3]], w=[GEM])
                        P.op("dve", lambda e, ds=ds, t=t: e.tensor_tensor(out=mprev[:, ds], in0=mnew[:, ds], in1=GBL[:, t, ds], op=ALU.subtract), r=[mnew, GBL], w=[mprev])
                if not sample:
                    P.op("sp", lambda e, s=s: e.dma_start(out=o_m[s:s + 1, :], in_=mprev[0:1, :]), r=[mprev], dma="om")

        def mod_rest(hh):
            pass

        for h in range(NH):
            wq = load_w(h * 256); proj_fm(wq, QT, "rq", 0)
            wk = load_w(1024 + h * 256); proj_fm(wk, KT, "rk", 0)
            wv = load_w(2048 + h * 256); wg = load_w(3072 + h * 256); proj_tm(wv, wg, AF.Silu)
            make_ktm()
            mod_rest(h)
            mixer(h, "ret")
        gates_pre()
        for h in range(NH):
            wq = load_w(4096 + h * 256); proj_fm(wq, QT, "mq", h * 256)
            wk = load_w(5120 + h * 256); proj_fm(wk, KT, "mk", 1024 + h * 256)
            wv = load_w(6144 + h * 256); wg = load_w(7168 + h * 256); proj_tm(wv, wg, AF.Sigmoid)
            make_ktm(1.0 / 16.0)
            mod_rest(4 + h)
            mixer(h, "ml")

        MT = sbt(O_HT, [16, Town], BF16)
        for t in range(NTO):
            transpose_tile(MIXO[:, t, :], MIXO.part(t, NTO), MT, MT, t * 128)
        rb = Bump(O_PROJ, SZ_PROJ)
        r_g1 = rb([D], F32); r_l1g = rb([D], F32); r_l1b = rb([D], F32); r_sc2 = rb([D], F32); r_sh2 = rb([D], F32)
        load_modrow(r_g1, 2, mg); load_row(r_l1g, ln1_g); load_row(r_l1b, ln1_b)
        load_modrow(r_sh2, 3, mg); load_modrow(r_sc2, 4, mg, plus1=True)
        YA = [sbt((O_MIXO if t < 4 else O_MIX) + (t % 4) * 8192, [D], F32) for t in range(NTO)]
        hb = Bump(O_HT + 32768, 32768)
        xts = [hb([D], F32) for _ in range(2)]
        h2s = [hb([D], BF16) for _ in range(2)]
        wsl2 = [sbt(O_W + k * 8192, [16, 256], BF16) for k in range(2)]
        for cbk in range(8):
            ws = wsl2[cbk % 2]
            P.op("pool", lambda e, ws=ws, cbk=cbk: e.dma_start(out=ws[:], in_=w_out[:, cbk * 256:(cbk + 1) * 256].rearrange("(kc p) n -> p kc n", p=128)), w=[ws], dma="wsl%d" % (cbk % 2))
            for t in range(NTO):
                ps = psn()
                for kc in range(KC):
                    P.op("pe", lambda e, ps=ps, kc=kc, t=t, ws=ws: e.matmul(ps[:, 0:256], lhsT=MT[:, kc, t * 128:(t + 1) * 128], rhs=ws[:, kc, :], start=(kc == 0), stop=(kc == KC - 1)), r=[MT, ws], w=[ps])
                P.op("dve", lambda e, ps=ps, t=t, cbk=cbk: e.tensor_tensor(out=YA[t][:, cbk * 256:(cbk + 1) * 256], in0=ps[:, 0:256], in1=r_g1[:, cbk * 256:(cbk + 1) * 256], op=ALU.mult), r=[ps, r_g1], w=[YA[t]])
        smo = [sbt(O_MIX + 32768 + k * 256, [64], F32) for k in range(2)]

        def o_tile(t):
            def g(k):
                xt = xts[k]; h2 = h2s[k]; ya = YA[t]
                st, mv, rstd, nmr = small4(smo[k])
                P.op("sp", lambda e: e.dma_start(out=xt[:], in_=x_own[t * 128:(t + 1) * 128, :]), w=[xt], dma="xo%d" % k)
                yield
                P.op("dve", lambda e: e.scalar_tensor_tensor(out=ya[:], in0=xt[:], scalar=ALPHA, in1=ya[:], op0=ALU.mult, op1=ALU.add), r=[xt, ya], w=[ya])
                yield
                ln_stats(ya.ap, ya, st, mv, rstd, None, D)
                yield
                P.op("dve", lambda e: e.scalar_tensor_tensor(out=ya[:], in0=ya[:], scalar=mv[:, 0:1], in1=r_l1g[:], op0=ALU.subtract, op1=ALU.mult), r=[ya, mv, r_l1g], w=[ya])
                yield
                P.op("dve", lambda e: e.scalar_tensor_tensor(out=ya[:], in0=ya[:], scalar=rstd[:], in1=r_l1b[:], op0=ALU.mult, op1=ALU.add), r=[ya, rstd, r_l1b], w=[ya])
                yield
                P.op("sp", lambda e: e.dma_start(out=x1_d[x1off + t * 128: x1off + (t + 1) * 128, :], in_=ya[:]), r=[ya], w=[DR("x1", x1off // 128 + t, x1off // 128 + t + 1)], dma="x1o%d" % t)
                ln_stats(ya.ap, ya, st, mv, rstd, None, D)
                yield
                P.op("dve", lambda e: e.scalar_tensor_tensor(out=xt[:], in0=ya[:], scalar=mv[:, 0:1], in1=r_sc2[:], op0=ALU.subtract, op1=ALU.mult), r=[ya, mv, r_sc2], w=[xt])
                yield
                P.op("dve", lambda e: e.scalar_tensor_tensor(out=h2[:], in0=xt[:], scalar=rstd[:], in1=r_sh2[:], op0=ALU.mult, op1=ALU.add), r=[xt, rstd, r_sh2], w=[h2])
                yield
                yield from transpose_tile_g(h2.ap, h2, MT, MT, t * 128)
            return g
        run_pipe([o_tile(t) for t in range(NTO)], 2)

        H2T = MT
        achunks = []
        for k in range(21):
            achunks.append(sbt(O_PROJ + k * Town * 2, [Town], BF16))
        for k in range(16):
            achunks.append(sbt(O_MIXO + k * Town * 2, [Town], BF16))
        for k in range(6):
            achunks.append(sbt(O_HT + 32768 + k * Town * 2, [Town], BF16))
        wd2 = sbt(O_HT + 32768 + 6 * 2048, [NFF, 128], BF16)
        wd1 = sbt(O_W, [NFF, 128], BF16)
        fb = Bump(O_MIX, SZ_MIX)
        zt = [fb([512], F32) for _ in range(2)]
        z2 = [fb([512], F32) for _ in range(2)]
        wup = [sbt(O_W + k * 8192, [16, 256], BF16) for k in range(2)]
        nb_ = 512 // Lc
        v3 = lambda ap: ap.rearrange("p (a b) -> p a b", a=nb_)
        for c in range(NFF):
            ws = wup[c % 2]
            P.op("pool", lambda e, ws=ws, c=c: e.dma_start(out=ws[:, :, 0:128], in_=w_up[:, c * 128:(c + 1) * 128].rearrange("(kc p) n -> p kc n", p=128)), w=[ws], dma="wup%d" % (c % 2))
            P.op("pool", lambda e, ws=ws, c=c: e.dma_start(out=ws[:, :, 128:256], in_=w_up[:, DFF + c * 128:DFF + (c + 1) * 128].rearrange("(kc p) n -> p kc n", p=128)), w=[ws], dma="wup%d" % (c % 2))
            w0 = CFF.ap[:, c * 3:c * 3 + 1]; w1 = CFF.ap[:, c * 3 + 1:c * 3 + 2]; w2 = CFF.ap[:, c * 3 + 2:c * 3 + 3]
            for tb in range(Town // 512):
                pu = psn()
                for kc in range(KC):
                    P.op("pe", lambda e, pu=pu, kc=kc, tb=tb, ws=ws: e.matmul(pu[:], lhsT=ws[:, kc, 0:128], rhs=H2T[:, kc, tb * 512:(tb + 1) * 512], start=(kc == 0), stop=(kc == KC - 1)), r=[ws, H2T], w=[pu])
                pg = psn()
                for kc in range(KC):
                    P.op("pe", lambda e, pg=pg, kc=kc, tb=tb, ws=ws: e.matmul(pg[:], lhsT=ws[:, kc, 128:256], rhs=H2T[:, kc, tb * 512:(tb + 1) * 512], start=(kc == 0), stop=(kc == KC - 1)), r=[ws, H2T], w=[pg])
                z = zt[tb % 2]; zz = z2[tb % 2]
                P.op("dve", lambda e, pu=pu, z=z, w1=w1: e.tensor_scalar(out=z[:], in0=pu[:], scalar1=w1, scalar2=None, op0=ALU.mult), r=[pu, CFF], w=[z])
                P.op("dve", lambda e, pu=pu, z=z, w0=w0: e.scalar_tensor_tensor(out=v3(z[:])[:, :, 1:Lc], in0=v3(pu[:])[:, :, 0:Lc - 1], scalar=w0, in1=v3(z[:])[:, :, 1:Lc], op0=ALU.mult, op1=ALU.add), r=[pu, CFF, z], w=[z])
                P.op("dve", lambda e, pu=pu, z=z, w2=w2: e.scalar_tensor_tensor(out=v3(z[:])[:, :, 0:Lc - 1], in0=v3(pu[:])[:, :, 1:Lc], scalar=w2, in1=v3(z[:])[:, :, 0:Lc - 1], op0=ALU.mult, op1=ALU.add), r=[pu, CFF, z], w=[z])
                P.op("act", lambda e, z=z, zz=zz: e.activation(out=zz[:], in_=z[:], func=AF.Silu), r=[z], w=[zz])
                ac = achunks[c]
                P.op("dve", lambda e, pg=pg, zz=zz, ac=ac, tb=tb: e.tensor_tensor(out=ac[:, tb * 512:(tb + 1) * 512], in0=pg[:], in1=zz[:], op=ALU.mult), r=[pg, zz], w=[ac])
        fb = Bump(O_MIX, SZ_MIX)
        r_g2 = fb([D], F32); r_l2g = fb([D], F32); r_l2b = fb([D], F32)
        x1t = fb([D], F32); y2 = fb([D], F32)
        st, mv, rstd, nmr = ST_C, MV_C, RSTD_C, NMR_C
        load_modrow(r_g2, 5, mg); load_row(r_l2g, ln2_g); load_row(r_l2b, ln2_b)
        wds = [wd1, wd2]
        stg = [sbt(O_W + 11264 + k * 512, [128], F32) for k in range(8)]
        cnt = 0
        for cbk in range(16):
            ws = wds[cbk % 2]
            P.op("pool", lambda e, ws=ws, cbk=cbk: e.dma_start(out=ws[:], in_=w_down[:, cbk * 128:(cbk + 1) * 128].rearrange("(c p) n -> p c n", p=128)), w=[ws], dma="wd%d" % (cbk % 2))
            for t in range(NTO):
                ps = psn()
                for c in range(NFF):
                    P.op("pe", lambda e, ps=ps, c=c, t=t, ws=ws: e.matmul(ps[:, 0:128], lhsT=achunks[c][:, t * 128:(t + 1) * 128], rhs=ws[:, c, :], start=(c == 0), stop=(c == NFF - 1)), r=[achunks[c], ws], w=[ps])
                sg_ = stg[cnt % 8]
                P.op("dve", lambda e, ps=ps, cbk=cbk, sg_=sg_: e.tensor_tensor(out=sg_[:], in0=ps[:, 0:128], in1=r_g2[:, cbk * 128:(cbk + 1) * 128], op=ALU.mult), r=[ps, r_g2], w=[sg_])
                P.op("sp", lambda e, sg_=sg_, cbk=cbk, t=t: e.dma_start(out=y2_d[x1off + t * 128: x1off + (t + 1) * 128, cbk * 128:(cbk + 1) * 128], in_=sg_[:]), r=[sg_], w=[DR("y2", x1off // 128 + t, x1off // 128 + t + 1)], dma="stg%d" % (cnt % 8))
                cnt += 1
        for t in range(NTO):
            row = x1off // 128 + t
            P.op("sp", lambda e, t=t: e.dma_start(out=x1t[:], in_=x1_d[x1off + t * 128: x1off + (t + 1) * 128, :]), r=[DR("x1", row, row + 1)], w=[x1t], dma="x1t")
            P.op("sp", lambda e, t=t: e.dma_start(out=y2[:], in_=y2_d[x1off + t * 128: x1off + (t + 1) * 128, :]), r=[DR("y2", row, row + 1)], w=[y2], dma="y2t")
            P.op("dve", lambda e: e.scalar_tensor_tensor(out=y2[:], in0=x1t[:], scalar=ALPHA, in1=y2[:], op0=ALU.mult, op1=ALU.add), r=[x1t, y2], w=[y2])
            ln_stats(y2.ap, y2, st, mv, rstd, None, D)
            P.op("dve", lambda e: e.scalar_tensor_tensor(out=y2[:], in0=y2[:], scalar=mv[:, 0:1], in1=r_l2g[:], op0=ALU.subtract, op1=ALU.mult), r=[y2, mv, r_l2g], w=[y2])
            P.op("dve", lambda e: e.scalar_tensor_tensor(out=y2[:], in0=y2[:], scalar=rstd[:], in1=r_l2b[:], op0=ALU.mult, op1=ALU.add), r=[y2, rstd, r_l2b], w=[y2])
            P.op("sp", lambda e, t=t: e.dma_start(out=y_out[t * 128:(t + 1) * 128, :], in_=y2[:]), r=[y2], dma="yout")

    stage_mod(list(range(48)), O_PROJ, O_MIX)
    for gi in GROUPS:
        process_group(gi)
    P.emit()
    es.close()
    return nc


GROUPS = (0, 1)
_CACHE = {}


def kernel(x_prompt, x_sample, state_ret, state_mlstm_C, state_mlstm_n, state_mlstm_m, c, c_ctx,
           w_mod, b_mod, w_in, b_gate, conv_qk, ret_theta, gn_ret, gn_mlstm, w_out,
           ln1_g, ln1_b, w_up, conv_ff, w_down, ln2_g, ln2_b):
    f = lambda a: np.ascontiguousarray(np.asarray(a, dtype=np.float32))
    if "nc" not in _CACHE:
        _CACHE["nc"] = build_program()
    nc = _CACHE["nc"]
    cst = make_consts()
    xp_all = f(x_prompt).reshape(8, 1024, D)
    xs_all = f(x_sample)
    shared = dict(
        w_mod=f(w_mod)[0], b_mod=f(b_mod), w_in=f(w_in)[0], b_gate=f(b_gate),
        conv_qkT=f(f(conv_qk)[0].T.reshape(16, 128, 3).transpose(1, 0, 2).reshape(128, 48)),
        theta=f(ret_theta).reshape(1, 8), gn_ret=f(gn_ret).reshape(1, 1024), gn_ml=f(gn_mlstm).reshape(1, 1024),
        w_out=f(w_out)[0], ln1_g=f(ln1_g), ln1_b=f(ln1_b), w_up=f(w_up)[0],
        conv_ffT=f(f(conv_ff)[0].T.reshape(NFF, 128, 3).transpose(1, 0, 2).reshape(128, NFF * 3)),
        w_down=f(w_down)[0], ln2_g=f(ln2_g), ln2_b=f(ln2_b), cst=cst)
    in_maps = []
    for i in range(8):
        b, p = i // 4, i % 4
        fl = np.zeros((1, 4), np.float32); fl[0, p] = 1.0
        cT = np.stack([f(c_ctx).reshape(16, 128).T, f(c)[b].reshape(16, 128).T], axis=2).reshape(128, 32)
        m = dict(shared)
        m.update(xp=xp_all[i], xs=xs_all[b], xo=f(xs_all[b, p * 512:(p + 1) * 512]), flags=fl, cT=f(cT),
                 st_ret=f(state_ret)[b, 0], st_C=f(state_mlstm_C)[b, 0],
                 st_n=f(f(state_mlstm_n)[b, 0].reshape(2, 4, 2, 128).transpose(0, 1, 3, 2)),
                 st_m=f(state_mlstm_m)[b, 0].reshape(1, 8))
        in_maps.append(m)
    res = run_bass_kernel_spmd(nc, in_maps, core_ids=list(range(8)))
    R = res.results
    y_prompt = np.concatenate([R[i]["yp"] for i in range(8)], 0).reshape(32, 256, D)
    y_sample = np.stack([np.concatenate([R[b * 4 + p]["ys"] for p in range(4)], 0) for b in range(2)], 0)
    n_ret = np.concatenate([R[i]["o_ret"] for i in range(8)], 0)[:, None]
    n_C = np.concatenate([R[i]["o_C"] for i in range(8)], 0)[:, None]
    n_n = np.concatenate([R[i]["o_n"] for i in range(8)], 0).transpose(0, 1, 2, 4, 3).reshape(32, 2, 4, 256)[:, None]
    n_m = np.concatenate([R[i]["o_m"] for i in range(8)], 0).reshape(32, 2, 4)[:, None]
    return (y_prompt, y_sample, np.ascontiguousarray(n_ret), np.ascontiguousarray(n_C),
            np.ascontiguousarray(n_n), np.ascontiguousarray(n_m))
```

```python
import contextlib
import numpy as np
import concourse.bass as bass
import concourse.mybir as mybir
from concourse.bass_utils import run_bass_kernel_spmd

F32 = mybir.dt.float32
BF16 = mybir.dt.bfloat16
ALU = mybir.AluOpType
AF = mybir.ActivationFunctionType
AX = mybir.AxisListType

D = 2048
KC = 16
HD = 256
NH = 4
DFF = 5504
NFF = 43
NIN = 8208
ALPHA = 2.0 ** 0.25
BIG = 1.0e9
SAME_ENGINE_SYNC = True
DEBUG_IDS = set()
GRAN = 256


class Rg:
    __slots__ = ("space", "lo", "hi")

    def __init__(self, space, lo, hi):
        self.space, self.lo, self.hi = space, lo, hi


class Tl:
    def __init__(self, ap, space, lo, hi):
        self.ap, self.space, self.lo, self.hi = ap, space, lo, hi

    def __getitem__(self, k):
        return self.ap[k]

    @property
    def r(self):
        return Rg(self.space, self.lo, self.hi)

    def part(self, i, n, cnt=1):
        sz = (self.hi - self.lo) // n
        return Rg(self.space, self.lo + i * sz, self.lo + (i + cnt) * sz)


def _rg(x):
    return x.r if isinstance(x, Tl) else x


class _Rec:
    def __getattr__(self, name):
        def f(*a, **k):
            self.call = (name, a, k)
            return self
        return f


class Prog:
    ENGS = ("pe", "act", "dve", "pool", "sp")

    def __init__(self, nc):
        self.nc = nc
        self.ops = []
        self.st = {}

    def _cells(self, rg):
        if rg.space == "sb":
            return [("sb", g) for g in range(rg.lo // GRAN, (rg.hi + GRAN - 1) // GRAN)]
        return [(rg.space, g) for g in range(rg.lo, rg.hi)]

    def op(self, eng, fn, r=(), w=(), dma=None):
        oid = len(self.ops)
        deps = set()
        rc = [c for x in r for c in self._cells(_rg(x))]
        wc = [c for x in w for c in self._cells(_rg(x))]
        st = self.st
        for c in rc:
            s = st.get(c)
            if s is not None:
                if s[0] is not None:
                    deps.add(s[0])
                if c[0] == "ps":
                    for r_ in s[1]:
                        if self.ops[r_]["eng"] != eng:
                            deps.add(r_)
        for c in wc:
            s = st.get(c)
            if s is not None:
                if s[0] is not None:
                    deps.add(s[0])
                deps.update(s[1])
        for c in rc:
            s = st.get(c)
            if s is None:
                st[c] = [None, [oid]]
            else:
                s[1].append(oid)
        for c in wc:
            st[c] = [oid, []]
        deps.discard(oid)
        rec = _Rec()
        fn(rec)
        self.ops.append(dict(eng=eng, call=rec.call, deps=deps, dma=dma, sig=False, sigidx=None))
        return oid

    def emit(self):
        nc = self.nc
        ops = self.ops

        def dom(o):
            return ("dma", o["dma"]) if o["dma"] is not None else ("eng", o["eng"])

        def skip(p, engname):
            return p["dma"] is None and p["eng"] == engname and (engname == "pe" or not SAME_ENGINE_SYNC)

        for o in ops:
            for d in o["deps"]:
                p = ops[d]
                if skip(p, o["eng"]):
                    continue
                p["sig"] = True
            if o["dma"] is not None:
                o["sig"] = True
        counters = {}
        for o in ops:
            if o["sig"]:
                dm = dom(o)
                counters[dm] = counters.get(dm, 0) + 1
                o["sigidx"] = counters[dm]
        with contextlib.ExitStack() as es:
            sems = {}
            for i, dm in enumerate(counters.keys()):
                sems[dm] = es.enter_context(nc.semaphore("s%d" % i))
            block = es.enter_context(nc.Block())
            streams = {e: [] for e in self.ENGS}
            for o in ops:
                streams[o["eng"]].append(o)

            def run(engname, eng):
                waited = {}
                for o in streams[engname]:
                    need = {}
                    for d in o["deps"]:
                        p = ops[d]
                        if not p["sig"] or skip(p, engname):
                            continue
                        dm = dom(p)
                        if p["sigidx"] > need.get(dm, 0):
                            need[dm] = p["sigidx"]
                    for dm, v in need.items():
                        if waited.get(dm, 0) >= v:
                            continue
                        waited[dm] = v
                        eng.wait_ge(sems[dm], v * (16 if dm[0] == "dma" else 1))
                    nm, a_, k_ = o["call"]
                    ins = getattr(eng, nm)(*a_, **k_)
                    if DEBUG_IDS:
                        try:
                            iname = str(ins.ins.name) if hasattr(ins, "ins") else str(getattr(ins, "name", ""))
                        except Exception:
                            iname = "?"
                        if iname in DEBUG_IDS:
                            print("DEBUGID", iname, engname, nm, {kk: (getattr(vv, "ap", None), getattr(vv, "dtype", None)) if hasattr(vv, "ap") else vv for kk, vv in k_.items()})
                    if o["sig"]:
                        ins.then_inc(sems[dom(o)], 16 if o["dma"] is not None else 1)
                if engname == "sp":
                    for dm, cnt in counters.items():
                        if dm[0] == "dma":
                            eng.wait_ge(sems[dm], cnt * 16)

            block.tensor(lambda e: run("pe", e))
            block.scalar(lambda e: run("act", e))
            block.vector(lambda e: run("dve", e))
            block.gpsimd(lambda e: run("pool", e))
            block.sync(lambda e: run("sp", e))


def make_consts():
    p = np.arange(128)
    j = p[:, None].astype(np.float64)
    i = p[None, :].astype(np.float64)
    D1 = np.maximum(i - j, 0)
    U1 = (i >= j).astype(np.float64)
    D2 = np.maximum(j - i, 0)
    U2 = (j >= i).astype(np.float64)
    BIGF = BIG * (1 - U1) + np.log(16.0)
    BIGB = BIG * (1 - U2) + np.log(16.0)
    NEGF = -BIG * (1 - U2)
    NEGB = -BIG * (1 - U1)
    vec = np.stack([p + 1.0, 128.0 - p, 127.0 - p, p * 1.0], axis=1)
    inv = 10000.0 ** (-np.arange(64, dtype=np.float32) / 64.0)
    f = p % 64
    sgn = np.where(p < 64, 1.0, -1.0)
    rows = np.arange(32, dtype=np.float32)
    cols = np.arange(64, dtype=np.float32)
    angR = (rows[None, :] * inv[f][:, None]).astype(np.float32)
    angC = (cols[None, :] * inv[f][:, None]).astype(np.float32)
    cosR, sinR = np.cos(angR), np.sin(angR) * sgn[:, None]
    cosC, sinC = np.cos(angC), np.sin(angC) * sgn[:, None]
    rope = np.concatenate([cosR, sinR, cosC, sinC], axis=1)
    ropeK = rope / 16.0
    cst = np.concatenate([D1, U1, D2, U2, BIGF, BIGB, NEGF, NEGB, vec, rope, ropeK, np.eye(128), np.ones((128, 128))], axis=1)
    return np.ascontiguousarray(cst.astype(np.float32))


C_D1, C_U1, C_D2, C_U2, C_BIGF, C_BIGB, C_NEGF, C_NEGB = [k * 128 for k in range(8)]
C_VEC = 1024
C_ROPE = 1028
C_ROPEK = 1028 + 192
C_ID = 1028 + 384
C_ONES = C_ID + 128
NCST = C_ONES + 128

O_CONST = 0
SZ_CONST = 13312
O_HT = O_CONST + SZ_CONST
SZ_HT = 65536
O_W = O_HT + SZ_HT
SZ_W = 16384
O_PROJ = O_W + SZ_W
SZ_PROJ = 43008
O_MIXO = O_PROJ + SZ_PROJ
SZ_MIXO = 32768
O_MIX = O_MIXO + SZ_MIXO
SZ_MIX = 41728
ARENA = O_MIX + SZ_MIX


def build_program():
    nc = bass.Bass("TRN2", target_bir_lowering=False)

    def din(name, shape):
        return nc.dram_tensor(name, list(shape), F32, kind="ExternalInput").ap()

    def dout(name, shape):
        return nc.dram_tensor(name, list(shape), F32, kind="ExternalOutput").ap()

    xp = din("xp", [1024, D]); xs = din("xs", [2048, D]); xo = din("xo", [512, D])
    flags = din("flags", [1, 4]); cT = din("cT", [128, 32])
    st_ret = din("st_ret", [2, 4, 256, 256]); st_C = din("st_C", [2, 4, 256, 256])
    st_n = din("st_n", [2, 4, 128, 2]); st_m = din("st_m", [1, 8])
    w_mod = din("w_mod", [D, 6 * D]); b_mod = din("b_mod", [1, 6 * D]); w_in = din("w_in", [D, NIN])
    b_gate = din("b_gate", [1, 16]); conv_qkT = din("conv_qkT", [128, 48]); theta = din("theta", [1, 8])
    gn_ret = din("gn_ret", [1, 1024]); gn_ml = din("gn_ml", [1, 1024]); w_out = din("w_out", [D, D])
    ln1_g = din("ln1_g", [1, D]); ln1_b = din("ln1_b", [1, D]); w_up = din("w_up", [D, 2 * DFF])
    conv_ffT = din("conv_ffT", [128, NFF * 3]); w_down = din("w_down", [DFF, D])
    ln2_g = din("ln2_g", [1, D]); ln2_b = din("ln2_b", [1, D]); cst_d = din("cst", [128, NCST])
    yp = dout("yp", [1024, D]); ys = dout("ys", [512, D])
    o_ret = dout("o_ret", [4, 2, 4, 256, 256]); o_C = dout("o_C", [4, 2, 4, 256, 256])
    o_n = dout("o_n", [4, 2, 4, 128, 2]); o_m = dout("o_m", [4, 8])
    mod_d = nc.dram_tensor("mod_d", [2, 6 * D], F32).ap()
    x1_d = nc.dram_tensor("x1_d", [1536, D], F32).ap()
    y2_d = nc.dram_tensor("y2_d", [1536, D], F32).ap()

    P = Prog(nc)
    es = contextlib.ExitStack()
    A = es.enter_context(nc.sbuf_tensor("arena", [128, ARENA // 2], BF16))
    psf = [es.enter_context(nc.psum_tensor("psf%d" % i, [128, 512], F32)) for i in range(6)]
    psb = [es.enter_context(nc.psum_tensor("psb%d" % i, [128, 1024], BF16)) for i in range(2)]
    PSF = [Tl(psf[i][:], "ps", i, i + 1) for i in range(6)]
    PSB = [Tl(psb[i][:], "ps", 6 + i, 7 + i) for i in range(2)]
    rot = {"f": 0, "b": 0}

    held = set()

    def psn():
        for _ in range(6):
            rot["f"] = (rot["f"] + 1) % 6
            if rot["f"] not in held:
                return PSF[rot["f"]]
        raise AssertionError("no free psum bank")

    def psh():
        ps = psn()
        held.add(ps.lo)
        return ps

    def prel(ps):
        held.discard(ps.lo)

    def navail():
        return 6 - len(held)

    heldb = set()

    def psbn():
        for _ in range(2):
            rot["b"] = (rot["b"] + 1) % 2
            if rot["b"] not in heldb:
                return PSB[rot["b"]]
        raise AssertionError("no free bf16 psum bank")

    def run_pipe(facts, width=2):
        free = list(range(width)); active = []; i = 0
        while i < len(facts) or active:
            while free and i < len(facts):
                k = free.pop(0)
                active.append((facts[i](k), k))
                i += 1
            nxt = []
            for g, k in active:
                try:
                    next(g)
                    nxt.append((g, k))
                except StopIteration:
                    free.append(k)
            active = nxt

    def sbt(off, shape, dt, parts=128):
        esz = 4 if dt == F32 else 2
        n = 1
        for s in shape:
            n *= s
        nb = n * esz
        assert off % 4 == 0
        ap = A[0:parts, off // 2:(off + nb) // 2]
        if dt != BF16:
            ap = ap.bitcast(dt)
        if len(shape) == 2:
            ap = ap.rearrange("p (a b) -> p a b", a=shape[0])
        elif len(shape) == 3:
            ap = ap.rearrange("p (a b c) -> p a b c", a=shape[0], b=shape[1])
        return Tl(ap, "sb", off, off + nb)

    class Bump:
        def __init__(self, off, size):
            self.off, self.end = off, off + size

        def __call__(self, shape, dt, parts=128, align=GRAN):
            esz = 4 if dt == F32 else 2
            n = 1
            for s in shape:
                n *= s
            self.off = (self.off + align - 1) // align * align
            t = sbt(self.off, shape, dt, parts)
            self.off += n * esz
            assert self.off <= self.end, ("arena overflow", self.off, self.end)
            return t

    def DR(name, lo=0, hi=1):
        return Rg("dr_" + name, lo, hi)

    cb = Bump(O_CONST, SZ_CONST)
    CST = cb([NCST], F32)
    IDB = cb([128], BF16)
    LG = cb([8], F32)
    FLG = cb([4], F32)
    BG = cb([16], F32)
    GN = cb([256], F32)
    DEC = cb([8], F32)
    RM = cb([128], F32)
    RMT = cb([128], F32)
    SILC = cb([16, 2], BF16)
    WGT = cb([16, 16], BF16)
    CQK = cb([48], F32)
    CFF = cb([NFF * 3], F32)
    ST_C = cb([4, 6], F32); MV_C = cb([2], F32, align=4); RSTD_C = cb([1], F32, align=4); NMR_C = cb([1], F32, align=4)
    IDF = CST.ap[:, C_ID:C_ID + 128]
    ONESF = CST.ap[:, C_ONES:C_ONES + 128]

    def cstm(c0):
        return CST.ap[:, c0:c0 + 128]

    P.op("sp", lambda e: e.dma_start(out=CST[:], in_=cst_d[:, :]), w=[CST], dma="cst")
    P.op("sp", lambda e: e.dma_start(out=LG[:], in_=theta[0:1, :].broadcast_to([128, 8])), w=[LG], dma="lg")
    P.op("sp", lambda e: e.dma_start(out=FLG[:], in_=flags[0:1, :].broadcast_to([128, 4])), w=[FLG], dma="flg")
    P.op("sp", lambda e: e.dma_start(out=BG[:], in_=b_gate[0:1, :].broadcast_to([128, 16])), w=[BG], dma="bg")
    P.op("sp", lambda e: e.dma_start(out=CQK[:], in_=conv_qkT[:, :]), w=[CQK], dma="cqk")
    P.op("sp", lambda e: e.dma_start(out=CFF[:], in_=conv_ffT[:, :]), w=[CFF], dma="cff")
    P.op("pool", lambda e: e.dma_start(out=WGT[:], in_=w_in[:, 8192:8208].rearrange("(kc p) n -> p kc n", p=128)), w=[WGT], dma="wgt")
    P.op("dve", lambda e: e.tensor_copy(out=IDB[:], in_=IDF), r=[CST], w=[IDB])
    P.op("act", lambda e: e.activation(out=LG[:], in_=LG[:], func=AF.Exp, scale=-1.0), r=[LG], w=[LG])
    P.op("act", lambda e: e.activation(out=LG[:], in_=LG[:], func=AF.Ln, bias=1.0), r=[LG], w=[LG])
    P.op("dve", lambda e: e.tensor_scalar(out=LG[:], in0=LG[:], scalar1=-1.0, scalar2=None, op0=ALU.mult), r=[LG], w=[LG])

    mod_state = {"init": False}

    def stage_mod(blks, stage_off, small_off):
        ct = sbt(small_off, [32], F32)
        brow = [sbt(small_off + 256 + k * 1024, [256], F32, parts=2) for k in range(2)]
        mrow = [sbt(small_off + 256 + 2048 + k * 1024, [256], F32, parts=2) for k in range(2)]
        if not mod_state["init"]:
            mod_state["init"] = True
            P.op("sp", lambda e: e.dma_start(out=ct[:], in_=cT[:, :]), w=[ct], dma="ct")
            P.op("act", lambda e: e.activation(out=SILC[:].rearrange("p a b -> p (a b)"), in_=ct[:], func=AF.Silu), r=[ct], w=[SILC])
        wslots = [sbt(stage_off + k * 8192, [16, 256], BF16) for k in range(2)]
        for i, blk in enumerate(blks):
            ws = wslots[i % 2]; br = brow[i % 2]; mr = mrow[i % 2]
            c0 = blk * 256
            P.op("pool", lambda e, ws=ws, c0=c0: e.dma_start(out=ws[:], in_=w_mod[:, c0:c0 + 256].rearrange("(kc p) n -> p kc n", p=128)), w=[ws], dma="wm%d_%d" % (stage_off, i % 2))
            P.op("sp", lambda e, br=br, c0=c0: e.dma_start(out=br[:], in_=b_mod[0:1, c0:c0 + 256].broadcast_to([2, 256])), w=[br], dma="brow%d_%d" % (small_off, i % 2))
            ps = psn()
            for kc in range(KC):
                P.op("pe", lambda e, ps=ps, ws=ws, kc=kc: e.matmul(ps[0:2, 0:256], lhsT=SILC[:, kc, :], rhs=ws[:, kc, :], start=(kc == 0), stop=(kc == KC - 1)), r=[SILC, ws], w=[ps])
            P.op("dve", lambda e, ps=ps, mr=mr, br=br: e.tensor_tensor(out=mr[:], in0=ps[0:2, 0:256], in1=br[:], op=ALU.add), r=[ps, br], w=[mr])
            P.op("sp", lambda e, mr=mr, c0=c0: e.dma_start(out=mod_d[:, c0:c0 + 256], in_=mr[:]), r=[mr], w=[DR("mod", blk // 8, blk // 8 + 1)], dma="mrow%d_%d" % (small_off, i % 2))

    MOD_STAGE = O_HT + 32768
    MOD_SMALL = O_PROJ + 36864

    def mod_dma(blks):
        for i, blk in enumerate(blks):
            ws = sbt(MOD_STAGE + i * 8192, [16, 256], BF16)
            c0 = blk * 256
            P.op("pool", lambda e, ws=ws, c0=c0: e.dma_start(out=ws[:], in_=w_mod[:, c0:c0 + 256].rearrange("(kc p) n -> p kc n", p=128)), w=[ws], dma="wmr%d" % i)

    def mod_compute(blks):
        brow = [sbt(MOD_SMALL + k * 1024, [256], F32, parts=2) for k in range(2)]
        mrow = [sbt(MOD_SMALL + 2048 + k * 1024, [256], F32, parts=2) for k in range(2)]
        for i, blk in enumerate(blks):
            ws = sbt(MOD_STAGE + i * 8192, [16, 256], BF16)
            br = brow[i % 2]; mr = mrow[i % 2]
            c0 = blk * 256
            P.op("sp", lambda e, br=br, c0=c0: e.dma_start(out=br[:], in_=b_mod[0:1, c0:c0 + 256].broadcast_to([2, 256])), w=[br], dma="browr%d" % (i % 2))
            ps = psn()
            for kc in range(KC):
                P.op("pe", lambda e, ps=ps, ws=ws, kc=kc: e.matmul(ps[0:2, 0:256], lhsT=SILC[:, kc, :], rhs=ws[:, kc, :], start=(kc == 0), stop=(kc == KC - 1)), r=[SILC, ws], w=[ps])
            P.op("dve", lambda e, ps=ps, mr=mr, br=br: e.tensor_tensor(out=mr[:], in0=ps[0:2, 0:256], in1=br[:], op=ALU.add), r=[ps, br], w=[mr])
            P.op("sp", lambda e, mr=mr, c0=c0: e.dma_start(out=mod_d[:, c0:c0 + 256], in_=mr[:]), r=[mr], w=[DR("mod", blk // 8, blk // 8 + 1)], dma="mrowr%d" % (i % 2))

    def load_modrow(dst, q, g, plus1=False):
        P.op("sp", lambda e: e.dma_start(out=dst[:], in_=mod_d[g:g + 1, q * D:(q + 1) * D].broadcast_to([128, D])), r=[DR("mod", q, q + 1)], w=[dst], dma="mr_%d" % (dst.lo))
        if plus1:
            P.op("dve", lambda e: e.tensor_scalar(out=dst[:], in0=dst[:], scalar1=1.0, scalar2=None, op0=ALU.add), r=[dst], w=[dst])

    def load_row(dst, src):
        P.op("sp", lambda e: e.dma_start(out=dst[:], in_=src[0:1, :].broadcast_to([128, src.shape[1]])), w=[dst], dma="lr_%d" % (dst.lo))

    def ln_stats(x_ap, xr, st, mv, rstd, nmr, n):
        nch = max(1, n // 512)
        w_ = n // nch
        for c in range(nch):
            P.op("dve", lambda e, c=c: e.bn_stats(out=st[:, c, :], in_=x_ap[:, c * w_:(c + 1) * w_]), r=[xr], w=[st])
        P.op("dve", lambda e: e.bn_aggr(out=mv[:], in_=st[:, 0:nch, :].rearrange("p a b -> p (a b)")), r=[st], w=[mv])
        P.op("act", lambda e: e.activation(out=rstd[:], in_=mv[:, 1:2], func=AF.Ln, bias=1e-6), r=[mv], w=[rstd])
        P.op("act", lambda e: e.activation(out=rstd[:], in_=rstd[:], func=AF.Exp, scale=-0.5), r=[rstd], w=[rstd])
        if nmr is not None:
            P.op("dve", lambda e: e.scalar_tensor_tensor(out=nmr[:], in0=mv[:, 0:1], scalar=-1.0, in1=rstd[:], op0=ALU.mult, op1=ALU.mult), r=[mv, rstd], w=[nmr])

    def transpose_tile(src, src_r, dstT, dst_r, tcol):
        for _ in transpose_tile_g(src, src_r, dstT, dst_r, tcol):
            pass

    def transpose_tile_g(src, src_r, dstT, dst_r_, tcol):
        for half in range(2):
            while len(heldb) >= 2:
                yield
            pb = psbn()
            heldb.add(pb.lo - 6)
            dst_r = dstT.part(half, 2)
            for k in range(8):
                kc = half * 8 + k
                P.op("pe", lambda e, pb=pb, k=k, kc=kc: e.transpose(out=pb[:, k * 128:(k + 1) * 128], in_=src[:, kc * 128:(kc + 1) * 128], identity=IDB[:]), r=[src_r, IDB], w=[pb])
            eng = "act" if half == 0 else "dve"
            if eng == "act":
                P.op("act", lambda e, pb=pb, half=half: e.activation(out=dstT[:, half * 8:half * 8 + 8, tcol:tcol + 128], in_=pb[:].rearrange("p (a b) -> p a b", a=8), func=AF.Copy), r=[pb], w=[dst_r])
            else:
                P.op("dve", lambda e, pb=pb, half=half: e.tensor_copy(out=dstT[:, half * 8:half * 8 + 8, tcol:tcol + 128], in_=pb[:].rearrange("p (a b) -> p a b", a=8)), r=[pb], w=[dst_r])
            heldb.discard(pb.lo - 6)
            yield

    def process_group(gi):
        sample = gi == 1
        T = 2048 if sample else 1024
        nseq = 1 if sample else 4
        n = 16 if sample else 2
        NT = T // 128
        Town = 512 if sample else 1024
        NTO = Town // 128
        x_all = xs if sample else xp
        x_own = xo if sample else xp
        y_out = ys if sample else yp
        x1off = 1024 if sample else 0
        mg = 1 if sample else 0
        Lc = 64 if sample else 256
        HT = sbt(O_HT, [16, T], BF16)

        mb = Bump(O_MIX, SZ_MIX)
        r_sc = mb([D], F32); r_sh = mb([D], F32)
        sm_ = [mb([64], F32) for _ in range(2)]
        load_modrow(r_sh, 0, mg)
        load_modrow(r_sc, 1, mg, plus1=True)
        pbm = Bump(O_PROJ, SZ_PROJ)
        xts = [pbm([D], F32) for _ in range(2)]
        hts = [pbm([D], BF16) for _ in range(2)]

        def small4(tl):
            mk = lambda off, shape: Tl(sbt(tl.lo + off * 4, shape, F32).ap, "sb", tl.lo, tl.hi)
            return mk(0, [4, 6]), mk(24, [2]), mk(26, [1]), mk(27, [1])

        def a1_tile(t):
            def g(k):
                xt = xts[k]; ht = hts[k]
                st, mv, rstd, nmr = small4(sm_[k])
                P.op("sp", lambda e: e.dma_start(out=xt[:], in_=x_all[t * 128:(t + 1) * 128, :]), w=[xt], dma="xt%d" % k)
                yield
                ln_stats(xt.ap, xt, st, mv, rstd, None, D)
                yield
                P.op("dve", lambda e: e.scalar_tensor_tensor(out=xt[:], in0=xt[:], scalar=mv[:, 0:1], in1=r_sc[:], op0=ALU.subtract, op1=ALU.mult), r=[xt, mv, r_sc], w=[xt])
                yield
                P.op("dve", lambda e: e.scalar_tensor_tensor(out=ht[:], in0=xt[:], scalar=rstd[:], in1=r_sh[:], op0=ALU.mult, op1=ALU.add), r=[xt, rstd, r_sh], w=[ht])
                yield
                yield from transpose_tile_g(ht.ap, ht, HT, HT, t * 128)
            return g
        run_pipe([a1_tile(t) for t in range(NT)], 2)

        pbm = Bump(O_PROJ, SZ_PROJ)
        QT = pbm([2, T], BF16)
        KT = pbm([2, T], BF16)
        VG = pbm([NT, 520], BF16)
        KTM = pbm([NT, 256], BF16)
        mb = Bump(O_MIX, SZ_MIX)
        NSL = 1 if sample else 2
        MST = [[mb([2, 257], F32) for d in range(2)] for sl in range(NSL)]
        KX = [[mb([256], BF16) for d in range(2)] for sl in range(NSL)]
        SALL = []
        for sl in range(NSL):
            per_d = []
            for d in range(2):
                if sample and d == 0:
                    first = mb([2, 257], BF16)
                    rest = sbt(O_MIXO + 16384, [n - 1, 2, 257], BF16)
                    per_d.append([first] + [Tl(rest[:, c], "sb", rest.lo + c * 1028, rest.lo + (c + 1) * 1028) for c in range(n - 1)])
                else:
                    arr = mb([n, 2, 257], BF16)
                    per_d.append([Tl(arr[:, c], "sb", arr.lo + c * 1028, arr.lo + (c + 1) * 1028) for c in range(n)])
            SALL.append(per_d)
        Dm = mb([4, 128], F32)
        Am = mb([4, 128], F32)
        tmpc = [sbt(Dm.lo, [512], F32), sbt(Am.lo, [512], F32)]
        ATT = [sbt(Am.lo + k * 512, [2, 128], BF16) for k in range(2)]
        WTS = [sbt(Am.lo + 1024 + k * 512, [128], F32) for k in range(2)]
        DMS = [sbt(Dm.lo + k * 1024, [2, 128], F32) for k in range(2)]
        TOTT = [mb([2, 257], F32) for k in range(2)]
        TOT = [[Tl(TOTT[k][:, d, :], "sb", TOTT[k].lo + d * 1028, TOTT[k].lo + (d + 1) * 1028) for d in range(2)] for k in range(2)]
        XN = [mb([256], F32) for k in range(2)]
        SMALL = [mb([64], F32) for k in range(2)]
        GTS = mb([NT, 16], F32)
        LFN = mb([NT, 16], F32)
        GA = mb([NT, 8], F32); GNB = mb([NT, 8], F32); GMU = mb([NT, 8], F32); GSP = mb([NT, 8], F32)
        GWK = mb([NT, 8], F32); GSC = mb([NT, 8], F32); GEM = mb([NT, 8], F32); GBL = mb([NT, 8], F32)
        mprev = mb([8], F32); mnew = mb([8], F32, align=4); v8 = [mb([8], F32, align=4) for _ in range(4)]
        MIXO = sbt(O_MIXO, [NTO, D], BF16)
        wsl = [sbt(O_W + k * 8192, [16, 256], BF16) for k in range(2)]
        wcount = [0]

        def load_w(c0):
            ws = wsl[wcount[0] % 2]
            k = wcount[0] % 2
            wcount[0] += 1
            P.op("pool", lambda e: e.dma_start(out=ws[:], in_=w_in[:, c0:c0 + 256].rearrange("(kc p) n -> p kc n", p=128)), w=[ws], dma="wsl%d" % k)
            return ws

        P.op("dve", lambda e: e.memset(VG[:, :, 256:257], 1.0), w=[VG])

        def proj_fm(ws, dst, kind, chan0):
            for dc in range(2):
                for tb in range(T // 512):
                    ps = psn()
                    for kc in range(KC):
                        P.op("pe", lambda e, ps=ps, kc=kc, dc=dc, tb=tb: e.matmul(ps[:], lhsT=ws[:, kc, dc * 128:(dc + 1) * 128], rhs=HT[:, kc, tb * 512:(tb + 1) * 512], start=(kc == 0), stop=(kc == KC - 1)), r=[ws, HT], w=[ps])
                    dsl = dst[:, dc, tb * 512:(tb + 1) * 512]
                    if kind in ("rq", "rk"):
                        if not sample:
                            sc_ = 1.0 if kind == "rq" else 1.0 / 16.0
                            P.op("act", lambda e, ps=ps, dsl=dsl, sc_=sc_: e.activation(out=dsl, in_=ps[:], func=AF.Identity, scale=sc_), r=[ps], w=[dst])
                        else:
                            base = C_ROPE if kind == "rq" else C_ROPEK
                            if dc == 0:
                                cosb = CST.ap[:, base + 8 * tb: base + 8 * tb + 8].unsqueeze(2).to_broadcast([128, 8, 64])
                                sinb = CST.ap[:, base + 32 + 8 * tb: base + 32 + 8 * tb + 8].unsqueeze(2).to_broadcast([128, 8, 64])
                            else:
                                cosb = CST.ap[:, base + 64: base + 128].unsqueeze(1).to_broadcast([128, 8, 64])
                                sinb = CST.ap[:, base + 128: base + 192].unsqueeze(1).to_broadcast([128, 8, 64])
                            t1 = tmpc[0]; t2 = tmpc[1]
                            v3 = lambda ap: ap.rearrange("p (a b) -> p a b", a=8)
                            P.op("dve", lambda e, ps=ps, cosb=cosb: e.tensor_tensor(out=v3(t1[:]), in0=v3(ps[:]), in1=cosb, op=ALU.mult), r=[ps, CST], w=[t1])
                            P.op("dve", lambda e, ps=ps, sinb=sinb: e.tensor_tensor(out=v3(t2[:])[0:64], in0=v3(ps[:])[64:128], in1=sinb[64:128], op=ALU.mult), r=[ps, CST], w=[t2])
                            P.op("dve", lambda e, ps=ps, sinb=sinb: e.tensor_tensor(out=v3(t2[:])[64:128], in0=v3(ps[:])[0:64], in1=sinb[0:64], op=ALU.mult), r=[ps, CST], w=[t2])
                            P.op("dve", lambda e, dsl=dsl: e.tensor_tensor(out=dsl, in0=t1[:], in1=t2[:], op=ALU.add), r=[t1, t2], w=[dst])
                    else:
                        ch = chan0 // 128 + dc
                        w0 = CQK.ap[:, ch * 3 + 0: ch * 3 + 1]; w1 = CQK.ap[:, ch * 3 + 1: ch * 3 + 2]; w2 = CQK.ap[:, ch * 3 + 2: ch * 3 + 3]
                        z = tmpc[0]
                        nb_ = 512 // Lc
                        v3 = lambda ap: ap.rearrange("p (a b) -> p a b", a=nb_)
                        P.op("dve", lambda e, ps=ps, w1=w1: e.tensor_scalar(out=z[:], in0=ps[:], scalar1=w1, scalar2=None, op0=ALU.mult), r=[ps, CQK], w=[z])
                        P.op("dve", lambda e, ps=ps, w0=w0: e.scalar_tensor_tensor(out=v3(z[:])[:, :, 1:Lc], in0=v3(ps[:])[:, :, 0:Lc - 1], scalar=w0, in1=v3(z[:])[:, :, 1:Lc], op0=ALU.mult, op1=ALU.add), r=[ps, CQK, z], w=[z])
                        P.op("dve", lambda e, ps=ps, w2=w2: e.scalar_tensor_tensor(out=v3(z[:])[:, :, 0:Lc - 1], in0=v3(ps[:])[:, :, 1:Lc], scalar=w2, in1=v3(z[:])[:, :, 0:Lc - 1], op0=ALU.mult, op1=ALU.add), r=[ps, CQK, z], w=[z])
                        P.op("act", lambda e, dsl=dsl: e.activation(out=dsl, in_=z[:], func=AF.Silu), r=[z], w=[dst])

        def proj_tm(wv, wg, gfunc):
            for t in range(NT):
                ps = psn()
                for j, ws in enumerate((wv, wg)):
                    for kc in range(KC):
                        P.op("pe", lambda e, ps=ps, kc=kc, j=j, ws=ws, t=t: e.matmul(ps[:, j * 256:(j + 1) * 256], lhsT=HT[:, kc, t * 128:(t + 1) * 128], rhs=ws[:, kc, :], start=(kc == 0), stop=(kc == KC - 1)), r=[ws, HT], w=[ps])
                P.op("dve", lambda e, ps=ps, t=t: e.tensor_copy(out=VG[:, t, 0:256], in_=ps[:, 0:256]), r=[ps], w=[VG.part(t, NT)])
                P.op("act", lambda e, ps=ps, t=t: e.activation(out=VG[:, t, 264:520], in_=ps[:, 256:512], func=gfunc), r=[ps], w=[VG.part(t, NT)])

        def make_ktm(kscale=1.0):
            for t in range(NT):
                pb = psbn()
                for dc in range(2):
                    P.op("pe", lambda e, pb=pb, dc=dc, t=t: e.transpose(out=pb[:, dc * 128:(dc + 1) * 128], in_=KT[:, dc, t * 128:(t + 1) * 128], identity=IDB[:]), r=[KT, IDB], w=[pb])
                P.op("dve", lambda e, pb=pb, t=t: e.tensor_scalar(out=KTM[:, t, :], in0=pb[:, 0:256], scalar1=kscale, scalar2=None, op0=ALU.mult), r=[pb], w=[KTM.part(t, NT)])

        def smalls(k):
            o = SMALL[k].lo
            mk = lambda off, shape: Tl(sbt(o + off * 4, shape, F32).ap, "sb", SMALL[k].lo, SMALL[k].hi)
            return mk(0, [4, 6]), mk(24, [2]), mk(26, [1]), mk(27, [1]), mk(28, [2])

        def out_tail(k, t, hh):
            xn = XN[k]
            st, mv, rstd, nmr, _ = smalls(k)
            ln_stats(xn.ap, xn, st, mv, rstd, None, 256)
            yield
            P.op("dve", lambda e: e.scalar_tensor_tensor(out=xn[:], in0=xn[:], scalar=mv[:, 0:1], in1=GN[:], op0=ALU.subtract, op1=ALU.mult), r=[xn, mv, GN], w=[xn])
            yield
            col = hh * 256
            gate = VG[:, t, 264:520]
            gr = VG.part(t, NT)
            if not sample:
                P.op("dve", lambda e: e.scalar_tensor_tensor(out=MIXO[:, t, col:col + 256], in0=xn[:], scalar=rstd[:], in1=gate, op0=ALU.mult, op1=ALU.mult), r=[xn, rstd, gr], w=[MIXO.part(t, NTO)])
            else:
                pp, tp = t // 4, t % 4
                P.op("dve", lambda e: e.scalar_tensor_tensor(out=xn[:], in0=xn[:], scalar=rstd[:], in1=gate, op0=ALU.mult, op1=ALU.mult), r=[xn, rstd, gr], w=[xn])
                yield
                if pp == 0:
                    P.op("dve", lambda e: e.tensor_scalar(out=MIXO[:, tp, col:col + 256], in0=xn[:], scalar1=FLG[:, 0:1], scalar2=None, op0=ALU.mult), r=[xn, FLG], w=[MIXO.part(tp, NTO)])
                else:
                    P.op("dve", lambda e: e.scalar_tensor_tensor(out=MIXO[:, tp, col:col + 256], in0=xn[:], scalar=FLG[:, pp:pp + 1], in1=MIXO[:, tp, col:col + 256], op0=ALU.mult, op1=ALU.add), r=[xn, FLG, MIXO.part(tp, NTO)], w=[MIXO.part(tp, NTO)])
            yield

        def run_rr(gens):
            gens = list(gens)
            while gens:
                nxt = []
                for g in gens:
                    try:
                        next(g)
                        nxt.append(g)
                    except StopIteration:
                        pass
                gens = nxt

        def state_chain(sl, s, d, h, kind):
            ret = kind == "ret"
            ncol = 256 if ret else 257
            SM = MST[sl][d]; Kx = KX[sl][d]
            src4 = st_ret if ret else st_C
            srcn = None if ret else st_n
            if not sample:
                P.op("dve", lambda e: e.memset(SM[:], 0.0), w=[SM])
            else:
                P.op("sp", lambda e: e.dma_start(out=SM[:, :, 0:256], in_=src4[d, h].rearrange("(kc p) v -> p kc v", p=128)), w=[SM], dma="sm%d" % SM.lo)
                if srcn is not None:
                    P.op("sp", lambda e: e.dma_start(out=SM[:, :, 256:257], in_=srcn[d, h].unsqueeze(2), allow_slow_non_contiguous=True), w=[SM], dma="sm%d" % SM.lo)
            yield
            order = range(n) if d == 0 else range(n - 1, -1, -1)
            for c in order:
                t = s * n + c
                sa = SALL[sl][d][c]
                P.op("act", lambda e, sa=sa: e.activation(out=sa[:, :, 0:ncol], in_=SM[:, :, 0:ncol], func=AF.Copy), r=[SM], w=[sa])
                if ret:
                    sc_ap, sc_r = DEC.ap[:, 2 + d:3 + d], DEC
                    dk_ap, dk_r = DEC.ap[:, 4 + d:5 + d], DEC
                else:
                    sc_ap, sc_r = GWK.ap[:, t, d * 4 + h:d * 4 + h + 1], GWK
                    dk_ap, dk_r = GSC.ap[:, t, d * 4 + h:d * 4 + h + 1], GSC
                P.op("act", lambda e, t=t, sc_ap=sc_ap: e.activation(out=Kx[:], in_=KTM[:, t, :], func=AF.Identity, scale=sc_ap), r=[KTM.part(t, NT), sc_r], w=[Kx])
                yield
                if ret:
                    while navail() < 1:
                        yield
                    ps = psh()
                    for kc in range(2):
                        P.op("pe", lambda e, kc=kc, ps=ps, t=t: e.matmul(ps[:, kc * 256:(kc + 1) * 256], lhsT=Kx[:, kc * 128:(kc + 1) * 128], rhs=VG[:, t, 0:256], start=True, stop=True), r=[Kx, VG.part(t, NT)], w=[ps])
                    yield
                    P.op("dve", lambda e, ps=ps, dk_ap=dk_ap: e.scalar_tensor_tensor(out=SM[:, :, 0:256], in0=SM[:, :, 0:256], scalar=dk_ap, in1=ps[:].rearrange("p (a b) -> p a b", a=2), op0=ALU.mult, op1=ALU.add), r=[SM, ps, dk_r], w=[SM])
                    prel(ps)
                    yield
                else:
                    while navail() < 2:
                        yield
                    pss = [psh(), psh()]
                    for kc in range(2):
                        P.op("pe", lambda e, kc=kc, ps=pss[kc], t=t: e.matmul(ps[:, 0:257], lhsT=Kx[:, kc * 128:(kc + 1) * 128], rhs=VG[:, t, 0:257], start=True, stop=True), r=[Kx, VG.part(t, NT)], w=[pss[kc]])
                    yield
                    for kc in range(2):
                        P.op("dve", lambda e, ps=pss[kc], kc=kc, dk_ap=dk_ap: e.scalar_tensor_tensor(out=SM[:, kc, :], in0=SM[:, kc, :], scalar=dk_ap, in1=ps[:, 0:257], op0=ALU.mult, op1=ALU.add), r=[SM, pss[kc], dk_r], w=[SM])
                        prel(pss[kc])
                    yield
            if not sample:
                dst4 = o_ret if ret else o_C
                P.op("sp", lambda e: e.dma_start(out=dst4[s, d, h].rearrange("(kc p) v -> p kc v", p=128), in_=SM[:, :, 0:256]), r=[SM], dma="smo%d" % SM.lo)
                if not ret:
                    P.op("sp", lambda e: e.dma_start(out=o_n[s, d, h].unsqueeze(2), in_=SM[:, :, 256:257], allow_slow_non_contiguous=True), r=[SM], dma="smo%d" % SM.lo)
            yield

        def out_chunk_ret(k, sl, s, c, h):
            t = s * n + c
            tk = slice(t * 128, (t + 1) * 128)
            attm = ATT[k]; xn = XN[k]
            while navail() < 2:
                yield
            psA = psh()
            for kc in range(2):
                P.op("pe", lambda e, kc=kc: e.matmul(psA[:, 0:128], lhsT=KT[:, kc, tk], rhs=QT[:, kc, tk], start=(kc == 0), stop=(kc == 1)), r=[KT, QT], w=[psA])
            psS = psh()
            for d in range(2):
                sa = SALL[sl][d][c]
                for kc in range(2):
                    P.op("pe", lambda e, kc=kc, d=d, sa=sa: e.matmul(psS[:, d * 256:(d + 1) * 256], lhsT=QT[:, kc, tk], rhs=sa[:, kc, 0:256], start=(kc == 0), stop=(kc == 1)), r=[QT, sa], w=[psS])
            yield
            P.op("dve", lambda e: e.tensor_tensor(out=attm[:, 0, :], in0=psA[:, 0:128], in1=RM[:], op=ALU.mult), r=[psA, RM], w=[attm])
            prel(psA)
            yield
            while navail() < 1:
                yield
            psO = psh()
            P.op("pe", lambda e: e.matmul(psO[:, 0:256], lhsT=attm[:, 0, :], rhs=VG[:, t, 0:256], start=True, stop=True), r=[attm, VG.part(t, NT)], w=[psO])
            P.op("dve", lambda e: e.tensor_scalar(out=xn[:], in0=psS[:, 0:256], scalar1=DEC[:, 0:1], scalar2=None, op0=ALU.mult), r=[psS, DEC], w=[xn])
            yield
            P.op("dve", lambda e: e.scalar_tensor_tensor(out=xn[:], in0=psS[:, 256:512], scalar=DEC[:, 1:2], in1=xn[:], op0=ALU.mult, op1=ALU.add), r=[psS, DEC, xn], w=[xn])
            yield
            P.op("dve", lambda e: e.tensor_tensor(out=xn[:], in0=psO[:, 0:256], in1=xn[:], op=ALU.add), r=[psO, xn], w=[xn])
            prel(psS); prel(psO)
            yield
            yield from out_tail(k, t, h)

        def out_chunk_ml(k, sl, s, c, h):
            t = s * n + c
            tk = slice(t * 128, (t + 1) * 128)
            attm = ATT[k]; xn = XN[k]; Dk = DMS[k]; WT = WTS[k]
            _, _, _, _, dd = smalls(k)
            bigm = CST.ap[:, C_BIGF:C_BIGF + 256]
            while navail() < 1:
                yield
            psA = psh()
            for kc in range(2):
                P.op("pe", lambda e, kc=kc: e.matmul(psA[:, 0:128], lhsT=KT[:, kc, tk], rhs=QT[:, kc, tk], start=(kc == 0), stop=(kc == 1)), r=[KT, QT], w=[psA])
            mu2 = GMU.ap[:, t, h:h + 5:4]
            P.op("dve", lambda e: e.tensor_tensor(out=Dk[:], in0=IDF.unsqueeze(1).to_broadcast([128, 2, 128]), in1=mu2.unsqueeze(2).to_broadcast([128, 2, 128]), op=ALU.mult), r=[CST, GMU], w=[Dk])
            yield
            P.op("pe", lambda e: e.matmul(psA[:, 128:384], lhsT=ONESF, rhs=Dk[:].rearrange("p a b -> p (a b)"), start=True, stop=False), r=[CST, Dk], w=[psA])
            P.op("pe", lambda e: e.matmul(psA[:, 128:384], lhsT=IDF, rhs=bigm, start=False, stop=True), r=[CST], w=[psA])
            yield
            for d in range(2):
                col = d * 4 + h
                P.op("act", lambda e, d=d, col=col: e.activation(out=WT[:], in_=psA[:, 128 + d * 128:256 + d * 128], func=AF.Exp, bias=GA[:, t, col:col + 1], scale=-1.0), r=[psA, GA], w=[WT])
                yield
                P.op("dve", lambda e, d=d: e.tensor_tensor(out=attm[:, d, :], in0=psA[:, 0:128], in1=WT[:], op=ALU.mult), r=[psA, WT], w=[attm])
                yield
            prel(psA)
            for d in range(2):
                col = d * 4 + h
                sa = SALL[sl][d][c]
                td = TOT[k][d]
                while navail() < 2:
                    yield
                psN = psh()
                P.op("pe", lambda e, d=d, psN=psN: e.matmul(psN[:, 0:257], lhsT=attm[:, d, :], rhs=VG[:, t, 0:257], start=True, stop=True), r=[attm, VG.part(t, NT)], w=[psN])
                psI = psh()
                for kc in range(2):
                    P.op("pe", lambda e, kc=kc, psI=psI, sa=sa: e.matmul(psI[:, 0:257], lhsT=QT[:, kc, tk], rhs=sa[:, kc, :], start=(kc == 0), stop=(kc == 1)), r=[QT, sa], w=[psI])
                yield
                P.op("dve", lambda e, psI=psI, td=td, col=col: e.tensor_scalar(out=td[:], in0=psI[:, 0:257], scalar1=GSP[:, t, col:col + 1], scalar2=None, op0=ALU.mult), r=[psI, GSP], w=[td])
                prel(psI)
                yield
                P.op("dve", lambda e, psN=psN, td=td: e.tensor_tensor(out=td[:], in0=psN[:, 0:257], in1=td[:], op=ALU.add), r=[psN, td], w=[td])
                prel(psN)
                yield
            den2 = TOTT[k][:, :, 256:257]
            P.op("dve", lambda e: e.scalar_tensor_tensor(out=dd[:, 0:2].unsqueeze(2), in0=den2, scalar=-1.0, in1=den2, op0=ALU.mult, op1=ALU.max), r=[TOTT[k]], w=[dd])
            yield
            P.op("dve", lambda e: e.tensor_tensor(out=dd[:, 0:2], in0=dd[:, 0:2], in1=GEM.ap[:, t, h:h + 5:4], op=ALU.max), r=[dd, GEM], w=[dd])
            yield
            P.op("dve", lambda e: e.reciprocal(out=dd[:], in_=dd[:]), r=[dd], w=[dd])
            yield
            P.op("dve", lambda e: e.tensor_scalar(out=xn[:], in0=TOT[k][0][:, 0:256], scalar1=dd[:, 0:1], scalar2=None, op0=ALU.mult), r=[TOT[k][0], dd], w=[xn])
            yield
            P.op("dve", lambda e: e.scalar_tensor_tensor(out=xn[:], in0=TOT[k][1][:, 0:256], scalar=dd[:, 1:2], in1=xn[:], op0=ALU.mult, op1=ALU.add), r=[TOT[k][1], dd, xn], w=[xn])
            yield
            yield from out_tail(k, t, 4 + h)

        def mixer(h, kind):
            ret = kind == "ret"
            if ret:
                lgf = LG.ap[:, h:h + 1]; lgb = LG.ap[:, 4 + h:5 + h]
                vec = lambda k_: CST.ap[:, C_VEC + k_:C_VEC + k_ + 1]
                for col, (src, lg) in enumerate([(vec(0), lgf), (vec(1), lgb), (vec(2), lgf), (vec(3), lgb)]):
                    P.op("act", lambda e, col=col, src=src, lg=lg: e.activation(out=DEC[:, col:col + 1], in_=src, func=AF.Exp, scale=lg), r=[CST, LG], w=[DEC])
                P.op("act", lambda e: e.activation(out=DEC[:, 4:5], in_=lgf, func=AF.Exp, scale=128.0), r=[LG], w=[DEC])
                P.op("act", lambda e: e.activation(out=DEC[:, 5:6], in_=lgb, func=AF.Exp, scale=128.0), r=[LG], w=[DEC])
                P.op("act", lambda e: e.activation(out=RM[:], in_=cstm(C_D1), func=AF.Exp, scale=lgf), r=[CST, LG], w=[RM])
                P.op("dve", lambda e: e.tensor_tensor(out=RM[:], in0=RM[:], in1=cstm(C_U1), op=ALU.mult), r=[RM, CST], w=[RM])
                P.op("act", lambda e: e.activation(out=RMT[:], in_=cstm(C_D2), func=AF.Exp, scale=lgb), r=[CST, LG], w=[RMT])
                P.op("dve", lambda e: e.tensor_tensor(out=RMT[:], in0=RMT[:], in1=cstm(C_U2), op=ALU.mult), r=[RMT, CST], w=[RMT])
                P.op("dve", lambda e: e.tensor_tensor(out=RM[:], in0=RM[:], in1=RMT[:], op=ALU.add), r=[RM, RMT], w=[RM])
            gsrc = gn_ret if ret else gn_ml
            P.op("sp", lambda e: e.dma_start(out=GN[:], in_=gsrc[0:1, h * 256:(h + 1) * 256].broadcast_to([128, 256])), w=[GN], dma="gn")
            ocf = out_chunk_ret if ret else out_chunk_ml
            for s0 in range(0, nseq, NSL):
                run_rr([state_chain(sl, s0 + sl, d, h, kind) for sl in range(NSL) for d in range(2)])
                jobs = [(sl, s0 + sl, c) for sl in range(NSL) for c in range(n)]
                for j0 in range(0, len(jobs), 2):
                    run_rr([ocf(k, jobs[j0 + k][0], jobs[j0 + k][1], jobs[j0 + k][2], h) for k in range(min(2, len(jobs) - j0))])

        def gates_pre():
            for t in range(NT):
                ps = psn()
                for kc in range(KC):
                    P.op("pe", lambda e, ps=ps, kc=kc, t=t: e.matmul(ps[:, 0:16], lhsT=HT[:, kc, t * 128:(t + 1) * 128], rhs=WGT[:, kc, :], start=(kc == 0), stop=(kc == KC - 1)), r=[HT, WGT], w=[ps])
                P.op("dve", lambda e, ps=ps, t=t: e.tensor_tensor(out=GTS[:, t, :], in0=ps[:, 0:16], in1=BG[:], op=ALU.add), r=[ps, BG], w=[GTS])
            P.op("act", lambda e: e.activation(out=LFN[:], in_=GTS[:], func=AF.Exp, scale=-1.0), r=[GTS], w=[LFN])
            P.op("act", lambda e: e.activation(out=LFN[:], in_=LFN[:], func=AF.Ln, bias=1.0), r=[LFN], w=[LFN])
            for t in range(NT):
                ps = psn()
                P.op("pe", lambda e, ps=ps, t=t: e.matmul(ps[:, 0:4], lhsT=cstm(C_U1), rhs=LFN[:, t, 4:8], start=True, stop=True), r=[CST, LFN], w=[ps])
                P.op("pe", lambda e, ps=ps, t=t: e.matmul(ps[:, 4:8], lhsT=cstm(C_U2), rhs=LFN[:, t, 12:16], start=True, stop=True), r=[CST, LFN], w=[ps])
                P.op("pe", lambda e, ps=ps, t=t: e.matmul(ps[:, 8:12], lhsT=ONESF, rhs=LFN[:, t, 4:8], start=True, stop=True), r=[CST, LFN], w=[ps])
                P.op("pe", lambda e, ps=ps, t=t: e.matmul(ps[:, 12:16], lhsT=ONESF, rhs=LFN[:, t, 12:16], start=True, stop=True), r=[CST, LFN], w=[ps])
                P.op("dve", lambda e, ps=ps, t=t: e.tensor_copy(out=GNB[:, t, :], in_=ps[:, 0:8]), r=[ps], w=[GNB])
                P.op("dve", lambda e, ps=ps, t=t: e.tensor_copy(out=GBL[:, t, :], in_=ps[:, 8:16]), r=[ps], w=[GBL])
                P.op("dve", lambda e, t=t: e.tensor_tensor(out=GA[:, t, 0:4], in0=GTS[:, t, 0:4], in1=GNB[:, t, 0:4], op=ALU.add), r=[GTS, GNB], w=[GA])
                P.op("dve", lambda e, t=t: e.tensor_tensor(out=GA[:, t, 4:8], in0=GTS[:, t, 8:12], in1=GNB[:, t, 4:8], op=ALU.add), r=[GTS, GNB], w=[GA])
            for s in range(nseq):
                if not sample:
                    P.op("dve", lambda e: e.memset(mprev[:], 0.0), w=[mprev])
                else:
                    P.op("sp", lambda e: e.dma_start(out=mprev[:], in_=st_m[0:1, :].broadcast_to([128, 8])), w=[mprev], dma="mprev")
                for d in range(2):
                    ds = slice(d * 4, d * 4 + 4)
                    order = range(n) if d == 0 else range(n - 1, -1, -1)
                    neg = cstm(C_NEGF if d == 0 else C_NEGB)
                    for c in order:
                        t = s * n + c
                        P.op("dve", lambda e, t=t, ds=ds: e.tensor_tensor(out=Dm[:], in0=IDF.unsqueeze(1).to_broadcast([128, 4, 128]), in1=GA[:, t, ds].unsqueeze(2).to_broadcast([128, 4, 128]), op=ALU.mult), r=[CST, GA], w=[Dm])
                        ps = psn()
                        P.op("pe", lambda e, ps=ps: e.matmul(ps[:], lhsT=ONESF, rhs=Dm[:].rearrange("p a b -> p (a b)"), start=True, stop=True), r=[CST, Dm], w=[ps])
                        P.op("dve", lambda e, ps=ps, neg=neg: e.tensor_tensor(out=Am[:], in0=ps[:].rearrange("p (a b) -> p a b", a=4), in1=neg.unsqueeze(1).to_broadcast([128, 4, 128]), op=ALU.add), r=[ps, CST], w=[Am])
                        P.op("dve", lambda e: e.tensor_reduce(out=v8[0][:, 0:4], in_=Am[:], axis=AX.X, op=ALU.max), r=[Am], w=[v8[0]])
                        P.op("dve", lambda e, ps=ps: e.tensor_reduce(out=v8[1][:, 0:4], in_=ps[:].rearrange("p (a b) -> p a b", a=4), axis=AX.X, op=ALU.max), r=[ps], w=[v8[1]])
                        P.op("dve", lambda e, t=t, ds=ds: e.tensor_tensor(out=GMU[:, t, ds], in0=v8[0][:, 0:4], in1=mprev[:, ds], op=ALU.max), r=[v8[0], mprev], w=[GMU])
                        P.op("dve", lambda e, ds=ds: e.tensor_tensor(out=mnew[:, ds], in0=v8[1][:, 0:4], in1=mprev[:, ds], op=ALU.max), r=[v8[1], mprev], w=[mnew])
                        P.op("dve", lambda e, t=t, ds=ds: e.tensor_tensor(out=v8[2][:, 0:4], in0=mprev[:, ds], in1=GMU[:, t, ds], op=ALU.subtract), r=[mprev, GMU], w=[v8[2]])
                        P.op("act", lambda e, t=t, ds=ds: e.activation(out=GSP[:, t, ds], in_=v8[2][:, 0:4], func=AF.Exp), r=[v8[2]], w=[GSP])
                        P.op("dve", lambda e, t=t, ds=ds: e.tensor_tensor(out=v8[3][:, 0:4], in0=GA[:, t, ds], in1=mnew[:, ds], op=ALU.subtract), r=[GA, mnew], w=[v8[3]])
                        P.op("act", lambda e, t=t, ds=ds: e.activation(out=GWK[:, t, ds], in_=v8[3][:, 0:4], func=AF.Exp), r=[v8[3]], w=[GWK])
                        P.op("dve", lambda e, ds=ds: e.tensor_tensor(out=v8[2][:, 4:8], in0=mprev[:, ds], in1=mnew[:, ds], op=ALU.subtract), r=[mprev, mnew], w=[v8[2]])
                        P.op("act", lambda e, t=t, ds=ds: e.activation(out=GSC[:, t, ds], in_=v8[2][:, 4:8], func=AF.Exp), r=[v8[2]], w=[GSC])
                        P.op("dve", lambda e, t=t, ds=ds: e.tensor_tensor(out=v8[3][:, 4:8], in0=GNB[:, t, ds], in1=GMU[:, t, ds], op=ALU.subtract), r=[GNB, GMU], w=[v8[3]])
                        P.op("act", lambda e, t=t, ds=ds: e.activation(out=GEM[:, t, ds], in_=v8[3][:, 4:8], func=AF.Exp), r=[v8[3]], w=[GEM])
                        P.op("dve", lambda e, ds=ds, t=t: e.tensor_tensor(out=mprev[:, ds], in0=mnew[:, ds], in1=GBL[:, t, ds], op=ALU.subtract), r=[mnew, GBL], w=[mprev])
                if not sample:
                    P.op("sp", lambda e, s=s: e.dma_start(out=o_m[s:s + 1, :], in_=mprev[0:1, :]), r=[mprev], dma="om")

        def mod_blks(hh):
            return list(range(16 + hh * 4, 20 + hh * 4))

        for h in range(NH):
            if not sample:
                mod_dma(mod_blks(h))
            wq = load_w(h * 256); proj_fm(wq, QT, "rq", 0)
            wk = load_w(1024 + h * 256); proj_fm(wk, KT, "rk", 0)
            wv = load_w(2048 + h * 256); wg = load_w(3072 + h * 256); proj_tm(wv, wg, AF.Silu)
            make_ktm()
            mixer(h, "ret")
            if not sample:
                mod_compute(mod_blks(h))
        gates_pre()
        for h in range(NH):
            if not sample:
                mod_dma(mod_blks(4 + h))
            wq = load_w(4096 + h * 256); proj_fm(wq, QT, "mq", h * 256)
            wk = load_w(5120 + h * 256); proj_fm(wk, KT, "mk", 1024 + h * 256)
            wv = load_w(6144 + h * 256); wg = load_w(7168 + h * 256); proj_tm(wv, wg, AF.Sigmoid)
            make_ktm(1.0 / 16.0)
            mixer(h, "ml")
            if not sample:
                mod_compute(mod_blks(4 + h))

        MT = sbt(O_HT, [16, Town], BF16)
        for t in range(NTO):
            transpose_tile(MIXO[:, t, :], MIXO.part(t, NTO), MT, MT, t * 128)
        rb = Bump(O_PROJ, SZ_PROJ)
        r_g1 = rb([D], F32); r_l1g = rb([D], F32); r_l1b = rb([D], F32); r_sc2 = rb([D], F32); r_sh2 = rb([D], F32)
        load_modrow(r_g1, 2, mg); load_row(r_l1g, ln1_g); load_row(r_l1b, ln1_b)
        load_modrow(r_sh2, 3, mg); load_modrow(r_sc2, 4, mg, plus1=True)
        YA = [sbt((O_MIXO if t < 4 else O_MIX) + (t % 4) * 8192, [D], F32) for t in range(NTO)]
        hb = Bump(O_HT + 32768, 32768)
        xts = [hb([D], F32) for _ in range(2)]
        h2s = [hb([D], BF16) for _ in range(2)]
        wsl2 = [sbt(O_W + k * 8192, [16, 256], BF16) for k in range(2)]
        for cbk in range(8):
            ws = wsl2[cbk % 2]
            P.op("pool", lambda e, ws=ws, cbk=cbk: e.dma_start(out=ws[:], in_=w_out[:, cbk * 256:(cbk + 1) * 256].rearrange("(kc p) n -> p kc n", p=128)), w=[ws], dma="wsl%d" % (cbk % 2))
            for t in range(NTO):
                ps = psn()
                for kc in range(KC):
                    P.op("pe", lambda e, ps=ps, kc=kc, t=t, ws=ws: e.matmul(ps[:, 0:256], lhsT=MT[:, kc, t * 128:(t + 1) * 128], rhs=ws[:, kc, :], start=(kc == 0), stop=(kc == KC - 1)), r=[MT, ws], w=[ps])
                P.op("dve", lambda e, ps=ps, t=t, cbk=cbk: e.tensor_tensor(out=YA[t][:, cbk * 256:(cbk + 1) * 256], in0=ps[:, 0:256], in1=r_g1[:, cbk * 256:(cbk + 1) * 256], op=ALU.mult), r=[ps, r_g1], w=[YA[t]])
        smo = [sbt(O_MIX + 32768 + k * 256, [64], F32) for k in range(2)]

        def o_tile(t):
            def g(k):
                xt = xts[k]; h2 = h2s[k]; ya = YA[t]
                st, mv, rstd, nmr = small4(smo[k])
                P.op("sp", lambda e: e.dma_start(out=xt[:], in_=x_own[t * 128:(t + 1) * 128, :]), w=[xt], dma="xo%d" % k)
                yield
                P.op("dve", lambda e: e.scalar_tensor_tensor(out=ya[:], in0=xt[:], scalar=ALPHA, in1=ya[:], op0=ALU.mult, op1=ALU.add), r=[xt, ya], w=[ya])
                yield
                ln_stats(ya.ap, ya, st, mv, rstd, None, D)
                yield
                P.op("dve", lambda e: e.scalar_tensor_tensor(out=ya[:], in0=ya[:], scalar=mv[:, 0:1], in1=r_l1g[:], op0=ALU.subtract, op1=ALU.mult), r=[ya, mv, r_l1g], w=[ya])
                yield
                P.op("dve", lambda e: e.scalar_tensor_tensor(out=ya[:], in0=ya[:], scalar=rstd[:], in1=r_l1b[:], op0=ALU.mult, op1=ALU.add), r=[ya, rstd, r_l1b], w=[ya])
                yield
                P.op("sp", lambda e: e.dma_start(out=x1_d[x1off + t * 128: x1off + (t + 1) * 128, :], in_=ya[:]), r=[ya], w=[DR("x1", x1off // 128 + t, x1off // 128 + t + 1)], dma="x1o%d" % t)
                ln_stats(ya.ap, ya, st, mv, rstd, None, D)
                yield
                P.op("dve", lambda e: e.scalar_tensor_tensor(out=xt[:], in0=ya[:], scalar=mv[:, 0:1], in1=r_sc2[:], op0=ALU.subtract, op1=ALU.mult), r=[ya, mv, r_sc2], w=[xt])
                yield
                P.op("dve", lambda e: e.scalar_tensor_tensor(out=h2[:], in0=xt[:], scalar=rstd[:], in1=r_sh2[:], op0=ALU.mult, op1=ALU.add), r=[xt, rstd, r_sh2], w=[h2])
                yield
                yield from transpose_tile_g(h2.ap, h2, MT, MT, t * 128)
            return g
        run_pipe([o_tile(t) for t in range(NTO)], 2)

        H2T = MT
        achunks = []
        for k in range(21):
            achunks.append(sbt(O_PROJ + k * Town * 2, [Town], BF16))
        for k in range(16):
            achunks.append(sbt(O_MIXO + k * Town * 2, [Town], BF16))
        for k in range(6):
            achunks.append(sbt(O_HT + 32768 + k * Town * 2, [Town], BF16))
        wd2 = sbt(O_HT + 32768 + 6 * 2048, [NFF, 128], BF16)
        wd1 = sbt(O_W, [NFF, 128], BF16)
        fb = Bump(O_MIX, SZ_MIX)
        zt = [fb([512], F32) for _ in range(2)]
        z2 = [fb([512], F32) for _ in range(2)]
        wup = [sbt(O_W + k * 8192, [16, 256], BF16) for k in range(2)]
        nb_ = 512 // Lc
        v3 = lambda ap: ap.rearrange("p (a b) -> p a b", a=nb_)
        for c in range(NFF):
            ws = wup[c % 2]
            P.op("pool", lambda e, ws=ws, c=c: e.dma_start(out=ws[:, :, 0:128], in_=w_up[:, c * 128:(c + 1) * 128].rearrange("(kc p) n -> p kc n", p=128)), w=[ws], dma="wup%d" % (c % 2))
            P.op("pool", lambda e, ws=ws, c=c: e.dma_start(out=ws[:, :, 128:256], in_=w_up[:, DFF + c * 128:DFF + (c + 1) * 128].rearrange("(kc p) n -> p kc n", p=128)), w=[ws], dma="wup%d" % (c % 2))
            w0 = CFF.ap[:, c * 3:c * 3 + 1]; w1 = CFF.ap[:, c * 3 + 1:c * 3 + 2]; w2 = CFF.ap[:, c * 3 + 2:c * 3 + 3]
            for tb in range(Town // 512):
                pu = psn()
                for kc in range(KC):
                    P.op("pe", lambda e, pu=pu, kc=kc, tb=tb, ws=ws: e.matmul(pu[:], lhsT=ws[:, kc, 0:128], rhs=H2T[:, kc, tb * 512:(tb + 1) * 512], start=(kc == 0), stop=(kc == KC - 1)), r=[ws, H2T], w=[pu])
                pg = psn()
                for kc in range(KC):
                    P.op("pe", lambda e, pg=pg, kc=kc, tb=tb, ws=ws: e.matmul(pg[:], lhsT=ws[:, kc, 128:256], rhs=H2T[:, kc, tb * 512:(tb + 1) * 512], start=(kc == 0), stop=(kc == KC - 1)), r=[ws, H2T], w=[pg])
                z = zt[tb % 2]; zz = z2[tb % 2]
                P.op("dve", lambda e, pu=pu, z=z, w1=w1: e.tensor_scalar(out=z[:], in0=pu[:], scalar1=w1, scalar2=None, op0=ALU.mult), r=[pu, CFF], w=[z])
                P.op("dve", lambda e, pu=pu, z=z, w0=w0: e.scalar_tensor_tensor(out=v3(z[:])[:, :, 1:Lc], in0=v3(pu[:])[:, :, 0:Lc - 1], scalar=w0, in1=v3(z[:])[:, :, 1:Lc], op0=ALU.mult, op1=ALU.add), r=[pu, CFF, z], w=[z])
                P.op("dve", lambda e, pu=pu, z=z, w2=w2: e.scalar_tensor_tensor(out=v3(z[:])[:, :, 0:Lc - 1], in0=v3(pu[:])[:, :, 1:Lc], scalar=w2, in1=v3(z[:])[:, :, 0:Lc - 1], op0=ALU.mult, op1=ALU.add), r=[pu, CFF, z], w=[z])
                P.op("act", lambda e, z=z, zz=zz: e.activation(out=zz[:], in_=z[:], func=AF.Silu), r=[z], w=[zz])
                ac = achunks[c]
                P.op("dve", lambda e, pg=pg, zz=zz, ac=ac, tb=tb: e.tensor_tensor(out=ac[:, tb * 512:(tb + 1) * 512], in0=pg[:], in1=zz[:], op=ALU.mult), r=[pg, zz], w=[ac])
        fb = Bump(O_MIX, SZ_MIX)
        r_g2 = fb([D], F32); r_l2g = fb([D], F32); r_l2b = fb([D], F32)
        x1t = fb([D], F32); y2 = fb([D], F32)
        st, mv, rstd, nmr = ST_C, MV_C, RSTD_C, NMR_C
        load_modrow(r_g2, 5, mg); load_row(r_l2g, ln2_g); load_row(r_l2b, ln2_b)
        wds = [wd1, wd2]
        stg = [sbt(O_W + 11264 + k * 512, [128], F32) for k in range(8)]
        cnt = 0
        for cbk in range(16):
            ws = wds[cbk % 2]
            P.op("pool", lambda e, ws=ws, cbk=cbk: e.dma_start(out=ws[:], in_=w_down[:, cbk * 128:(cbk + 1) * 128].rearrange("(c p) n -> p c n", p=128)), w=[ws], dma="wd%d" % (cbk % 2))
            for t in range(NTO):
                ps = psn()
                for c in range(NFF):
                    P.op("pe", lambda e, ps=ps, c=c, t=t, ws=ws: e.matmul(ps[:, 0:128], lhsT=achunks[c][:, t * 128:(t + 1) * 128], rhs=ws[:, c, :], start=(c == 0), stop=(c == NFF - 1)), r=[achunks[c], ws], w=[ps])
                sg_ = stg[cnt % 8]
                P.op("dve", lambda e, ps=ps, cbk=cbk, sg_=sg_: e.tensor_tensor(out=sg_[:], in0=ps[:, 0:128], in1=r_g2[:, cbk * 128:(cbk + 1) * 128], op=ALU.mult), r=[ps, r_g2], w=[sg_])
                P.op("sp", lambda e, sg_=sg_, cbk=cbk, t=t: e.dma_start(out=y2_d[x1off + t * 128: x1off + (t + 1) * 128, cbk * 128:(cbk + 1) * 128], in_=sg_[:]), r=[sg_], w=[DR("y2", x1off // 128 + t, x1off // 128 + t + 1)], dma="stg%d" % (cnt % 8))
                cnt += 1
        for t in range(NTO):
            row = x1off // 128 + t
            P.op("sp", lambda e, t=t: e.dma_start(out=x1t[:], in_=x1_d[x1off + t * 128: x1off + (t + 1) * 128, :]), r=[DR("x1", row, row + 1)], w=[x1t], dma="x1t")
            P.op("sp", lambda e, t=t: e.dma_start(out=y2[:], in_=y2_d[x1off + t * 128: x1off + (t + 1) * 128, :]), r=[DR("y2", row, row + 1)], w=[y2], dma="y2t")
            P.op("dve", lambda e: e.scalar_tensor_tensor(out=y2[:], in0=x1t[:], scalar=ALPHA, in1=y2[:], op0=ALU.mult, op1=ALU.add), r=[x1t, y2], w=[y2])
            ln_stats(y2.ap, y2, st, mv, rstd, None, D)
            P.op("dve", lambda e: e.scalar_tensor_tensor(out=y2[:], in0=y2[:], scalar=mv[:, 0:1], in1=r_l2g[:], op0=ALU.subtract, op1=ALU.mult), r=[y2, mv, r_l2g], w=[y2])
            P.op("dve", lambda e: e.scalar_tensor_tensor(out=y2[:], in0=y2[:], scalar=rstd[:], in1=r_l2b[:], op0=ALU.mult, op1=ALU.add), r=[y2, rstd, r_l2b], w=[y2])
            P.op("sp", lambda e, t=t: e.dma_start(out=y_out[t * 128:(t + 1) * 128, :], in_=y2[:]), r=[y2], dma="yout")

    stage_mod(list(range(16)), O_PROJ, O_MIX)
    for gi in GROUPS:
        process_group(gi)
    P.emit()
    es.close()
    return nc


GROUPS = (0, 1)
_CACHE = {}


def kernel(x_prompt, x_sample, state_ret, state_mlstm_C, state_mlstm_n, state_mlstm_m, c, c_ctx,
           w_mod, b_mod, w_in, b_gate, conv_qk, ret_theta, gn_ret, gn_mlstm, w_out,
           ln1_g, ln1_b, w_up, conv_ff, w_down, ln2_g, ln2_b):
    f = lambda a: np.ascontiguousarray(np.asarray(a, dtype=np.float32))
    if "nc" not in _CACHE:
        _CACHE["nc"] = build_program()
    nc = _CACHE["nc"]
    cst = make_consts()
    xp_all = f(x_prompt).reshape(8, 1024, D)
    xs_all = f(x_sample)
    shared = dict(
        w_mod=f(w_mod)[0], b_mod=f(b_mod), w_in=f(w_in)[0], b_gate=f(b_gate),
        conv_qkT=f(f(conv_qk)[0].T.reshape(16, 128, 3).transpose(1, 0, 2).reshape(128, 48)),
        theta=f(ret_theta).reshape(1, 8), gn_ret=f(gn_ret).reshape(1, 1024), gn_ml=f(gn_mlstm).reshape(1, 1024),
        w_out=f(w_out)[0], ln1_g=f(ln1_g), ln1_b=f(ln1_b), w_up=f(w_up)[0],
        conv_ffT=f(f(conv_ff)[0].T.reshape(NFF, 128, 3).transpose(1, 0, 2).reshape(128, NFF * 3)),
        w_down=f(w_down)[0], ln2_g=f(ln2_g), ln2_b=f(ln2_b), cst=cst)
    in_maps = []
    for i in range(8):
        b, p = i // 4, i % 4
        fl = np.zeros((1, 4), np.float32); fl[0, p] = 1.0
        cT = np.stack([f(c_ctx).reshape(16, 128).T, f(c)[b].reshape(16, 128).T], axis=2).reshape(128, 32)
        m = dict(shared)
        m.update(xp=xp_all[i], xs=xs_all[b], xo=f(xs_all[b, p * 512:(p + 1) * 512]), flags=fl, cT=f(cT),
                 st_ret=f(state_ret)[b, 0], st_C=f(state_mlstm_C)[b, 0],
                 st_n=f(f(state_mlstm_n)[b, 0].reshape(2, 4, 2, 128).transpose(0, 1, 3, 2)),
                 st_m=f(state_mlstm_m)[b, 0].reshape(1, 8))
        in_maps.append(m)
    res = run_bass_kernel_spmd(nc, in_maps, core_ids=list(range(8)))
    R = res.results
    y_prompt = np.concatenate([R[i]["yp"] for i in range(8)], 0).reshape(32, 256, D)
    y_sample = np.stack([np.concatenate([R[b * 4 + p]["ys"] for p in range(4)], 0) for b in range(2)], 0)
    n_ret = np.concatenate([R[i]["o_ret"] for i in range(8)], 0)[:, None]
    n_C = np.concatenate([R[i]["o_C"] for i in range(8)], 0)[:, None]
    n_n = np.concatenate([R[i]["o_n"] for i in range(8)], 0).transpose(0, 1, 2, 4, 3).reshape(32, 2, 4, 256)[:, None]
    n_m = np.concatenate([R[i]["o_m"] for i in range(8)], 0).reshape(32, 2, 4)[:, None]
    return (y_prompt, y_sample, np.ascontiguousarray(n_ret), np.ascontiguousarray(n_C),
            np.ascontiguousarray(n_n), np.ascontiguousarray(n_m))
```

```python
import contextlib
import numpy as np
import concourse.bass as bass
import concourse.mybir as mybir
from concourse.bass_utils import run_bass_kernel_spmd

F32 = mybir.dt.float32
BF16 = mybir.dt.bfloat16
ALU = mybir.AluOpType
AF = mybir.ActivationFunctionType
AX = mybir.AxisListType

D = 2048
KC = 16
HD = 256
NH = 4
DFF = 5504
NFF = 43
NIN = 8208
ALPHA = 2.0 ** 0.25
BIG = 1.0e9
SAME_ENGINE_SYNC = True
DEBUG_IDS = set()
GRAN = 256


class Rg:
    __slots__ = ("space", "lo", "hi")

    def __init__(self, space, lo, hi):
        self.space, self.lo, self.hi = space, lo, hi


class Tl:
    def __init__(self, ap, space, lo, hi):
        self.ap, self.space, self.lo, self.hi = ap, space, lo, hi

    def __getitem__(self, k):
        return self.ap[k]

    @property
    def r(self):
        return Rg(self.space, self.lo, self.hi)

    def part(self, i, n, cnt=1):
        sz = (self.hi - self.lo) // n
        return Rg(self.space, self.lo + i * sz, self.lo + (i + cnt) * sz)


def _rg(x):
    return x.r if isinstance(x, Tl) else x


class _Rec:
    def __getattr__(self, name):
        def f(*a, **k):
            self.call = (name, a, k)
            return self
        return f


class Prog:
    ENGS = ("pe", "act", "dve", "pool", "sp")

    def __init__(self, nc):
        self.nc = nc
        self.ops = []
        self.st = {}

    def _cells(self, rg):
        if rg.space == "sb":
            return [("sb", g) for g in range(rg.lo // GRAN, (rg.hi + GRAN - 1) // GRAN)]
        return [(rg.space, g) for g in range(rg.lo, rg.hi)]

    def op(self, eng, fn, r=(), w=(), dma=None):
        oid = len(self.ops)
        deps = set()
        rc = [c for x in r for c in self._cells(_rg(x))]
        wc = [c for x in w for c in self._cells(_rg(x))]
        st = self.st
        for c in rc:
            s = st.get(c)
            if s is not None:
                if s[0] is not None:
                    deps.add(s[0])
                if c[0] == "ps":
                    for r_ in s[1]:
                        if self.ops[r_]["eng"] != eng:
                            deps.add(r_)
        for c in wc:
            s = st.get(c)
            if s is not None:
                if s[0] is not None:
                    deps.add(s[0])
                deps.update(s[1])
        for c in rc:
            s = st.get(c)
            if s is None:
                st[c] = [None, [oid]]
            else:
                s[1].append(oid)
        for c in wc:
            st[c] = [oid, []]
        deps.discard(oid)
        rec = _Rec()
        fn(rec)
        self.ops.append(dict(eng=eng, call=rec.call, deps=deps, dma=dma, sig=False, sigidx=None))
        return oid

    def emit(self):
        nc = self.nc
        ops = self.ops

        def dom(o):
            return ("dma", o["dma"]) if o["dma"] is not None else ("eng", o["eng"])

        def skip(p, engname):
            return p["dma"] is None and p["eng"] == engname and (engname == "pe" or not SAME_ENGINE_SYNC)

        for o in ops:
            for d in o["deps"]:
                p = ops[d]
                if skip(p, o["eng"]):
                    continue
                p["sig"] = True
            if o["dma"] is not None:
                o["sig"] = True
        counters = {}
        for o in ops:
            if o["sig"]:
                dm = dom(o)
                counters[dm] = counters.get(dm, 0) + 1
                o["sigidx"] = counters[dm]
        with contextlib.ExitStack() as es:
            sems = {}
            for i, dm in enumerate(counters.keys()):
                sems[dm] = es.enter_context(nc.semaphore("s%d" % i))
            block = es.enter_context(nc.Block())
            streams = {e: [] for e in self.ENGS}
            for o in ops:
                streams[o["eng"]].append(o)

            def run(engname, eng):
                waited = {}
                for o in streams[engname]:
                    need = {}
                    for d in o["deps"]:
                        p = ops[d]
                        if not p["sig"] or skip(p, engname):
                            continue
                        dm = dom(p)
                        if p["sigidx"] > need.get(dm, 0):
                            need[dm] = p["sigidx"]
                    for dm, v in need.items():
                        if waited.get(dm, 0) >= v:
                            continue
                        waited[dm] = v
                        eng.wait_ge(sems[dm], v * (16 if dm[0] == "dma" else 1))
                    nm, a_, k_ = o["call"]
                    ins = getattr(eng, nm)(*a_, **k_)
                    if DEBUG_IDS:
                        try:
                            iname = str(ins.ins.name) if hasattr(ins, "ins") else str(getattr(ins, "name", ""))
                        except Exception:
                            iname = "?"
                        if iname in DEBUG_IDS:
                            print("DEBUGID", iname, engname, nm, {kk: (getattr(vv, "ap", None), getattr(vv, "dtype", None)) if hasattr(vv, "ap") else vv for kk, vv in k_.items()})
                    if o["sig"]:
                        ins.then_inc(sems[dom(o)], 16 if o["dma"] is not None else 1)
                if engname == "sp":
                    for dm, cnt in counters.items():
                        if dm[0] == "dma":
                            eng.wait_ge(sems[dm], cnt * 16)

            block.tensor(lambda e: run("pe", e))
            block.scalar(lambda e: run("act", e))
            block.vector(lambda e: run("dve", e))
            block.gpsimd(lambda e: run("pool", e))
            block.sync(lambda e: run("sp", e))


def make_consts():
    p = np.arange(128)
    j = p[:, None].astype(np.float64)
    i = p[None, :].astype(np.float64)
    D1 = np.maximum(i - j, 0)
    U1 = (i >= j).astype(np.float64)
    D2 = np.maximum(j - i, 0)
    U2 = (j >= i).astype(np.float64)
    BIGF = BIG * (1 - U1) + np.log(16.0)
    BIGB = BIG * (1 - U2) + np.log(16.0)
    NEGF = -BIG * (1 - U2)
    NEGB = -BIG * (1 - U1)
    vec = np.stack([p + 1.0, 128.0 - p, 127.0 - p, p * 1.0], axis=1)
    inv = 10000.0 ** (-np.arange(64, dtype=np.float32) / 64.0)
    f = p % 64
    sgn = np.where(p < 64, 1.0, -1.0)
    rows = np.arange(32, dtype=np.float32)
    cols = np.arange(64, dtype=np.float32)
    angR = (rows[None, :] * inv[f][:, None]).astype(np.float32)
    angC = (cols[None, :] * inv[f][:, None]).astype(np.float32)
    cosR, sinR = np.cos(angR), np.sin(angR) * sgn[:, None]
    cosC, sinC = np.cos(angC), np.sin(angC) * sgn[:, None]
    rope = np.concatenate([cosR, sinR, cosC, sinC], axis=1)
    ropeK = rope / 16.0
    cst = np.concatenate([D1, U1, D2, U2, BIGF, BIGB, NEGF, NEGB, vec, rope, ropeK, np.eye(128), np.ones((128, 128))], axis=1)
    return np.ascontiguousarray(cst.astype(np.float32))


C_D1, C_U1, C_D2, C_U2, C_BIGF, C_BIGB, C_NEGF, C_NEGB = [k * 128 for k in range(8)]
C_VEC = 1024
C_ROPE = 1028
C_ROPEK = 1028 + 192
C_ID = 1028 + 384
C_ONES = C_ID + 128
NCST = C_ONES + 128

O_CONST = 0
SZ_CONST = 13312
O_HT = O_CONST + SZ_CONST
SZ_HT = 65536
O_W = O_HT + SZ_HT
SZ_W = 16384
O_PROJ = O_W + SZ_W
SZ_PROJ = 43008
O_MIXO = O_PROJ + SZ_PROJ
SZ_MIXO = 32768
O_MIX = O_MIXO + SZ_MIXO
SZ_MIX = 41728
ARENA = O_MIX + SZ_MIX


def build_program():
    nc = bass.Bass("TRN2", target_bir_lowering=False)

    def din(name, shape):
        return nc.dram_tensor(name, list(shape), F32, kind="ExternalInput").ap()

    def dout(name, shape):
        return nc.dram_tensor(name, list(shape), F32, kind="ExternalOutput").ap()

    xp = din("xp", [1024, D]); xs = din("xs", [2048, D]); xo = din("xo", [512, D])
    flags = din("flags", [1, 4]); cT = din("cT", [128, 32])
    st_ret = din("st_ret", [2, 4, 256, 256]); st_C = din("st_C", [2, 4, 256, 256])
    st_n = din("st_n", [2, 4, 128, 2]); st_m = din("st_m", [1, 8])
    w_mod = din("w_mod", [D, 6 * D]); b_mod = din("b_mod", [1, 6 * D]); w_in = din("w_in", [D, NIN])
    b_gate = din("b_gate", [1, 16]); conv_qkT = din("conv_qkT", [128, 48]); theta = din("theta", [1, 8])
    gn_ret = din("gn_ret", [1, 1024]); gn_ml = din("gn_ml", [1, 1024]); w_out = din("w_out", [D, D])
    ln1_g = din("ln1_g", [1, D]); ln1_b = din("ln1_b", [1, D]); w_up = din("w_up", [D, 2 * DFF])
    conv_ffT = din("conv_ffT", [128, NFF * 3]); w_down = din("w_down", [DFF, D])
    ln2_g = din("ln2_g", [1, D]); ln2_b = din("ln2_b", [1, D]); cst_d = din("cst", [128, NCST])
    yp = dout("yp", [1024, D]); ys = dout("ys", [512, D])
    o_ret = dout("o_ret", [4, 2, 4, 256, 256]); o_C = dout("o_C", [4, 2, 4, 256, 256])
    o_n = dout("o_n", [4, 2, 4, 128, 2]); o_m = dout("o_m", [4, 8])
    mod_d = nc.dram_tensor("mod_d", [2, 6 * D], F32).ap()
    x1_d = nc.dram_tensor("x1_d", [1536, D], F32).ap()
    y2_d = nc.dram_tensor("y2_d", [1536, D], F32).ap()

    P = Prog(nc)
    es = contextlib.ExitStack()
    A = es.enter_context(nc.sbuf_tensor("arena", [128, ARENA // 2], BF16))
    psf = [es.enter_context(nc.psum_tensor("psf%d" % i, [128, 512], F32)) for i in range(6)]
    psb = [es.enter_context(nc.psum_tensor("psb%d" % i, [128, 1024], BF16)) for i in range(2)]
    PSF = [Tl(psf[i][:], "ps", i, i + 1) for i in range(6)]
    PSB = [Tl(psb[i][:], "ps", 6 + i, 7 + i) for i in range(2)]
    rot = {"f": 0, "b": 0}

    held = set()

    def psn():
        for _ in range(6):
            rot["f"] = (rot["f"] + 1) % 6
            if rot["f"] not in held:
                return PSF[rot["f"]]
        raise AssertionError("no free psum bank")

    def psh():
        ps = psn()
        held.add(ps.lo)
        return ps

    def prel(ps):
        held.discard(ps.lo)

    def navail():
        return 6 - len(held)

    heldb = set()

    def psbn():
        for _ in range(2):
            rot["b"] = (rot["b"] + 1) % 2
            if rot["b"] not in heldb:
                return PSB[rot["b"]]
        raise AssertionError("no free bf16 psum bank")

    def run_pipe(facts, width=2):
        free = list(range(width)); active = []; i = 0
        while i < len(facts) or active:
            while free and i < len(facts):
                k = free.pop(0)
                active.append((facts[i](k), k))
                i += 1
            nxt = []
            for g, k in active:
                try:
                    next(g)
                    nxt.append((g, k))
                except StopIteration:
                    free.append(k)
            active = nxt

    def sbt(off, shape, dt, parts=128):
        esz = 4 if dt == F32 else 2
        n = 1
        for s in shape:
            n *= s
        nb = n * esz
        assert off % 4 == 0
        ap = A[0:parts, off // 2:(off + nb) // 2]
        if dt != BF16:
            ap = ap.bitcast(dt)
        if len(shape) == 2:
            ap = ap.rearrange("p (a b) -> p a b", a=shape[0])
        elif len(shape) == 3:
            ap = ap.rearrange("p (a b c) -> p a b c", a=shape[0], b=shape[1])
        return Tl(ap, "sb", off, off + nb)

    class Bump:
        def __init__(self, off, size):
            self.off, self.end = off, off + size

        def __call__(self, shape, dt, parts=128, align=GRAN):
            esz = 4 if dt == F32 else 2
            n = 1
            for s in shape:
                n *= s
            self.off = (self.off + align - 1) // align * align
            t = sbt(self.off, shape, dt, parts)
            self.off += n * esz
            assert self.off <= self.end, ("arena overflow", self.off, self.end)
            return t

    def DR(name, lo=0, hi=1):
        return Rg("dr_" + name, lo, hi)

    cb = Bump(O_CONST, SZ_CONST)
    CST = cb([NCST], F32)
    IDB = cb([128], BF16)
    LG = cb([8], F32)
    FLG = cb([4], F32)
    BG = cb([16], F32)
    GN = cb([256], F32)
    DEC = cb([8], F32)
    RM = cb([128], F32)
    RMT = cb([128], F32)
    SILC = cb([16, 2], BF16)
    WGT = cb([16, 16], BF16)
    CQK = cb([48], F32)
    CFF = cb([NFF * 3], F32)
    ST_C = cb([4, 6], F32); MV_C = cb([2], F32, align=4); RSTD_C = cb([1], F32, align=4); NMR_C = cb([1], F32, align=4)
    IDF = CST.ap[:, C_ID:C_ID + 128]
    ONESF = CST.ap[:, C_ONES:C_ONES + 128]

    def cstm(c0):
        return CST.ap[:, c0:c0 + 128]

    P.op("sp", lambda e: e.dma_start(out=CST[:], in_=cst_d[:, :]), w=[CST], dma="cst")
    P.op("sp", lambda e: e.dma_start(out=LG[:], in_=theta[0:1, :].broadcast_to([128, 8])), w=[LG], dma="lg")
    P.op("sp", lambda e: e.dma_start(out=FLG[:], in_=flags[0:1, :].broadcast_to([128, 4])), w=[FLG], dma="flg")
    P.op("sp", lambda e: e.dma_start(out=BG[:], in_=b_gate[0:1, :].broadcast_to([128, 16])), w=[BG], dma="bg")
    P.op("sp", lambda e: e.dma_start(out=CQK[:], in_=conv_qkT[:, :]), w=[CQK], dma="cqk")
    P.op("sp", lambda e: e.dma_start(out=CFF[:], in_=conv_ffT[:, :]), w=[CFF], dma="cff")
    P.op("pool", lambda e: e.dma_start(out=WGT[:], in_=w_in[:, 8192:8208].rearrange("(kc p) n -> p kc n", p=128)), w=[WGT], dma="wgt")
    P.op("dve", lambda e: e.tensor_copy(out=IDB[:], in_=IDF), r=[CST], w=[IDB])
    P.op("act", lambda e: e.activation(out=LG[:], in_=LG[:], func=AF.Exp, scale=-1.0), r=[LG], w=[LG])
    P.op("act", lambda e: e.activation(out=LG[:], in_=LG[:], func=AF.Ln, bias=1.0), r=[LG], w=[LG])
    P.op("dve", lambda e: e.tensor_scalar(out=LG[:], in0=LG[:], scalar1=-1.0, scalar2=None, op0=ALU.mult), r=[LG], w=[LG])

    mod_state = {"init": False}

    def stage_mod(blks, stage_off, small_off):
        ct = sbt(small_off, [32], F32)
        brow = [sbt(small_off + 256 + k * 1024, [256], F32, parts=2) for k in range(2)]
        mrow = [sbt(small_off + 256 + 2048 + k * 1024, [256], F32, parts=2) for k in range(2)]
        if not mod_state["init"]:
            mod_state["init"] = True
            P.op("sp", lambda e: e.dma_start(out=ct[:], in_=cT[:, :]), w=[ct], dma="ct")
            P.op("act", lambda e: e.activation(out=SILC[:].rearrange("p a b -> p (a b)"), in_=ct[:], func=AF.Silu), r=[ct], w=[SILC])
        wslots = [sbt(stage_off + k * 8192, [16, 256], BF16) for k in range(2)]
        for i, blk in enumerate(blks):
            ws = wslots[i % 2]; br = brow[i % 2]; mr = mrow[i % 2]
            c0 = blk * 256
            P.op("pool", lambda e, ws=ws, c0=c0: e.dma_start(out=ws[:], in_=w_mod[:, c0:c0 + 256].rearrange("(kc p) n -> p kc n", p=128)), w=[ws], dma="wm%d_%d" % (stage_off, i % 2))
            P.op("sp", lambda e, br=br, c0=c0: e.dma_start(out=br[:], in_=b_mod[0:1, c0:c0 + 256].broadcast_to([2, 256])), w=[br], dma="brow%d_%d" % (small_off, i % 2))
            ps = psn()
            for kc in range(KC):
                P.op("pe", lambda e, ps=ps, ws=ws, kc=kc: e.matmul(ps[0:2, 0:256], lhsT=SILC[:, kc, :], rhs=ws[:, kc, :], start=(kc == 0), stop=(kc == KC - 1)), r=[SILC, ws], w=[ps])
            P.op("dve", lambda e, ps=ps, mr=mr, br=br: e.tensor_tensor(out=mr[:], in0=ps[0:2, 0:256], in1=br[:], op=ALU.add), r=[ps, br], w=[mr])
            P.op("sp", lambda e, mr=mr, c0=c0: e.dma_start(out=mod_d[:, c0:c0 + 256], in_=mr[:]), r=[mr], w=[DR("mod", blk // 8, blk // 8 + 1)], dma="mrow%d_%d" % (small_off, i % 2))

    MOD_STAGE = O_HT + 32768
    MOD_SMALL = O_PROJ + 36864

    def mod_dma(blks):
        for i, blk in enumerate(blks):
            ws = sbt(MOD_STAGE + i * 8192, [16, 256], BF16)
            c0 = blk * 256
            P.op("pool", lambda e, ws=ws, c0=c0: e.dma_start(out=ws[:], in_=w_mod[:, c0:c0 + 256].rearrange("(kc p) n -> p kc n", p=128)), w=[ws], dma="wmr%d" % i)

    def mod_compute(blks):
        brow = [sbt(MOD_SMALL + k * 1024, [256], F32, parts=2) for k in range(2)]
        mrow = [sbt(MOD_SMALL + 2048 + k * 1024, [256], F32, parts=2) for k in range(2)]
        for i, blk in enumerate(blks):
            ws = sbt(MOD_STAGE + i * 8192, [16, 256], BF16)
            br = brow[i % 2]; mr = mrow[i % 2]
            c0 = blk * 256
            P.op("sp", lambda e, br=br, c0=c0: e.dma_start(out=br[:], in_=b_mod[0:1, c0:c0 + 256].broadcast_to([2, 256])), w=[br], dma="browr%d" % (i % 2))
            ps = psn()
            for kc in range(KC):
                P.op("pe", lambda e, ps=ps, ws=ws, kc=kc: e.matmul(ps[0:2, 0:256], lhsT=SILC[:, kc, :], rhs=ws[:, kc, :], start=(kc == 0), stop=(kc == KC - 1)), r=[SILC, ws], w=[ps])
            P.op("dve", lambda e, ps=ps, mr=mr, br=br: e.tensor_tensor(out=mr[:], in0=ps[0:2, 0:256], in1=br[:], op=ALU.add), r=[ps, br], w=[mr])
            P.op("sp", lambda e, mr=mr, c0=c0: e.dma_start(out=mod_d[:, c0:c0 + 256], in_=mr[:]), r=[mr], w=[DR("mod", blk // 8, blk // 8 + 1)], dma="mrowr%d" % (i % 2))

    def load_modrow(dst, q, g, plus1=False):
        P.op("sp", lambda e: e.dma_start(out=dst[:], in_=mod_d[g:g + 1, q * D:(q + 1) * D].broadcast_to([128, D])), r=[DR("mod", q, q + 1)], w=[dst], dma="mr_%d" % (dst.lo))
        if plus1:
            P.op("dve", lambda e: e.tensor_scalar(out=dst[:], in0=dst[:], scalar1=1.0, scalar2=None, op0=ALU.add), r=[dst], w=[dst])

    def load_row(dst, src):
        P.op("sp", lambda e: e.dma_start(out=dst[:], in_=src[0:1, :].broadcast_to([128, src.shape[1]])), w=[dst], dma="lr_%d" % (dst.lo))

    def ln_stats(x_ap, xr, st, mv, rstd, nmr, n):
        nch = max(1, n // 512)
        w_ = n // nch
        for c in range(nch):
            P.op("dve", lambda e, c=c: e.bn_stats(out=st[:, c, :], in_=x_ap[:, c * w_:(c + 1) * w_]), r=[xr], w=[st])
        P.op("dve", lambda e: e.bn_aggr(out=mv[:], in_=st[:, 0:nch, :].rearrange("p a b -> p (a b)")), r=[st], w=[mv])
        P.op("act", lambda e: e.activation(out=rstd[:], in_=mv[:, 1:2], func=AF.Ln, bias=1e-6), r=[mv], w=[rstd])
        P.op("act", lambda e: e.activation(out=rstd[:], in_=rstd[:], func=AF.Exp, scale=-0.5), r=[rstd], w=[rstd])
        if nmr is not None:
            P.op("dve", lambda e: e.scalar_tensor_tensor(out=nmr[:], in0=mv[:, 0:1], scalar=-1.0, in1=rstd[:], op0=ALU.mult, op1=ALU.mult), r=[mv, rstd], w=[nmr])

    def transpose_tile(src, src_r, dstT, dst_r, tcol):
        for _ in transpose_tile_g(src, src_r, dstT, dst_r, tcol):
            pass

    def transpose_tile_g(src, src_r, dstT, dst_r_, tcol):
        for half in range(2):
            while len(heldb) >= 2:
                yield
            pb = psbn()
            heldb.add(pb.lo - 6)
            dst_r = dstT.part(half, 2)
            for k in range(8):
                kc = half * 8 + k
                P.op("pe", lambda e, pb=pb, k=k, kc=kc: e.transpose(out=pb[:, k * 128:(k + 1) * 128], in_=src[:, kc * 128:(kc + 1) * 128], identity=IDB[:]), r=[src_r, IDB], w=[pb])
            eng = "act" if half == 0 else "dve"
            if eng == "act":
                P.op("act", lambda e, pb=pb, half=half: e.activation(out=dstT[:, half * 8:half * 8 + 8, tcol:tcol + 128], in_=pb[:].rearrange("p (a b) -> p a b", a=8), func=AF.Copy), r=[pb], w=[dst_r])
            else:
                P.op("dve", lambda e, pb=pb, half=half: e.tensor_copy(out=dstT[:, half * 8:half * 8 + 8, tcol:tcol + 128], in_=pb[:].rearrange("p (a b) -> p a b", a=8)), r=[pb], w=[dst_r])
            heldb.discard(pb.lo - 6)
            yield

    def process_group(gi):
        sample = gi == 1
        T = 2048 if sample else 1024
        nseq = 1 if sample else 4
        n = 16 if sample else 2
        NT = T // 128
        Town = 512 if sample else 1024
        NTO = Town // 128
        x_all = xs if sample else xp
        x_own = xo if sample else xp
        y_out = ys if sample else yp
        x1off = 1024 if sample else 0
        mg = 1 if sample else 0
        Lc = 64 if sample else 256
        HT = sbt(O_HT, [16, T], BF16)

        mb = Bump(O_MIX, SZ_MIX)
        r_sc = mb([D], F32); r_sh = mb([D], F32)
        sm_ = [mb([64], F32) for _ in range(2)]
        load_modrow(r_sh, 0, mg)
        load_modrow(r_sc, 1, mg, plus1=True)
        pbm = Bump(O_PROJ, SZ_PROJ)
        xts = [pbm([D], F32) for _ in range(2)]
        hts = [pbm([D], BF16) for _ in range(2)]

        def small4(tl):
            mk = lambda off, shape: Tl(sbt(tl.lo + off * 4, shape, F32).ap, "sb", tl.lo, tl.hi)
            return mk(0, [4, 6]), mk(24, [2]), mk(26, [1]), mk(27, [1])

        def a1_tile(t):
            def g(k):
                xt = xts[k]; ht = hts[k]
                st, mv, rstd, nmr = small4(sm_[k])
                P.op("sp", lambda e: e.dma_start(out=xt[:], in_=x_all[t * 128:(t + 1) * 128, :]), w=[xt], dma="xt%d" % k)
                yield
                ln_stats(xt.ap, xt, st, mv, rstd, None, D)
                yield
                P.op("dve", lambda e: e.scalar_tensor_tensor(out=xt[:], in0=xt[:], scalar=mv[:, 0:1], in1=r_sc[:], op0=ALU.subtract, op1=ALU.mult), r=[xt, mv, r_sc], w=[xt])
                yield
                P.op("dve", lambda e: e.scalar_tensor_tensor(out=ht[:], in0=xt[:], scalar=rstd[:], in1=r_sh[:], op0=ALU.mult, op1=ALU.add), r=[xt, rstd, r_sh], w=[ht])
                yield
                yield from transpose_tile_g(ht.ap, ht, HT, HT, t * 128)
            return g
        run_pipe([a1_tile(t) for t in range(NT)], 2)

        pbm = Bump(O_PROJ, SZ_PROJ)
        QT = pbm([2, T], BF16)
        KT = pbm([2, T], BF16)
        VG = pbm([NT, 520], BF16)
        KTM = pbm([NT, 256], BF16)
        mb = Bump(O_MIX, SZ_MIX)
        NSL = 1 if sample else 2
        MST = [[mb([2, 257], F32) for d in range(2)] for sl in range(NSL)]
        KX = [[mb([256], BF16) for d in range(2)] for sl in range(NSL)]
        SALL = []
        for sl in range(NSL):
            per_d = []
            for d in range(2):
                if sample and d == 0:
                    first = mb([2, 257], BF16)
                    rest = sbt(O_MIXO + 16384, [n - 1, 2, 257], BF16)
                    per_d.append([first] + [Tl(rest[:, c], "sb", rest.lo + c * 1028, rest.lo + (c + 1) * 1028) for c in range(n - 1)])
                else:
                    arr = mb([n, 2, 257], BF16)
                    per_d.append([Tl(arr[:, c], "sb", arr.lo + c * 1028, arr.lo + (c + 1) * 1028) for c in range(n)])
            SALL.append(per_d)
        Dm = mb([4, 128], F32)
        Am = mb([4, 128], F32)
        tmpc = [sbt(Dm.lo, [512], F32), sbt(Am.lo, [512], F32)]
        ATT = [sbt(Am.lo + k * 512, [2, 128], BF16) for k in range(2)]
        WTS = [sbt(Am.lo + 1024 + k * 512, [128], F32) for k in range(2)]
        DMS = [sbt(Dm.lo + k * 1024, [2, 128], F32) for k in range(2)]
        TOTT = [mb([2, 257], F32) for k in range(2)]
        TOT = [[Tl(TOTT[k][:, d, :], "sb", TOTT[k].lo + d * 1028, TOTT[k].lo + (d + 1) * 1028) for d in range(2)] for k in range(2)]
        XN = [mb([256], F32) for k in range(2)]
        SMALL = [mb([64], F32) for k in range(2)]
        GTS = mb([NT, 16], F32)
        LFN = mb([NT, 16], F32)
        GA = mb([NT, 8], F32); GNB = mb([NT, 8], F32); GMU = mb([NT, 8], F32); GSP = mb([NT, 8], F32)
        GWK = mb([NT, 8], F32); GSC = mb([NT, 8], F32); GEM = mb([NT, 8], F32); GBL = mb([NT, 8], F32)
        mprev = mb([8], F32); mnew = mb([8], F32, align=4); v8 = [mb([8], F32, align=4) for _ in range(4)]
        MIXO = sbt(O_MIXO, [NTO, D], BF16)
        wsl = [sbt(O_W + k * 8192, [16, 256], BF16) for k in range(2)]
        wcount = [0]

        def load_w(c0):
            ws = wsl[wcount[0] % 2]
            k = wcount[0] % 2
            wcount[0] += 1
            P.op("pool", lambda e: e.dma_start(out=ws[:], in_=w_in[:, c0:c0 + 256].rearrange("(kc p) n -> p kc n", p=128)), w=[ws], dma="wsl%d" % k)
            return ws

        P.op("dve", lambda e: e.memset(VG[:, :, 256:257], 1.0), w=[VG])

        def proj_fm(ws, dst, kind, chan0):
            for dc in range(2):
                for tb in range(T // 512):
                    ps = psn()
                    for kc in range(KC):
                        P.op("pe", lambda e, ps=ps, kc=kc, dc=dc, tb=tb: e.matmul(ps[:], lhsT=ws[:, kc, dc * 128:(dc + 1) * 128], rhs=HT[:, kc, tb * 512:(tb + 1) * 512], start=(kc == 0), stop=(kc == KC - 1)), r=[ws, HT], w=[ps])
                    dsl = dst[:, dc, tb * 512:(tb + 1) * 512]
                    if kind in ("rq", "rk"):
                        if not sample:
                            sc_ = 1.0 if kind == "rq" else 1.0 / 16.0
                            P.op("act", lambda e, ps=ps, dsl=dsl, sc_=sc_: e.activation(out=dsl, in_=ps[:], func=AF.Identity, scale=sc_), r=[ps], w=[dst])
                        else:
                            base = C_ROPE if kind == "rq" else C_ROPEK
                            if dc == 0:
                                cosb = CST.ap[:, base + 8 * tb: base + 8 * tb + 8].unsqueeze(2).to_broadcast([128, 8, 64])
                                sinb = CST.ap[:, base + 32 + 8 * tb: base + 32 + 8 * tb + 8].unsqueeze(2).to_broadcast([128, 8, 64])
                            else:
                                cosb = CST.ap[:, base + 64: base + 128].unsqueeze(1).to_broadcast([128, 8, 64])
                                sinb = CST.ap[:, base + 128: base + 192].unsqueeze(1).to_broadcast([128, 8, 64])
                            t1 = tmpc[0]; t2 = tmpc[1]
                            v3 = lambda ap: ap.rearrange("p (a b) -> p a b", a=8)
                            P.op("dve", lambda e, ps=ps, cosb=cosb: e.tensor_tensor(out=v3(t1[:]), in0=v3(ps[:]), in1=cosb, op=ALU.mult), r=[ps, CST], w=[t1])
                            P.op("dve", lambda e, ps=ps, sinb=sinb: e.tensor_tensor(out=v3(t2[:])[0:64], in0=v3(ps[:])[64:128], in1=sinb[64:128], op=ALU.mult), r=[ps, CST], w=[t2])
                            P.op("dve", lambda e, ps=ps, sinb=sinb: e.tensor_tensor(out=v3(t2[:])[64:128], in0=v3(ps[:])[0:64], in1=sinb[0:64], op=ALU.mult), r=[ps, CST], w=[t2])
                            P.op("dve", lambda e, dsl=dsl: e.tensor_tensor(out=dsl, in0=t1[:], in1=t2[:], op=ALU.add), r=[t1, t2], w=[dst])
                    else:
                        ch = chan0 // 128 + dc
                        w0 = CQK.ap[:, ch * 3 + 0: ch * 3 + 1]; w1 = CQK.ap[:, ch * 3 + 1: ch * 3 + 2]; w2 = CQK.ap[:, ch * 3 + 2: ch * 3 + 3]
                        z = tmpc[0]
                        nb_ = 512 // Lc
                        v3 = lambda ap: ap.rearrange("p (a b) -> p a b", a=nb_)
                        P.op("dve", lambda e, ps=ps, w1=w1: e.tensor_scalar(out=z[:], in0=ps[:], scalar1=w1, scalar2=None, op0=ALU.mult), r=[ps, CQK], w=[z])
                        P.op("dve", lambda e, ps=ps, w0=w0: e.scalar_tensor_tensor(out=v3(z[:])[:, :, 1:Lc], in0=v3(ps[:])[:, :, 0:Lc - 1], scalar=w0, in1=v3(z[:])[:, :, 1:Lc], op0=ALU.mult, op1=ALU.add), r=[ps, CQK, z], w=[z])
                        P.op("dve", lambda e, ps=ps, w2=w2: e.scalar_tensor_tensor(out=v3(z[:])[:, :, 0:Lc - 1], in0=v3(ps[:])[:, :, 1:Lc], scalar=w2, in1=v3(z[:])[:, :, 0:Lc - 1], op0=ALU.mult, op1=ALU.add), r=[ps, CQK, z], w=[z])
                        P.op("act", lambda e, dsl=dsl: e.activation(out=dsl, in_=z[:], func=AF.Silu), r=[z], w=[dst])

        def proj_tm(wv, wg, gfunc):
            for t in range(NT):
                ps = psn()
                for j, ws in enumerate((wv, wg)):
                    for kc in range(KC):
                        P.op("pe", lambda e, ps=ps, kc=kc, j=j, ws=ws, t=t: e.matmul(ps[:, j * 256:(j + 1) * 256], lhsT=HT[:, kc, t * 128:(t + 1) * 128], rhs=ws[:, kc, :], start=(kc == 0), stop=(kc == KC - 1)), r=[ws, HT], w=[ps])
                P.op("dve", lambda e, ps=ps, t=t: e.tensor_copy(out=VG[:, t, 0:256], in_=ps[:, 0:256]), r=[ps], w=[VG.part(t, NT)])
                P.op("act", lambda e, ps=ps, t=t: e.activation(out=VG[:, t, 264:520], in_=ps[:, 256:512], func=gfunc), r=[ps], w=[VG.part(t, NT)])

        def make_ktm(kscale=1.0):
            for t in range(NT):
                pb = psbn()
                for dc in range(2):
                    P.op("pe", lambda e, pb=pb, dc=dc, t=t: e.transpose(out=pb[:, dc * 128:(dc + 1) * 128], in_=KT[:, dc, t * 128:(t + 1) * 128], identity=IDB[:]), r=[KT, IDB], w=[pb])
                P.op("dve", lambda e, pb=pb, t=t: e.tensor_scalar(out=KTM[:, t, :], in0=pb[:, 0:256], scalar1=kscale, scalar2=None, op0=ALU.mult), r=[pb], w=[KTM.part(t, NT)])

        def smalls(k):
            o = SMALL[k].lo
            mk = lambda off, shape: Tl(sbt(o + off * 4, shape, F32).ap, "sb", SMALL[k].lo, SMALL[k].hi)
            return mk(0, [4, 6]), mk(24, [2]), mk(26, [1]), mk(27, [1]), mk(28, [2])

        def out_tail(k, t, hh):
            xn = XN[k]
            st, mv, rstd, nmr, _ = smalls(k)
            ln_stats(xn.ap, xn, st, mv, rstd, None, 256)
            yield
            P.op("dve", lambda e: e.scalar_tensor_tensor(out=xn[:], in0=xn[:], scalar=mv[:, 0:1], in1=GN[:], op0=ALU.subtract, op1=ALU.mult), r=[xn, mv, GN], w=[xn])
            yield
            col = hh * 256
            gate = VG[:, t, 264:520]
            gr = VG.part(t, NT)
            if not sample:
                P.op("dve", lambda e: e.scalar_tensor_tensor(out=MIXO[:, t, col:col + 256], in0=xn[:], scalar=rstd[:], in1=gate, op0=ALU.mult, op1=ALU.mult), r=[xn, rstd, gr], w=[MIXO.part(t, NTO)])
            else:
                pp, tp = t // 4, t % 4
                P.op("dve", lambda e: e.scalar_tensor_tensor(out=xn[:], in0=xn[:], scalar=rstd[:], in1=gate, op0=ALU.mult, op1=ALU.mult), r=[xn, rstd, gr], w=[xn])
                yield
                if pp == 0:
                    P.op("dve", lambda e: e.tensor_scalar(out=MIXO[:, tp, col:col + 256], in0=xn[:], scalar1=FLG[:, 0:1], scalar2=None, op0=ALU.mult), r=[xn, FLG], w=[MIXO.part(tp, NTO)])
                else:
                    P.op("dve", lambda e: e.scalar_tensor_tensor(out=MIXO[:, tp, col:col + 256], in0=xn[:], scalar=FLG[:, pp:pp + 1], in1=MIXO[:, tp, col:col + 256], op0=ALU.mult, op1=ALU.add), r=[xn, FLG, MIXO.part(tp, NTO)], w=[MIXO.part(tp, NTO)])
            yield

        def run_rr(gens):
            gens = list(gens)
            while gens:
                nxt = []
                for g in gens:
                    try:
                        next(g)
                        nxt.append(g)
                    except StopIteration:
                        pass
                gens = nxt

        def state_chain(sl, s, d, h, kind):
            ret = kind == "ret"
            ncol = 256 if ret else 257
            SM = MST[sl][d]; Kx = KX[sl][d]
            src4 = st_ret if ret else st_C
            srcn = None if ret else st_n
            if not sample:
                P.op("dve", lambda e: e.memset(SM[:], 0.0), w=[SM])
            else:
                P.op("sp", lambda e: e.dma_start(out=SM[:, :, 0:256], in_=src4[d, h].rearrange("(kc p) v -> p kc v", p=128)), w=[SM], dma="sm%d" % SM.lo)
                if srcn is not None:
                    P.op("sp", lambda e: e.dma_start(out=SM[:, :, 256:257], in_=srcn[d, h].unsqueeze(2), allow_slow_non_contiguous=True), w=[SM], dma="sm%d" % SM.lo)
            yield
            order = range(n) if d == 0 else range(n - 1, -1, -1)
            for c in order:
                t = s * n + c
                sa = SALL[sl][d][c]
                P.op("act", lambda e, sa=sa: e.activation(out=sa[:, :, 0:ncol], in_=SM[:, :, 0:ncol], func=AF.Copy), r=[SM], w=[sa])
                if ret:
                    sc_ap, sc_r = DEC.ap[:, 2 + d:3 + d], DEC
                    dk_ap, dk_r = DEC.ap[:, 4 + d:5 + d], DEC
                else:
                    sc_ap, sc_r = GWK.ap[:, t, d * 4 + h:d * 4 + h + 1], GWK
                    dk_ap, dk_r = GSC.ap[:, t, d * 4 + h:d * 4 + h + 1], GSC
                P.op("act", lambda e, t=t, sc_ap=sc_ap: e.activation(out=Kx[:], in_=KTM[:, t, :], func=AF.Identity, scale=sc_ap), r=[KTM.part(t, NT), sc_r], w=[Kx])
                yield
                if ret:
                    while navail() < 1:
                        yield
                    ps = psh()
                    for kc in range(2):
                        P.op("pe", lambda e, kc=kc, ps=ps, t=t: e.matmul(ps[:, kc * 256:(kc + 1) * 256], lhsT=Kx[:, kc * 128:(kc + 1) * 128], rhs=VG[:, t, 0:256], start=True, stop=True), r=[Kx, VG.part(t, NT)], w=[ps])
                    yield
                    P.op("dve", lambda e, ps=ps, dk_ap=dk_ap: e.scalar_tensor_tensor(out=SM[:, :, 0:256], in0=SM[:, :, 0:256], scalar=dk_ap, in1=ps[:].rearrange("p (a b) -> p a b", a=2), op0=ALU.mult, op1=ALU.add), r=[SM, ps, dk_r], w=[SM])
                    prel(ps)
                    yield
                else:
                    while navail() < 2:
                        yield
                    pss = [psh(), psh()]
                    for kc in range(2):
                        P.op("pe", lambda e, kc=kc, ps=pss[kc], t=t: e.matmul(ps[:, 0:257], lhsT=Kx[:, kc * 128:(kc + 1) * 128], rhs=VG[:, t, 0:257], start=True, stop=True), r=[Kx, VG.part(t, NT)], w=[pss[kc]])
                    yield
                    for kc in range(2):
                        P.op("dve", lambda e, ps=pss[kc], kc=kc, dk_ap=dk_ap: e.scalar_tensor_tensor(out=SM[:, kc, :], in0=SM[:, kc, :], scalar=dk_ap, in1=ps[:, 0:257], op0=ALU.mult, op1=ALU.add), r=[SM, pss[kc], dk_r], w=[SM])
                        prel(pss[kc])
                    yield
            if not sample:
                dst4 = o_ret if ret else o_C
                P.op("sp", lambda e: e.dma_start(out=dst4[s, d, h].rearrange("(kc p) v -> p kc v", p=128), in_=SM[:, :, 0:256]), r=[SM], dma="smo%d" % SM.lo)
                if not ret:
                    P.op("sp", lambda e: e.dma_start(out=o_n[s, d, h].unsqueeze(2), in_=SM[:, :, 256:257], allow_slow_non_contiguous=True), r=[SM], dma="smo%d" % SM.lo)
            yield

        def out_chunk_ret(k, sl, s, c, h):
            t = s * n + c
            tk = slice(t * 128, (t + 1) * 128)
            attm = ATT[k]; xn = XN[k]
            while navail() < 2:
                yield
            psA = psh()
            for kc in range(2):
                P.op("pe", lambda e, kc=kc: e.matmul(psA[:, 0:128], lhsT=KT[:, kc, tk], rhs=QT[:, kc, tk], start=(kc == 0), stop=(kc == 1)), r=[KT, QT], w=[psA])
            psS = psh()
            for d in range(2):
                sa = SALL[sl][d][c]
                for kc in range(2):
                    P.op("pe", lambda e, kc=kc, d=d, sa=sa: e.matmul(psS[:, d * 256:(d + 1) * 256], lhsT=QT[:, kc, tk], rhs=sa[:, kc, 0:256], start=(kc == 0), stop=(kc == 1)), r=[QT, sa], w=[psS])
            yield
            P.op("dve", lambda e: e.tensor_tensor(out=attm[:, 0, :], in0=psA[:, 0:128], in1=RM[:], op=ALU.mult), r=[psA, RM], w=[attm])
            prel(psA)
            yield
            while navail() < 1:
                yield
            psO = psh()
            P.op("pe", lambda e: e.matmul(psO[:, 0:256], lhsT=attm[:, 0, :], rhs=VG[:, t, 0:256], start=True, stop=True), r=[attm, VG.part(t, NT)], w=[psO])
            P.op("dve", lambda e: e.tensor_scalar(out=xn[:], in0=psS[:, 0:256], scalar1=DEC[:, 0:1], scalar2=None, op0=ALU.mult), r=[psS, DEC], w=[xn])
            yield
            P.op("dve", lambda e: e.scalar_tensor_tensor(out=xn[:], in0=psS[:, 256:512], scalar=DEC[:, 1:2], in1=xn[:], op0=ALU.mult, op1=ALU.add), r=[psS, DEC, xn], w=[xn])
            yield
            P.op("dve", lambda e: e.tensor_tensor(out=xn[:], in0=psO[:, 0:256], in1=xn[:], op=ALU.add), r=[psO, xn], w=[xn])
            prel(psS); prel(psO)
            yield
            yield from out_tail(k, t, h)

        def out_chunk_ml(k, sl, s, c, h):
            t = s * n + c
            tk = slice(t * 128, (t + 1) * 128)
            attm = ATT[k]; xn = XN[k]; Dk = DMS[k]; WT = WTS[k]
            _, _, _, _, dd = smalls(k)
            bigm = CST.ap[:, C_BIGF:C_BIGF + 256]
            while navail() < 1:
                yield
            psA = psh()
            for kc in range(2):
                P.op("pe", lambda e, kc=kc: e.matmul(psA[:, 0:128], lhsT=KT[:, kc, tk], rhs=QT[:, kc, tk], start=(kc == 0), stop=(kc == 1)), r=[KT, QT], w=[psA])
            mu2 = GMU.ap[:, t, h:h + 5:4]
            P.op("dve", lambda e: e.tensor_tensor(out=Dk[:], in0=IDF.unsqueeze(1).to_broadcast([128, 2, 128]), in1=mu2.unsqueeze(2).to_broadcast([128, 2, 128]), op=ALU.mult), r=[CST, GMU], w=[Dk])
            yield
            P.op("pe", lambda e: e.matmul(psA[:, 128:384], lhsT=ONESF, rhs=Dk[:].rearrange("p a b -> p (a b)"), start=True, stop=False), r=[CST, Dk], w=[psA])
            P.op("pe", lambda e: e.matmul(psA[:, 128:384], lhsT=IDF, rhs=bigm, start=False, stop=True), r=[CST], w=[psA])
            yield
            for d in range(2):
                col = d * 4 + h
                P.op("act", lambda e, d=d, col=col: e.activation(out=WT[:], in_=psA[:, 128 + d * 128:256 + d * 128], func=AF.Exp, bias=GA[:, t, col:col + 1], scale=-1.0), r=[psA, GA], w=[WT])
                yield
                P.op("dve", lambda e, d=d: e.tensor_tensor(out=attm[:, d, :], in0=psA[:, 0:128], in1=WT[:], op=ALU.mult), r=[psA, WT], w=[attm])
                yield
            prel(psA)
            for d in range(2):
                col = d * 4 + h
                sa = SALL[sl][d][c]
                td = TOT[k][d]
                while navail() < 2:
                    yield
                psN = psh()
                P.op("pe", lambda e, d=d, psN=psN: e.matmul(psN[:, 0:257], lhsT=attm[:, d, :], rhs=VG[:, t, 0:257], start=True, stop=True), r=[attm, VG.part(t, NT)], w=[psN])
                psI = psh()
                for kc in range(2):
                    P.op("pe", lambda e, kc=kc, psI=psI, sa=sa: e.matmul(psI[:, 0:257], lhsT=QT[:, kc, tk], rhs=sa[:, kc, :], start=(kc == 0), stop=(kc == 1)), r=[QT, sa], w=[psI])
                yield
                P.op("dve", lambda e, psI=psI, td=td, col=col: e.tensor_scalar(out=td[:], in0=psI[:, 0:257], scalar1=GSP[:, t, col:col + 1], scalar2=None, op0=ALU.mult), r=[psI, GSP], w=[td])
                prel(psI)
                yield
                P.op("dve", lambda e, psN=psN, td=td: e.tensor_tensor(out=td[:], in0=psN[:, 0:257], in1=td[:], op=ALU.add), r=[psN, td], w=[td])
                prel(psN)
                yield
            den2 = TOTT[k][:, :, 256:257]
            P.op("dve", lambda e: e.scalar_tensor_tensor(out=dd[:, 0:2].unsqueeze(2), in0=den2, scalar=-1.0, in1=den2, op0=ALU.mult, op1=ALU.max), r=[TOTT[k]], w=[dd])
            yield
            P.op("dve", lambda e: e.tensor_tensor(out=dd[:, 0:2], in0=dd[:, 0:2], in1=GEM.ap[:, t, h:h + 5:4], op=ALU.max), r=[dd, GEM], w=[dd])
            yield
            P.op("dve", lambda e: e.reciprocal(out=dd[:], in_=dd[:]), r=[dd], w=[dd])
            yield
            P.op("dve", lambda e: e.tensor_scalar(out=xn[:], in0=TOT[k][0][:, 0:256], scalar1=dd[:, 0:1], scalar2=None, op0=ALU.mult), r=[TOT[k][0], dd], w=[xn])
            yield
            P.op("dve", lambda e: e.scalar_tensor_tensor(out=xn[:], in0=TOT[k][1][:, 0:256], scalar=dd[:, 1:2], in1=xn[:], op0=ALU.mult, op1=ALU.add), r=[TOT[k][1], dd, xn], w=[xn])
            yield
            yield from out_tail(k, t, 4 + h)

        def mixer(h, kind):
            ret = kind == "ret"
            if ret:
                lgf = LG.ap[:, h:h + 1]; lgb = LG.ap[:, 4 + h:5 + h]
                vec = lambda k_: CST.ap[:, C_VEC + k_:C_VEC + k_ + 1]
                for col, (src, lg) in enumerate([(vec(0), lgf), (vec(1), lgb), (vec(2), lgf), (vec(3), lgb)]):
                    P.op("act", lambda e, col=col, src=src, lg=lg: e.activation(out=DEC[:, col:col + 1], in_=src, func=AF.Exp, scale=lg), r=[CST, LG], w=[DEC])
                P.op("act", lambda e: e.activation(out=DEC[:, 4:5], in_=lgf, func=AF.Exp, scale=128.0), r=[LG], w=[DEC])
                P.op("act", lambda e: e.activation(out=DEC[:, 5:6], in_=lgb, func=AF.Exp, scale=128.0), r=[LG], w=[DEC])
                P.op("act", lambda e: e.activation(out=RM[:], in_=cstm(C_D1), func=AF.Exp, scale=lgf), r=[CST, LG], w=[RM])
                P.op("dve", lambda e: e.tensor_tensor(out=RM[:], in0=RM[:], in1=cstm(C_U1), op=ALU.mult), r=[RM, CST], w=[RM])
                P.op("act", lambda e: e.activation(out=RMT[:], in_=cstm(C_D2), func=AF.Exp, scale=lgb), r=[CST, LG], w=[RMT])
                P.op("dve", lambda e: e.tensor_tensor(out=RMT[:], in0=RMT[:], in1=cstm(C_U2), op=ALU.mult), r=[RMT, CST], w=[RMT])
                P.op("dve", lambda e: e.tensor_tensor(out=RM[:], in0=RM[:], in1=RMT[:], op=ALU.add), r=[RM, RMT], w=[RM])
            gsrc = gn_ret if ret else gn_ml
            P.op("sp", lambda e: e.dma_start(out=GN[:], in_=gsrc[0:1, h * 256:(h + 1) * 256].broadcast_to([128, 256])), w=[GN], dma="gn")
            ocf = out_chunk_ret if ret else out_chunk_ml
            for s0 in range(0, nseq, NSL):
                run_rr([state_chain(sl, s0 + sl, d, h, kind) for sl in range(NSL) for d in range(2)])
                jobs = [(sl, s0 + sl, c) for sl in range(NSL) for c in range(n)]
                for j0 in range(0, len(jobs), 2):
                    run_rr([ocf(k, jobs[j0 + k][0], jobs[j0 + k][1], jobs[j0 + k][2], h) for k in range(min(2, len(jobs) - j0))])

        def gates_pre():
            for t in range(NT):
                ps = psn()
                for kc in range(KC):
                    P.op("pe", lambda e, ps=ps, kc=kc, t=t: e.matmul(ps[:, 0:16], lhsT=HT[:, kc, t * 128:(t + 1) * 128], rhs=WGT[:, kc, :], start=(kc == 0), stop=(kc == KC - 1)), r=[HT, WGT], w=[ps])
                P.op("dve", lambda e, ps=ps, t=t: e.tensor_tensor(out=GTS[:, t, :], in0=ps[:, 0:16], in1=BG[:], op=ALU.add), r=[ps, BG], w=[GTS])
            P.op("act", lambda e: e.activation(out=LFN[:], in_=GTS[:], func=AF.Exp, scale=-1.0), r=[GTS], w=[LFN])
            P.op("act", lambda e: e.activation(out=LFN[:], in_=LFN[:], func=AF.Ln, bias=1.0), r=[LFN], w=[LFN])
            for t in range(NT):
                ps = psn()
                P.op("pe", lambda e, ps=ps, t=t: e.matmul(ps[:, 0:4], lhsT=cstm(C_U1), rhs=LFN[:, t, 4:8], start=True, stop=True), r=[CST, LFN], w=[ps])
                P.op("pe", lambda e, ps=ps, t=t: e.matmul(ps[:, 4:8], lhsT=cstm(C_U2), rhs=LFN[:, t, 12:16], start=True, stop=True), r=[CST, LFN], w=[ps])
                P.op("pe", lambda e, ps=ps, t=t: e.matmul(ps[:, 8:12], lhsT=ONESF, rhs=LFN[:, t, 4:8], start=True, stop=True), r=[CST, LFN], w=[ps])
                P.op("pe", lambda e, ps=ps, t=t: e.matmul(ps[:, 12:16], lhsT=ONESF, rhs=LFN[:, t, 12:16], start=True, stop=True), r=[CST, LFN], w=[ps])
                P.op("dve", lambda e, ps=ps, t=t: e.tensor_copy(out=GNB[:, t, :], in_=ps[:, 0:8]), r=[ps], w=[GNB])
                P.op("dve", lambda e, ps=ps, t=t: e.tensor_copy(out=GBL[:, t, :], in_=ps[:, 8:16]), r=[ps], w=[GBL])
                P.op("dve", lambda e, t=t: e.tensor_tensor(out=GA[:, t, 0:4], in0=GTS[:, t, 0:4], in1=GNB[:, t, 0:4], op=ALU.add), r=[GTS, GNB], w=[GA])
                P.op("dve", lambda e, t=t: e.tensor_tensor(out=GA[:, t, 4:8], in0=GTS[:, t, 8:12], in1=GNB[:, t, 4:8], op=ALU.add), r=[GTS, GNB], w=[GA])
            for s in range(nseq):
                if not sample:
                    P.op("dve", lambda e: e.memset(mprev[:], 0.0), w=[mprev])
                else:
                    P.op("sp", lambda e: e.dma_start(out=mprev[:], in_=st_m[0:1, :].broadcast_to([128, 8])), w=[mprev], dma="mprev")
                for d in range(2):
                    ds = slice(d * 4, d * 4 + 4)
                    order = range(n) if d == 0 else range(n - 1, -1, -1)
                    neg = cstm(C_NEGF if d == 0 else C_NEGB)
                    for c in order:
                        t = s * n + c
                        P.op("dve", lambda e, t=t, ds=ds: e.tensor_tensor(out=Dm[:], in0=IDF.unsqueeze(1).to_broadcast([128, 4, 128]), in1=GA[:, t, ds].unsqueeze(2).to_broadcast([128, 4, 128]), op=ALU.mult), r=[CST, GA], w=[Dm])
                        ps = psn()
                        P.op("pe", lambda e, ps=ps: e.matmul(ps[:], lhsT=ONESF, rhs=Dm[:].rearrange("p a b -> p (a b)"), start=True, stop=True), r=[CST, Dm], w=[ps])
                        P.op("dve", lambda e, ps=ps, neg=neg: e.tensor_tensor(out=Am[:], in0=ps[:].rearrange("p (a b) -> p a b", a=4), in1=neg.unsqueeze(1).to_broadcast([128, 4, 128]), op=ALU.add), r=[ps, CST], w=[Am])
                        P.op("dve", lambda e: e.tensor_reduce(out=v8[0][:, 0:4], in_=Am[:], axis=AX.X, op=ALU.max), r=[Am], w=[v8[0]])
                        P.op("dve", lambda e, ps=ps: e.tensor_reduce(out=v8[1][:, 0:4], in_=ps[:].rearrange("p (a b) -> p a b", a=4), axis=AX.X, op=ALU.max), r=[ps], w=[v8[1]])
                        P.op("dve", lambda e, t=t, ds=ds: e.tensor_tensor(out=GMU[:, t, ds], in0=v8[0][:, 0:4], in1=mprev[:, ds], op=ALU.max), r=[v8[0], mprev], w=[GMU])
                        P.op("dve", lambda e, ds=ds: e.tensor_tensor(out=mnew[:, ds], in0=v8[1][:, 0:4], in1=mprev[:, ds], op=ALU.max), r=[v8[1], mprev], w=[mnew])
                        P.op("dve", lambda e, t=t, ds=ds: e.tensor_tensor(out=v8[2][:, 0:4], in0=mprev[:, ds], in1=GMU[:, t, ds], op=ALU.subtract), r=[mprev, GMU], w=[v8[2]])
                        P.op("act", lambda e, t=t, ds=ds: e.activation(out=GSP[:, t, ds], in_=v8[2][:, 0:4], func=AF.Exp), r=[v8[2]], w=[GSP])
                        P.op("dve", lambda e, t=t, ds=ds: e.tensor_tensor(out=v8[3][:, 0:4], in0=GA[:, t, ds], in1=mnew[:, ds], op=ALU.subtract), r=[GA, mnew], w=[v8[3]])
                        P.op("act", lambda e, t=t, ds=ds: e.activation(out=GWK[:, t, ds], in_=v8[3][:, 0:4], func=AF.Exp), r=[v8[3]], w=[GWK])
                        P.op("dve", lambda e, ds=ds: e.tensor_tensor(out=v8[2][:, 4:8], in0=mprev[:, ds], in1=mnew[:, ds], op=ALU.subtract), r=[mprev, mnew], w=[v8[2]])
                        P.op("act", lambda e, t=t, ds=ds: e.activation(out=GSC[:, t, ds], in_=v8[2][:, 4:8], func=AF.Exp), r=[v8[2]], w=[GSC])
                        P.op("dve", lambda e, t=t, ds=ds: e.tensor_tensor(out=v8[3][:, 4:8], in0=GNB[:, t, ds], in1=GMU[:, t, ds], op=ALU.subtract), r=[GNB, GMU], w=[v8[3]])
                        P.op("act", lambda e, t=t, ds=ds: e.activation(out=GEM[:, t, ds], in_=v8[3][:, 4:8], func=AF.Exp), r=[v8[3]], w=[GEM])
                        P.op("dve", lambda e, ds=ds, t=t: e.tensor_tensor(out=mprev[:, ds], in0=mnew[:, ds], in1=GBL[:, t, ds], op=ALU.subtract), r=[mnew, GBL], w=[mprev])
                if not sample:
                    P.op("sp", lambda e, s=s: e.dma_start(out=o_m[s:s + 1, :], in_=mprev[0:1, :]), r=[mprev], dma="om")

        def mod_blks(hh):
            return list(range(16 + hh * 4, 20 + hh * 4))

        for h in range(NH):
            wq = load_w(h * 256); proj_fm(wq, QT, "rq", 0)
            wk = load_w(1024 + h * 256); proj_fm(wk, KT, "rk", 0)
            wv = load_w(2048 + h * 256); wg = load_w(3072 + h * 256)
            if not sample:
                mod_dma(mod_blks(h))
            proj_tm(wv, wg, AF.Silu)
            make_ktm()
            mixer(h, "ret")
            if not sample:
                mod_compute(mod_blks(h))
        gates_pre()
        for h in range(NH):
            wq = load_w(4096 + h * 256); proj_fm(wq, QT, "mq", h * 256)
            wk = load_w(5120 + h * 256); proj_fm(wk, KT, "mk", 1024 + h * 256)
            wv = load_w(6144 + h * 256); wg = load_w(7168 + h * 256)
            if not sample:
                mod_dma(mod_blks(4 + h))
            proj_tm(wv, wg, AF.Sigmoid)
            make_ktm(1.0 / 16.0)
            mixer(h, "ml")
            if not sample:
                mod_compute(mod_blks(4 + h))

        MT = sbt(O_HT, [16, Town], BF16)
        for t in range(NTO):
            transpose_tile(MIXO[:, t, :], MIXO.part(t, NTO), MT, MT, t * 128)
        rb = Bump(O_PROJ, SZ_PROJ)
        r_g1 = rb([D], F32); r_l1g = rb([D], F32); r_l1b = rb([D], F32); r_sc2 = rb([D], F32); r_sh2 = rb([D], F32)
        load_modrow(r_g1, 2, mg); load_row(r_l1g, ln1_g); load_row(r_l1b, ln1_b)
        load_modrow(r_sh2, 3, mg); load_modrow(r_sc2, 4, mg, plus1=True)
        YA = [sbt((O_MIXO if t < 4 else O_MIX) + (t % 4) * 8192, [D], F32) for t in range(NTO)]
        hb = Bump(O_HT + 32768, 32768)
        xts = [hb([D], F32) for _ in range(2)]
        h2s = [hb([D], BF16) for _ in range(2)]
        wsl2 = [sbt(O_W + k * 8192, [16, 256], BF16) for k in range(2)]
        for cbk in range(8):
            ws = wsl2[cbk % 2]
            P.op("pool", lambda e, ws=ws, cbk=cbk: e.dma_start(out=ws[:], in_=w_out[:, cbk * 256:(cbk + 1) * 256].rearrange("(kc p) n -> p kc n", p=128)), w=[ws], dma="wsl%d" % (cbk % 2))
            for t in range(NTO):
                ps = psn()
                for kc in range(KC):
                    P.op("pe", lambda e, ps=ps, kc=kc, t=t, ws=ws: e.matmul(ps[:, 0:256], lhsT=MT[:, kc, t * 128:(t + 1) * 128], rhs=ws[:, kc, :], start=(kc == 0), stop=(kc == KC - 1)), r=[MT, ws], w=[ps])
                P.op("dve", lambda e, ps=ps, t=t, cbk=cbk: e.tensor_tensor(out=YA[t][:, cbk * 256:(cbk + 1) * 256], in0=ps[:, 0:256], in1=r_g1[:, cbk * 256:(cbk + 1) * 256], op=ALU.mult), r=[ps, r_g1], w=[YA[t]])
        smo = [sbt(O_MIX + 32768 + k * 256, [64], F32) for k in range(2)]

        def o_tile(t):
            def g(k):
                xt = xts[k]; h2 = h2s[k]; ya = YA[t]
                st, mv, rstd, nmr = small4(smo[k])
                P.op("sp", lambda e: e.dma_start(out=xt[:], in_=x_own[t * 128:(t + 1) * 128, :]), w=[xt], dma="xo%d" % k)
                yield
                P.op("dve", lambda e: e.scalar_tensor_tensor(out=ya[:], in0=xt[:], scalar=ALPHA, in1=ya[:], op0=ALU.mult, op1=ALU.add), r=[xt, ya], w=[ya])
                yield
                ln_stats(ya.ap, ya, st, mv, rstd, None, D)
                yield
                P.op("dve", lambda e: e.scalar_tensor_tensor(out=ya[:], in0=ya[:], scalar=mv[:, 0:1], in1=r_l1g[:], op0=ALU.subtract, op1=ALU.mult), r=[ya, mv, r_l1g], w=[ya])
                yield
                P.op("dve", lambda e: e.scalar_tensor_tensor(out=ya[:], in0=ya[:], scalar=rstd[:], in1=r_l1b[:], op0=ALU.mult, op1=ALU.add), r=[ya, rstd, r_l1b], w=[ya])
                yield
                P.op("sp", lambda e: e.dma_start(out=x1_d[x1off + t * 128: x1off + (t + 1) * 128, :], in_=ya[:]), r=[ya], w=[DR("x1", x1off // 128 + t, x1off // 128 + t + 1)], dma="x1o%d" % t)
                ln_stats(ya.ap, ya, st, mv, rstd, None, D)
                yield
                P.op("dve", lambda e: e.scalar_tensor_tensor(out=xt[:], in0=ya[:], scalar=mv[:, 0:1], in1=r_sc2[:], op0=ALU.subtract, op1=ALU.mult), r=[ya, mv, r_sc2], w=[xt])
                yield
                P.op("dve", lambda e: e.scalar_tensor_tensor(out=h2[:], in0=xt[:], scalar=rstd[:], in1=r_sh2[:], op0=ALU.mult, op1=ALU.add), r=[xt, rstd, r_sh2], w=[h2])
                yield
                yield from transpose_tile_g(h2.ap, h2, MT, MT, t * 128)
            return g
        run_pipe([o_tile(t) for t in range(NTO)], 2)

        H2T = MT
        achunks = []
        for k in range(21):
            achunks.append(sbt(O_PROJ + k * Town * 2, [Town], BF16))
        for k in range(16):
            achunks.append(sbt(O_MIXO + k * Town * 2, [Town], BF16))
        for k in range(6):
            achunks.append(sbt(O_HT + 32768 + k * Town * 2, [Town], BF16))
        wd2 = sbt(O_HT + 32768 + 6 * 2048, [NFF, 128], BF16)
        wd1 = sbt(O_W, [NFF, 128], BF16)
        fb = Bump(O_MIX, SZ_MIX)
        zt = [fb([512], F32) for _ in range(2)]
        z2 = [fb([512], F32) for _ in range(2)]
        wup = [sbt(O_W + k * 8192, [16, 256], BF16) for k in range(2)]
        nb_ = 512 // Lc
        v3 = lambda ap: ap.rearrange("p (a b) -> p a b", a=nb_)
        for c in range(NFF):
            ws = wup[c % 2]
            P.op("pool", lambda e, ws=ws, c=c: e.dma_start(out=ws[:, :, 0:128], in_=w_up[:, c * 128:(c + 1) * 128].rearrange("(kc p) n -> p kc n", p=128)), w=[ws], dma="wup%d" % (c % 2))
            P.op("pool", lambda e, ws=ws, c=c: e.dma_start(out=ws[:, :, 128:256], in_=w_up[:, DFF + c * 128:DFF + (c + 1) * 128].rearrange("(kc p) n -> p kc n", p=128)), w=[ws], dma="wup%d" % (c % 2))
            w0 = CFF.ap[:, c * 3:c * 3 + 1]; w1 = CFF.ap[:, c * 3 + 1:c * 3 + 2]; w2 = CFF.ap[:, c * 3 + 2:c * 3 + 3]
            for tb in range(Town // 512):
                pu = psn()
                for kc in range(KC):
                    P.op("pe", lambda e, pu=pu, kc=kc, tb=tb, ws=ws: e.matmul(pu[:], lhsT=ws[:, kc, 0:128], rhs=H2T[:, kc, tb * 512:(tb + 1) * 512], start=(kc == 0), stop=(kc == KC - 1)), r=[ws, H2T], w=[pu])
                pg = psn()
                for kc in range(KC):
                    P.op("pe", lambda e, pg=pg, kc=kc, tb=tb, ws=ws: e.matmul(pg[:], lhsT=ws[:, kc, 128:256], rhs=H2T[:, kc, tb * 512:(tb + 1) * 512], start=(kc == 0), stop=(kc == KC - 1)), r=[ws, H2T], w=[pg])
                z = zt[tb % 2]; zz = z2[tb % 2]
                P.op("dve", lambda e, pu=pu, z=z, w1=w1: e.tensor_scalar(out=z[:], in0=pu[:], scalar1=w1, scalar2=None, op0=ALU.mult), r=[pu, CFF], w=[z])
                P.op("dve", lambda e, pu=pu, z=z, w0=w0: e.scalar_tensor_tensor(out=v3(z[:])[:, :, 1:Lc], in0=v3(pu[:])[:, :, 0:Lc - 1], scalar=w0, in1=v3(z[:])[:, :, 1:Lc], op0=ALU.mult, op1=ALU.add), r=[pu, CFF, z], w=[z])
                P.op("dve", lambda e, pu=pu, z=z, w2=w2: e.scalar_tensor_tensor(out=v3(z[:])[:, :, 0:Lc - 1], in0=v3(pu[:])[:, :, 1:Lc], scalar=w2, in1=v3(z[:])[:, :, 0:Lc - 1], op0=ALU.mult, op1=ALU.add), r=[pu, CFF, z], w=[z])
                P.op("act", lambda e, z=z, zz=zz: e.activation(out=zz[:], in_=z[:], func=AF.Silu), r=[z], w=[zz])
                ac = achunks[c]
                P.op("dve", lambda e, pg=pg, zz=zz, ac=ac, tb=tb: e.tensor_tensor(out=ac[:, tb * 512:(tb + 1) * 512], in0=pg[:], in1=zz[:], op=ALU.mult), r=[pg, zz], w=[ac])
        fb = Bump(O_MIX, SZ_MIX)
        r_g2 = fb([D], F32); r_l2g = fb([D], F32); r_l2b = fb([D], F32)
        x1t = fb([D], F32); y2 = fb([D], F32)
        st, mv, rstd, nmr = ST_C, MV_C, RSTD_C, NMR_C
        load_modrow(r_g2, 5, mg); load_row(r_l2g, ln2_g); load_row(r_l2b, ln2_b)
        wds = [wd1, wd2]
        stg = [sbt(O_W + 11264 + k * 512, [128], F32) for k in range(8)]
        cnt = 0
        for cbk in range(16):
            ws = wds[cbk % 2]
            P.op("pool", lambda e, ws=ws, cbk=cbk: e.dma_start(out=ws[:], in_=w_down[:, cbk * 128:(cbk + 1) * 128].rearrange("(c p) n -> p c n", p=128)), w=[ws], dma="wd%d" % (cbk % 2))
            for t in range(NTO):
                ps = psn()
                for c in range(NFF):
                    P.op("pe", lambda e, ps=ps, c=c, t=t, ws=ws: e.matmul(ps[:, 0:128], lhsT=achunks[c][:, t * 128:(t + 1) * 128], rhs=ws[:, c, :], start=(c == 0), stop=(c == NFF - 1)), r=[achunks[c], ws], w=[ps])
                sg_ = stg[cnt % 8]
                P.op("dve", lambda e, ps=ps, cbk=cbk, sg_=sg_: e.tensor_tensor(out=sg_[:], in0=ps[:, 0:128], in1=r_g2[:, cbk * 128:(cbk + 1) * 128], op=ALU.mult), r=[ps, r_g2], w=[sg_])
                P.op("sp", lambda e, sg_=sg_, cbk=cbk, t=t: e.dma_start(out=y2_d[x1off + t * 128: x1off + (t + 1) * 128, cbk * 128:(cbk + 1) * 128], in_=sg_[:]), r=[sg_], w=[DR("y2", x1off // 128 + t, x1off // 128 + t + 1)], dma="stg%d" % (cnt % 8))
                cnt += 1
        for t in range(NTO):
            row = x1off // 128 + t
            P.op("sp", lambda e, t=t: e.dma_start(out=x1t[:], in_=x1_d[x1off + t * 128: x1off + (t + 1) * 128, :]), r=[DR("x1", row, row + 1)], w=[x1t], dma="x1t")
            P.op("sp", lambda e, t=t: e.dma_start(out=y2[:], in_=y2_d[x1off + t * 128: x1off + (t + 1) * 128, :]), r=[DR("y2", row, row + 1)], w=[y2], dma="y2t")
            P.op("dve", lambda e: e.scalar_tensor_tensor(out=y2[:], in0=x1t[:], scalar=ALPHA, in1=y2[:], op0=ALU.mult, op1=ALU.add), r=[x1t, y2], w=[y2])
            ln_stats(y2.ap, y2, st, mv, rstd, None, D)
            P.op("dve", lambda e: e.scalar_tensor_tensor(out=y2[:], in0=y2[:], scalar=mv[:, 0:1], in1=r_l2g[:], op0=ALU.subtract, op1=ALU.mult), r=[y2, mv, r_l2g], w=[y2])
            P.op("dve", lambda e: e.scalar_tensor_tensor(out=y2[:], in0=y2[:], scalar=rstd[:], in1=r_l2b[:], op0=ALU.mult, op1=ALU.add), r=[y2, rstd, r_l2b], w=[y2])
            P.op("sp", lambda e, t=t: e.dma_start(out=y_out[t * 128:(t + 1) * 128, :], in_=y2[:]), r=[y2], dma="yout")

    stage_mod(list(range(16)), O_PROJ, O_MIX)
    for gi in GROUPS:
        process_group(gi)
    P.emit()
    es.close()
    return nc


GROUPS = (0, 1)
_CACHE = {}


def kernel(x_prompt, x_sample, state_ret, state_mlstm_C, state_mlstm_n, state_mlstm_m, c, c_ctx,
           w_mod, b_mod, w_in, b_gate, conv_qk, ret_theta, gn_ret, gn_mlstm, w_out,
           ln1_g, ln1_b, w_up, conv_ff, w_down, ln2_g, ln2_b):
    f = lambda a: np.ascontiguousarray(np.asarray(a, dtype=np.float32))
    if "nc" not in _CACHE:
        _CACHE["nc"] = build_program()
    nc = _CACHE["nc"]
    cst = make_consts()
    xp_all = f(x_prompt).reshape(8, 1024, D)
    xs_all = f(x_sample)
    shared = dict(
        w_mod=f(w_mod)[0], b_mod=f(b_mod), w_in=f(w_in)[0], b_gate=f(b_gate),
        conv_qkT=f(f(conv_qk)[0].T.reshape(16, 128, 3).transpose(1, 0, 2).reshape(128, 48)),
        theta=f(ret_theta).reshape(1, 8), gn_ret=f(gn_ret).reshape(1, 1024), gn_ml=f(gn_mlstm).reshape(1, 1024),
        w_out=f(w_out)[0], ln1_g=f(ln1_g), ln1_b=f(ln1_b), w_up=f(w_up)[0],
        conv_ffT=f(f(conv_ff)[0].T.reshape(NFF, 128, 3).transpose(1, 0, 2).reshape(128, NFF * 3)),
        w_down=f(w_down)[0], ln2_g=f(ln2_g), ln2_b=f(ln2_b), cst=cst)
    in_maps = []
    for i in range(8):
        b, p = i // 4, i % 4
        fl = np.zeros((1, 4), np.float32); fl[0, p] = 1.0
        cT = np.stack([f(c_ctx).reshape(16, 128).T, f(c)[b].reshape(16, 128).T], axis=2).reshape(128, 32)
        m = dict(shared)
        m.update(xp=xp_all[i], xs=xs_all[b], xo=f(xs_all[b, p * 512:(p + 1) * 512]), flags=fl, cT=f(cT),
                 st_ret=f(state_ret)[b, 0], st_C=f(state_mlstm_C)[b, 0],
                 st_n=f(f(state_mlstm_n)[b, 0].reshape(2, 4, 2, 128).transpose(0, 1, 3, 2)),
                 st_m=f(state_mlstm_m)[b, 0].reshape(1, 8))
        in_maps.append(m)
    res = run_bass_kernel_spmd(nc, in_maps, core_ids=list(range(8)))
    R = res.results
    y_prompt = np.concatenate([R[i]["yp"] for i in range(8)], 0).reshape(32, 256, D)
    y_sample = np.stack([np.concatenate([R[b * 4 + p]["ys"] for p in range(4)], 0) for b in range(2)], 0)
    n_ret = np.concatenate([R[i]["o_ret"] for i in range(8)], 0)[:, None]
    n_C = np.concatenate([R[i]["o_C"] for i in range(8)], 0)[:, None]
    n_n = np.concatenate([R[i]["o_n"] for i in range(8)], 0).transpose(0, 1, 2, 4, 3).reshape(32, 2, 4, 256)[:, None]
    n_m = np.concatenate([R[i]["o_m"] for i in range(8)], 0).reshape(32, 2, 4)[:, None]
    return (y_prompt, y_sample, np.ascontiguousarray(n_ret), np.ascontiguousarray(n_C),
            np.ascontiguousarray(n_n), np.ascontiguousarray(n_m))
```

```python
import contextlib
import numpy as np
import concourse.bass as bass
import concourse.mybir as mybir
from concourse.bass_utils import run_bass_kernel_spmd

F32 = mybir.dt.float32
BF16 = mybir.dt.bfloat16
ALU = mybir.AluOpType
AF = mybir.ActivationFunctionType
AX = mybir.AxisListType

D = 2048
KC = 16
HD = 256
NH = 4
DFF = 5504
NFF = 43
NIN = 8208
ALPHA = 2.0 ** 0.25
BIG = 1.0e9
SAME_ENGINE_SYNC = True
DEBUG_IDS = set()
GRAN = 256


class Rg:
    __slots__ = ("space", "lo", "hi")

    def __init__(self, space, lo, hi):
        self.space, self.lo, self.hi = space, lo, hi


class Tl:
    def __init__(self, ap, space, lo, hi):
        self.ap, self.space, self.lo, self.hi = ap, space, lo, hi

    def __getitem__(self, k):
        return self.ap[k]

    @property
    def r(self):
        return Rg(self.space, self.lo, self.hi)

    def part(self, i, n, cnt=1):
        sz = (self.hi - self.lo) // n
        return Rg(self.space, self.lo + i * sz, self.lo + (i + cnt) * sz)


def _rg(x):
    return x.r if isinstance(x, Tl) else x


class _Rec:
    def __getattr__(self, name):
        def f(*a, **k):
            self.call = (name, a, k)
            return self
        return f


class Prog:
    ENGS = ("pe", "act", "dve", "pool", "sp")

    def __init__(self, nc):
        self.nc = nc
        self.ops = []
        self.st = {}

    def _cells(self, rg):
        if rg.space == "sb":
            return [("sb", g) for g in range(rg.lo // GRAN, (rg.hi + GRAN - 1) // GRAN)]
        return [(rg.space, g) for g in range(rg.lo, rg.hi)]

    def op(self, eng, fn, r=(), w=(), dma=None):
        oid = len(self.ops)
        deps = set()
        rc = [c for x in r for c in self._cells(_rg(x))]
        wc = [c for x in w for c in self._cells(_rg(x))]
        st = self.st
        for c in rc:
            s = st.get(c)
            if s is not None:
                if s[0] is not None:
                    deps.add(s[0])
                if c[0] == "ps":
                    for r_ in s[1]:
                        if self.ops[r_]["eng"] != eng:
                            deps.add(r_)
        for c in wc:
            s = st.get(c)
            if s is not None:
                if s[0] is not None:
                    deps.add(s[0])
                deps.update(s[1])
        for c in rc:
            s = st.get(c)
            if s is None:
                st[c] = [None, [oid]]
            else:
                s[1].append(oid)
        for c in wc:
            st[c] = [oid, []]
        deps.discard(oid)
        rec = _Rec()
        fn(rec)
        self.ops.append(dict(eng=eng, call=rec.call, deps=deps, dma=dma, sig=False, sigidx=None))
        return oid

    def emit(self):
        nc = self.nc
        ops = self.ops

        def dom(o):
            return ("dma", o["dma"]) if o["dma"] is not None else ("eng", o["eng"])

        def skip(p, engname):
            return p["dma"] is None and p["eng"] == engname and (engname == "pe" or not SAME_ENGINE_SYNC)

        for o in ops:
            for d in o["deps"]:
                p = ops[d]
                if skip(p, o["eng"]):
                    continue
                p["sig"] = True
            if o["dma"] is not None:
                o["sig"] = True
        counters = {}
        for o in ops:
            if o["sig"]:
                dm = dom(o)
                counters[dm] = counters.get(dm, 0) + 1
                o["sigidx"] = counters[dm]
        with contextlib.ExitStack() as es:
            sems = {}
            for i, dm in enumerate(counters.keys()):
                sems[dm] = es.enter_context(nc.semaphore("s%d" % i))
            block = es.enter_context(nc.Block())
            streams = {e: [] for e in self.ENGS}
            for o in ops:
                streams[o["eng"]].append(o)

            def run(engname, eng):
                waited = {}
                for o in streams[engname]:
                    need = {}
                    for d in o["deps"]:
                        p = ops[d]
                        if not p["sig"] or skip(p, engname):
                            continue
                        dm = dom(p)
                        if p["sigidx"] > need.get(dm, 0):
                            need[dm] = p["sigidx"]
                    for dm, v in need.items():
                        if waited.get(dm, 0) >= v:
                            continue
                        waited[dm] = v
                        eng.wait_ge(sems[dm], v * (16 if dm[0] == "dma" else 1))
                    nm, a_, k_ = o["call"]
                    ins = getattr(eng, nm)(*a_, **k_)
                    if DEBUG_IDS:
                        try:
                            iname = str(ins.ins.name) if hasattr(ins, "ins") else str(getattr(ins, "name", ""))
                        except Exception:
                            iname = "?"
                        if iname in DEBUG_IDS:
                            print("DEBUGID", iname, engname, nm, {kk: (getattr(vv, "ap", None), getattr(vv, "dtype", None)) if hasattr(vv, "ap") else vv for kk, vv in k_.items()})
                    if o["sig"]:
                        ins.then_inc(sems[dom(o)], 16 if o["dma"] is not None else 1)
                if engname == "sp":
                    for dm, cnt in counters.items():
                        if dm[0] == "dma":
                            eng.wait_ge(sems[dm], cnt * 16)

            block.tensor(lambda e: run("pe", e))
            block.scalar(lambda e: run("act", e))
            block.vector(lambda e: run("dve", e))
            block.gpsimd(lambda e: run("pool", e))
            block.sync(lambda e: run("sp", e))


def make_consts():
    p = np.arange(128)
    j = p[:, None].astype(np.float64)
    i = p[None, :].astype(np.float64)
    D1 = np.maximum(i - j, 0)
    U1 = (i >= j).astype(np.float64)
    D2 = np.maximum(j - i, 0)
    U2 = (j >= i).astype(np.float64)
    BIGF = BIG * (1 - U1) + np.log(16.0)
    BIGB = BIG * (1 - U2) + np.log(16.0)
    NEGF = -BIG * (1 - U2)
    NEGB = -BIG * (1 - U1)
    vec = np.stack([p + 1.0, 128.0 - p, 127.0 - p, p * 1.0], axis=1)
    inv = 10000.0 ** (-np.arange(64, dtype=np.float32) / 64.0)
    f = p % 64
    sgn = np.where(p < 64, 1.0, -1.0)
    rows = np.arange(32, dtype=np.float32)
    cols = np.arange(64, dtype=np.float32)
    angR = (rows[None, :] * inv[f][:, None]).astype(np.float32)
    angC = (cols[None, :] * inv[f][:, None]).astype(np.float32)
    cosR, sinR = np.cos(angR), np.sin(angR) * sgn[:, None]
    cosC, sinC = np.cos(angC), np.sin(angC) * sgn[:, None]
    rope = np.concatenate([cosR, sinR, cosC, sinC], axis=1)
    ropeK = rope / 16.0
    cst = np.concatenate([D1, U1, D2, U2, BIGF, BIGB, NEGF, NEGB, vec, rope, ropeK, np.eye(128), np.ones((128, 128))], axis=1)
    return np.ascontiguousarray(cst.astype(np.float32))


C_D1, C_U1, C_D2, C_U2, C_BIGF, C_BIGB, C_NEGF, C_NEGB = [k * 128 for k in range(8)]
C_VEC = 1024
C_ROPE = 1028
C_ROPEK = 1028 + 192
C_ID = 1028 + 384
C_ONES = C_ID + 128
NCST = C_ONES + 128

O_CONST = 0
SZ_CONST = 13312
O_HT = O_CONST + SZ_CONST
SZ_HT = 65536
O_W = O_HT + SZ_HT
SZ_W = 16384
O_PROJ = O_W + SZ_W
SZ_PROJ = 43008
O_MIXO = O_PROJ + SZ_PROJ
SZ_MIXO = 32768
O_MIX = O_MIXO + SZ_MIXO
SZ_MIX = 41728
ARENA = O_MIX + SZ_MIX


def build_program():
    nc = bass.Bass("TRN2", target_bir_lowering=False)

    def din(name, shape):
        return nc.dram_tensor(name, list(shape), F32, kind="ExternalInput").ap()

    def dout(name, shape):
        return nc.dram_tensor(name, list(shape), F32, kind="ExternalOutput").ap()

    xp = din("xp", [1024, D]); xs = din("xs", [2048, D]); xo = din("xo", [512, D])
    flags = din("flags", [1, 4]); cT = din("cT", [128, 32])
    st_ret = din("st_ret", [2, 4, 256, 256]); st_C = din("st_C", [2, 4, 256, 256])
    st_n = din("st_n", [2, 4, 128, 2]); st_m = din("st_m", [1, 8])
    w_mod = din("w_mod", [D, 6 * D]); b_mod = din("b_mod", [1, 6 * D]); w_in = din("w_in", [D, NIN])
    b_gate = din("b_gate", [1, 16]); conv_qkT = din("conv_qkT", [128, 48]); theta = din("theta", [1, 8])
    gn_ret = din("gn_ret", [1, 1024]); gn_ml = din("gn_ml", [1, 1024]); w_out = din("w_out", [D, D])
    ln1_g = din("ln1_g", [1, D]); ln1_b = din("ln1_b", [1, D]); w_up = din("w_up", [D, 2 * DFF])
    conv_ffT = din("conv_ffT", [128, NFF * 3]); w_down = din("w_down", [DFF, D])
    ln2_g = din("ln2_g", [1, D]); ln2_b = din("ln2_b", [1, D]); cst_d = din("cst", [128, NCST])
    yp = dout("yp", [1024, D]); ys = dout("ys", [512, D])
    o_ret = dout("o_ret", [4, 2, 4, 256, 256]); o_C = dout("o_C", [4, 2, 4, 256, 256])
    o_n = dout("o_n", [4, 2, 4, 128, 2]); o_m = dout("o_m", [4, 8])
    mod_d = nc.dram_tensor("mod_d", [2, 6 * D], F32).ap()
    x1_d = nc.dram_tensor("x1_d", [1536, D], F32).ap()
    y2_d = nc.dram_tensor("y2_d", [1536, D], F32).ap()

    P = Prog(nc)
    es = contextlib.ExitStack()
    A = es.enter_context(nc.sbuf_tensor("arena", [128, ARENA // 2], BF16))
    psf = [es.enter_context(nc.psum_tensor("psf%d" % i, [128, 512], F32)) for i in range(6)]
    psb = [es.enter_context(nc.psum_tensor("psb%d" % i, [128, 1024], BF16)) for i in range(2)]
    PSF = [Tl(psf[i][:], "ps", i, i + 1) for i in range(6)]
    PSB = [Tl(psb[i][:], "ps", 6 + i, 7 + i) for i in range(2)]
    rot = {"f": 0, "b": 0}

    held = set()

    def psn():
        for _ in range(6):
            rot["f"] = (rot["f"] + 1) % 6
            if rot["f"] not in held:
                return PSF[rot["f"]]
        raise AssertionError("no free psum bank")

    def psh():
        ps = psn()
        held.add(ps.lo)
        return ps

    def prel(ps):
        held.discard(ps.lo)

    def navail():
        return 6 - len(held)

    heldb = set()

    def psbn():
        for _ in range(2):
            rot["b"] = (rot["b"] + 1) % 2
            if rot["b"] not in heldb:
                return PSB[rot["b"]]
        raise AssertionError("no free bf16 psum bank")

    def run_pipe(facts, width=2):
        free = list(range(width)); active = []; i = 0
        while i < len(facts) or active:
            while free and i < len(facts):
                k = free.pop(0)
                active.append((facts[i](k), k))
                i += 1
            nxt = []
            for g, k in active:
                try:
                    next(g)
                    nxt.append((g, k))
                except StopIteration:
                    free.append(k)
            active = nxt

    def sbt(off, shape, dt, parts=128):
        esz = 4 if dt == F32 else 2
        n = 1
        for s in shape:
            n *= s
        nb = n * esz
        assert off % 4 == 0
        ap = A[0:parts, off // 2:(off + nb) // 2]
        if dt != BF16:
            ap = ap.bitcast(dt)
        if len(shape) == 2:
            ap = ap.rearrange("p (a b) -> p a b", a=shape[0])
        elif len(shape) == 3:
            ap = ap.rearrange("p (a b c) -> p a b c", a=shape[0], b=shape[1])
        return Tl(ap, "sb", off, off + nb)

    class Bump:
        def __init__(self, off, size):
            self.off, self.end = off, off + size

        def __call__(self, shape, dt, parts=128, align=GRAN):
            esz = 4 if dt == F32 else 2
            n = 1
            for s in shape:
                n *= s
            self.off = (self.off + align - 1) // align * align
            t = sbt(self.off, shape, dt, parts)
            self.off += n * esz
            assert self.off <= self.end, ("arena overflow", self.off, self.end)
            return t

    def DR(name, lo=0, hi=1):
        return Rg("dr_" + name, lo, hi)

    cb = Bump(O_CONST, SZ_CONST)
    CST = cb([NCST], F32)
    IDB = cb([128], BF16)
    LG = cb([8], F32)
    FLG = cb([4], F32)
    BG = cb([16], F32)
    GN = cb([256], F32)
    DEC = cb([8], F32)
    RM = cb([128], F32)
    RMT = cb([128], F32)
    SILC = cb([16, 2], BF16)
    WGT = cb([16, 16], BF16)
    CQK = cb([48], F32)
    CFF = cb([NFF * 3], F32)
    ST_C = cb([4, 6], F32); MV_C = cb([2], F32, align=4); RSTD_C = cb([1], F32, align=4); NMR_C = cb([1], F32, align=4)
    IDF = CST.ap[:, C_ID:C_ID + 128]
    ONESF = CST.ap[:, C_ONES:C_ONES + 128]

    def cstm(c0):
        return CST.ap[:, c0:c0 + 128]

    P.op("sp", lambda e: e.dma_start(out=CST[:], in_=cst_d[:, :]), w=[CST], dma="cst")
    P.op("sp", lambda e: e.dma_start(out=LG[:], in_=theta[0:1, :].broadcast_to([128, 8])), w=[LG], dma="lg")
    P.op("sp", lambda e: e.dma_start(out=FLG[:], in_=flags[0:1, :].broadcast_to([128, 4])), w=[FLG], dma="flg")
    P.op("sp", lambda e: e.dma_start(out=BG[:], in_=b_gate[0:1, :].broadcast_to([128, 16])), w=[BG], dma="bg")
    P.op("sp", lambda e: e.dma_start(out=CQK[:], in_=conv_qkT[:, :]), w=[CQK], dma="cqk")
    P.op("sp", lambda e: e.dma_start(out=CFF[:], in_=conv_ffT[:, :]), w=[CFF], dma="cff")
    P.op("pool", lambda e: e.dma_start(out=WGT[:], in_=w_in[:, 8192:8208].rearrange("(kc p) n -> p kc n", p=128)), w=[WGT], dma="wgt")
    P.op("dve", lambda e: e.tensor_copy(out=IDB[:], in_=IDF), r=[CST], w=[IDB])
    P.op("act", lambda e: e.activation(out=LG[:], in_=LG[:], func=AF.Exp, scale=-1.0), r=[LG], w=[LG])
    P.op("act", lambda e: e.activation(out=LG[:], in_=LG[:], func=AF.Ln, bias=1.0), r=[LG], w=[LG])
    P.op("dve", lambda e: e.tensor_scalar(out=LG[:], in0=LG[:], scalar1=-1.0, scalar2=None, op0=ALU.mult), r=[LG], w=[LG])

    mod_state = {"init": False}

    def stage_mod(blks, stage_off, small_off):
        ct = sbt(small_off, [32], F32)
        brow = [sbt(small_off + 256 + k * 1024, [256], F32, parts=2) for k in range(2)]
        mrow = [sbt(small_off + 256 + 2048 + k * 1024, [256], F32, parts=2) for k in range(2)]
        if not mod_state["init"]:
            mod_state["init"] = True
            P.op("sp", lambda e: e.dma_start(out=ct[:], in_=cT[:, :]), w=[ct], dma="ct")
            P.op("act", lambda e: e.activation(out=SILC[:].rearrange("p a b -> p (a b)"), in_=ct[:], func=AF.Silu), r=[ct], w=[SILC])
        wslots = [sbt(stage_off + k * 8192, [16, 256], BF16) for k in range(2)]
        for i, blk in enumerate(blks):
            ws = wslots[i % 2]; br = brow[i % 2]; mr = mrow[i % 2]
            c0 = blk * 256
            P.op("pool", lambda e, ws=ws, c0=c0: e.dma_start(out=ws[:], in_=w_mod[:, c0:c0 + 256].rearrange("(kc p) n -> p kc n", p=128)), w=[ws], dma="wm%d_%d" % (stage_off, i % 2))
            P.op("sp", lambda e, br=br, c0=c0: e.dma_start(out=br[:], in_=b_mod[0:1, c0:c0 + 256].broadcast_to([2, 256])), w=[br], dma="brow%d_%d" % (small_off, i % 2))
            ps = psn()
            for kc in range(KC):
                P.op("pe", lambda e, ps=ps, ws=ws, kc=kc: e.matmul(ps[0:2, 0:256], lhsT=SILC[:, kc, :], rhs=ws[:, kc, :], start=(kc == 0), stop=(kc == KC - 1)), r=[SILC, ws], w=[ps])
            P.op("dve", lambda e, ps=ps, mr=mr, br=br: e.tensor_tensor(out=mr[:], in0=ps[0:2, 0:256], in1=br[:], op=ALU.add), r=[ps, br], w=[mr])
            P.op("sp", lambda e, mr=mr, c0=c0: e.dma_start(out=mod_d[:, c0:c0 + 256], in_=mr[:]), r=[mr], w=[DR("mod", blk // 8, blk // 8 + 1)], dma="mrow%d_%d" % (small_off, i % 2))

    MOD_STAGE = O_HT + 32768
    MOD_SMALL = O_PROJ + 36864

    def mod_dma(blks):
        for i, blk in enumerate(blks):
            ws = sbt(MOD_STAGE + i * 8192, [16, 256], BF16)
            c0 = blk * 256
            P.op("pool", lambda e, ws=ws, c0=c0: e.dma_start(out=ws[:], in_=w_mod[:, c0:c0 + 256].rearrange("(kc p) n -> p kc n", p=128)), w=[ws], dma="wmr%d" % i)

    def mod_compute(blks):
        brow = [sbt(MOD_SMALL + k * 1024, [256], F32, parts=2) for k in range(2)]
        mrow = [sbt(MOD_SMALL + 2048 + k * 1024, [256], F32, parts=2) for k in range(2)]
        for i, blk in enumerate(blks):
            ws = sbt(MOD_STAGE + i * 8192, [16, 256], BF16)
            br = brow[i % 2]; mr = mrow[i % 2]
            c0 = blk * 256
            P.op("sp", lambda e, br=br, c0=c0: e.dma_start(out=br[:], in_=b_mod[0:1, c0:c0 + 256].broadcast_to([2, 256])), w=[br], dma="browr%d" % (i % 2))
            ps = psn()
            for kc in range(KC):
                P.op("pe", lambda e, ps=ps, ws=ws, kc=kc: e.matmul(ps[0:2, 0:256], lhsT=SILC[:, kc, :], rhs=ws[:, kc, :], start=(kc == 0), stop=(kc == KC - 1)), r=[SILC, ws], w=[ps])
            P.op("dve", lambda e, ps=ps, mr=mr, br=br: e.tensor_tensor(out=mr[:], in0=ps[0:2, 0:256], in1=br[:], op=ALU.add), r=[ps, br], w=[mr])
            P.op("sp", lambda e, mr=mr, c0=c0: e.dma_start(out=mod_d[:, c0:c0 + 256], in_=mr[:]), r=[mr], w=[DR("mod", blk // 8, blk // 8 + 1)], dma="mrowr%d" % (i % 2))

    def load_modrow(dst, q, g, plus1=False):
        P.op("sp", lambda e: e.dma_start(out=dst[:], in_=mod_d[g:g + 1, q * D:(q + 1) * D].broadcast_to([128, D])), r=[DR("mod", q, q + 1)], w=[dst], dma="mr_%d" % (dst.lo))
        if plus1:
            P.op("dve", lambda e: e.tensor_scalar(out=dst[:], in0=dst[:], scalar1=1.0, scalar2=None, op0=ALU.add), r=[dst], w=[dst])

    def load_row(dst, src):
        P.op("sp", lambda e: e.dma_start(out=dst[:], in_=src[0:1, :].broadcast_to([128, src.shape[1]])), w=[dst], dma="lr_%d" % (dst.lo))

    def ln_stats(x_ap, xr, st, mv, rstd, nmr, n):
        nch = max(1, n // 512)
        w_ = n // nch
        for c in range(nch):
            P.op("dve", lambda e, c=c: e.bn_stats(out=st[:, c, :], in_=x_ap[:, c * w_:(c + 1) * w_]), r=[xr], w=[st])
        P.op("dve", lambda e: e.bn_aggr(out=mv[:], in_=st[:, 0:nch, :].rearrange("p a b -> p (a b)")), r=[st], w=[mv])
        P.op("act", lambda e: e.activation(out=rstd[:], in_=mv[:, 1:2], func=AF.Ln, bias=1e-6), r=[mv], w=[rstd])
        P.op("act", lambda e: e.activation(out=rstd[:], in_=rstd[:], func=AF.Exp, scale=-0.5), r=[rstd], w=[rstd])
        if nmr is not None:
            P.op("dve", lambda e: e.scalar_tensor_tensor(out=nmr[:], in0=mv[:, 0:1], scalar=-1.0, in1=rstd[:], op0=ALU.mult, op1=ALU.mult), r=[mv, rstd], w=[nmr])

    def transpose_tile(src, src_r, dstT, dst_r, tcol):
        for _ in transpose_tile_g(src, src_r, dstT, dst_r, tcol):
            pass

    def transpose_tile_g(src, src_r, dstT, dst_r_, tcol):
        for half in range(2):
            while len(heldb) >= 2:
                yield
            pb = psbn()
            heldb.add(pb.lo - 6)
            dst_r = dstT.part(half, 2)
            for k in range(8):
                kc = half * 8 + k
                P.op("pe", lambda e, pb=pb, k=k, kc=kc: e.transpose(out=pb[:, k * 128:(k + 1) * 128], in_=src[:, kc * 128:(kc + 1) * 128], identity=IDB[:]), r=[src_r, IDB], w=[pb])
            eng = "act" if half == 0 else "dve"
            if eng == "act":
                P.op("act", lambda e, pb=pb, half=half: e.activation(out=dstT[:, half * 8:half * 8 + 8, tcol:tcol + 128], in_=pb[:].rearrange("p (a b) -> p a b", a=8), func=AF.Copy), r=[pb], w=[dst_r])
            else:
                P.op("dve", lambda e, pb=pb, half=half: e.tensor_copy(out=dstT[:, half * 8:half * 8 + 8, tcol:tcol + 128], in_=pb[:].rearrange("p (a b) -> p a b", a=8)), r=[pb], w=[dst_r])
            heldb.discard(pb.lo - 6)
            yield

    def process_group(gi):
        sample = gi == 1
        T = 2048 if sample else 1024
        nseq = 1 if sample else 4
        n = 16 if sample else 2
        NT = T // 128
        Town = 512 if sample else 1024
        NTO = Town // 128
        x_all = xs if sample else xp
        x_own = xo if sample else xp
        y_out = ys if sample else yp
        x1off = 1024 if sample else 0
        mg = 1 if sample else 0
        Lc = 64 if sample else 256
        HT = sbt(O_HT, [16, T], BF16)

        mb = Bump(O_MIX, SZ_MIX)
        r_sc = mb([D], F32); r_sh = mb([D], F32)
        sm_ = [mb([64], F32) for _ in range(2)]
        load_modrow(r_sh, 0, mg)
        load_modrow(r_sc, 1, mg, plus1=True)
        pbm = Bump(O_PROJ, SZ_PROJ)
        xts = [pbm([D], F32) for _ in range(2)]
        hts = [pbm([D], BF16) for _ in range(2)]

        def small4(tl):
            mk = lambda off, shape: Tl(sbt(tl.lo + off * 4, shape, F32).ap, "sb", tl.lo, tl.hi)
            return mk(0, [4, 6]), mk(24, [2]), mk(26, [1]), mk(27, [1])

        def a1_tile(t):
            def g(k):
                xt = xts[k]; ht = hts[k]
                st, mv, rstd, nmr = small4(sm_[k])
                P.op("sp", lambda e: e.dma_start(out=xt[:], in_=x_all[t * 128:(t + 1) * 128, :]), w=[xt], dma="xt%d" % k)
                yield
                ln_stats(xt.ap, xt, st, mv, rstd, None, D)
                yield
                P.op("dve", lambda e: e.scalar_tensor_tensor(out=xt[:], in0=xt[:], scalar=mv[:, 0:1], in1=r_sc[:], op0=ALU.subtract, op1=ALU.mult), r=[xt, mv, r_sc], w=[xt])
                yield
                P.op("dve", lambda e: e.scalar_tensor_tensor(out=ht[:], in0=xt[:], scalar=rstd[:], in1=r_sh[:], op0=ALU.mult, op1=ALU.add), r=[xt, rstd, r_sh], w=[ht])
                yield
                yield from transpose_tile_g(ht.ap, ht, HT, HT, t * 128)
            return g
        run_pipe([a1_tile(t) for t in range(NT)], 2)

        pbm = Bump(O_PROJ, SZ_PROJ)
        QT = pbm([2, T], BF16)
        KT = pbm([2, T], BF16)
        VG = pbm([NT, 520], BF16)
        KTM = pbm([NT, 256], BF16)
        mb = Bump(O_MIX, SZ_MIX)
        NSL = 1 if sample else 2
        MST = [[mb([2, 257], F32) for d in range(2)] for sl in range(NSL)]
        KX = [[mb([256], BF16) for d in range(2)] for sl in range(NSL)]
        SALL = []
        for sl in range(NSL):
            per_d = []
            for d in range(2):
                if sample and d == 0:
                    first = mb([2, 257], BF16)
                    rest = sbt(O_MIXO + 16384, [n - 1, 2, 257], BF16)
                    per_d.append([first] + [Tl(rest[:, c], "sb", rest.lo + c * 1028, rest.lo + (c + 1) * 1028) for c in range(n - 1)])
                else:
                    arr = mb([n, 2, 257], BF16)
                    per_d.append([Tl(arr[:, c], "sb", arr.lo + c * 1028, arr.lo + (c + 1) * 1028) for c in range(n)])
            SALL.append(per_d)
        Dm = mb([4, 128], F32)
        Am = mb([4, 128], F32)
        tmpc = [sbt(Dm.lo, [512], F32), sbt(Am.lo, [512], F32)]
        ATT = [sbt(Am.lo + k * 512, [2, 128], BF16) for k in range(2)]
        WTS = [sbt(Am.lo + 1024 + k * 512, [128], F32) for k in range(2)]
        DMS = [sbt(Dm.lo + k * 1024, [2, 128], F32) for k in range(2)]
        TOTT = [mb([2, 257], F32) for k in range(2)]
        TOT = [[Tl(TOTT[k][:, d, :], "sb", TOTT[k].lo + d * 1028, TOTT[k].lo + (d + 1) * 1028) for d in range(2)] for k in range(2)]
        XN = [mb([256], F32) for k in range(2)]
        SMALL = [mb([64], F32) for k in range(2)]
        GTS = mb([NT, 16], F32)
        LFN = mb([NT, 16], F32)
        GA = mb([NT, 8], F32); GNB = mb([NT, 8], F32); GMU = mb([NT, 8], F32); GSP = mb([NT, 8], F32)
        GWK = mb([NT, 8], F32); GSC = mb([NT, 8], F32); GEM = mb([NT, 8], F32); GBL = mb([NT, 8], F32)
        mprev = mb([8], F32); mnew = mb([8], F32, align=4); v8 = [mb([8], F32, align=4) for _ in range(4)]
        MIXO = sbt(O_MIXO, [NTO, D], BF16)
        wsl = [sbt(O_W + k * 8192, [16, 256], BF16) for k in range(2)]
        wcount = [0]

        def load_w(c0):
            ws = wsl[wcount[0] % 2]
            k = wcount[0] % 2
            wcount[0] += 1
            P.op("pool", lambda e: e.dma_start(out=ws[:], in_=w_in[:, c0:c0 + 256].rearrange("(kc p) n -> p kc n", p=128)), w=[ws], dma="wsl%d" % k)
            return ws

        P.op("dve", lambda e: e.memset(VG[:, :, 256:257], 1.0), w=[VG])

        def proj_fm(ws, dst, kind, chan0):
            for dc in range(2):
                for tb in range(T // 512):
                    ps = psn()
                    for kc in range(KC):
                        P.op("pe", lambda e, ps=ps, kc=kc, dc=dc, tb=tb: e.matmul(ps[:], lhsT=ws[:, kc, dc * 128:(dc + 1) * 128], rhs=HT[:, kc, tb * 512:(tb + 1) * 512], start=(kc == 0), stop=(kc == KC - 1)), r=[ws, HT], w=[ps])
                    dsl = dst[:, dc, tb * 512:(tb + 1) * 512]
                    if kind in ("rq", "rk"):
                        if not sample:
                            sc_ = 1.0 if kind == "rq" else 1.0 / 16.0
                            P.op("act", lambda e, ps=ps, dsl=dsl, sc_=sc_: e.activation(out=dsl, in_=ps[:], func=AF.Identity, scale=sc_), r=[ps], w=[dst])
                        else:
                            base = C_ROPE if kind == "rq" else C_ROPEK
                            if dc == 0:
                                cosb = CST.ap[:, base + 8 * tb: base + 8 * tb + 8].unsqueeze(2).to_broadcast([128, 8, 64])
                                sinb = CST.ap[:, base + 32 + 8 * tb: base + 32 + 8 * tb + 8].unsqueeze(2).to_broadcast([128, 8, 64])
                            else:
                                cosb = CST.ap[:, base + 64: base + 128].unsqueeze(1).to_broadcast([128, 8, 64])
                                sinb = CST.ap[:, base + 128: base + 192].unsqueeze(1).to_broadcast([128, 8, 64])
                            t1 = tmpc[0]; t2 = tmpc[1]
                            v3 = lambda ap: ap.rearrange("p (a b) -> p a b", a=8)
                            P.op("dve", lambda e, ps=ps, cosb=cosb: e.tensor_tensor(out=v3(t1[:]), in0=v3(ps[:]), in1=cosb, op=ALU.mult), r=[ps, CST], w=[t1])
                            P.op("dve", lambda e, ps=ps, sinb=sinb: e.tensor_tensor(out=v3(t2[:])[0:64], in0=v3(ps[:])[64:128], in1=sinb[64:128], op=ALU.mult), r=[ps, CST], w=[t2])
                            P.op("dve", lambda e, ps=ps, sinb=sinb: e.tensor_tensor(out=v3(t2[:])[64:128], in0=v3(ps[:])[0:64], in1=sinb[0:64], op=ALU.mult), r=[ps, CST], w=[t2])
                            P.op("dve", lambda e, dsl=dsl: e.tensor_tensor(out=dsl, in0=t1[:], in1=t2[:], op=ALU.add), r=[t1, t2], w=[dst])
                    else:
                        ch = chan0 // 128 + dc
                        w0 = CQK.ap[:, ch * 3 + 0: ch * 3 + 1]; w1 = CQK.ap[:, ch * 3 + 1: ch * 3 + 2]; w2 = CQK.ap[:, ch * 3 + 2: ch * 3 + 3]
                        z = tmpc[0]
                        nb_ = 512 // Lc
                        v3 = lambda ap: ap.rearrange("p (a b) -> p a b", a=nb_)
                        P.op("dve", lambda e, ps=ps, w1=w1: e.tensor_scalar(out=z[:], in0=ps[:], scalar1=w1, scalar2=None, op0=ALU.mult), r=[ps, CQK], w=[z])
                        P.op("dve", lambda e, ps=ps, w0=w0: e.scalar_tensor_tensor(out=v3(z[:])[:, :, 1:Lc], in0=v3(ps[:])[:, :, 0:Lc - 1], scalar=w0, in1=v3(z[:])[:, :, 1:Lc], op0=ALU.mult, op1=ALU.add), r=[ps, CQK, z], w=[z])
                        P.op("dve", lambda e, ps=ps, w2=w2: e.scalar_tensor_tensor(out=v3(z[:])[:, :, 0:Lc - 1], in0=v3(ps[:])[:, :, 1:Lc], scalar=w2, in1=v3(z[:])[:, :, 0:Lc - 1], op0=ALU.mult, op1=ALU.add), r=[ps, CQK, z], w=[z])
                        P.op("act", lambda e, dsl=dsl: e.activation(out=dsl, in_=z[:], func=AF.Silu), r=[z], w=[dst])

        def proj_tm(wv, wg, gfunc):
            for t in range(NT):
                ps = psn()
                for j, ws in enumerate((wv, wg)):
                    for kc in range(KC):
                        P.op("pe", lambda e, ps=ps, kc=kc, j=j, ws=ws, t=t: e.matmul(ps[:, j * 256:(j + 1) * 256], lhsT=HT[:, kc, t * 128:(t + 1) * 128], rhs=ws[:, kc, :], start=(kc == 0), stop=(kc == KC - 1)), r=[ws, HT], w=[ps])
                P.op("dve", lambda e, ps=ps, t=t: e.tensor_copy(out=VG[:, t, 0:256], in_=ps[:, 0:256]), r=[ps], w=[VG.part(t, NT)])
                P.op("act", lambda e, ps=ps, t=t: e.activation(out=VG[:, t, 264:520], in_=ps[:, 256:512], func=gfunc), r=[ps], w=[VG.part(t, NT)])

        def make_ktm(kscale=1.0):
            for t in range(NT):
                pb = psbn()
                for dc in range(2):
                    P.op("pe", lambda e, pb=pb, dc=dc, t=t: e.transpose(out=pb[:, dc * 128:(dc + 1) * 128], in_=KT[:, dc, t * 128:(t + 1) * 128], identity=IDB[:]), r=[KT, IDB], w=[pb])
                P.op("dve", lambda e, pb=pb, t=t: e.tensor_scalar(out=KTM[:, t, :], in0=pb[:, 0:256], scalar1=kscale, scalar2=None, op0=ALU.mult), r=[pb], w=[KTM.part(t, NT)])

        def smalls(k):
            o = SMALL[k].lo
            mk = lambda off, shape: Tl(sbt(o + off * 4, shape, F32).ap, "sb", SMALL[k].lo, SMALL[k].hi)
            return mk(0, [4, 6]), mk(24, [2]), mk(26, [1]), mk(27, [1]), mk(28, [2])

        def out_tail(k, t, hh):
            xn = XN[k]
            st, mv, rstd, nmr, _ = smalls(k)
            ln_stats(xn.ap, xn, st, mv, rstd, None, 256)
            yield
            P.op("dve", lambda e: e.scalar_tensor_tensor(out=xn[:], in0=xn[:], scalar=mv[:, 0:1], in1=GN[:], op0=ALU.subtract, op1=ALU.mult), r=[xn, mv, GN], w=[xn])
            yield
            col = hh * 256
            gate = VG[:, t, 264:520]
            gr = VG.part(t, NT)
            if not sample:
                P.op("dve", lambda e: e.scalar_tensor_tensor(out=MIXO[:, t, col:col + 256], in0=xn[:], scalar=rstd[:], in1=gate, op0=ALU.mult, op1=ALU.mult), r=[xn, rstd, gr], w=[MIXO.part(t, NTO)])
            else:
                pp, tp = t // 4, t % 4
                P.op("dve", lambda e: e.scalar_tensor_tensor(out=xn[:], in0=xn[:], scalar=rstd[:], in1=gate, op0=ALU.mult, op1=ALU.mult), r=[xn, rstd, gr], w=[xn])
                yield
                if pp == 0:
                    P.op("dve", lambda e: e.tensor_scalar(out=MIXO[:, tp, col:col + 256], in0=xn[:], scalar1=FLG[:, 0:1], scalar2=None, op0=ALU.mult), r=[xn, FLG], w=[MIXO.part(tp, NTO)])
                else:
                    P.op("dve", lambda e: e.scalar_tensor_tensor(out=MIXO[:, tp, col:col + 256], in0=xn[:], scalar=FLG[:, pp:pp + 1], in1=MIXO[:, tp, col:col + 256], op0=ALU.mult, op1=ALU.add), r=[xn, FLG, MIXO.part(tp, NTO)], w=[MIXO.part(tp, NTO)])
            yield

        def run_rr(gens):
            gens = list(gens)
            while gens:
                nxt = []
                for g in gens:
                    try:
                        next(g)
                        nxt.append(g)
                    except StopIteration:
                        pass
                gens = nxt

        def state_chain(sl, s, d, h, kind):
            ret = kind == "ret"
            ncol = 256 if ret else 257
            SM = MST[sl][d]; Kx = KX[sl][d]
            src4 = st_ret if ret else st_C
            srcn = None if ret else st_n
            if not sample:
                P.op("dve", lambda e: e.memset(SM[:], 0.0), w=[SM])
            else:
                P.op("sp", lambda e: e.dma_start(out=SM[:, :, 0:256], in_=src4[d, h].rearrange("(kc p) v -> p kc v", p=128)), w=[SM], dma="sm%d" % SM.lo)
                if srcn is not None:
                    P.op("sp", lambda e: e.dma_start(out=SM[:, :, 256:257], in_=srcn[d, h].unsqueeze(2), allow_slow_non_contiguous=True), w=[SM], dma="sm%d" % SM.lo)
            yield
            order = range(n) if d == 0 else range(n - 1, -1, -1)
            for c in order:
                t = s * n + c
                sa = SALL[sl][d][c]
                P.op("act", lambda e, sa=sa: e.activation(out=sa[:, :, 0:ncol], in_=SM[:, :, 0:ncol], func=AF.Copy), r=[SM], w=[sa])
                if ret:
                    sc_ap, sc_r = DEC.ap[:, 2 + d:3 + d], DEC
                    dk_ap, dk_r = DEC.ap[:, 4 + d:5 + d], DEC
                else:
                    sc_ap, sc_r = GWK.ap[:, t, d * 4 + h:d * 4 + h + 1], GWK
                    dk_ap, dk_r = GSC.ap[:, t, d * 4 + h:d * 4 + h + 1], GSC
                P.op("act", lambda e, t=t, sc_ap=sc_ap: e.activation(out=Kx[:], in_=KTM[:, t, :], func=AF.Identity, scale=sc_ap), r=[KTM.part(t, NT), sc_r], w=[Kx])
                yield
                if ret:
                    while navail() < 1:
                        yield
                    ps = psh()
                    for kc in range(2):
                        P.op("pe", lambda e, kc=kc, ps=ps, t=t: e.matmul(ps[:, kc * 256:(kc + 1) * 256], lhsT=Kx[:, kc * 128:(kc + 1) * 128], rhs=VG[:, t, 0:256], start=True, stop=True), r=[Kx, VG.part(t, NT)], w=[ps])
                    yield
                    P.op("dve", lambda e, ps=ps, dk_ap=dk_ap: e.scalar_tensor_tensor(out=SM[:, :, 0:256], in0=SM[:, :, 0:256], scalar=dk_ap, in1=ps[:].rearrange("p (a b) -> p a b", a=2), op0=ALU.mult, op1=ALU.add), r=[SM, ps, dk_r], w=[SM])
                    prel(ps)
                    yield
                else:
                    while navail() < 2:
                        yield
                    pss = [psh(), psh()]
                    for kc in range(2):
                        P.op("pe", lambda e, kc=kc, ps=pss[kc], t=t: e.matmul(ps[:, 0:257], lhsT=Kx[:, kc * 128:(kc + 1) * 128], rhs=VG[:, t, 0:257], start=True, stop=True), r=[Kx, VG.part(t, NT)], w=[pss[kc]])
                    yield
                    for kc in range(2):
                        P.op("dve", lambda e, ps=pss[kc], kc=kc, dk_ap=dk_ap: e.scalar_tensor_tensor(out=SM[:, kc, :], in0=SM[:, kc, :], scalar=dk_ap, in1=ps[:, 0:257], op0=ALU.mult, op1=ALU.add), r=[SM, pss[kc], dk_r], w=[SM])
                        prel(pss[kc])
                    yield
            if not sample:
                dst4 = o_ret if ret else o_C
                P.op("sp", lambda e: e.dma_start(out=dst4[s, d, h].rearrange("(kc p) v -> p kc v", p=128), in_=SM[:, :, 0:256]), r=[SM], dma="smo%d" % SM.lo)
                if not ret:
                    P.op("sp", lambda e: e.dma_start(out=o_n[s, d, h].unsqueeze(2), in_=SM[:, :, 256:257], allow_slow_non_contiguous=True), r=[SM], dma="smo%d" % SM.lo)
            yield

        def out_chunk_ret(k, sl, s, c, h):
            t = s * n + c
            tk = slice(t * 128, (t + 1) * 128)
            attm = ATT[k]; xn = XN[k]
            while navail() < 2:
                yield
            psA = psh()
            for kc in range(2):
                P.op("pe", lambda e, kc=kc: e.matmul(psA[:, 0:128], lhsT=KT[:, kc, tk], rhs=QT[:, kc, tk], start=(kc == 0), stop=(kc == 1)), r=[KT, QT], w=[psA])
            psS = psh()
            for d in range(2):
                sa = SALL[sl][d][c]
                for kc in range(2):
                    P.op("pe", lambda e, kc=kc, d=d, sa=sa: e.matmul(psS[:, d * 256:(d + 1) * 256], lhsT=QT[:, kc, tk], rhs=sa[:, kc, 0:256], start=(kc == 0), stop=(kc == 1)), r=[QT, sa], w=[psS])
            yield
            P.op("dve", lambda e: e.tensor_tensor(out=attm[:, 0, :], in0=psA[:, 0:128], in1=RM[:], op=ALU.mult), r=[psA, RM], w=[attm])
            prel(psA)
            yield
            while navail() < 1:
                yield
            psO = psh()
            P.op("pe", lambda e: e.matmul(psO[:, 0:256], lhsT=attm[:, 0, :], rhs=VG[:, t, 0:256], start=True, stop=True), r=[attm, VG.part(t, NT)], w=[psO])
            P.op("dve", lambda e: e.tensor_scalar(out=xn[:], in0=psS[:, 0:256], scalar1=DEC[:, 0:1], scalar2=None, op0=ALU.mult), r=[psS, DEC], w=[xn])
            yield
            P.op("dve", lambda e: e.scalar_tensor_tensor(out=xn[:], in0=psS[:, 256:512], scalar=DEC[:, 1:2], in1=xn[:], op0=ALU.mult, op1=ALU.add), r=[psS, DEC, xn], w=[xn])
            yield
            P.op("dve", lambda e: e.tensor_tensor(out=xn[:], in0=psO[:, 0:256], in1=xn[:], op=ALU.add), r=[psO, xn], w=[xn])
            prel(psS); prel(psO)
            yield
            yield from out_tail(k, t, h)

        def out_chunk_ml(k, sl, s, c, h):
            t = s * n + c
            tk = slice(t * 128, (t + 1) * 128)
            attm = ATT[k]; xn = XN[k]; Dk = DMS[k]; WT = WTS[k]
            _, _, _, _, dd = smalls(k)
            bigm = CST.ap[:, C_BIGF:C_BIGF + 256]
            while navail() < 1:
                yield
            psA = psh()
            for kc in range(2):
                P.op("pe", lambda e, kc=kc: e.matmul(psA[:, 0:128], lhsT=KT[:, kc, tk], rhs=QT[:, kc, tk], start=(kc == 0), stop=(kc == 1)), r=[KT, QT], w=[psA])
            mu2 = GMU.ap[:, t, h:h + 5:4]
            P.op("dve", lambda e: e.tensor_tensor(out=Dk[:], in0=IDF.unsqueeze(1).to_broadcast([128, 2, 128]), in1=mu2.unsqueeze(2).to_broadcast([128, 2, 128]), op=ALU.mult), r=[CST, GMU], w=[Dk])
            yield
            P.op("pe", lambda e: e.matmul(psA[:, 128:384], lhsT=ONESF, rhs=Dk[:].rearrange("p a b -> p (a b)"), start=True, stop=False), r=[CST, Dk], w=[psA])
            P.op("pe", lambda e: e.matmul(psA[:, 128:384], lhsT=IDF, rhs=bigm, start=False, stop=True), r=[CST], w=[psA])
            yield
            for d in range(2):
                col = d * 4 + h
                P.op("act", lambda e, d=d, col=col: e.activation(out=WT[:], in_=psA[:, 128 + d * 128:256 + d * 128], func=AF.Exp, bias=GA[:, t, col:col + 1], scale=-1.0), r=[psA, GA], w=[WT])
                yield
                P.op("dve", lambda e, d=d: e.tensor_tensor(out=attm[:, d, :], in0=psA[:, 0:128], in1=WT[:], op=ALU.mult), r=[psA, WT], w=[attm])
                yield
            prel(psA)
            for d in range(2):
                col = d * 4 + h
                sa = SALL[sl][d][c]
                td = TOT[k][d]
                while navail() < 2:
                    yield
                psN = psh()
                P.op("pe", lambda e, d=d, psN=psN: e.matmul(psN[:, 0:257], lhsT=attm[:, d, :], rhs=VG[:, t, 0:257], start=True, stop=True), r=[attm, VG.part(t, NT)], w=[psN])
                psI = psh()
                for kc in range(2):
                    P.op("pe", lambda e, kc=kc, psI=psI, sa=sa: e.matmul(psI[:, 0:257], lhsT=QT[:, kc, tk], rhs=sa[:, kc, :], start=(kc == 0), stop=(kc == 1)), r=[QT, sa], w=[psI])
                yield
                P.op("dve", lambda e, psI=psI, td=td, col=col: e.tensor_scalar(out=td[:], in0=psI[:, 0:257], scalar1=GSP[:, t, col:col + 1], scalar2=None, op0=ALU.mult), r=[psI, GSP], w=[td])
                prel(psI)
                yield
                P.op("dve", lambda e, psN=psN, td=td: e.tensor_tensor(out=td[:], in0=psN[:, 0:257], in1=td[:], op=ALU.add), r=[psN, td], w=[td])
                prel(psN)
                yield
            den2 = TOTT[k][:, :, 256:257]
            P.op("dve", lambda e: e.scalar_tensor_tensor(out=dd[:, 0:2].unsqueeze(2), in0=den2, scalar=-1.0, in1=den2, op0=ALU.mult, op1=ALU.max), r=[TOTT[k]], w=[dd])
            yield
            P.op("dve", lambda e: e.tensor_tensor(out=dd[:, 0:2], in0=dd[:, 0:2], in1=GEM.ap[:, t, h:h + 5:4], op=ALU.max), r=[dd, GEM], w=[dd])
            yield
            P.op("dve", lambda e: e.reciprocal(out=dd[:], in_=dd[:]), r=[dd], w=[dd])
            yield
            P.op("dve", lambda e: e.tensor_scalar(out=xn[:], in0=TOT[k][0][:, 0:256], scalar1=dd[:, 0:1], scalar2=None, op0=ALU.mult), r=[TOT[k][0], dd], w=[xn])
            yield
            P.op("dve", lambda e: e.scalar_tensor_tensor(out=xn[:], in0=TOT[k][1][:, 0:256], scalar=dd[:, 1:2], in1=xn[:], op0=ALU.mult, op1=ALU.add), r=[TOT[k][1], dd, xn], w=[xn])
            yield
            yield from out_tail(k, t, 4 + h)

        def mixer(h, kind):
            ret = kind == "ret"
            if ret:
                lgf = LG.ap[:, h:h + 1]; lgb = LG.ap[:, 4 + h:5 + h]
                vec = lambda k_: CST.ap[:, C_VEC + k_:C_VEC + k_ + 1]
                for col, (src, lg) in enumerate([(vec(0), lgf), (vec(1), lgb), (vec(2), lgf), (vec(3), lgb)]):
                    P.op("act", lambda e, col=col, src=src, lg=lg: e.activation(out=DEC[:, col:col + 1], in_=src, func=AF.Exp, scale=lg), r=[CST, LG], w=[DEC])
                P.op("act", lambda e: e.activation(out=DEC[:, 4:5], in_=lgf, func=AF.Exp, scale=128.0), r=[LG], w=[DEC])
                P.op("act", lambda e: e.activation(out=DEC[:, 5:6], in_=lgb, func=AF.Exp, scale=128.0), r=[LG], w=[DEC])
                P.op("act", lambda e: e.activation(out=RM[:], in_=cstm(C_D1), func=AF.Exp, scale=lgf), r=[CST, LG], w=[RM])
                P.op("dve", lambda e: e.tensor_tensor(out=RM[:], in0=RM[:], in1=cstm(C_U1), op=ALU.mult), r=[RM, CST], w=[RM])
                P.op("act", lambda e: e.activation(out=RMT[:], in_=cstm(C_D2), func=AF.Exp, scale=lgb), r=[CST, LG], w=[RMT])
                P.op("dve", lambda e: e.tensor_tensor(out=RMT[:], in0=RMT[:], in1=cstm(C_U2), op=ALU.mult), r=[RMT, CST], w=[RMT])
                P.op("dve", lambda e: e.tensor_tensor(out=RM[:], in0=RM[:], in1=RMT[:], op=ALU.add), r=[RM, RMT], w=[RM])
            gsrc = gn_ret if ret else gn_ml
            P.op("sp", lambda e: e.dma_start(out=GN[:], in_=gsrc[0:1, h * 256:(h + 1) * 256].broadcast_to([128, 256])), w=[GN], dma="gn")
            ocf = out_chunk_ret if ret else out_chunk_ml
            for s0 in range(0, nseq, NSL):
                run_rr([state_chain(sl, s0 + sl, d, h, kind) for sl in range(NSL) for d in range(2)])
                jobs = [(sl, s0 + sl, c) for sl in range(NSL) for c in range(n)]
                for j0 in range(0, len(jobs), 2):
                    run_rr([ocf(k, jobs[j0 + k][0], jobs[j0 + k][1], jobs[j0 + k][2], h) for k in range(min(2, len(jobs) - j0))])

        def gates_pre():
            for t in range(NT):
                ps = psn()
                for kc in range(KC):
                    P.op("pe", lambda e, ps=ps, kc=kc, t=t: e.matmul(ps[:, 0:16], lhsT=HT[:, kc, t * 128:(t + 1) * 128], rhs=WGT[:, kc, :], start=(kc == 0), stop=(kc == KC - 1)), r=[HT, WGT], w=[ps])
                P.op("dve", lambda e, ps=ps, t=t: e.tensor_tensor(out=GTS[:, t, :], in0=ps[:, 0:16], in1=BG[:], op=ALU.add), r=[ps, BG], w=[GTS])
            P.op("act", lambda e: e.activation(out=LFN[:], in_=GTS[:], func=AF.Exp, scale=-1.0), r=[GTS], w=[LFN])
            P.op("act", lambda e: e.activation(out=LFN[:], in_=LFN[:], func=AF.Ln, bias=1.0), r=[LFN], w=[LFN])
            for t in range(NT):
                ps = psn()
                P.op("pe", lambda e, ps=ps, t=t: e.matmul(ps[:, 0:4], lhsT=cstm(C_U1), rhs=LFN[:, t, 4:8], start=True, stop=True), r=[CST, LFN], w=[ps])
                P.op("pe", lambda e, ps=ps, t=t: e.matmul(ps[:, 4:8], lhsT=cstm(C_U2), rhs=LFN[:, t, 12:16], start=True, stop=True), r=[CST, LFN], w=[ps])
                P.op("pe", lambda e, ps=ps, t=t: e.matmul(ps[:, 8:12], lhsT=ONESF, rhs=LFN[:, t, 4:8], start=True, stop=True), r=[CST, LFN], w=[ps])
                P.op("pe", lambda e, ps=ps, t=t: e.matmul(ps[:, 12:16], lhsT=ONESF, rhs=LFN[:, t, 12:16], start=True, stop=True), r=[CST, LFN], w=[ps])
                P.op("dve", lambda e, ps=ps, t=t: e.tensor_copy(out=GNB[:, t, :], in_=ps[:, 0:8]), r=[ps], w=[GNB])
                P.op("dve", lambda e, ps=ps, t=t: e.tensor_copy(out=GBL[:, t, :], in_=ps[:, 8:16]), r=[ps], w=[GBL])
                P.op("dve", lambda e, t=t: e.tensor_tensor(out=GA[:, t, 0:4], in0=GTS[:, t, 0:4], in1=GNB[:, t, 0:4], op=ALU.add), r=[GTS, GNB], w=[GA])
                P.op("dve", lambda e, t=t: e.tensor_tensor(out=GA[:, t, 4:8], in0=GTS[:, t, 8:12], in1=GNB[:, t, 4:8], op=ALU.add), r=[GTS, GNB], w=[GA])
            GD = [Dm, sbt(XN[0].lo, [4, 128], F32)]
            GAm = [Am, sbt(TOTT[0].lo, [4, 128], F32)]
            GV8 = [v8, [sbt(SMALL[0].lo + j * 32, [8], F32) for j in range(4)]]
            MPV = [mprev, sbt(SMALL[1].lo, [8], F32)]
            MNW = [mnew, sbt(SMALL[1].lo + 32, [8], F32)]
            for s in range(nseq):
                for d in range(2):
                    if not sample:
                        P.op("dve", lambda e, d=d: e.memset(MPV[d][:], 0.0), w=[MPV[d]])
                    else:
                        P.op("sp", lambda e, d=d: e.dma_start(out=MPV[d][:], in_=st_m[0:1, :].broadcast_to([128, 8])), w=[MPV[d]], dma="mprev%d" % d)

                def gchain(d, s=s):
                    ds = slice(d * 4, d * 4 + 4)
                    order = range(n) if d == 0 else range(n - 1, -1, -1)
                    neg = cstm(C_NEGF if d == 0 else C_NEGB)
                    Dm_, Am_, v8_, mprev_, mnew_ = GD[d], GAm[d], GV8[d], MPV[d], MNW[d]
                    for c in order:
                        t = s * n + c
                        yield
                        P.op("dve", lambda e, t=t: e.tensor_tensor(out=Dm_[:], in0=IDF.unsqueeze(1).to_broadcast([128, 4, 128]), in1=GA[:, t, ds].unsqueeze(2).to_broadcast([128, 4, 128]), op=ALU.mult), r=[CST, GA], w=[Dm_])
                        ps = psn()
                        P.op("pe", lambda e, ps=ps: e.matmul(ps[:], lhsT=ONESF, rhs=Dm_[:].rearrange("p a b -> p (a b)"), start=True, stop=True), r=[CST, Dm_], w=[ps])
                        P.op("dve", lambda e, ps=ps: e.tensor_tensor(out=Am_[:], in0=ps[:].rearrange("p (a b) -> p a b", a=4), in1=neg.unsqueeze(1).to_broadcast([128, 4, 128]), op=ALU.add), r=[ps, CST], w=[Am_])
                        P.op("dve", lambda e: e.tensor_reduce(out=v8_[0][:, 0:4], in_=Am_[:], axis=AX.X, op=ALU.max), r=[Am_], w=[v8_[0]])
                        P.op("dve", lambda e, ps=ps: e.tensor_reduce(out=v8_[1][:, 0:4], in_=ps[:].rearrange("p (a b) -> p a b", a=4), axis=AX.X, op=ALU.max), r=[ps], w=[v8_[1]])
                        yield
                        P.op("dve", lambda e, t=t: e.tensor_tensor(out=GMU[:, t, ds], in0=v8_[0][:, 0:4], in1=mprev_[:, ds], op=ALU.max), r=[v8_[0], mprev_], w=[GMU])
                        P.op("dve", lambda e: e.tensor_tensor(out=mnew_[:, ds], in0=v8_[1][:, 0:4], in1=mprev_[:, ds], op=ALU.max), r=[v8_[1], mprev_], w=[mnew_])
                        P.op("dve", lambda e, t=t: e.tensor_tensor(out=v8_[2][:, 0:4], in0=mprev_[:, ds], in1=GMU[:, t, ds], op=ALU.subtract), r=[mprev_, GMU], w=[v8_[2]])
                        P.op("act", lambda e, t=t: e.activation(out=GSP[:, t, ds], in_=v8_[2][:, 0:4], func=AF.Exp), r=[v8_[2]], w=[GSP])
                        P.op("dve", lambda e, t=t: e.tensor_tensor(out=v8_[3][:, 0:4], in0=GA[:, t, ds], in1=mnew_[:, ds], op=ALU.subtract), r=[GA, mnew_], w=[v8_[3]])
                        P.op("act", lambda e, t=t: e.activation(out=GWK[:, t, ds], in_=v8_[3][:, 0:4], func=AF.Exp), r=[v8_[3]], w=[GWK])
                        yield
                        P.op("dve", lambda e: e.tensor_tensor(out=v8_[2][:, 4:8], in0=mprev_[:, ds], in1=mnew_[:, ds], op=ALU.subtract), r=[mprev_, mnew_], w=[v8_[2]])
                        P.op("act", lambda e, t=t: e.activation(out=GSC[:, t, ds], in_=v8_[2][:, 4:8], func=AF.Exp), r=[v8_[2]], w=[GSC])
                        P.op("dve", lambda e, t=t: e.tensor_tensor(out=v8_[3][:, 4:8], in0=GNB[:, t, ds], in1=GMU[:, t, ds], op=ALU.subtract), r=[GNB, GMU], w=[v8_[3]])
                        P.op("act", lambda e, t=t: e.activation(out=GEM[:, t, ds], in_=v8_[3][:, 4:8], func=AF.Exp), r=[v8_[3]], w=[GEM])
                        P.op("dve", lambda e, t=t: e.tensor_tensor(out=mprev_[:, ds], in0=mnew_[:, ds], in1=GBL[:, t, ds], op=ALU.subtract), r=[mnew_, GBL], w=[mprev_])
                    yield

                gens = [gchain(0), gchain(1)]
                while gens:
                    nxt = []
                    for g in gens:
                        try:
                            next(g)
                            nxt.append(g)
                        except StopIteration:
                            pass
                    gens = nxt
                if not sample:
                    P.op("dve", lambda e: e.tensor_copy(out=MPV[0][:, 4:8], in_=MPV[1][:, 4:8]), r=[MPV[1]], w=[MPV[0]])
                    P.op("sp", lambda e, s=s: e.dma_start(out=o_m[s:s + 1, :], in_=MPV[0][0:1, :]), r=[MPV[0]], dma="om")

        def mod_blks(hh):
            return list(range(16 + hh * 4, 20 + hh * 4))

        for h in range(NH):
            wq = load_w(h * 256); proj_fm(wq, QT, "rq", 0)
            wk = load_w(1024 + h * 256); proj_fm(wk, KT, "rk", 0)
            wv = load_w(2048 + h * 256); wg = load_w(3072 + h * 256)
            if not sample:
                mod_dma(mod_blks(h))
            proj_tm(wv, wg, AF.Silu)
            make_ktm()
            mixer(h, "ret")
            if not sample:
                mod_compute(mod_blks(h))
        gates_pre()
        for h in range(NH):
            wq = load_w(4096 + h * 256); proj_fm(wq, QT, "mq", h * 256)
            wk = load_w(5120 + h * 256); proj_fm(wk, KT, "mk", 1024 + h * 256)
            wv = load_w(6144 + h * 256); wg = load_w(7168 + h * 256)
            if not sample:
                mod_dma(mod_blks(4 + h))
            proj_tm(wv, wg, AF.Sigmoid)
            make_ktm(1.0 / 16.0)
            mixer(h, "ml")
            if not sample:
                mod_compute(mod_blks(4 + h))

        MT = sbt(O_HT, [16, Town], BF16)
        for t in range(NTO):
            transpose_tile(MIXO[:, t, :], MIXO.part(t, NTO), MT, MT, t * 128)
        rb = Bump(O_PROJ, SZ_PROJ)
        r_g1 = rb([D], F32); r_l1g = rb([D], F32); r_l1b = rb([D], F32); r_sc2 = rb([D], F32); r_sh2 = rb([D], F32)
        load_modrow(r_g1, 2, mg); load_row(r_l1g, ln1_g); load_row(r_l1b, ln1_b)
        load_modrow(r_sh2, 3, mg); load_modrow(r_sc2, 4, mg, plus1=True)
        YA = [sbt((O_MIXO if t < 4 else O_MIX) + (t % 4) * 8192, [D], F32) for t in range(NTO)]
        hb = Bump(O_HT + 32768, 32768)
        xts = [hb([D], F32) for _ in range(2)]
        h2s = [hb([D], BF16) for _ in range(2)]
        wsl2 = [sbt(O_W + k * 8192, [16, 256], BF16) for k in range(2)]
        for cbk in range(8):
            ws = wsl2[cbk % 2]
            P.op("pool", lambda e, ws=ws, cbk=cbk: e.dma_start(out=ws[:], in_=w_out[:, cbk * 256:(cbk + 1) * 256].rearrange("(kc p) n -> p kc n", p=128)), w=[ws], dma="wsl%d" % (cbk % 2))
            for t in range(NTO):
                ps = psn()
                for kc in range(KC):
                    P.op("pe", lambda e, ps=ps, kc=kc, t=t, ws=ws: e.matmul(ps[:, 0:256], lhsT=MT[:, kc, t * 128:(t + 1) * 128], rhs=ws[:, kc, :], start=(kc == 0), stop=(kc == KC - 1)), r=[MT, ws], w=[ps])
                P.op("dve", lambda e, ps=ps, t=t, cbk=cbk: e.tensor_tensor(out=YA[t][:, cbk * 256:(cbk + 1) * 256], in0=ps[:, 0:256], in1=r_g1[:, cbk * 256:(cbk + 1) * 256], op=ALU.mult), r=[ps, r_g1], w=[YA[t]])
        smo = [sbt(O_MIX + 32768 + k * 256, [64], F32) for k in range(2)]

        def o_tile(t):
            def g(k):
                xt = xts[k]; h2 = h2s[k]; ya = YA[t]
                st, mv, rstd, nmr = small4(smo[k])
                P.op("sp", lambda e: e.dma_start(out=xt[:], in_=x_own[t * 128:(t + 1) * 128, :]), w=[xt], dma="xo%d" % k)
                yield
                P.op("dve", lambda e: e.scalar_tensor_tensor(out=ya[:], in0=xt[:], scalar=ALPHA, in1=ya[:], op0=ALU.mult, op1=ALU.add), r=[xt, ya], w=[ya])
                yield
                ln_stats(ya.ap, ya, st, mv, rstd, None, D)
                yield
                P.op("dve", lambda e: e.scalar_tensor_tensor(out=ya[:], in0=ya[:], scalar=mv[:, 0:1], in1=r_l1g[:], op0=ALU.subtract, op1=ALU.mult), r=[ya, mv, r_l1g], w=[ya])
                yield
                P.op("dve", lambda e: e.scalar_tensor_tensor(out=ya[:], in0=ya[:], scalar=rstd[:], in1=r_l1b[:], op0=ALU.mult, op1=ALU.add), r=[ya, rstd, r_l1b], w=[ya])
                yield
                P.op("sp", lambda e: e.dma_start(out=x1_d[x1off + t * 128: x1off + (t + 1) * 128, :], in_=ya[:]), r=[ya], w=[DR("x1", x1off // 128 + t, x1off // 128 + t + 1)], dma="x1o%d" % t)
                ln_stats(ya.ap, ya, st, mv, rstd, None, D)
                yield
                P.op("dve", lambda e: e.scalar_tensor_tensor(out=xt[:], in0=ya[:], scalar=mv[:, 0:1], in1=r_sc2[:], op0=ALU.subtract, op1=ALU.mult), r=[ya, mv, r_sc2], w=[xt])
                yield
                P.op("dve", lambda e: e.scalar_tensor_tensor(out=h2[:], in0=xt[:], scalar=rstd[:], in1=r_sh2[:], op0=ALU.mult, op1=ALU.add), r=[xt, rstd, r_sh2], w=[h2])
                yield
                yield from transpose_tile_g(h2.ap, h2, MT, MT, t * 128)
            return g
        run_pipe([o_tile(t) for t in range(NTO)], 2)

        H2T = MT
        achunks = []
        for k in range(21):
            achunks.append(sbt(O_PROJ + k * Town * 2, [Town], BF16))
        for k in range(16):
            achunks.append(sbt(O_MIXO + k * Town * 2, [Town], BF16))
        for k in range(6):
            achunks.append(sbt(O_HT + 32768 + k * Town * 2, [Town], BF16))
        wd2 = sbt(O_HT + 32768 + 6 * 2048, [NFF, 128], BF16)
        wd1 = sbt(O_W, [NFF, 128], BF16)
        fb = Bump(O_MIX, SZ_MIX)
        zt = [fb([512], F32) for _ in range(2)]
        z2 = [fb([512], F32) for _ in range(2)]
        wup = [sbt(O_W + k * 8192, [16, 256], BF16) for k in range(2)]
        nb_ = 512 // Lc
        v3 = lambda ap: ap.rearrange("p (a b) -> p a b", a=nb_)
        for c in range(NFF):
            ws = wup[c % 2]
            P.op("pool", lambda e, ws=ws, c=c: e.dma_start(out=ws[:, :, 0:128], in_=w_up[:, c * 128:(c + 1) * 128].rearrange("(kc p) n -> p kc n", p=128)), w=[ws], dma="wup%d" % (c % 2))
            P.op("pool", lambda e, ws=ws, c=c: e.dma_start(out=ws[:, :, 128:256], in_=w_up[:, DFF + c * 128:DFF + (c + 1) * 128].rearrange("(kc p) n -> p kc n", p=128)), w=[ws], dma="wup%d" % (c % 2))
            w0 = CFF.ap[:, c * 3:c * 3 + 1]; w1 = CFF.ap[:, c * 3 + 1:c * 3 + 2]; w2 = CFF.ap[:, c * 3 + 2:c * 3 + 3]
            for tb in range(Town // 512):
                pu = psn()
                for kc in range(KC):
                    P.op("pe", lambda e, pu=pu, kc=kc, tb=tb, ws=ws: e.matmul(pu[:], lhsT=ws[:, kc, 0:128], rhs=H2T[:, kc, tb * 512:(tb + 1) * 512], start=(kc == 0), stop=(kc == KC - 1)), r=[ws, H2T], w=[pu])
                pg = psn()
                for kc in range(KC):
                    P.op("pe", lambda e, pg=pg, kc=kc, tb=tb, ws=ws: e.matmul(pg[:], lhsT=ws[:, kc, 128:256], rhs=H2T[:, kc, tb * 512:(tb + 1) * 512], start=(kc == 0), stop=(kc == KC - 1)), r=[ws, H2T], w=[pg])
                z = zt[(c * (Town // 512) + tb) % 2]; zz = z2[(c * (Town // 512) + tb) % 2]
                P.op("dve", lambda e, pu=pu, z=z, w1=w1: e.tensor_scalar(out=z[:], in0=pu[:], scalar1=w1, scalar2=None, op0=ALU.mult), r=[pu, CFF], w=[z])
                P.op("dve", lambda e, pu=pu, z=z, w0=w0: e.scalar_tensor_tensor(out=v3(z[:])[:, :, 1:Lc], in0=v3(pu[:])[:, :, 0:Lc - 1], scalar=w0, in1=v3(z[:])[:, :, 1:Lc], op0=ALU.mult, op1=ALU.add), r=[pu, CFF, z], w=[z])
                P.op("dve", lambda e, pu=pu, z=z, w2=w2: e.scalar_tensor_tensor(out=v3(z[:])[:, :, 0:Lc - 1], in0=v3(pu[:])[:, :, 1:Lc], scalar=w2, in1=v3(z[:])[:, :, 0:Lc - 1], op0=ALU.mult, op1=ALU.add), r=[pu, CFF, z], w=[z])
                P.op("act", lambda e, z=z, zz=zz: e.activation(out=zz[:], in_=z[:], func=AF.Silu), r=[z], w=[zz])
                ac = achunks[c]
                P.op("dve", lambda e, pg=pg, zz=zz, ac=ac, tb=tb: e.tensor_tensor(out=ac[:, tb * 512:(tb + 1) * 512], in0=pg[:], in1=zz[:], op=ALU.mult), r=[pg, zz], w=[ac])
        fb = Bump(O_MIX, SZ_MIX)
        r_g2 = fb([D], F32); r_l2g = fb([D], F32); r_l2b = fb([D], F32)
        x1t = fb([D], F32); y2 = fb([D], F32)
        st, mv, rstd, nmr = ST_C, MV_C, RSTD_C, NMR_C
        load_modrow(r_g2, 5, mg); load_row(r_l2g, ln2_g); load_row(r_l2b, ln2_b)
        wds = [wd1, wd2]
        stg = [sbt(O_W + 11264 + k * 512, [128], F32) for k in range(8)]
        cnt = 0
        for cbk in range(16):
            ws = wds[cbk % 2]
            P.op("pool", lambda e, ws=ws, cbk=cbk: e.dma_start(out=ws[:], in_=w_down[:, cbk * 128:(cbk + 1) * 128].rearrange("(c p) n -> p c n", p=128)), w=[ws], dma="wd%d" % (cbk % 2))
            for t in range(NTO):
                ps = psn()
                for c in range(NFF):
                    P.op("pe", lambda e, ps=ps, c=c, t=t, ws=ws: e.matmul(ps[:, 0:128], lhsT=achunks[c][:, t * 128:(t + 1) * 128], rhs=ws[:, c, :], start=(c == 0), stop=(c == NFF - 1)), r=[achunks[c], ws], w=[ps])
                sg_ = stg[cnt % 8]
                P.op("dve", lambda e, ps=ps, cbk=cbk, sg_=sg_: e.tensor_tensor(out=sg_[:], in0=ps[:, 0:128], in1=r_g2[:, cbk * 128:(cbk + 1) * 128], op=ALU.mult), r=[ps, r_g2], w=[sg_])
                P.op("sp", lambda e, sg_=sg_, cbk=cbk, t=t: e.dma_start(out=y2_d[x1off + t * 128: x1off + (t + 1) * 128, cbk * 128:(cbk + 1) * 128], in_=sg_[:]), r=[sg_], w=[DR("y2", x1off // 128 + t, x1off // 128 + t + 1)], dma="stg%d" % (cnt % 8))
                cnt += 1
        for t in range(NTO):
            row = x1off // 128 + t
            P.op("sp", lambda e, t=t: e.dma_start(out=x1t[:], in_=x1_d[x1off + t * 128: x1off + (t + 1) * 128, :]), r=[DR("x1", row, row + 1)], w=[x1t], dma="x1t")
            P.op("sp", lambda e, t=t: e.dma_start(out=y2[:], in_=y2_d[x1off + t * 128: x1off + (t + 1) * 128, :]), r=[DR("y2", row, row + 1)], w=[y2], dma="y2t")
            P.op("dve", lambda e: e.scalar_tensor_tensor(out=y2[:], in0=x1t[:], scalar=ALPHA, in1=y2[:], op0=ALU.mult, op1=ALU.add), r=[x1t, y2], w=[y2])
            ln_stats(y2.ap, y2, st, mv, rstd, None, D)
            P.op("dve", lambda e: e.scalar_tensor_tensor(out=y2[:], in0=y2[:], scalar=mv[:, 0:1], in1=r_l2g[:], op0=ALU.subtract, op1=ALU.mult), r=[y2, mv, r_l2g], w=[y2])
            P.op("dve", lambda e: e.scalar_tensor_tensor(out=y2[:], in0=y2[:], scalar=rstd[:], in1=r_l2b[:], op0=ALU.mult, op1=ALU.add), r=[y2, rstd, r_l2b], w=[y2])
            P.op("sp", lambda e, t=t: e.dma_start(out=y_out[t * 128:(t + 1) * 128, :], in_=y2[:]), r=[y2], dma="yout")

    stage_mod(list(range(16)), O_PROJ, O_MIX)
    for gi in GROUPS:
        process_group(gi)
    P.emit()
    es.close()
    return nc


GROUPS = (0, 1)
_CACHE = {}


def kernel(x_prompt, x_sample, state_ret, state_mlstm_C, state_mlstm_n, state_mlstm_m, c, c_ctx,
           w_mod, b_mod, w_in, b_gate, conv_qk, ret_theta, gn_ret, gn_mlstm, w_out,
           ln1_g, ln1_b, w_up, conv_ff, w_down, ln2_g, ln2_b):
    f = lambda a: np.ascontiguousarray(np.asarray(a, dtype=np.float32))
    if "nc" not in _CACHE:
        _CACHE["nc"] = build_program()
    nc = _CACHE["nc"]
    cst = make_consts()
    xp_all = f(x_prompt).reshape(8, 1024, D)
    xs_all = f(x_sample)
    shared = dict(
        w_mod=f(w_mod)[0], b_mod=f(b_mod), w_in=f(w_in)[0], b_gate=f(b_gate),
        conv_qkT=f(f(conv_qk)[0].T.reshape(16, 128, 3).transpose(1, 0, 2).reshape(128, 48)),
        theta=f(ret_theta).reshape(1, 8), gn_ret=f(gn_ret).reshape(1, 1024), gn_ml=f(gn_mlstm).reshape(1, 1024),
        w_out=f(w_out)[0], ln1_g=f(ln1_g), ln1_b=f(ln1_b), w_up=f(w_up)[0],
        conv_ffT=f(f(conv_ff)[0].T.reshape(NFF, 128, 3).transpose(1, 0, 2).reshape(128, NFF * 3)),
        w_down=f(w_down)[0], ln2_g=f(ln2_g), ln2_b=f(ln2_b), cst=cst)
    in_maps = []
    for i in range(8):
        b, p = i // 4, i % 4
        fl = np.zeros((1, 4), np.float32); fl[0, p] = 1.0
        cT = np.stack([f(c_ctx).reshape(16, 128).T, f(c)[b].reshape(16, 128).T], axis=2).reshape(128, 32)
        m = dict(shared)
        m.update(xp=xp_all[i], xs=xs_all[b], xo=f(xs_all[b, p * 512:(p + 1) * 512]), flags=fl, cT=f(cT),
                 st_ret=f(state_ret)[b, 0], st_C=f(state_mlstm_C)[b, 0],
                 st_n=f(f(state_mlstm_n)[b, 0].reshape(2, 4, 2, 128).transpose(0, 1, 3, 2)),
                 st_m=f(state_mlstm_m)[b, 0].reshape(1, 8))
        in_maps.append(m)
    res = run_bass_kernel_spmd(nc, in_maps, core_ids=list(range(8)))
    R = res.results
    y_prompt = np.concatenate([R[i]["yp"] for i in range(8)], 0).reshape(32, 256, D)
    y_sample = np.stack([np.concatenate([R[b * 4 + p]["ys"] for p in range(4)], 0) for b in range(2)], 0)
    n_ret = np.concatenate([R[i]["o_ret"] for i in range(8)], 0)[:, None]
    n_C = np.concatenate([R[i]["o_C"] for i in range(8)], 0)[:, None]
    n_n = np.concatenate([R[i]["o_n"] for i in range(8)], 0).transpose(0, 1, 2, 4, 3).reshape(32, 2, 4, 256)[:, None]
    n_m = np.concatenate([R[i]["o_m"] for i in range(8)], 0).reshape(32, 2, 4)[:, None]
    return (y_prompt, y_sample, np.ascontiguousarray(n_ret), np.ascontiguousarray(n_C),
            np.ascontiguousarray(n_n), np.ascontiguousarray(n_m))
```

```python
import contextlib
import numpy as np
import concourse.bass as bass
import concourse.mybir as mybir
from concourse.bass_utils import run_bass_kernel_spmd

F32 = mybir.dt.float32
BF16 = mybir.dt.bfloat16
ALU = mybir.AluOpType
AF = mybir.ActivationFunctionType
AX = mybir.AxisListType

D = 2048
KC = 16
HD = 256
NH = 4
DFF = 5504
NFF = 43
NIN = 8208
ALPHA = 2.0 ** 0.25
BIG = 1.0e9
SAME_ENGINE_SYNC = True
DEBUG_IDS = set()
GRAN = 256


class Rg:
    __slots__ = ("space", "lo", "hi")

    def __init__(self, space, lo, hi):
        self.space, self.lo, self.hi = space, lo, hi


class Tl:
    def __init__(self, ap, space, lo, hi):
        self.ap, self.space, self.lo, self.hi = ap, space, lo, hi

    def __getitem__(self, k):
        return self.ap[k]

    @property
    def r(self):
        return Rg(self.space, self.lo, self.hi)

    def part(self, i, n, cnt=1):
        sz = (self.hi - self.lo) // n
        return Rg(self.space, self.lo + i * sz, self.lo + (i + cnt) * sz)


def _rg(x):
    return x.r if isinstance(x, Tl) else x


class _Rec:
    def __getattr__(self, name):
        def f(*a, **k):
            self.call = (name, a, k)
            return self
        return f


class Prog:
    ENGS = ("pe", "act", "dve", "pool", "sp")

    def __init__(self, nc):
        self.nc = nc
        self.ops = []
        self.st = {}

    def _cells(self, rg):
        if rg.space == "sb":
            return [("sb", g) for g in range(rg.lo // GRAN, (rg.hi + GRAN - 1) // GRAN)]
        return [(rg.space, g) for g in range(rg.lo, rg.hi)]

    def op(self, eng, fn, r=(), w=(), dma=None):
        oid = len(self.ops)
        deps = set()
        rc = [c for x in r for c in self._cells(_rg(x))]
        wc = [c for x in w for c in self._cells(_rg(x))]
        st = self.st
        for c in rc:
            s = st.get(c)
            if s is not None:
                if s[0] is not None:
                    deps.add(s[0])
                if c[0] == "ps":
                    for r_ in s[1]:
                        if self.ops[r_]["eng"] != eng:
                            deps.add(r_)
        for c in wc:
            s = st.get(c)
            if s is not None:
                if s[0] is not None:
                    deps.add(s[0])
                deps.update(s[1])
        for c in rc:
            s = st.get(c)
            if s is None:
                st[c] = [None, [oid]]
            else:
                s[1].append(oid)
        for c in wc:
            st[c] = [oid, []]
        deps.discard(oid)
        rec = _Rec()
        fn(rec)
        self.ops.append(dict(eng=eng, call=rec.call, deps=deps, dma=dma, sig=False, sigidx=None))
        return oid

    def emit(self):
        nc = self.nc
        ops = self.ops

        def dom(o):
            return ("dma", o["dma"]) if o["dma"] is not None else ("eng", o["eng"])

        def skip(p, engname):
            return p["dma"] is None and p["eng"] == engname and (engname == "pe" or not SAME_ENGINE_SYNC)

        for o in ops:
            for d in o["deps"]:
                p = ops[d]
                if skip(p, o["eng"]):
                    continue
                p["sig"] = True
            if o["dma"] is not None:
                o["sig"] = True
        counters = {}
        for o in ops:
            if o["sig"]:
                dm = dom(o)
                counters[dm] = counters.get(dm, 0) + 1
                o["sigidx"] = counters[dm]
        with contextlib.ExitStack() as es:
            sems = {}
            for i, dm in enumerate(counters.keys()):
                sems[dm] = es.enter_context(nc.semaphore("s%d" % i))
            block = es.enter_context(nc.Block())
            streams = {e: [] for e in self.ENGS}
            for o in ops:
                streams[o["eng"]].append(o)

            def run(engname, eng):
                waited = {}
                for o in streams[engname]:
                    need = {}
                    for d in o["deps"]:
                        p = ops[d]
                        if not p["sig"] or skip(p, engname):
                            continue
                        dm = dom(p)
                        if p["sigidx"] > need.get(dm, 0):
                            need[dm] = p["sigidx"]
                    for dm, v in need.items():
                        if waited.get(dm, 0) >= v:
                            continue
                        waited[dm] = v
                        eng.wait_ge(sems[dm], v * (16 if dm[0] == "dma" else 1))
                    nm, a_, k_ = o["call"]
                    ins = getattr(eng, nm)(*a_, **k_)
                    if DEBUG_IDS:
                        try:
                            iname = str(ins.ins.name) if hasattr(ins, "ins") else str(getattr(ins, "name", ""))
                        except Exception:
                            iname = "?"
                        if iname in DEBUG_IDS:
                            print("DEBUGID", iname, engname, nm, {kk: (getattr(vv, "ap", None), getattr(vv, "dtype", None)) if hasattr(vv, "ap") else vv for kk, vv in k_.items()})
                    if o["sig"]:
                        ins.then_inc(sems[dom(o)], 16 if o["dma"] is not None else 1)
                if engname == "sp":
                    for dm, cnt in counters.items():
                        if dm[0] == "dma":
                            eng.wait_ge(sems[dm], cnt * 16)

            block.tensor(lambda e: run("pe", e))
            block.scalar(lambda e: run("act", e))
            block.vector(lambda e: run("dve", e))
            block.gpsimd(lambda e: run("pool", e))
            block.sync(lambda e: run("sp", e))


def make_consts():
    p = np.arange(128)
    j = p[:, None].astype(np.float64)
    i = p[None, :].astype(np.float64)
    D1 = np.maximum(i - j, 0)
    U1 = (i >= j).astype(np.float64)
    D2 = np.maximum(j - i, 0)
    U2 = (j >= i).astype(np.float64)
    BIGF = BIG * (1 - U1) + np.log(16.0)
    BIGB = BIG * (1 - U2) + np.log(16.0)
    NEGF = -BIG * (1 - U2)
    NEGB = -BIG * (1 - U1)
    vec = np.stack([p + 1.0, 128.0 - p, 127.0 - p, p * 1.0], axis=1)
    inv = 10000.0 ** (-np.arange(64, dtype=np.float32) / 64.0)
    f = p % 64
    sgn = np.where(p < 64, 1.0, -1.0)
    rows = np.arange(32, dtype=np.float32)
    cols = np.arange(64, dtype=np.float32)
    angR = (rows[None, :] * inv[f][:, None]).astype(np.float32)
    angC = (cols[None, :] * inv[f][:, None]).astype(np.float32)
    cosR, sinR = np.cos(angR), np.sin(angR) * sgn[:, None]
    cosC, sinC = np.cos(angC), np.sin(angC) * sgn[:, None]
    rope = np.concatenate([cosR, sinR, cosC, sinC], axis=1)
    ropeK = rope / 16.0
    cst = np.concatenate([D1, U1, D2, U2, BIGF, BIGB, NEGF, NEGB, vec, rope, ropeK, np.eye(128), np.ones((128, 128))], axis=1)
    return np.ascontiguousarray(cst.astype(np.float32))


C_D1, C_U1, C_D2, C_U2, C_BIGF, C_BIGB, C_NEGF, C_NEGB = [k * 128 for k in range(8)]
C_VEC = 1024
C_ROPE = 1028
C_ROPEK = 1028 + 192
C_ID = 1028 + 384
C_ONES = C_ID + 128
NCST = C_ONES + 128

O_CONST = 0
SZ_CONST = 13312
O_HT = O_CONST + SZ_CONST
SZ_HT = 65536
O_W = O_HT + SZ_HT
SZ_W = 16384
O_PROJ = O_W + SZ_W
SZ_PROJ = 43008
O_MIXO = O_PROJ + SZ_PROJ
SZ_MIXO = 32768
O_MIX = O_MIXO + SZ_MIXO
SZ_MIX = 41728
ARENA = O_MIX + SZ_MIX


def build_program():
    nc = bass.Bass("TRN2", target_bir_lowering=False)

    def din(name, shape):
        return nc.dram_tensor(name, list(shape), F32, kind="ExternalInput").ap()

    def dout(name, shape):
        return nc.dram_tensor(name, list(shape), F32, kind="ExternalOutput").ap()

    xp = din("xp", [1024, D]); xs = din("xs", [2048, D]); xo = din("xo", [512, D])
    flags = din("flags", [1, 4]); cT = din("cT", [128, 32])
    st_ret = din("st_ret", [2, 4, 256, 256]); st_C = din("st_C", [2, 4, 256, 256])
    st_n = din("st_n", [2, 4, 128, 2]); st_m = din("st_m", [1, 8])
    w_mod = din("w_mod", [D, 6 * D]); b_mod = din("b_mod", [1, 6 * D]); w_in = din("w_in", [D, NIN])
    b_gate = din("b_gate", [1, 16]); conv_qkT = din("conv_qkT", [128, 48]); theta = din("theta", [1, 8])
    gn_ret = din("gn_ret", [1, 1024]); gn_ml = din("gn_ml", [1, 1024]); w_out = din("w_out", [D, D])
    ln1_g = din("ln1_g", [1, D]); ln1_b = din("ln1_b", [1, D]); w_up = din("w_up", [D, 2 * DFF])
    conv_ffT = din("conv_ffT", [128, NFF * 3]); w_down = din("w_down", [DFF, D])
    ln2_g = din("ln2_g", [1, D]); ln2_b = din("ln2_b", [1, D]); cst_d = din("cst", [128, NCST])
    yp = dout("yp", [1024, D]); ys = dout("ys", [512, D])
    o_ret = dout("o_ret", [4, 2, 4, 256, 256]); o_C = dout("o_C", [4, 2, 4, 256, 256])
    o_n = dout("o_n", [4, 2, 4, 128, 2]); o_m = dout("o_m", [4, 8])
    mod_d = nc.dram_tensor("mod_d", [2, 6 * D], F32).ap()
    x1_d = nc.dram_tensor("x1_d", [1536, D], F32).ap()
    y2_d = nc.dram_tensor("y2_d", [1536, D], F32).ap()

    P = Prog(nc)
    es = contextlib.ExitStack()
    A = es.enter_context(nc.sbuf_tensor("arena", [128, ARENA // 2], BF16))
    psf = [es.enter_context(nc.psum_tensor("psf%d" % i, [128, 512], F32)) for i in range(6)]
    psb = [es.enter_context(nc.psum_tensor("psb%d" % i, [128, 1024], BF16)) for i in range(2)]
    PSF = [Tl(psf[i][:], "ps", i, i + 1) for i in range(6)]
    PSB = [Tl(psb[i][:], "ps", 6 + i, 7 + i) for i in range(2)]
    rot = {"f": 0, "b": 0}

    held = set()

    def psn():
        for _ in range(6):
            rot["f"] = (rot["f"] + 1) % 6
            if rot["f"] not in held:
                return PSF[rot["f"]]
        raise AssertionError("no free psum bank")

    def psh():
        ps = psn()
        held.add(ps.lo)
        return ps

    def prel(ps):
        held.discard(ps.lo)

    def navail():
        return 6 - len(held)

    heldb = set()

    def psbn():
        for _ in range(2):
            rot["b"] = (rot["b"] + 1) % 2
            if rot["b"] not in heldb:
                return PSB[rot["b"]]
        raise AssertionError("no free bf16 psum bank")

    def run_pipe(facts, width=2):
        free = list(range(width)); active = []; i = 0
        while i < len(facts) or active:
            while free and i < len(facts):
                k = free.pop(0)
                active.append((facts[i](k), k))
                i += 1
            nxt = []
            for g, k in active:
                try:
                    next(g)
                    nxt.append((g, k))
                except StopIteration:
                    free.append(k)
            active = nxt

    def sbt(off, shape, dt, parts=128):
        esz = 4 if dt == F32 else 2
        n = 1
        for s in shape:
            n *= s
        nb = n * esz
        assert off % 4 == 0
        ap = A[0:parts, off // 2:(off + nb) // 2]
        if dt != BF16:
            ap = ap.bitcast(dt)
        if len(shape) == 2:
            ap = ap.rearrange("p (a b) -> p a b", a=shape[0])
        elif len(shape) == 3:
            ap = ap.rearrange("p (a b c) -> p a b c", a=shape[0], b=shape[1])
        return Tl(ap, "sb", off, off + nb)

    class Bump:
        def __init__(self, off, size):
            self.off, self.end = off, off + size

        def __call__(self, shape, dt, parts=128, align=GRAN):
            esz = 4 if dt == F32 else 2
            n = 1
            for s in shape:
                n *= s
            self.off = (self.off + align - 1) // align * align
            t = sbt(self.off, shape, dt, parts)
            self.off += n * esz
            assert self.off <= self.end, ("arena overflow", self.off, self.end)
            return t

    def DR(name, lo=0, hi=1):
        return Rg("dr_" + name, lo, hi)

    cb = Bump(O_CONST, SZ_CONST)
    CST = cb([NCST], F32)
    IDB = cb([128], BF16)
    LG = cb([8], F32)
    FLG = cb([4], F32)
    BG = cb([16], F32)
    GN = cb([256], F32)
    DEC = cb([8], F32)
    RM = cb([128], F32)
    RMT = cb([128], F32)
    SILC = cb([16, 2], BF16)
    WGT = cb([16, 16], BF16)
    CQK = cb([48], F32)
    CFF = cb([NFF * 3], F32)
    ST_C = cb([4, 6], F32); MV_C = cb([2], F32, align=4); RSTD_C = cb([1], F32, align=4); NMR_C = cb([1], F32, align=4)
    IDF = CST.ap[:, C_ID:C_ID + 128]
    ONESF = CST.ap[:, C_ONES:C_ONES + 128]

    def cstm(c0):
        return CST.ap[:, c0:c0 + 128]

    P.op("sp", lambda e: e.dma_start(out=CST[:], in_=cst_d[:, :]), w=[CST], dma="cst")
    P.op("sp", lambda e: e.dma_start(out=LG[:], in_=theta[0:1, :].broadcast_to([128, 8])), w=[LG], dma="lg")
    P.op("sp", lambda e: e.dma_start(out=FLG[:], in_=flags[0:1, :].broadcast_to([128, 4])), w=[FLG], dma="flg")
    P.op("sp", lambda e: e.dma_start(out=BG[:], in_=b_gate[0:1, :].broadcast_to([128, 16])), w=[BG], dma="bg")
    P.op("sp", lambda e: e.dma_start(out=CQK[:], in_=conv_qkT[:, :]), w=[CQK], dma="cqk")
    P.op("sp", lambda e: e.dma_start(out=CFF[:], in_=conv_ffT[:, :]), w=[CFF], dma="cff")
    P.op("pool", lambda e: e.dma_start(out=WGT[:], in_=w_in[:, 8192:8208].rearrange("(kc p) n -> p kc n", p=128)), w=[WGT], dma="wgt")
    P.op("dve", lambda e: e.tensor_copy(out=IDB[:], in_=IDF), r=[CST], w=[IDB])
    P.op("act", lambda e: e.activation(out=LG[:], in_=LG[:], func=AF.Exp, scale=-1.0), r=[LG], w=[LG])
    P.op("act", lambda e: e.activation(out=LG[:], in_=LG[:], func=AF.Ln, bias=1.0), r=[LG], w=[LG])
    P.op("dve", lambda e: e.tensor_scalar(out=LG[:], in0=LG[:], scalar1=-1.0, scalar2=None, op0=ALU.mult), r=[LG], w=[LG])

    mod_state = {"init": False}

    def stage_mod(blks, stage_off, small_off):
        ct = sbt(small_off, [32], F32)
        brow = [sbt(small_off + 256 + k * 1024, [256], F32, parts=2) for k in range(2)]
        mrow = [sbt(small_off + 256 + 2048 + k * 1024, [256], F32, parts=2) for k in range(2)]
        if not mod_state["init"]:
            mod_state["init"] = True
            P.op("sp", lambda e: e.dma_start(out=ct[:], in_=cT[:, :]), w=[ct], dma="ct")
            P.op("act", lambda e: e.activation(out=SILC[:].rearrange("p a b -> p (a b)"), in_=ct[:], func=AF.Silu), r=[ct], w=[SILC])
        wslots = [sbt(stage_off + k * 8192, [16, 256], BF16) for k in range(2)]
        for i, blk in enumerate(blks):
            ws = wslots[i % 2]; br = brow[i % 2]; mr = mrow[i % 2]
            c0 = blk * 256
            P.op("pool", lambda e, ws=ws, c0=c0: e.dma_start(out=ws[:], in_=w_mod[:, c0:c0 + 256].rearrange("(kc p) n -> p kc n", p=128)), w=[ws], dma="wm%d_%d" % (stage_off, i % 2))
            P.op("sp", lambda e, br=br, c0=c0: e.dma_start(out=br[:], in_=b_mod[0:1, c0:c0 + 256].broadcast_to([2, 256])), w=[br], dma="brow%d_%d" % (small_off, i % 2))
            ps = psn()
            for kc in range(KC):
                P.op("pe", lambda e, ps=ps, ws=ws, kc=kc: e.matmul(ps[0:2, 0:256], lhsT=SILC[:, kc, :], rhs=ws[:, kc, :], start=(kc == 0), stop=(kc == KC - 1)), r=[SILC, ws], w=[ps])
            P.op("dve", lambda e, ps=ps, mr=mr, br=br: e.tensor_tensor(out=mr[:], in0=ps[0:2, 0:256], in1=br[:], op=ALU.add), r=[ps, br], w=[mr])
            P.op("sp", lambda e, mr=mr, c0=c0: e.dma_start(out=mod_d[:, c0:c0 + 256], in_=mr[:]), r=[mr], w=[DR("mod", blk // 8, blk // 8 + 1)], dma="mrow%d_%d" % (small_off, i % 2))

    MOD_STAGE = O_HT + 32768
    MOD_SMALL = O_PROJ + 36864

    def mod_dma(blks):
        for i, blk in enumerate(blks):
            ws = sbt(MOD_STAGE + i * 8192, [16, 256], BF16)
            c0 = blk * 256
            P.op("pool", lambda e, ws=ws, c0=c0: e.dma_start(out=ws[:], in_=w_mod[:, c0:c0 + 256].rearrange("(kc p) n -> p kc n", p=128)), w=[ws], dma="wmr%d" % i)

    def mod_compute(blks):
        brow = [sbt(MOD_SMALL + k * 1024, [256], F32, parts=2) for k in range(2)]
        mrow = [sbt(MOD_SMALL + 2048 + k * 1024, [256], F32, parts=2) for k in range(2)]
        for i, blk in enumerate(blks):
            ws = sbt(MOD_STAGE + i * 8192, [16, 256], BF16)
            br = brow[i % 2]; mr = mrow[i % 2]
            c0 = blk * 256
            P.op("sp", lambda e, br=br, c0=c0: e.dma_start(out=br[:], in_=b_mod[0:1, c0:c0 + 256].broadcast_to([2, 256])), w=[br], dma="browr%d" % (i % 2))
            ps = psn()
            for kc in range(KC):
                P.op("pe", lambda e, ps=ps, ws=ws, kc=kc: e.matmul(ps[0:2, 0:256], lhsT=SILC[:, kc, :], rhs=ws[:, kc, :], start=(kc == 0), stop=(kc == KC - 1)), r=[SILC, ws], w=[ps])
            P.op("dve", lambda e, ps=ps, mr=mr, br=br: e.tensor_tensor(out=mr[:], in0=ps[0:2, 0:256], in1=br[:], op=ALU.add), r=[ps, br], w=[mr])
            P.op("sp", lambda e, mr=mr, c0=c0: e.dma_start(out=mod_d[:, c0:c0 + 256], in_=mr[:]), r=[mr], w=[DR("mod", blk // 8, blk // 8 + 1)], dma="mrowr%d" % (i % 2))

    def load_modrow(dst, q, g, plus1=False):
        P.op("sp", lambda e: e.dma_start(out=dst[:], in_=mod_d[g:g + 1, q * D:(q + 1) * D].broadcast_to([128, D])), r=[DR("mod", q, q + 1)], w=[dst], dma="mr_%d" % (dst.lo))
        if plus1:
            P.op("dve", lambda e: e.tensor_scalar(out=dst[:], in0=dst[:], scalar1=1.0, scalar2=None, op0=ALU.add), r=[dst], w=[dst])

    def load_row(dst, src):
        P.op("sp", lambda e: e.dma_start(out=dst[:], in_=src[0:1, :].broadcast_to([128, src.shape[1]])), w=[dst], dma="lr_%d" % (dst.lo))

    def ln_stats(x_ap, xr, st, mv, rstd, nmr, n):
        nch = max(1, n // 512)
        w_ = n // nch
        for c in range(nch):
            P.op("dve", lambda e, c=c: e.bn_stats(out=st[:, c, :], in_=x_ap[:, c * w_:(c + 1) * w_]), r=[xr], w=[st])
        P.op("dve", lambda e: e.bn_aggr(out=mv[:], in_=st[:, 0:nch, :].rearrange("p a b -> p (a b)")), r=[st], w=[mv])
        P.op("act", lambda e: e.activation(out=rstd[:], in_=mv[:, 1:2], func=AF.Ln, bias=1e-6), r=[mv], w=[rstd])
        P.op("act", lambda e: e.activation(out=rstd[:], in_=rstd[:], func=AF.Exp, scale=-0.5), r=[rstd], w=[rstd])
        if nmr is not None:
            P.op("dve", lambda e: e.scalar_tensor_tensor(out=nmr[:], in0=mv[:, 0:1], scalar=-1.0, in1=rstd[:], op0=ALU.mult, op1=ALU.mult), r=[mv, rstd], w=[nmr])

    def transpose_tile(src, src_r, dstT, dst_r, tcol):
        for _ in transpose_tile_g(src, src_r, dstT, dst_r, tcol):
            pass

    def transpose_tile_g(src, src_r, dstT, dst_r_, tcol):
        for half in range(2):
            while len(heldb) >= 2:
                yield
            pb = psbn()
            heldb.add(pb.lo - 6)
            dst_r = dstT.part(half, 2)
            for k in range(8):
                kc = half * 8 + k
                P.op("pe", lambda e, pb=pb, k=k, kc=kc: e.transpose(out=pb[:, k * 128:(k + 1) * 128], in_=src[:, kc * 128:(kc + 1) * 128], identity=IDB[:]), r=[src_r, IDB], w=[pb])
            eng = "act" if half == 0 else "dve"
            if eng == "act":
                P.op("act", lambda e, pb=pb, half=half: e.activation(out=dstT[:, half * 8:half * 8 + 8, tcol:tcol + 128], in_=pb[:].rearrange("p (a b) -> p a b", a=8), func=AF.Copy), r=[pb], w=[dst_r])
            else:
                P.op("dve", lambda e, pb=pb, half=half: e.tensor_copy(out=dstT[:, half * 8:half * 8 + 8, tcol:tcol + 128], in_=pb[:].rearrange("p (a b) -> p a b", a=8)), r=[pb], w=[dst_r])
            heldb.discard(pb.lo - 6)
            yield

    def process_group(gi):
        sample = gi == 1
        T = 2048 if sample else 1024
        nseq = 1 if sample else 4
        n = 16 if sample else 2
        NT = T // 128
        Town = 512 if sample else 1024
        NTO = Town // 128
        x_all = xs if sample else xp
        x_own = xo if sample else xp
        y_out = ys if sample else yp
        x1off = 1024 if sample else 0
        mg = 1 if sample else 0
        Lc = 64 if sample else 256
        HT = sbt(O_HT, [16, T], BF16)

        mb = Bump(O_MIX, SZ_MIX)
        r_sc = mb([D], F32); r_sh = mb([D], F32)
        sm_ = [mb([64], F32) for _ in range(2)]
        load_modrow(r_sh, 0, mg)
        load_modrow(r_sc, 1, mg, plus1=True)
        pbm = Bump(O_PROJ, SZ_PROJ)
        xts = [pbm([D], F32) for _ in range(2)]
        hts = [pbm([D], BF16) for _ in range(2)]

        def small4(tl):
            mk = lambda off, shape: Tl(sbt(tl.lo + off * 4, shape, F32).ap, "sb", tl.lo, tl.hi)
            return mk(0, [4, 6]), mk(24, [2]), mk(26, [1]), mk(27, [1])

        def a1_tile(t):
            def g(k):
                xt = xts[k]; ht = hts[k]
                st, mv, rstd, nmr = small4(sm_[k])
                P.op("sp", lambda e: e.dma_start(out=xt[:], in_=x_all[t * 128:(t + 1) * 128, :]), w=[xt], dma="xt%d" % k)
                yield
                ln_stats(xt.ap, xt, st, mv, rstd, None, D)
                yield
                P.op("dve", lambda e: e.scalar_tensor_tensor(out=xt[:], in0=xt[:], scalar=mv[:, 0:1], in1=r_sc[:], op0=ALU.subtract, op1=ALU.mult), r=[xt, mv, r_sc], w=[xt])
                yield
                P.op("dve", lambda e: e.scalar_tensor_tensor(out=ht[:], in0=xt[:], scalar=rstd[:], in1=r_sh[:], op0=ALU.mult, op1=ALU.add), r=[xt, rstd, r_sh], w=[ht])
                yield
                yield from transpose_tile_g(ht.ap, ht, HT, HT, t * 128)
            return g
        run_pipe([a1_tile(t) for t in range(NT)], 2)

        pbm = Bump(O_PROJ, SZ_PROJ)
        QT = pbm([2, T], BF16)
        KT = pbm([2, T], BF16)
        VG = pbm([NT, 520], BF16)
        KTM = pbm([NT, 256], BF16)
        mb = Bump(O_MIX, SZ_MIX)
        NSL = 1 if sample else 2
        MST = [[mb([2, 257], F32) for d in range(2)] for sl in range(NSL)]
        KX = [[mb([256], BF16) for d in range(2)] for sl in range(NSL)]
        SALL = []
        for sl in range(NSL):
            per_d = []
            for d in range(2):
                if sample and d == 0:
                    first = mb([2, 257], BF16)
                    rest = sbt(O_MIXO + 16384, [n - 1, 2, 257], BF16)
                    per_d.append([first] + [Tl(rest[:, c], "sb", rest.lo + c * 1028, rest.lo + (c + 1) * 1028) for c in range(n - 1)])
                else:
                    arr = mb([n, 2, 257], BF16)
                    per_d.append([Tl(arr[:, c], "sb", arr.lo + c * 1028, arr.lo + (c + 1) * 1028) for c in range(n)])
            SALL.append(per_d)
        Dm = mb([4, 128], F32)
        Am = mb([4, 128], F32)
        tmpc = [sbt(Dm.lo, [512], F32), sbt(Am.lo, [512], F32)]
        ATT = [sbt(Am.lo + k * 512, [2, 128], BF16) for k in range(2)]
        WTS = [sbt(Am.lo + 1024 + k * 512, [128], F32) for k in range(2)]
        DMS = [sbt(Dm.lo + k * 1024, [2, 128], F32) for k in range(2)]
        TOTT = [mb([2, 257], F32) for k in range(2)]
        TOT = [[Tl(TOTT[k][:, d, :], "sb", TOTT[k].lo + d * 1028, TOTT[k].lo + (d + 1) * 1028) for d in range(2)] for k in range(2)]
        XN = [mb([256], F32) for k in range(2)]
        SMALL = [mb([64], F32) for k in range(2)]
        GTS = mb([NT, 16], F32)
        LFN = mb([NT, 16], F32)
        GA = mb([2, 16, 4], F32); GNB = mb([2, 16, 4], F32); GMU = mb([2, 16, 4], F32); GSP = mb([2, 16, 4], F32)
        GWK = mb([2, 16, 4], F32); GSC = mb([2, 16, 4], F32); GEM = mb([2, 16, 4], F32); GBL = mb([2, 16, 4], F32)
        mprev = mb([8], F32); mnew = mb([8], F32, align=4); v8 = [mb([8], F32, align=4) for _ in range(4)]
        MIXO = sbt(O_MIXO, [NTO, D], BF16)
        wsl = [sbt(O_W + k * 8192, [16, 256], BF16) for k in range(2)]
        wcount = [0]

        def load_w(c0):
            ws = wsl[wcount[0] % 2]
            k = wcount[0] % 2
            wcount[0] += 1
            P.op("pool", lambda e: e.dma_start(out=ws[:], in_=w_in[:, c0:c0 + 256].rearrange("(kc p) n -> p kc n", p=128)), w=[ws], dma="wsl%d" % k)
            return ws

        P.op("dve", lambda e: e.memset(VG[:, :, 256:257], 1.0), w=[VG])

        def proj_fm(ws, dst, kind, chan0):
            for dc in range(2):
                for tb in range(T // 512):
                    ps = psn()
                    for kc in range(KC):
                        P.op("pe", lambda e, ps=ps, kc=kc, dc=dc, tb=tb: e.matmul(ps[:], lhsT=ws[:, kc, dc * 128:(dc + 1) * 128], rhs=HT[:, kc, tb * 512:(tb + 1) * 512], start=(kc == 0), stop=(kc == KC - 1)), r=[ws, HT], w=[ps])
                    dsl = dst[:, dc, tb * 512:(tb + 1) * 512]
                    if kind in ("rq", "rk"):
                        if not sample:
                            sc_ = 1.0 if kind == "rq" else 1.0 / 16.0
                            P.op("act", lambda e, ps=ps, dsl=dsl, sc_=sc_: e.activation(out=dsl, in_=ps[:], func=AF.Identity, scale=sc_), r=[ps], w=[dst])
                        else:
                            base = C_ROPE if kind == "rq" else C_ROPEK
                            if dc == 0:
                                cosb = CST.ap[:, base + 8 * tb: base + 8 * tb + 8].unsqueeze(2).to_broadcast([128, 8, 64])
                                sinb = CST.ap[:, base + 32 + 8 * tb: base + 32 + 8 * tb + 8].unsqueeze(2).to_broadcast([128, 8, 64])
                            else:
                                cosb = CST.ap[:, base + 64: base + 128].unsqueeze(1).to_broadcast([128, 8, 64])
                                sinb = CST.ap[:, base + 128: base + 192].unsqueeze(1).to_broadcast([128, 8, 64])
                            t1 = tmpc[0]; t2 = tmpc[1]
                            v3 = lambda ap: ap.rearrange("p (a b) -> p a b", a=8)
                            P.op("dve", lambda e, ps=ps, cosb=cosb: e.tensor_tensor(out=v3(t1[:]), in0=v3(ps[:]), in1=cosb, op=ALU.mult), r=[ps, CST], w=[t1])
                            P.op("dve", lambda e, ps=ps, sinb=sinb: e.tensor_tensor(out=v3(t2[:])[0:64], in0=v3(ps[:])[64:128], in1=sinb[64:128], op=ALU.mult), r=[ps, CST], w=[t2])
                            P.op("dve", lambda e, ps=ps, sinb=sinb: e.tensor_tensor(out=v3(t2[:])[64:128], in0=v3(ps[:])[0:64], in1=sinb[0:64], op=ALU.mult), r=[ps, CST], w=[t2])
                            P.op("dve", lambda e, dsl=dsl: e.tensor_tensor(out=dsl, in0=t1[:], in1=t2[:], op=ALU.add), r=[t1, t2], w=[dst])
                    else:
                        ch = chan0 // 128 + dc
                        w0 = CQK.ap[:, ch * 3 + 0: ch * 3 + 1]; w1 = CQK.ap[:, ch * 3 + 1: ch * 3 + 2]; w2 = CQK.ap[:, ch * 3 + 2: ch * 3 + 3]
                        z = tmpc[0]
                        nb_ = 512 // Lc
                        v3 = lambda ap: ap.rearrange("p (a b) -> p a b", a=nb_)
                        P.op("dve", lambda e, ps=ps, w1=w1: e.tensor_scalar(out=z[:], in0=ps[:], scalar1=w1, scalar2=None, op0=ALU.mult), r=[ps, CQK], w=[z])
                        P.op("dve", lambda e, ps=ps, w0=w0: e.scalar_tensor_tensor(out=v3(z[:])[:, :, 1:Lc], in0=v3(ps[:])[:, :, 0:Lc - 1], scalar=w0, in1=v3(z[:])[:, :, 1:Lc], op0=ALU.mult, op1=ALU.add), r=[ps, CQK, z], w=[z])
                        P.op("dve", lambda e, ps=ps, w2=w2: e.scalar_tensor_tensor(out=v3(z[:])[:, :, 0:Lc - 1], in0=v3(ps[:])[:, :, 1:Lc], scalar=w2, in1=v3(z[:])[:, :, 0:Lc - 1], op0=ALU.mult, op1=ALU.add), r=[ps, CQK, z], w=[z])
                        P.op("act", lambda e, dsl=dsl: e.activation(out=dsl, in_=z[:], func=AF.Silu), r=[z], w=[dst])

        def proj_tm(wv, wg, gfunc):
            for t in range(NT):
                ps = psn()
                for j, ws in enumerate((wv, wg)):
                    for kc in range(KC):
                        P.op("pe", lambda e, ps=ps, kc=kc, j=j, ws=ws, t=t: e.matmul(ps[:, j * 256:(j + 1) * 256], lhsT=HT[:, kc, t * 128:(t + 1) * 128], rhs=ws[:, kc, :], start=(kc == 0), stop=(kc == KC - 1)), r=[ws, HT], w=[ps])
                P.op("dve", lambda e, ps=ps, t=t: e.tensor_copy(out=VG[:, t, 0:256], in_=ps[:, 0:256]), r=[ps], w=[VG.part(t, NT)])
                P.op("act", lambda e, ps=ps, t=t: e.activation(out=VG[:, t, 264:520], in_=ps[:, 256:512], func=gfunc), r=[ps], w=[VG.part(t, NT)])

        def make_ktm(kscale=1.0):
            for t in range(NT):
                pb = psbn()
                for dc in range(2):
                    P.op("pe", lambda e, pb=pb, dc=dc, t=t: e.transpose(out=pb[:, dc * 128:(dc + 1) * 128], in_=KT[:, dc, t * 128:(t + 1) * 128], identity=IDB[:]), r=[KT, IDB], w=[pb])
                P.op("dve", lambda e, pb=pb, t=t: e.tensor_scalar(out=KTM[:, t, :], in0=pb[:, 0:256], scalar1=kscale, scalar2=None, op0=ALU.mult), r=[pb], w=[KTM.part(t, NT)])

        def smalls(k):
            o = SMALL[k].lo
            mk = lambda off, shape: Tl(sbt(o + off * 4, shape, F32).ap, "sb", SMALL[k].lo, SMALL[k].hi)
            return mk(0, [4, 6]), mk(24, [2]), mk(26, [1]), mk(27, [1]), mk(28, [2])

        def out_tail(k, t, hh):
            xn = XN[k]
            st, mv, rstd, nmr, _ = smalls(k)
            ln_stats(xn.ap, xn, st, mv, rstd, None, 256)
            yield
            P.op("dve", lambda e: e.scalar_tensor_tensor(out=xn[:], in0=xn[:], scalar=mv[:, 0:1], in1=GN[:], op0=ALU.subtract, op1=ALU.mult), r=[xn, mv, GN], w=[xn])
            yield
            col = hh * 256
            gate = VG[:, t, 264:520]
            gr = VG.part(t, NT)
            if not sample:
                P.op("dve", lambda e: e.scalar_tensor_tensor(out=MIXO[:, t, col:col + 256], in0=xn[:], scalar=rstd[:], in1=gate, op0=ALU.mult, op1=ALU.mult), r=[xn, rstd, gr], w=[MIXO.part(t, NTO)])
            else:
                pp, tp = t // 4, t % 4
                P.op("dve", lambda e: e.scalar_tensor_tensor(out=xn[:], in0=xn[:], scalar=rstd[:], in1=gate, op0=ALU.mult, op1=ALU.mult), r=[xn, rstd, gr], w=[xn])
                yield
                if pp == 0:
                    P.op("dve", lambda e: e.tensor_scalar(out=MIXO[:, tp, col:col + 256], in0=xn[:], scalar1=FLG[:, 0:1], scalar2=None, op0=ALU.mult), r=[xn, FLG], w=[MIXO.part(tp, NTO)])
                else:
                    P.op("dve", lambda e: e.scalar_tensor_tensor(out=MIXO[:, tp, col:col + 256], in0=xn[:], scalar=FLG[:, pp:pp + 1], in1=MIXO[:, tp, col:col + 256], op0=ALU.mult, op1=ALU.add), r=[xn, FLG, MIXO.part(tp, NTO)], w=[MIXO.part(tp, NTO)])
            yield

        def run_rr(gens):
            gens = list(gens)
            while gens:
                nxt = []
                for g in gens:
                    try:
                        next(g)
                        nxt.append(g)
                    except StopIteration:
                        pass
                gens = nxt

        def state_chain(sl, s, d, h, kind):
            ret = kind == "ret"
            ncol = 256 if ret else 257
            SM = MST[sl][d]; Kx = KX[sl][d]
            src4 = st_ret if ret else st_C
            srcn = None if ret else st_n
            if not sample:
                P.op("dve", lambda e: e.memset(SM[:], 0.0), w=[SM])
            else:
                P.op("sp", lambda e: e.dma_start(out=SM[:, :, 0:256], in_=src4[d, h].rearrange("(kc p) v -> p kc v", p=128)), w=[SM], dma="sm%d" % SM.lo)
                if srcn is not None:
                    P.op("sp", lambda e: e.dma_start(out=SM[:, :, 256:257], in_=srcn[d, h].unsqueeze(2), allow_slow_non_contiguous=True), w=[SM], dma="sm%d" % SM.lo)
            yield
            order = range(n) if d == 0 else range(n - 1, -1, -1)
            for c in order:
                t = s * n + c
                sa = SALL[sl][d][c]
                P.op("act", lambda e, sa=sa: e.activation(out=sa[:, :, 0:ncol], in_=SM[:, :, 0:ncol], func=AF.Copy), r=[SM], w=[sa])
                if ret:
                    sc_ap, sc_r = DEC.ap[:, 2 + d:3 + d], DEC
                    dk_ap, dk_r = DEC.ap[:, 4 + d:5 + d], DEC
                else:
                    sc_ap, sc_r = GWK.ap[:, d, t, h:h + 1], GWK.part(d, 2)
                    dk_ap, dk_r = GSC.ap[:, d, t, h:h + 1], GSC.part(d, 2)
                P.op("act", lambda e, t=t, sc_ap=sc_ap: e.activation(out=Kx[:], in_=KTM[:, t, :], func=AF.Identity, scale=sc_ap), r=[KTM.part(t, NT), sc_r], w=[Kx])
                yield
                if ret:
                    while navail() < 1:
                        yield
                    ps = psh()
                    for kc in range(2):
                        P.op("pe", lambda e, kc=kc, ps=ps, t=t: e.matmul(ps[:, kc * 256:(kc + 1) * 256], lhsT=Kx[:, kc * 128:(kc + 1) * 128], rhs=VG[:, t, 0:256], start=True, stop=True), r=[Kx, VG.part(t, NT)], w=[ps])
                    yield
                    P.op("dve", lambda e, ps=ps, dk_ap=dk_ap: e.scalar_tensor_tensor(out=SM[:, :, 0:256], in0=SM[:, :, 0:256], scalar=dk_ap, in1=ps[:].rearrange("p (a b) -> p a b", a=2), op0=ALU.mult, op1=ALU.add), r=[SM, ps, dk_r], w=[SM])
                    prel(ps)
                    yield
                else:
                    while navail() < 2:
                        yield
                    pss = [psh(), psh()]
                    for kc in range(2):
                        P.op("pe", lambda e, kc=kc, ps=pss[kc], t=t: e.matmul(ps[:, 0:257], lhsT=Kx[:, kc * 128:(kc + 1) * 128], rhs=VG[:, t, 0:257], start=True, stop=True), r=[Kx, VG.part(t, NT)], w=[pss[kc]])
                    yield
                    for kc in range(2):
                        P.op("dve", lambda e, ps=pss[kc], kc=kc, dk_ap=dk_ap: e.scalar_tensor_tensor(out=SM[:, kc, :], in0=SM[:, kc, :], scalar=dk_ap, in1=ps[:, 0:257], op0=ALU.mult, op1=ALU.add), r=[SM, pss[kc], dk_r], w=[SM])
                        prel(pss[kc])
                    yield
            if not sample:
                dst4 = o_ret if ret else o_C
                P.op("sp", lambda e: e.dma_start(out=dst4[s, d, h].rearrange("(kc p) v -> p kc v", p=128), in_=SM[:, :, 0:256]), r=[SM], dma="smo%d" % SM.lo)
                if not ret:
                    P.op("sp", lambda e: e.dma_start(out=o_n[s, d, h].unsqueeze(2), in_=SM[:, :, 256:257], allow_slow_non_contiguous=True), r=[SM], dma="smo%d" % SM.lo)
            yield

        def out_chunk_ret(k, sl, s, c, h):
            t = s * n + c
            tk = slice(t * 128, (t + 1) * 128)
            attm = ATT[k]; xn = XN[k]
            while navail() < 2:
                yield
            psA = psh()
            for kc in range(2):
                P.op("pe", lambda e, kc=kc: e.matmul(psA[:, 0:128], lhsT=KT[:, kc, tk], rhs=QT[:, kc, tk], start=(kc == 0), stop=(kc == 1)), r=[KT, QT], w=[psA])
            psS = psh()
            for d in range(2):
                sa = SALL[sl][d][c]
                for kc in range(2):
                    P.op("pe", lambda e, kc=kc, d=d, sa=sa: e.matmul(psS[:, d * 256:(d + 1) * 256], lhsT=QT[:, kc, tk], rhs=sa[:, kc, 0:256], start=(kc == 0), stop=(kc == 1)), r=[QT, sa], w=[psS])
            yield
            P.op("dve", lambda e: e.tensor_tensor(out=attm[:, 0, :], in0=psA[:, 0:128], in1=RM[:], op=ALU.mult), r=[psA, RM], w=[attm])
            prel(psA)
            yield
            while navail() < 1:
                yield
            psO = psh()
            P.op("pe", lambda e: e.matmul(psO[:, 0:256], lhsT=attm[:, 0, :], rhs=VG[:, t, 0:256], start=True, stop=True), r=[attm, VG.part(t, NT)], w=[psO])
            P.op("dve", lambda e: e.tensor_scalar(out=xn[:], in0=psS[:, 0:256], scalar1=DEC[:, 0:1], scalar2=None, op0=ALU.mult), r=[psS, DEC], w=[xn])
            yield
            P.op("dve", lambda e: e.scalar_tensor_tensor(out=xn[:], in0=psS[:, 256:512], scalar=DEC[:, 1:2], in1=xn[:], op0=ALU.mult, op1=ALU.add), r=[psS, DEC, xn], w=[xn])
            yield
            P.op("dve", lambda e: e.tensor_tensor(out=xn[:], in0=psO[:, 0:256], in1=xn[:], op=ALU.add), r=[psO, xn], w=[xn])
            prel(psS); prel(psO)
            yield
            yield from out_tail(k, t, h)

        def out_chunk_ml(k, sl, s, c, h):
            t = s * n + c
            tk = slice(t * 128, (t + 1) * 128)
            attm = ATT[k]; xn = XN[k]; Dk = DMS[k]; WT = WTS[k]
            _, _, _, _, dd = smalls(k)
            bigm = CST.ap[:, C_BIGF:C_BIGF + 256]
            while navail() < 1:
                yield
            psA = psh()
            for kc in range(2):
                P.op("pe", lambda e, kc=kc: e.matmul(psA[:, 0:128], lhsT=KT[:, kc, tk], rhs=QT[:, kc, tk], start=(kc == 0), stop=(kc == 1)), r=[KT, QT], w=[psA])
            mu2 = GMU.ap[:, :, t, h]
            P.op("dve", lambda e: e.tensor_tensor(out=Dk[:], in0=IDF.unsqueeze(1).to_broadcast([128, 2, 128]), in1=mu2.unsqueeze(2).to_broadcast([128, 2, 128]), op=ALU.mult), r=[CST, GMU], w=[Dk])
            yield
            P.op("pe", lambda e: e.matmul(psA[:, 128:384], lhsT=ONESF, rhs=Dk[:].rearrange("p a b -> p (a b)"), start=True, stop=False), r=[CST, Dk], w=[psA])
            P.op("pe", lambda e: e.matmul(psA[:, 128:384], lhsT=IDF, rhs=bigm, start=False, stop=True), r=[CST], w=[psA])
            yield
            for d in range(2):
                col = d * 4 + h
                P.op("act", lambda e, d=d, col=col: e.activation(out=WT[:], in_=psA[:, 128 + d * 128:256 + d * 128], func=AF.Exp, bias=GA[:, d, t, h:h + 1], scale=-1.0), r=[psA, GA], w=[WT])
                yield
                P.op("dve", lambda e, d=d: e.tensor_tensor(out=attm[:, d, :], in0=psA[:, 0:128], in1=WT[:], op=ALU.mult), r=[psA, WT], w=[attm])
                yield
            prel(psA)
            for d in range(2):
                col = d * 4 + h
                sa = SALL[sl][d][c]
                td = TOT[k][d]
                while navail() < 2:
                    yield
                psN = psh()
                P.op("pe", lambda e, d=d, psN=psN: e.matmul(psN[:, 0:257], lhsT=attm[:, d, :], rhs=VG[:, t, 0:257], start=True, stop=True), r=[attm, VG.part(t, NT)], w=[psN])
                psI = psh()
                for kc in range(2):
                    P.op("pe", lambda e, kc=kc, psI=psI, sa=sa: e.matmul(psI[:, 0:257], lhsT=QT[:, kc, tk], rhs=sa[:, kc, :], start=(kc == 0), stop=(kc == 1)), r=[QT, sa], w=[psI])
                yield
                P.op("dve", lambda e, psI=psI, td=td, col=col: e.tensor_scalar(out=td[:], in0=psI[:, 0:257], scalar1=GSP[:, d, t, h:h + 1], scalar2=None, op0=ALU.mult), r=[psI, GSP], w=[td])
                prel(psI)
                yield
                P.op("dve", lambda e, psN=psN, td=td: e.tensor_tensor(out=td[:], in0=psN[:, 0:257], in1=td[:], op=ALU.add), r=[psN, td], w=[td])
                prel(psN)
                yield
            den2 = TOTT[k][:, :, 256:257]
            P.op("dve", lambda e: e.scalar_tensor_tensor(out=dd[:, 0:2].unsqueeze(2), in0=den2, scalar=-1.0, in1=den2, op0=ALU.mult, op1=ALU.max), r=[TOTT[k]], w=[dd])
            yield
            P.op("dve", lambda e: e.tensor_tensor(out=dd[:, 0:2], in0=dd[:, 0:2], in1=GEM.ap[:, :, t, h], op=ALU.max), r=[dd, GEM], w=[dd])
            yield
            P.op("dve", lambda e: e.reciprocal(out=dd[:], in_=dd[:]), r=[dd], w=[dd])
            yield
            P.op("dve", lambda e: e.tensor_scalar(out=xn[:], in0=TOT[k][0][:, 0:256], scalar1=dd[:, 0:1], scalar2=None, op0=ALU.mult), r=[TOT[k][0], dd], w=[xn])
            yield
            P.op("dve", lambda e: e.scalar_tensor_tensor(out=xn[:], in0=TOT[k][1][:, 0:256], scalar=dd[:, 1:2], in1=xn[:], op0=ALU.mult, op1=ALU.add), r=[TOT[k][1], dd, xn], w=[xn])
            yield
            yield from out_tail(k, t, 4 + h)

        def mixer(h, kind):
            ret = kind == "ret"
            if ret:
                lgf = LG.ap[:, h:h + 1]; lgb = LG.ap[:, 4 + h:5 + h]
                vec = lambda k_: CST.ap[:, C_VEC + k_:C_VEC + k_ + 1]
                for col, (src, lg) in enumerate([(vec(0), lgf), (vec(1), lgb), (vec(2), lgf), (vec(3), lgb)]):
                    P.op("act", lambda e, col=col, src=src, lg=lg: e.activation(out=DEC[:, col:col + 1], in_=src, func=AF.Exp, scale=lg), r=[CST, LG], w=[DEC])
                P.op("act", lambda e: e.activation(out=DEC[:, 4:5], in_=lgf, func=AF.Exp, scale=128.0), r=[LG], w=[DEC])
                P.op("act", lambda e: e.activation(out=DEC[:, 5:6], in_=lgb, func=AF.Exp, scale=128.0), r=[LG], w=[DEC])
                P.op("act", lambda e: e.activation(out=RM[:], in_=cstm(C_D1), func=AF.Exp, scale=lgf), r=[CST, LG], w=[RM])
                P.op("dve", lambda e: e.tensor_tensor(out=RM[:], in0=RM[:], in1=cstm(C_U1), op=ALU.mult), r=[RM, CST], w=[RM])
                P.op("act", lambda e: e.activation(out=RMT[:], in_=cstm(C_D2), func=AF.Exp, scale=lgb), r=[CST, LG], w=[RMT])
                P.op("dve", lambda e: e.tensor_tensor(out=RMT[:], in0=RMT[:], in1=cstm(C_U2), op=ALU.mult), r=[RMT, CST], w=[RMT])
                P.op("dve", lambda e: e.tensor_tensor(out=RM[:], in0=RM[:], in1=RMT[:], op=ALU.add), r=[RM, RMT], w=[RM])
            gsrc = gn_ret if ret else gn_ml
            P.op("sp", lambda e: e.dma_start(out=GN[:], in_=gsrc[0:1, h * 256:(h + 1) * 256].broadcast_to([128, 256])), w=[GN], dma="gn")
            ocf = out_chunk_ret if ret else out_chunk_ml
            for s0 in range(0, nseq, NSL):
                run_rr([state_chain(sl, s0 + sl, d, h, kind) for sl in range(NSL) for d in range(2)])
                jobs = [(sl, s0 + sl, c) for sl in range(NSL) for c in range(n)]
                for j0 in range(0, len(jobs), 2):
                    run_rr([ocf(k, jobs[j0 + k][0], jobs[j0 + k][1], jobs[j0 + k][2], h) for k in range(min(2, len(jobs) - j0))])

        def gates_pre():
            for t in range(NT):
                ps = psn()
                for kc in range(KC):
                    P.op("pe", lambda e, ps=ps, kc=kc, t=t: e.matmul(ps[:, 0:16], lhsT=HT[:, kc, t * 128:(t + 1) * 128], rhs=WGT[:, kc, :], start=(kc == 0), stop=(kc == KC - 1)), r=[HT, WGT], w=[ps])
                P.op("dve", lambda e, ps=ps, t=t: e.tensor_tensor(out=GTS[:, t, :], in0=ps[:, 0:16], in1=BG[:], op=ALU.add), r=[ps, BG], w=[GTS])
            P.op("act", lambda e: e.activation(out=LFN[:], in_=GTS[:], func=AF.Exp, scale=-1.0), r=[GTS], w=[LFN])
            P.op("act", lambda e: e.activation(out=LFN[:], in_=LFN[:], func=AF.Ln, bias=1.0), r=[LFN], w=[LFN])
            for t in range(NT):
                ps = psn()
                P.op("pe", lambda e, ps=ps, t=t: e.matmul(ps[:, 0:4], lhsT=cstm(C_U1), rhs=LFN[:, t, 4:8], start=True, stop=True), r=[CST, LFN], w=[ps])
                P.op("pe", lambda e, ps=ps, t=t: e.matmul(ps[:, 4:8], lhsT=cstm(C_U2), rhs=LFN[:, t, 12:16], start=True, stop=True), r=[CST, LFN], w=[ps])
                P.op("pe", lambda e, ps=ps, t=t: e.matmul(ps[:, 8:12], lhsT=ONESF, rhs=LFN[:, t, 4:8], start=True, stop=True), r=[CST, LFN], w=[ps])
                P.op("pe", lambda e, ps=ps, t=t: e.matmul(ps[:, 12:16], lhsT=ONESF, rhs=LFN[:, t, 12:16], start=True, stop=True), r=[CST, LFN], w=[ps])
                P.op("dve", lambda e, ps=ps, t=t: e.tensor_copy(out=GNB[:, :, t, :], in_=ps[:, 0:8].rearrange("p (a b) -> p a b", a=2)), r=[ps], w=[GNB])
                P.op("dve", lambda e, ps=ps, t=t: e.tensor_copy(out=GBL[:, :, t, :], in_=ps[:, 8:16].rearrange("p (a b) -> p a b", a=2)), r=[ps], w=[GBL])
                P.op("dve", lambda e, t=t: e.tensor_tensor(out=GA[:, 0, t, :], in0=GTS[:, t, 0:4], in1=GNB[:, 0, t, :], op=ALU.add), r=[GTS, GNB], w=[GA])
                P.op("dve", lambda e, t=t: e.tensor_tensor(out=GA[:, 1, t, :], in0=GTS[:, t, 8:12], in1=GNB[:, 1, t, :], op=ALU.add), r=[GTS, GNB], w=[GA])
            GD = [Dm, sbt(XN[0].lo, [4, 128], F32)]
            GAm = [Am, sbt(TOTT[0].lo, [4, 128], F32)]
            GV8 = [v8, [sbt(SMALL[0].lo + j * 32, [8], F32) for j in range(4)]]
            MPV = [mprev, sbt(SMALL[1].lo, [8], F32)]
            MNW = [mnew, sbt(SMALL[1].lo + 32, [8], F32)]
            for s in range(nseq):
                for d in range(2):
                    if not sample:
                        P.op("dve", lambda e, d=d: e.memset(MPV[d][:], 0.0), w=[MPV[d]])
                    else:
                        P.op("sp", lambda e, d=d: e.dma_start(out=MPV[d][:], in_=st_m[0:1, :].broadcast_to([128, 8])), w=[MPV[d]], dma="mprev%d" % d)

                def gchain(d, s=s):
                    ds = slice(d * 4, d * 4 + 4)
                    order = range(n) if d == 0 else range(n - 1, -1, -1)
                    neg = cstm(C_NEGF if d == 0 else C_NEGB)
                    Dm_, Am_, v8_, mprev_, mnew_ = GD[d], GAm[d], GV8[d], MPV[d], MNW[d]
                    for c in order:
                        t = s * n + c
                        yield
                        P.op("dve", lambda e, t=t: e.tensor_tensor(out=Dm_[:], in0=IDF.unsqueeze(1).to_broadcast([128, 4, 128]), in1=GA[:, d, t, :].unsqueeze(2).to_broadcast([128, 4, 128]), op=ALU.mult), r=[CST, GA.part(d, 2)], w=[Dm_])
                        ps = psn()
                        P.op("pe", lambda e, ps=ps: e.matmul(ps[:], lhsT=ONESF, rhs=Dm_[:].rearrange("p a b -> p (a b)"), start=True, stop=True), r=[CST, Dm_], w=[ps])
                        P.op("dve", lambda e, ps=ps: e.tensor_tensor(out=Am_[:], in0=ps[:].rearrange("p (a b) -> p a b", a=4), in1=neg.unsqueeze(1).to_broadcast([128, 4, 128]), op=ALU.add), r=[ps, CST], w=[Am_])
                        P.op("dve", lambda e: e.tensor_reduce(out=v8_[0][:, 0:4], in_=Am_[:], axis=AX.X, op=ALU.max), r=[Am_], w=[v8_[0]])
                        P.op("dve", lambda e, ps=ps: e.tensor_reduce(out=v8_[1][:, 0:4], in_=ps[:].rearrange("p (a b) -> p a b", a=4), axis=AX.X, op=ALU.max), r=[ps], w=[v8_[1]])
                        yield
                        P.op("dve", lambda e, t=t: e.tensor_tensor(out=GMU[:, d, t, :], in0=v8_[0][:, 0:4], in1=mprev_[:, ds], op=ALU.max), r=[v8_[0], mprev_], w=[GMU.part(d, 2)])
                        P.op("dve", lambda e: e.tensor_tensor(out=mnew_[:, ds], in0=v8_[1][:, 0:4], in1=mprev_[:, ds], op=ALU.max), r=[v8_[1], mprev_], w=[mnew_])
                        P.op("dve", lambda e, t=t: e.tensor_tensor(out=v8_[2][:, 0:4], in0=mprev_[:, ds], in1=GMU[:, d, t, :], op=ALU.subtract), r=[mprev_, GMU.part(d, 2)], w=[v8_[2]])
                        P.op("act", lambda e, t=t: e.activation(out=GSP[:, d, t, :], in_=v8_[2][:, 0:4], func=AF.Exp), r=[v8_[2]], w=[GSP.part(d, 2)])
                        P.op("dve", lambda e, t=t: e.tensor_tensor(out=v8_[3][:, 0:4], in0=GA[:, d, t, :], in1=mnew_[:, ds], op=ALU.subtract), r=[GA.part(d, 2), mnew_], w=[v8_[3]])
                        P.op("act", lambda e, t=t: e.activation(out=GWK[:, d, t, :], in_=v8_[3][:, 0:4], func=AF.Exp), r=[v8_[3]], w=[GWK.part(d, 2)])
                        yield
                        P.op("dve", lambda e: e.tensor_tensor(out=v8_[2][:, 4:8], in0=mprev_[:, ds], in1=mnew_[:, ds], op=ALU.subtract), r=[mprev_, mnew_], w=[v8_[2]])
                        P.op("act", lambda e, t=t: e.activation(out=GSC[:, d, t, :], in_=v8_[2][:, 4:8], func=AF.Exp), r=[v8_[2]], w=[GSC.part(d, 2)])
                        P.op("dve", lambda e, t=t: e.tensor_tensor(out=v8_[3][:, 4:8], in0=GNB[:, d, t, :], in1=GMU[:, d, t, :], op=ALU.subtract), r=[GNB.part(d, 2), GMU.part(d, 2)], w=[v8_[3]])
                        P.op("act", lambda e, t=t: e.activation(out=GEM[:, d, t, :], in_=v8_[3][:, 4:8], func=AF.Exp), r=[v8_[3]], w=[GEM.part(d, 2)])
                        P.op("dve", lambda e, t=t: e.tensor_tensor(out=mprev_[:, ds], in0=mnew_[:, ds], in1=GBL[:, d, t, :], op=ALU.subtract), r=[mnew_, GBL.part(d, 2)], w=[mprev_])
                    yield

                gens = [gchain(0), gchain(1)]
                while gens:
                    nxt = []
                    for g in gens:
                        try:
                            next(g)
                            nxt.append(g)
                        except StopIteration:
                            pass
                    gens = nxt
                if not sample:
                    P.op("dve", lambda e: e.tensor_copy(out=MPV[0][:, 4:8], in_=MPV[1][:, 4:8]), r=[MPV[1]], w=[MPV[0]])
                    P.op("sp", lambda e, s=s: e.dma_start(out=o_m[s:s + 1, :], in_=MPV[0][0:1, :]), r=[MPV[0]], dma="om")

        def mod_blks(hh):
            return list(range(16 + hh * 4, 20 + hh * 4))

        for h in range(NH):
            wq = load_w(h * 256); proj_fm(wq, QT, "rq", 0)
            wk = load_w(1024 + h * 256); proj_fm(wk, KT, "rk", 0)
            wv = load_w(2048 + h * 256); wg = load_w(3072 + h * 256)
            if not sample:
                mod_dma(mod_blks(h))
            proj_tm(wv, wg, AF.Silu)
            make_ktm()
            mixer(h, "ret")
            if not sample:
                mod_compute(mod_blks(h))
        gates_pre()
        for h in range(NH):
            wq = load_w(4096 + h * 256); proj_fm(wq, QT, "mq", h * 256)
            wk = load_w(5120 + h * 256); proj_fm(wk, KT, "mk", 1024 + h * 256)
            wv = load_w(6144 + h * 256); wg = load_w(7168 + h * 256)
            if not sample:
                mod_dma(mod_blks(4 + h))
            proj_tm(wv, wg, AF.Sigmoid)
            make_ktm(1.0 / 16.0)
            mixer(h, "ml")
            if not sample:
                mod_compute(mod_blks(4 + h))

        MT = sbt(O_HT, [16, Town], BF16)
        for t in range(NTO):
            transpose_tile(MIXO[:, t, :], MIXO.part(t, NTO), MT, MT, t * 128)
        rb = Bump(O_PROJ, SZ_PROJ)
        r_g1 = rb([D], F32); r_l1g = rb([D], F32); r_l1b = rb([D], F32); r_sc2 = rb([D], F32); r_sh2 = rb([D], F32)
        load_modrow(r_g1, 2, mg); load_row(r_l1g, ln1_g); load_row(r_l1b, ln1_b)
        load_modrow(r_sh2, 3, mg); load_modrow(r_sc2, 4, mg, plus1=True)
        YA = [sbt((O_MIXO if t < 4 else O_MIX) + (t % 4) * 8192, [D], F32) for t in range(NTO)]
        hb = Bump(O_HT + 32768, 32768)
        xts = [hb([D], F32) for _ in range(2)]
        h2s = [hb([D], BF16) for _ in range(2)]
        wsl2 = [sbt(O_W + k * 8192, [16, 256], BF16) for k in range(2)]
        for cbk in range(8):
            ws = wsl2[cbk % 2]
            P.op("pool", lambda e, ws=ws, cbk=cbk: e.dma_start(out=ws[:], in_=w_out[:, cbk * 256:(cbk + 1) * 256].rearrange("(kc p) n -> p kc n", p=128)), w=[ws], dma="wsl%d" % (cbk % 2))
            for t in range(NTO):
                ps = psn()
                for kc in range(KC):
                    P.op("pe", lambda e, ps=ps, kc=kc, t=t, ws=ws: e.matmul(ps[:, 0:256], lhsT=MT[:, kc, t * 128:(t + 1) * 128], rhs=ws[:, kc, :], start=(kc == 0), stop=(kc == KC - 1)), r=[MT, ws], w=[ps])
                P.op("dve", lambda e, ps=ps, t=t, cbk=cbk: e.tensor_tensor(out=YA[t][:, cbk * 256:(cbk + 1) * 256], in0=ps[:, 0:256], in1=r_g1[:, cbk * 256:(cbk + 1) * 256], op=ALU.mult), r=[ps, r_g1], w=[YA[t]])
        smo = [sbt(O_MIX + 32768 + k * 256, [64], F32) for k in range(2)]

        def o_tile(t):
            def g(k):
                xt = xts[k]; h2 = h2s[k]; ya = YA[t]
                st, mv, rstd, nmr = small4(smo[k])
                P.op("sp", lambda e: e.dma_start(out=xt[:], in_=x_own[t * 128:(t + 1) * 128, :]), w=[xt], dma="xo%d" % k)
                yield
                P.op("dve", lambda e: e.scalar_tensor_tensor(out=ya[:], in0=xt[:], scalar=ALPHA, in1=ya[:], op0=ALU.mult, op1=ALU.add), r=[xt, ya], w=[ya])
                yield
                ln_stats(ya.ap, ya, st, mv, rstd, None, D)
                yield
                P.op("dve", lambda e: e.scalar_tensor_tensor(out=ya[:], in0=ya[:], scalar=mv[:, 0:1], in1=r_l1g[:], op0=ALU.subtract, op1=ALU.mult), r=[ya, mv, r_l1g], w=[ya])
                yield
                P.op("dve", lambda e: e.scalar_tensor_tensor(out=ya[:], in0=ya[:], scalar=rstd[:], in1=r_l1b[:], op0=ALU.mult, op1=ALU.add), r=[ya, rstd, r_l1b], w=[ya])
                yield
                P.op("sp", lambda e: e.dma_start(out=x1_d[x1off + t * 128: x1off + (t + 1) * 128, :], in_=ya[:]), r=[ya], w=[DR("x1", x1off // 128 + t, x1off // 128 + t + 1)], dma="x1o%d" % t)
                ln_stats(ya.ap, ya, st, mv, rstd, None, D)
                yield
                P.op("dve", lambda e: e.scalar_tensor_tensor(out=xt[:], in0=ya[:], scalar=mv[:, 0:1], in1=r_sc2[:], op0=ALU.subtract, op1=ALU.mult), r=[ya, mv, r_sc2], w=[xt])
                yield
                P.op("dve", lambda e: e.scalar_tensor_tensor(out=h2[:], in0=xt[:], scalar=rstd[:], in1=r_sh2[:], op0=ALU.mult, op1=ALU.add), r=[xt, rstd, r_sh2], w=[h2])
                yield
                yield from transpose_tile_g(h2.ap, h2, MT, MT, t * 128)
            return g
        run_pipe([o_tile(t) for t in range(NTO)], 2)

        H2T = MT
        achunks = []
        for k in range(21):
            achunks.append(sbt(O_PROJ + k * Town * 2, [Town], BF16))
        for k in range(16):
            achunks.append(sbt(O_MIXO + k * Town * 2, [Town], BF16))
        for k in range(6):
            achunks.append(sbt(O_HT + 32768 + k * Town * 2, [Town], BF16))
        wd2 = sbt(O_HT + 32768 + 6 * 2048, [NFF, 128], BF16)
        wd1 = sbt(O_W, [NFF, 128], BF16)
        fb = Bump(O_MIX, SZ_MIX)
        zt = [fb([512], F32) for _ in range(2)]
        z2 = [fb([512], F32) for _ in range(2)]
        wup = [sbt(O_W + k * 8192, [16, 256], BF16) for k in range(2)]
        nb_ = 512 // Lc
        v3 = lambda ap: ap.rearrange("p (a b) -> p a b", a=nb_)
        for c in range(NFF):
            ws = wup[c % 2]
            P.op("pool", lambda e, ws=ws, c=c: e.dma_start(out=ws[:, :, 0:128], in_=w_up[:, c * 128:(c + 1) * 128].rearrange("(kc p) n -> p kc n", p=128)), w=[ws], dma="wup%d" % (c % 2))
            P.op("pool", lambda e, ws=ws, c=c: e.dma_start(out=ws[:, :, 128:256], in_=w_up[:, DFF + c * 128:DFF + (c + 1) * 128].rearrange("(kc p) n -> p kc n", p=128)), w=[ws], dma="wup%d" % (c % 2))
            w0 = CFF.ap[:, c * 3:c * 3 + 1]; w1 = CFF.ap[:, c * 3 + 1:c * 3 + 2]; w2 = CFF.ap[:, c * 3 + 2:c * 3 + 3]
            for tb in range(Town // 512):
                pu = psn()
                for kc in range(KC):
                    P.op("pe", lambda e, pu=pu, kc=kc, tb=tb, ws=ws: e.matmul(pu[:], lhsT=ws[:, kc, 0:128], rhs=H2T[:, kc, tb * 512:(tb + 1) * 512], start=(kc == 0), stop=(kc == KC - 1)), r=[ws, H2T], w=[pu])
                pg = psn()
                for kc in range(KC):
                    P.op("pe", lambda e, pg=pg, kc=kc, tb=tb, ws=ws: e.matmul(pg[:], lhsT=ws[:, kc, 128:256], rhs=H2T[:, kc, tb * 512:(tb + 1) * 512], start=(kc == 0), stop=(kc == KC - 1)), r=[ws, H2T], w=[pg])
                z = zt[(c * (Town // 512) + tb) % 2]; zz = z2[(c * (Town // 512) + tb) % 2]
                P.op("dve", lambda e, pu=pu, z=z, w1=w1: e.tensor_scalar(out=z[:], in0=pu[:], scalar1=w1, scalar2=None, op0=ALU.mult), r=[pu, CFF], w=[z])
                P.op("dve", lambda e, pu=pu, z=z, w0=w0: e.scalar_tensor_tensor(out=v3(z[:])[:, :, 1:Lc], in0=v3(pu[:])[:, :, 0:Lc - 1], scalar=w0, in1=v3(z[:])[:, :, 1:Lc], op0=ALU.mult, op1=ALU.add), r=[pu, CFF, z], w=[z])
                P.op("dve", lambda e, pu=pu, z=z, w2=w2: e.scalar_tensor_tensor(out=v3(z[:])[:, :, 0:Lc - 1], in0=v3(pu[:])[:, :, 1:Lc], scalar=w2, in1=v3(z[:])[:, :, 0:Lc - 1], op0=ALU.mult, op1=ALU.add), r=[pu, CFF, z], w=[z])
                P.op("act", lambda e, z=z, zz=zz: e.activation(out=zz[:], in_=z[:], func=AF.Silu), r=[z], w=[zz])
                ac = achunks[c]
                P.op("dve", lambda e, pg=pg, zz=zz, ac=ac, tb=tb: e.tensor_tensor(out=ac[:, tb * 512:(tb + 1) * 512], in0=pg[:], in1=zz[:], op=ALU.mult), r=[pg, zz], w=[ac])
        fb = Bump(O_MIX, SZ_MIX)
        r_g2 = fb([D], F32); r_l2g = fb([D], F32); r_l2b = fb([D], F32)
        x1t = fb([D], F32); y2 = fb([D], F32)
        st, mv, rstd, nmr = ST_C, MV_C, RSTD_C, NMR_C
        load_modrow(r_g2, 5, mg); load_row(r_l2g, ln2_g); load_row(r_l2b, ln2_b)
        wds = [wd1, wd2]
        stg = [sbt(O_W + 11264 + k * 512, [128], F32) for k in range(8)]
        cnt = 0
        for cbk in range(16):
            ws = wds[cbk % 2]
            P.op("pool", lambda e, ws=ws, cbk=cbk: e.dma_start(out=ws[:], in_=w_down[:, cbk * 128:(cbk + 1) * 128].rearrange("(c p) n -> p c n", p=128)), w=[ws], dma="wd%d" % (cbk % 2))
            for t in range(NTO):
                ps = psn()
                for c in range(NFF):
                    P.op("pe", lambda e, ps=ps, c=c, t=t, ws=ws: e.matmul(ps[:, 0:128], lhsT=achunks[c][:, t * 128:(t + 1) * 128], rhs=ws[:, c, :], start=(c == 0), stop=(c == NFF - 1)), r=[achunks[c], ws], w=[ps])
                sg_ = stg[cnt % 8]
                P.op("dve", lambda e, ps=ps, cbk=cbk, sg_=sg_: e.tensor_tensor(out=sg_[:], in0=ps[:, 0:128], in1=r_g2[:, cbk * 128:(cbk + 1) * 128], op=ALU.mult), r=[ps, r_g2], w=[sg_])
                P.op("sp", lambda e, sg_=sg_, cbk=cbk, t=t: e.dma_start(out=y2_d[x1off + t * 128: x1off + (t + 1) * 128, cbk * 128:(cbk + 1) * 128], in_=sg_[:]), r=[sg_], w=[DR("y2", x1off // 128 + t, x1off // 128 + t + 1)], dma="stg%d" % (cnt % 8))
                cnt += 1
        for t in range(NTO):
            row = x1off // 128 + t
            P.op("sp", lambda e, t=t: e.dma_start(out=x1t[:], in_=x1_d[x1off + t * 128: x1off + (t + 1) * 128, :]), r=[DR("x1", row, row + 1)], w=[x1t], dma="x1t")
            P.op("sp", lambda e, t=t: e.dma_start(out=y2[:], in_=y2_d[x1off + t * 128: x1off + (t + 1) * 128, :]), r=[DR("y2", row, row + 1)], w=[y2], dma="y2t")
            P.op("dve", lambda e: e.scalar_tensor_tensor(out=y2[:], in0=x1t[:], scalar=ALPHA, in1=y2[:], op0=ALU.mult, op1=ALU.add), r=[x1t, y2], w=[y2])
            ln_stats(y2.ap, y2, st, mv, rstd, None, D)
            P.op("dve", lambda e: e.scalar_tensor_tensor(out=y2[:], in0=y2[:], scalar=mv[:, 0:1], in1=r_l2g[:], op0=ALU.subtract, op1=ALU.mult), r=[y2, mv, r_l2g], w=[y2])
            P.op("dve", lambda e: e.scalar_tensor_tensor(out=y2[:], in0=y2[:], scalar=rstd[:], in1=r_l2b[:], op0=ALU.mult, op1=ALU.add), r=[y2, rstd, r_l2b], w=[y2])
            P.op("sp", lambda e, t=t: e.dma_start(out=y_out[t * 128:(t + 1) * 128, :], in_=y2[:]), r=[y2], dma="yout")

    stage_mod(list(range(16)), O_PROJ, O_MIX)
    for gi in GROUPS:
        process_group(gi)
    P.emit()
    es.close()
    return nc


GROUPS = (0, 1)
_CACHE = {}


def kernel(x_prompt, x_sample, state_ret, state_mlstm_C, state_mlstm_n, state_mlstm_m, c, c_ctx,
           w_mod, b_mod, w_in, b_gate, conv_qk, ret_theta, gn_ret, gn_mlstm, w_out,
           ln1_g, ln1_b, w_up, conv_ff, w_down, ln2_g, ln2_b):
    f = lambda a: np.ascontiguousarray(np.asarray(a, dtype=np.float32))
    if "nc" not in _CACHE:
        _CACHE["nc"] = build_program()
    nc = _CACHE["nc"]
    cst = make_consts()
    xp_all = f(x_prompt).reshape(8, 1024, D)
    xs_all = f(x_sample)
    shared = dict(
        w_mod=f(w_mod)[0], b_mod=f(b_mod), w_in=f(w_in)[0], b_gate=f(b_gate),
        conv_qkT=f(f(conv_qk)[0].T.reshape(16, 128, 3).transpose(1, 0, 2).reshape(128, 48)),
        theta=f(ret_theta).reshape(1, 8), gn_ret=f(gn_ret).reshape(1, 1024), gn_ml=f(gn_mlstm).reshape(1, 1024),
        w_out=f(w_out)[0], ln1_g=f(ln1_g), ln1_b=f(ln1_b), w_up=f(w_up)[0],
        conv_ffT=f(f(conv_ff)[0].T.reshape(NFF, 128, 3).transpose(1, 0, 2).reshape(128, NFF * 3)),
        w_down=f(w_down)[0], ln2_g=f(ln2_g), ln2_b=f(ln2_b), cst=cst)
    in_maps = []
    for i in range(8):
        b, p = i // 4, i % 4
        fl = np.zeros((1, 4), np.float32); fl[0, p] = 1.0
        cT = np.stack([f(c_ctx).reshape(16, 128).T, f(c)[b].reshape(16, 128).T], axis=2).reshape(128, 32)
        m = dict(shared)
        m.update(xp=xp_all[i], xs=xs_all[b], xo=f(xs_all[b, p * 512:(p + 1) * 512]), flags=fl, cT=f(cT),
                 st_ret=f(state_ret)[b, 0], st_C=f(state_mlstm_C)[b, 0],
                 st_n=f(f(state_mlstm_n)[b, 0].reshape(2, 4, 2, 128).transpose(0, 1, 3, 2)),
                 st_m=f(state_mlstm_m)[b, 0].reshape(1, 8))
        in_maps.append(m)
    res = run_bass_kernel_spmd(nc, in_maps, core_ids=list(range(8)))
    R = res.results
    y_prompt = np.concatenate([R[i]["yp"] for i in range(8)], 0).reshape(32, 256, D)
    y_sample = np.stack([np.concatenate([R[b * 4 + p]["ys"] for p in range(4)], 0) for b in range(2)], 0)
    n_ret = np.concatenate([R[i]["o_ret"] for i in range(8)], 0)[:, None]
    n_C = np.concatenate([R[i]["o_C"] for i in range(8)], 0)[:, None]
    n_n = np.concatenate([R[i]["o_n"] for i in range(8)], 0).transpose(0, 1, 2, 4, 3).reshape(32, 2, 4, 256)[:, None]
    n_m = np.concatenate([R[i]["o_m"] for i in range(8)], 0).reshape(32, 2, 4)[:, None]
    return (y_prompt, y_sample, np.ascontiguousarray(n_ret), np.ascontiguousarray(n_C),
            np.ascontiguousarray(n_n), np.ascontiguousarray(n_m))
```

```python
import contextlib
import numpy as np
import concourse.bass as bass
import concourse.mybir as mybir
from concourse.bass_utils import run_bass_kernel_spmd

F32 = mybir.dt.float32
BF16 = mybir.dt.bfloat16
ALU = mybir.AluOpType
AF = mybir.ActivationFunctionType
AX = mybir.AxisListType

D = 2048
KC = 16
HD = 256
NH = 4
DFF = 5504
NFF = 43
NIN = 8208
ALPHA = 2.0 ** 0.25
BIG = 1.0e9
SAME_ENGINE_SYNC = True
DEBUG_IDS = set()
GRAN = 256


class Rg:
    __slots__ = ("space", "lo", "hi")

    def __init__(self, space, lo, hi):
        self.space, self.lo, self.hi = space, lo, hi


class Tl:
    def __init__(self, ap, space, lo, hi):
        self.ap, self.space, self.lo, self.hi = ap, space, lo, hi

    def __getitem__(self, k):
        return self.ap[k]

    @property
    def r(self):
        return Rg(self.space, self.lo, self.hi)

    def part(self, i, n, cnt=1):
        sz = (self.hi - self.lo) // n
        return Rg(self.space, self.lo + i * sz, self.lo + (i + cnt) * sz)


def _rg(x):
    return x.r if isinstance(x, Tl) else x


class _Rec:
    def __getattr__(self, name):
        def f(*a, **k):
            self.call = (name, a, k)
            return self
        return f


class Prog:
    ENGS = ("pe", "act", "dve", "pool", "sp")

    def __init__(self, nc):
        self.nc = nc
        self.ops = []
        self.st = {}

    def _cells(self, rg):
        if rg.space == "sb":
            return [("sb", g) for g in range(rg.lo // GRAN, (rg.hi + GRAN - 1) // GRAN)]
        return [(rg.space, g) for g in range(rg.lo, rg.hi)]

    def op(self, eng, fn, r=(), w=(), dma=None):
        oid = len(self.ops)
        deps = set()
        rc = [c for x in r for c in self._cells(_rg(x))]
        wc = [c for x in w for c in self._cells(_rg(x))]
        st = self.st
        for c in rc:
            s = st.get(c)
            if s is not None:
                if s[0] is not None:
                    deps.add(s[0])
                if c[0] == "ps":
                    for r_ in s[1]:
                        if self.ops[r_]["eng"] != eng:
                            deps.add(r_)
        for c in wc:
            s = st.get(c)
            if s is not None:
                if s[0] is not None:
                    deps.add(s[0])
                deps.update(s[1])
        for c in rc:
            s = st.get(c)
            if s is None:
                st[c] = [None, [oid]]
            else:
                s[1].append(oid)
        for c in wc:
            st[c] = [oid, []]
        deps.discard(oid)
        rec = _Rec()
        fn(rec)
        self.ops.append(dict(eng=eng, call=rec.call, deps=deps, dma=dma, sig=False, sigidx=None))
        return oid

    def emit(self):
        nc = self.nc
        ops = self.ops

        def dom(o):
            return ("dma", o["dma"]) if o["dma"] is not None else ("eng", o["eng"])

        def skip(p, engname):
            return p["dma"] is None and p["eng"] == engname and (engname == "pe" or not SAME_ENGINE_SYNC)

        for o in ops:
            for d in o["deps"]:
                p = ops[d]
                if skip(p, o["eng"]):
                    continue
                p["sig"] = True
            if o["dma"] is not None:
                o["sig"] = True
        counters = {}
        for o in ops:
            if o["sig"]:
                dm = dom(o)
                counters[dm] = counters.get(dm, 0) + 1
                o["sigidx"] = counters[dm]
        with contextlib.ExitStack() as es:
            sems = {}
            for i, dm in enumerate(counters.keys()):
                sems[dm] = es.enter_context(nc.semaphore("s%d" % i))
            block = es.enter_context(nc.Block())
            streams = {e: [] for e in self.ENGS}
            for o in ops:
                streams[o["eng"]].append(o)

            def run(engname, eng):
                waited = {}
                for o in streams[engname]:
                    need = {}
                    for d in o["deps"]:
                        p = ops[d]
                        if not p["sig"] or skip(p, engname):
                            continue
                        dm = dom(p)
                        if p["sigidx"] > need.get(dm, 0):
                            need[dm] = p["sigidx"]
                    for dm, v in need.items():
                        if waited.get(dm, 0) >= v:
                            continue
                        waited[dm] = v
                        eng.wait_ge(sems[dm], v * (16 if dm[0] == "dma" else 1))
                    nm, a_, k_ = o["call"]
                    ins = getattr(eng, nm)(*a_, **k_)
                    if DEBUG_IDS:
                        try:
                            iname = str(ins.ins.name) if hasattr(ins, "ins") else str(getattr(ins, "name", ""))
                        except Exception:
                            iname = "?"
                        if iname in DEBUG_IDS:
                            print("DEBUGID", iname, engname, nm, {kk: (getattr(vv, "ap", None), getattr(vv, "dtype", None)) if hasattr(vv, "ap") else vv for kk, vv in k_.items()})
                    if o["sig"]:
                        ins.then_inc(sems[dom(o)], 16 if o["dma"] is not None else 1)
                if engname == "sp":
                    for dm, cnt in counters.items():
                        if dm[0] == "dma":
                            eng.wait_ge(sems[dm], cnt * 16)

            block.tensor(lambda e: run("pe", e))
            block.scalar(lambda e: run("act", e))
            block.vector(lambda e: run("dve", e))
            block.gpsimd(lambda e: run("pool", e))
            block.sync(lambda e: run("sp", e))


def make_consts():
    p = np.arange(128)
    j = p[:, None].astype(np.float64)
    i = p[None, :].astype(np.float64)
    D1 = np.maximum(i - j, 0)
    U1 = (i >= j).astype(np.float64)
    D2 = np.maximum(j - i, 0)
    U2 = (j >= i).astype(np.float64)
    BIGF = BIG * (1 - U1) + np.log(16.0)
    BIGB = BIG * (1 - U2) + np.log(16.0)
    NEGF = -BIG * (1 - U2)
    NEGB = -BIG * (1 - U1)
    vec = np.stack([p + 1.0, 128.0 - p, 127.0 - p, p * 1.0], axis=1)
    inv = 10000.0 ** (-np.arange(64, dtype=np.float32) / 64.0)
    f = p % 64
    sgn = np.where(p < 64, 1.0, -1.0)
    rows = np.arange(32, dtype=np.float32)
    cols = np.arange(64, dtype=np.float32)
    angR = (rows[None, :] * inv[f][:, None]).astype(np.float32)
    angC = (cols[None, :] * inv[f][:, None]).astype(np.float32)
    cosR, sinR = np.cos(angR), np.sin(angR) * sgn[:, None]
    cosC, sinC = np.cos(angC), np.sin(angC) * sgn[:, None]
    rope = np.concatenate([cosR, sinR, cosC, sinC], axis=1)
    ropeK = rope / 16.0
    cst = np.concatenate([D1, U1, D2, U2, BIGF, BIGB, NEGF, NEGB, vec, rope, ropeK, np.eye(128), np.ones((128, 128))], axis=1)
    return np.ascontiguousarray(cst.astype(np.float32))


C_D1, C_U1, C_D2, C_U2, C_BIGF, C_BIGB, C_NEGF, C_NEGB = [k * 128 for k in range(8)]
C_VEC = 1024
C_ROPE = 1028
C_ROPEK = 1028 + 192
C_ID = 1028 + 384
C_ONES = C_ID + 128
NCST = C_ONES + 128

O_CONST = 0
SZ_CONST = 13312
O_HT = O_CONST + SZ_CONST
SZ_HT = 65536
O_W = O_HT + SZ_HT
SZ_W = 16384
O_PROJ = O_W + SZ_W
SZ_PROJ = 43008
O_MIXO = O_PROJ + SZ_PROJ
SZ_MIXO = 32768
O_MIX = O_MIXO + SZ_MIXO
SZ_MIX = 41728
ARENA = O_MIX + SZ_MIX


def build_program():
    nc = bass.Bass("TRN2", target_bir_lowering=False)

    def din(name, shape):
        return nc.dram_tensor(name, list(shape), F32, kind="ExternalInput").ap()

    def dout(name, shape):
        return nc.dram_tensor(name, list(shape), F32, kind="ExternalOutput").ap()

    xp = din("xp", [1024, D]); xs = din("xs", [2048, D]); xo = din("xo", [512, D])
    flags = din("flags", [1, 4]); cT = din("cT", [128, 32])
    st_ret = din("st_ret", [2, 4, 256, 256]); st_C = din("st_C", [2, 4, 256, 256])
    st_n = din("st_n", [2, 4, 128, 2]); st_m = din("st_m", [1, 8])
    w_mod = din("w_mod", [D, 6 * D]); b_mod = din("b_mod", [1, 6 * D]); w_in = din("w_in", [D, NIN])
    b_gate = din("b_gate", [1, 16]); conv_qkT = din("conv_qkT", [128, 48]); theta = din("theta", [1, 8])
    gn_ret = din("gn_ret", [1, 1024]); gn_ml = din("gn_ml", [1, 1024]); w_out = din("w_out", [D, D])
    ln1_g = din("ln1_g", [1, D]); ln1_b = din("ln1_b", [1, D]); w_up = din("w_up", [D, 2 * DFF])
    conv_ffT = din("conv_ffT", [128, NFF * 3]); w_down = din("w_down", [DFF, D])
    ln2_g = din("ln2_g", [1, D]); ln2_b = din("ln2_b", [1, D]); cst_d = din("cst", [128, NCST])
    yp = dout("yp", [1024, D]); ys = dout("ys", [512, D])
    o_ret = dout("o_ret", [4, 2, 4, 256, 256]); o_C = dout("o_C", [4, 2, 4, 256, 256])
    o_n = dout("o_n", [4, 2, 4, 128, 2]); o_m = dout("o_m", [4, 8])
    mod_d = nc.dram_tensor("mod_d", [2, 6 * D], F32).ap()
    x1_d = nc.dram_tensor("x1_d", [1536, D], F32).ap()
    y2_d = nc.dram_tensor("y2_d", [1536, D], F32).ap()

    P = Prog(nc)
    es = contextlib.ExitStack()
    A = es.enter_context(nc.sbuf_tensor("arena", [128, ARENA // 2], BF16))
    psf = [es.enter_context(nc.psum_tensor("psf%d" % i, [128, 512], F32)) for i in range(6)]
    psb = [es.enter_context(nc.psum_tensor("psb%d" % i, [128, 1024], BF16)) for i in range(2)]
    PSF = [Tl(psf[i][:], "ps", i, i + 1) for i in range(6)]
    PSB = [Tl(psb[i][:], "ps", 6 + i, 7 + i) for i in range(2)]
    rot = {"f": 0, "b": 0}

    held = set()

    def psn():
        for _ in range(6):
            rot["f"] = (rot["f"] + 1) % 6
            if rot["f"] not in held:
                return PSF[rot["f"]]
        raise AssertionError("no free psum bank")

    def psh():
        ps = psn()
        held.add(ps.lo)
        return ps

    def prel(ps):
        held.discard(ps.lo)

    def navail():
        return 6 - len(held)

    heldb = set()

    def psbn():
        for _ in range(2):
            rot["b"] = (rot["b"] + 1) % 2
            if rot["b"] not in heldb:
                return PSB[rot["b"]]
        raise AssertionError("no free bf16 psum bank")

    def run_pipe(facts, width=2):
        free = list(range(width)); active = []; i = 0
        while i < len(facts) or active:
            while free and i < len(facts):
                k = free.pop(0)
                active.append((facts[i](k), k))
                i += 1
            nxt = []
            for g, k in active:
                try:
                    next(g)
                    nxt.append((g, k))
                except StopIteration:
                    free.append(k)
            active = nxt

    def sbt(off, shape, dt, parts=128):
        esz = 4 if dt == F32 else 2
        n = 1
        for s in shape:
            n *= s
        nb = n * esz
        assert off % 4 == 0
        ap = A[0:parts, off // 2:(off + nb) // 2]
        if dt != BF16:
            ap = ap.bitcast(dt)
        if len(shape) == 2:
            ap = ap.rearrange("p (a b) -> p a b", a=shape[0])
        elif len(shape) == 3:
            ap = ap.rearrange("p (a b c) -> p a b c", a=shape[0], b=shape[1])
        return Tl(ap, "sb", off, off + nb)

    class Bump:
        def __init__(self, off, size):
            self.off, self.end = off, off + size

        def __call__(self, shape, dt, parts=128, align=GRAN):
            esz = 4 if dt == F32 else 2
            n = 1
            for s in shape:
                n *= s
            self.off = (self.off + align - 1) // align * align
            t = sbt(self.off, shape, dt, parts)
            self.off += n * esz
            assert self.off <= self.end, ("arena overflow", self.off, self.end)
            return t

    def DR(name, lo=0, hi=1):
        return Rg("dr_" + name, lo, hi)

    cb = Bump(O_CONST, SZ_CONST)
    CST = cb([NCST], F32)
    IDB = cb([128], BF16)
    LG = cb([8], F32)
    FLG = cb([4], F32)
    BG = cb([16], F32)
    GN = cb([256], F32)
    DEC = cb([8], F32)
    RM = cb([128], F32)
    RMT = cb([128], F32)
    SILC = cb([16, 2], BF16)
    WGT = cb([16, 16], BF16)
    CQK = cb([48], F32)
    CFF = cb([NFF * 3], F32)
    ST_C = cb([4, 6], F32); MV_C = cb([2], F32, align=4); RSTD_C = cb([1], F32, align=4); NMR_C = cb([1], F32, align=4)
    IDF = CST.ap[:, C_ID:C_ID + 128]
    ONESF = CST.ap[:, C_ONES:C_ONES + 128]

    def cstm(c0):
        return CST.ap[:, c0:c0 + 128]

    P.op("sp", lambda e: e.dma_start(out=CST[:], in_=cst_d[:, :]), w=[CST], dma="cst")
    P.op("sp", lambda e: e.dma_start(out=LG[:], in_=theta[0:1, :].broadcast_to([128, 8])), w=[LG], dma="lg")
    P.op("sp", lambda e: e.dma_start(out=FLG[:], in_=flags[0:1, :].broadcast_to([128, 4])), w=[FLG], dma="flg")
    P.op("sp", lambda e: e.dma_start(out=BG[:], in_=b_gate[0:1, :].broadcast_to([128, 16])), w=[BG], dma="bg")
    P.op("sp", lambda e: e.dma_start(out=CQK[:], in_=conv_qkT[:, :]), w=[CQK], dma="cqk")
    P.op("sp", lambda e: e.dma_start(out=CFF[:], in_=conv_ffT[:, :]), w=[CFF], dma="cff")
    P.op("pool", lambda e: e.dma_start(out=WGT[:], in_=w_in[:, 8192:8208].rearrange("(kc p) n -> p kc n", p=128)), w=[WGT], dma="wgt")
    P.op("dve", lambda e: e.tensor_copy(out=IDB[:], in_=IDF), r=[CST], w=[IDB])
    P.op("act", lambda e: e.activation(out=LG[:], in_=LG[:], func=AF.Exp, scale=-1.0), r=[LG], w=[LG])
    P.op("act", lambda e: e.activation(out=LG[:], in_=LG[:], func=AF.Ln, bias=1.0), r=[LG], w=[LG])
    P.op("dve", lambda e: e.tensor_scalar(out=LG[:], in0=LG[:], scalar1=-1.0, scalar2=None, op0=ALU.mult), r=[LG], w=[LG])

    mod_state = {"init": False}

    def stage_mod(blks, stage_off, small_off):
        ct = sbt(small_off, [32], F32)
        brow = [sbt(small_off + 256 + k * 1024, [256], F32, parts=2) for k in range(2)]
        mrow = [sbt(small_off + 256 + 2048 + k * 1024, [256], F32, parts=2) for k in range(2)]
        if not mod_state["init"]:
            mod_state["init"] = True
            P.op("sp", lambda e: e.dma_start(out=ct[:], in_=cT[:, :]), w=[ct], dma="ct")
            P.op("act", lambda e: e.activation(out=SILC[:].rearrange("p a b -> p (a b)"), in_=ct[:], func=AF.Silu), r=[ct], w=[SILC])
        wslots = [sbt(stage_off + k * 8192, [16, 256], BF16) for k in range(2)]
        for i, blk in enumerate(blks):
            ws = wslots[i % 2]; br = brow[i % 2]; mr = mrow[i % 2]
            c0 = blk * 256
            P.op("pool", lambda e, ws=ws, c0=c0: e.dma_start(out=ws[:], in_=w_mod[:, c0:c0 + 256].rearrange("(kc p) n -> p kc n", p=128)), w=[ws], dma="wm%d_%d" % (stage_off, i % 2))
            P.op("sp", lambda e, br=br, c0=c0: e.dma_start(out=br[:], in_=b_mod[0:1, c0:c0 + 256].broadcast_to([2, 256])), w=[br], dma="brow%d_%d" % (small_off, i % 2))
            ps = psn()
            for kc in range(KC):
                P.op("pe", lambda e, ps=ps, ws=ws, kc=kc: e.matmul(ps[0:2, 0:256], lhsT=SILC[:, kc, :], rhs=ws[:, kc, :], start=(kc == 0), stop=(kc == KC - 1)), r=[SILC, ws], w=[ps])
            P.op("dve", lambda e, ps=ps, mr=mr, br=br: e.tensor_tensor(out=mr[:], in0=ps[0:2, 0:256], in1=br[:], op=ALU.add), r=[ps, br], w=[mr])
            P.op("sp", lambda e, mr=mr, c0=c0: e.dma_start(out=mod_d[:, c0:c0 + 256], in_=mr[:]), r=[mr], w=[DR("mod", blk // 8, blk // 8 + 1)], dma="mrow%d_%d" % (small_off, i % 2))

    MOD_STAGE = O_HT + 32768
    MOD_SMALL = O_PROJ + 36864

    def mod_dma(blks):
        for i, blk in enumerate(blks):
            ws = sbt(MOD_STAGE + i * 8192, [16, 256], BF16)
            c0 = blk * 256
            P.op("pool", lambda e, ws=ws, c0=c0: e.dma_start(out=ws[:], in_=w_mod[:, c0:c0 + 256].rearrange("(kc p) n -> p kc n", p=128)), w=[ws], dma="wmr%d" % i)

    def mod_compute(blks):
        brow = [sbt(MOD_SMALL + k * 1024, [256], F32, parts=2) for k in range(2)]
        mrow = [sbt(MOD_SMALL + 2048 + k * 1024, [256], F32, parts=2) for k in range(2)]
        for i, blk in enumerate(blks):
            ws = sbt(MOD_STAGE + i * 8192, [16, 256], BF16)
            br = brow[i % 2]; mr = mrow[i % 2]
            c0 = blk * 256
            P.op("sp", lambda e, br=br, c0=c0: e.dma_start(out=br[:], in_=b_mod[0:1, c0:c0 + 256].broadcast_to([2, 256])), w=[br], dma="browr%d" % (i % 2))
            ps = psn()
            for kc in range(KC):
                P.op("pe", lambda e, ps=ps, ws=ws, kc=kc: e.matmul(ps[0:2, 0:256], lhsT=SILC[:, kc, :], rhs=ws[:, kc, :], start=(kc == 0), stop=(kc == KC - 1)), r=[SILC, ws], w=[ps])
            P.op("dve", lambda e, ps=ps, mr=mr, br=br: e.tensor_tensor(out=mr[:], in0=ps[0:2, 0:256], in1=br[:], op=ALU.add), r=[ps, br], w=[mr])
            P.op("sp", lambda e, mr=mr, c0=c0: e.dma_start(out=mod_d[:, c0:c0 + 256], in_=mr[:]), r=[mr], w=[DR("mod", blk // 8, blk // 8 + 1)], dma="mrowr%d" % (i % 2))

    def load_modrow(dst, q, g, plus1=False):
        P.op("sp", lambda e: e.dma_start(out=dst[:], in_=mod_d[g:g + 1, q * D:(q + 1) * D].broadcast_to([128, D])), r=[DR("mod", q, q + 1)], w=[dst], dma="mr_%d" % (dst.lo))
        if plus1:
            P.op("dve", lambda e: e.tensor_scalar(out=dst[:], in0=dst[:], scalar1=1.0, scalar2=None, op0=ALU.add), r=[dst], w=[dst])

    def load_row(dst, src):
        P.op("sp", lambda e: e.dma_start(out=dst[:], in_=src[0:1, :].broadcast_to([128, src.shape[1]])), w=[dst], dma="lr_%d" % (dst.lo))

    def ln_stats(x_ap, xr, st, mv, rstd, nmr, n):
        nch = max(1, n // 512)
        w_ = n // nch
        for c in range(nch):
            P.op("dve", lambda e, c=c: e.bn_stats(out=st[:, c, :], in_=x_ap[:, c * w_:(c + 1) * w_]), r=[xr], w=[st])
        P.op("dve", lambda e: e.bn_aggr(out=mv[:], in_=st[:, 0:nch, :].rearrange("p a b -> p (a b)")), r=[st], w=[mv])
        P.op("act", lambda e: e.activation(out=rstd[:], in_=mv[:, 1:2], func=AF.Ln, bias=1e-6), r=[mv], w=[rstd])
        P.op("act", lambda e: e.activation(out=rstd[:], in_=rstd[:], func=AF.Exp, scale=-0.5), r=[rstd], w=[rstd])
        if nmr is not None:
            P.op("dve", lambda e: e.scalar_tensor_tensor(out=nmr[:], in0=mv[:, 0:1], scalar=-1.0, in1=rstd[:], op0=ALU.mult, op1=ALU.mult), r=[mv, rstd], w=[nmr])

    def transpose_tile(src, src_r, dstT, dst_r, tcol):
        for _ in transpose_tile_g(src, src_r, dstT, dst_r, tcol):
            pass

    def transpose_tile_g(src, src_r, dstT, dst_r_, tcol):
        for half in range(2):
            while len(heldb) >= 2:
                yield
            pb = psbn()
            heldb.add(pb.lo - 6)
            dst_r = dstT.part(half, 2)
            for k in range(8):
                kc = half * 8 + k
                P.op("pe", lambda e, pb=pb, k=k, kc=kc: e.transpose(out=pb[:, k * 128:(k + 1) * 128], in_=src[:, kc * 128:(kc + 1) * 128], identity=IDB[:]), r=[src_r, IDB], w=[pb])
            eng = "act"
            if eng == "act":
                P.op("act", lambda e, pb=pb, half=half: e.activation(out=dstT[:, half * 8:half * 8 + 8, tcol:tcol + 128], in_=pb[:].rearrange("p (a b) -> p a b", a=8), func=AF.Copy), r=[pb], w=[dst_r])
            else:
                P.op("dve", lambda e, pb=pb, half=half: e.tensor_copy(out=dstT[:, half * 8:half * 8 + 8, tcol:tcol + 128], in_=pb[:].rearrange("p (a b) -> p a b", a=8)), r=[pb], w=[dst_r])
            heldb.discard(pb.lo - 6)
            yield

    def process_group(gi):
        sample = gi == 1
        T = 2048 if sample else 1024
        nseq = 1 if sample else 4
        n = 16 if sample else 2
        NT = T // 128
        Town = 512 if sample else 1024
        NTO = Town // 128
        x_all = xs if sample else xp
        x_own = xo if sample else xp
        y_out = ys if sample else yp
        x1off = 1024 if sample else 0
        mg = 1 if sample else 0
        Lc = 64 if sample else 256
        HT = sbt(O_HT, [16, T], BF16)

        mb = Bump(O_MIX, SZ_MIX)
        r_sc = mb([D], F32); r_sh = mb([D], F32)
        sm_ = [mb([64], F32) for _ in range(2)]
        load_modrow(r_sh, 0, mg)
        load_modrow(r_sc, 1, mg, plus1=True)
        pbm = Bump(O_PROJ, SZ_PROJ)
        xts = [pbm([D], F32) for _ in range(2)]
        hts = [pbm([D], BF16) for _ in range(2)]

        def small4(tl):
            mk = lambda off, shape: Tl(sbt(tl.lo + off * 4, shape, F32).ap, "sb", tl.lo, tl.hi)
            return mk(0, [4, 6]), mk(24, [2]), mk(26, [1]), mk(27, [1])

        def a1_tile(t):
            def g(k):
                xt = xts[k]; ht = hts[k]
                st, mv, rstd, nmr = small4(sm_[k])
                P.op("sp", lambda e: e.dma_start(out=xt[:], in_=x_all[t * 128:(t + 1) * 128, :]), w=[xt], dma="xt%d" % k)
                yield
                ln_stats(xt.ap, xt, st, mv, rstd, None, D)
                yield
                P.op("dve", lambda e: e.scalar_tensor_tensor(out=xt[:], in0=xt[:], scalar=mv[:, 0:1], in1=r_sc[:], op0=ALU.subtract, op1=ALU.mult), r=[xt, mv, r_sc], w=[xt])
                yield
                P.op("dve", lambda e: e.scalar_tensor_tensor(out=ht[:], in0=xt[:], scalar=rstd[:], in1=r_sh[:], op0=ALU.mult, op1=ALU.add), r=[xt, rstd, r_sh], w=[ht])
                yield
                yield from transpose_tile_g(ht.ap, ht, HT, HT, t * 128)
            return g
        run_pipe([a1_tile(t) for t in range(NT)], 2)

        pbm = Bump(O_PROJ, SZ_PROJ)
        QT = pbm([2, T], BF16)
        KT = pbm([2, T], BF16)
        VG = pbm([NT, 520], BF16)
        KTM = pbm([NT, 256], BF16)
        mb = Bump(O_MIX, SZ_MIX)
        NSL = 1 if sample else 2
        MST = [[mb([2, 257], F32) for d in range(2)] for sl in range(NSL)]
        KX = [[mb([256], BF16) for d in range(2)] for sl in range(NSL)]
        SALL = []
        for sl in range(NSL):
            per_d = []
            for d in range(2):
                if sample and d == 0:
                    first = mb([2, 257], BF16)
                    rest = sbt(O_MIXO + 16384, [n - 1, 2, 257], BF16)
                    per_d.append([first] + [Tl(rest[:, c], "sb", rest.lo + c * 1028, rest.lo + (c + 1) * 1028) for c in range(n - 1)])
                else:
                    arr = mb([n, 2, 257], BF16)
                    per_d.append([Tl(arr[:, c], "sb", arr.lo + c * 1028, arr.lo + (c + 1) * 1028) for c in range(n)])
            SALL.append(per_d)
        Dm = mb([4, 128], F32)
        Am = mb([4, 128], F32)
        tmpc = [sbt(Dm.lo, [512], F32), sbt(Am.lo, [512], F32)]
        ATT = [sbt(Am.lo + k * 512, [2, 128], BF16) for k in range(2)]
        WTS = [sbt(Am.lo + 1024 + k * 512, [128], F32) for k in range(2)]
        DMS = [sbt(Dm.lo + k * 1024, [2, 128], F32) for k in range(2)]
        TOTT = [mb([2, 257], F32) for k in range(2)]
        TOT = [[Tl(TOTT[k][:, d, :], "sb", TOTT[k].lo + d * 1028, TOTT[k].lo + (d + 1) * 1028) for d in range(2)] for k in range(2)]
        XN = [mb([256], F32) for k in range(2)]
        SMALL = [mb([64], F32) for k in range(2)]
        GTS = mb([NT, 16], F32)
        LFN = mb([NT, 16], F32)
        GA = mb([2, 16, 4], F32); GNB = mb([2, 16, 4], F32); GMU = mb([2, 16, 4], F32); GSP = mb([2, 16, 4], F32)
        GWK = mb([2, 16, 4], F32); GSC = mb([2, 16, 4], F32); GEM = mb([2, 16, 4], F32); GBL = mb([2, 16, 4], F32)
        mprev = mb([8], F32); mnew = mb([8], F32, align=4); v8 = [mb([8], F32, align=4) for _ in range(4)]
        MIXO = sbt(O_MIXO, [NTO, D], BF16)
        wsl = [sbt(O_W + k * 8192, [16, 256], BF16) for k in range(2)]
        wcount = [0]

        def load_w(c0):
            ws = wsl[wcount[0] % 2]
            k = wcount[0] % 2
            wcount[0] += 1
            P.op("pool", lambda e: e.dma_start(out=ws[:], in_=w_in[:, c0:c0 + 256].rearrange("(kc p) n -> p kc n", p=128)), w=[ws], dma="wsl%d" % k)
            return ws

        P.op("dve", lambda e: e.memset(VG[:, :, 256:257], 1.0), w=[VG])

        def proj_fm(ws, dst, kind, chan0):
            for dc in range(2):
                for tb in range(T // 512):
                    ps = psn()
                    for kc in range(KC):
                        P.op("pe", lambda e, ps=ps, kc=kc, dc=dc, tb=tb: e.matmul(ps[:], lhsT=ws[:, kc, dc * 128:(dc + 1) * 128], rhs=HT[:, kc, tb * 512:(tb + 1) * 512], start=(kc == 0), stop=(kc == KC - 1)), r=[ws, HT], w=[ps])
                    dsl = dst[:, dc, tb * 512:(tb + 1) * 512]
                    if kind in ("rq", "rk"):
                        if not sample:
                            sc_ = 1.0 if kind == "rq" else 1.0 / 16.0
                            P.op("act", lambda e, ps=ps, dsl=dsl, sc_=sc_: e.activation(out=dsl, in_=ps[:], func=AF.Identity, scale=sc_), r=[ps], w=[dst])
                        else:
                            base = C_ROPE if kind == "rq" else C_ROPEK
                            if dc == 0:
                                cosb = CST.ap[:, base + 8 * tb: base + 8 * tb + 8].unsqueeze(2).to_broadcast([128, 8, 64])
                                sinb = CST.ap[:, base + 32 + 8 * tb: base + 32 + 8 * tb + 8].unsqueeze(2).to_broadcast([128, 8, 64])
                            else:
                                cosb = CST.ap[:, base + 64: base + 128].unsqueeze(1).to_broadcast([128, 8, 64])
                                sinb = CST.ap[:, base + 128: base + 192].unsqueeze(1).to_broadcast([128, 8, 64])
                            t1 = tmpc[0]; t2 = tmpc[1]
                            v3 = lambda ap: ap.rearrange("p (a b) -> p a b", a=8)
                            P.op("dve", lambda e, ps=ps, cosb=cosb: e.tensor_tensor(out=v3(t1[:]), in0=v3(ps[:]), in1=cosb, op=ALU.mult), r=[ps, CST], w=[t1])
                            P.op("dve", lambda e, ps=ps, sinb=sinb: e.tensor_tensor(out=v3(t2[:])[0:64], in0=v3(ps[:])[64:128], in1=sinb[64:128], op=ALU.mult), r=[ps, CST], w=[t2])
                            P.op("dve", lambda e, ps=ps, sinb=sinb: e.tensor_tensor(out=v3(t2[:])[64:128], in0=v3(ps[:])[0:64], in1=sinb[0:64], op=ALU.mult), r=[ps, CST], w=[t2])
                            P.op("dve", lambda e, dsl=dsl: e.tensor_tensor(out=dsl, in0=t1[:], in1=t2[:], op=ALU.add), r=[t1, t2], w=[dst])
                    else:
                        ch = chan0 // 128 + dc
                        w0 = CQK.ap[:, ch * 3 + 0: ch * 3 + 1]; w1 = CQK.ap[:, ch * 3 + 1: ch * 3 + 2]; w2 = CQK.ap[:, ch * 3 + 2: ch * 3 + 3]
                        z = tmpc[0]
                        nb_ = 512 // Lc
                        v3 = lambda ap: ap.rearrange("p (a b) -> p a b", a=nb_)
                        P.op("dve", lambda e, ps=ps, w1=w1: e.tensor_scalar(out=z[:], in0=ps[:], scalar1=w1, scalar2=None, op0=ALU.mult), r=[ps, CQK], w=[z])
                        P.op("dve", lambda e, ps=ps, w0=w0: e.scalar_tensor_tensor(out=v3(z[:])[:, :, 1:Lc], in0=v3(ps[:])[:, :, 0:Lc - 1], scalar=w0, in1=v3(z[:])[:, :, 1:Lc], op0=ALU.mult, op1=ALU.add), r=[ps, CQK, z], w=[z])
                        P.op("dve", lambda e, ps=ps, w2=w2: e.scalar_tensor_tensor(out=v3(z[:])[:, :, 0:Lc - 1], in0=v3(ps[:])[:, :, 1:Lc], scalar=w2, in1=v3(z[:])[:, :, 0:Lc - 1], op0=ALU.mult, op1=ALU.add), r=[ps, CQK, z], w=[z])
                        P.op("act", lambda e, dsl=dsl: e.activation(out=dsl, in_=z[:], func=AF.Silu), r=[z], w=[dst])

        def proj_tm(wv, wg, gfunc):
            for t in range(NT):
                ps = psn()
                for j, ws in enumerate((wv, wg)):
                    for kc in range(KC):
                        P.op("pe", lambda e, ps=ps, kc=kc, j=j, ws=ws, t=t: e.matmul(ps[:, j * 256:(j + 1) * 256], lhsT=HT[:, kc, t * 128:(t + 1) * 128], rhs=ws[:, kc, :], start=(kc == 0), stop=(kc == KC - 1)), r=[ws, HT], w=[ps])
                P.op("act", lambda e, ps=ps, t=t: e.activation(out=VG[:, t, 0:256], in_=ps[:, 0:256], func=AF.Copy), r=[ps], w=[VG.part(t, NT)])
                P.op("act", lambda e, ps=ps, t=t: e.activation(out=VG[:, t, 264:520], in_=ps[:, 256:512], func=gfunc), r=[ps], w=[VG.part(t, NT)])

        def make_ktm(kscale=1.0):
            for t in range(NT):
                pb = psbn()
                for dc in range(2):
                    P.op("pe", lambda e, pb=pb, dc=dc, t=t: e.transpose(out=pb[:, dc * 128:(dc + 1) * 128], in_=KT[:, dc, t * 128:(t + 1) * 128], identity=IDB[:]), r=[KT, IDB], w=[pb])
                P.op("act", lambda e, pb=pb, t=t: e.activation(out=KTM[:, t, :], in_=pb[:, 0:256], func=AF.Identity, scale=kscale), r=[pb], w=[KTM.part(t, NT)])

        def smalls(k):
            o = SMALL[k].lo
            mk = lambda off, shape: Tl(sbt(o + off * 4, shape, F32).ap, "sb", SMALL[k].lo, SMALL[k].hi)
            return mk(0, [4, 6]), mk(24, [2]), mk(26, [1]), mk(27, [1]), mk(28, [2])

        def out_tail(k, t, hh):
            xn = XN[k]
            st, mv, rstd, nmr, _ = smalls(k)
            ln_stats(xn.ap, xn, st, mv, rstd, None, 256)
            yield
            P.op("dve", lambda e: e.scalar_tensor_tensor(out=xn[:], in0=xn[:], scalar=mv[:, 0:1], in1=GN[:], op0=ALU.subtract, op1=ALU.mult), r=[xn, mv, GN], w=[xn])
            yield
            col = hh * 256
            gate = VG[:, t, 264:520]
            gr = VG.part(t, NT)
            if not sample:
                P.op("dve", lambda e: e.scalar_tensor_tensor(out=MIXO[:, t, col:col + 256], in0=xn[:], scalar=rstd[:], in1=gate, op0=ALU.mult, op1=ALU.mult), r=[xn, rstd, gr], w=[MIXO.part(t, NTO)])
            else:
                pp, tp = t // 4, t % 4
                P.op("dve", lambda e: e.scalar_tensor_tensor(out=xn[:], in0=xn[:], scalar=rstd[:], in1=gate, op0=ALU.mult, op1=ALU.mult), r=[xn, rstd, gr], w=[xn])
                yield
                if pp == 0:
                    P.op("dve", lambda e: e.tensor_scalar(out=MIXO[:, tp, col:col + 256], in0=xn[:], scalar1=FLG[:, 0:1], scalar2=None, op0=ALU.mult), r=[xn, FLG], w=[MIXO.part(tp, NTO)])
                else:
                    P.op("dve", lambda e: e.scalar_tensor_tensor(out=MIXO[:, tp, col:col + 256], in0=xn[:], scalar=FLG[:, pp:pp + 1], in1=MIXO[:, tp, col:col + 256], op0=ALU.mult, op1=ALU.add), r=[xn, FLG, MIXO.part(tp, NTO)], w=[MIXO.part(tp, NTO)])
            yield

        def run_rr(gens):
            gens = list(gens)
            while gens:
                nxt = []
                for g in gens:
                    try:
                        next(g)
                        nxt.append(g)
                    except StopIteration:
                        pass
                gens = nxt

        def state_chain(sl, s, d, h, kind):
            ret = kind == "ret"
            ncol = 256 if ret else 257
            SM = MST[sl][d]; Kx = KX[sl][d]
            src4 = st_ret if ret else st_C
            srcn = None if ret else st_n
            if not sample:
                P.op("dve", lambda e: e.memset(SM[:], 0.0), w=[SM])
            else:
                P.op("sp", lambda e: e.dma_start(out=SM[:, :, 0:256], in_=src4[d, h].rearrange("(kc p) v -> p kc v", p=128)), w=[SM], dma="sm%d" % SM.lo)
                if srcn is not None:
                    P.op("sp", lambda e: e.dma_start(out=SM[:, :, 256:257], in_=srcn[d, h].unsqueeze(2), allow_slow_non_contiguous=True), w=[SM], dma="sm%d" % SM.lo)
            yield
            order = range(n) if d == 0 else range(n - 1, -1, -1)
            for c in order:
                t = s * n + c
                sa = SALL[sl][d][c]
                P.op("act", lambda e, sa=sa: e.activation(out=sa[:, :, 0:ncol], in_=SM[:, :, 0:ncol], func=AF.Copy), r=[SM], w=[sa])
                if ret:
                    sc_ap, sc_r = DEC.ap[:, 2 + d:3 + d], DEC
                    dk_ap, dk_r = DEC.ap[:, 4 + d:5 + d], DEC
                else:
                    sc_ap, sc_r = GWK.ap[:, d, t, h:h + 1], GWK.part(d, 2)
                    dk_ap, dk_r = GSC.ap[:, d, t, h:h + 1], GSC.part(d, 2)
                P.op("act", lambda e, t=t, sc_ap=sc_ap: e.activation(out=Kx[:], in_=KTM[:, t, :], func=AF.Identity, scale=sc_ap), r=[KTM.part(t, NT), sc_r], w=[Kx])
                yield
                if ret:
                    while navail() < 1:
                        yield
                    ps = psh()
                    for kc in range(2):
                        P.op("pe", lambda e, kc=kc, ps=ps, t=t: e.matmul(ps[:, kc * 256:(kc + 1) * 256], lhsT=Kx[:, kc * 128:(kc + 1) * 128], rhs=VG[:, t, 0:256], start=True, stop=True), r=[Kx, VG.part(t, NT)], w=[ps])
                    yield
                    P.op("dve", lambda e, ps=ps, dk_ap=dk_ap: e.scalar_tensor_tensor(out=SM[:, :, 0:256], in0=SM[:, :, 0:256], scalar=dk_ap, in1=ps[:].rearrange("p (a b) -> p a b", a=2), op0=ALU.mult, op1=ALU.add), r=[SM, ps, dk_r], w=[SM])
                    prel(ps)
                    yield
                else:
                    while navail() < 2:
                        yield
                    pss = [psh(), psh()]
                    for kc in range(2):
                        P.op("pe", lambda e, kc=kc, ps=pss[kc], t=t: e.matmul(ps[:, 0:257], lhsT=Kx[:, kc * 128:(kc + 1) * 128], rhs=VG[:, t, 0:257], start=True, stop=True), r=[Kx, VG.part(t, NT)], w=[pss[kc]])
                    yield
                    for kc in range(2):
                        P.op("dve", lambda e, ps=pss[kc], kc=kc, dk_ap=dk_ap: e.scalar_tensor_tensor(out=SM[:, kc, :], in0=SM[:, kc, :], scalar=dk_ap, in1=ps[:, 0:257], op0=ALU.mult, op1=ALU.add), r=[SM, pss[kc], dk_r], w=[SM])
                        prel(pss[kc])
                    yield
            if not sample:
                dst4 = o_ret if ret else o_C
                P.op("sp", lambda e: e.dma_start(out=dst4[s, d, h].rearrange("(kc p) v -> p kc v", p=128), in_=SM[:, :, 0:256]), r=[SM], dma="smo%d" % SM.lo)
                if not ret:
                    P.op("sp", lambda e: e.dma_start(out=o_n[s, d, h].unsqueeze(2), in_=SM[:, :, 256:257], allow_slow_non_contiguous=True), r=[SM], dma="smo%d" % SM.lo)
            yield

        def out_chunk_ret(k, sl, s, c, h):
            t = s * n + c
            tk = slice(t * 128, (t + 1) * 128)
            attm = ATT[k]; xn = XN[k]
            while navail() < 2:
                yield
            psA = psh()
            for kc in range(2):
                P.op("pe", lambda e, kc=kc: e.matmul(psA[:, 0:128], lhsT=KT[:, kc, tk], rhs=QT[:, kc, tk], start=(kc == 0), stop=(kc == 1)), r=[KT, QT], w=[psA])
            psS = psh()
            for d in range(2):
                sa = SALL[sl][d][c]
                for kc in range(2):
                    P.op("pe", lambda e, kc=kc, d=d, sa=sa: e.matmul(psS[:, d * 256:(d + 1) * 256], lhsT=QT[:, kc, tk], rhs=sa[:, kc, 0:256], start=(kc == 0), stop=(kc == 1)), r=[QT, sa], w=[psS])
            yield
            P.op("dve", lambda e: e.tensor_tensor(out=attm[:, 0, :], in0=psA[:, 0:128], in1=RM[:], op=ALU.mult), r=[psA, RM], w=[attm])
            prel(psA)
            yield
            while navail() < 1:
                yield
            psO = psh()
            P.op("pe", lambda e: e.matmul(psO[:, 0:256], lhsT=attm[:, 0, :], rhs=VG[:, t, 0:256], start=True, stop=True), r=[attm, VG.part(t, NT)], w=[psO])
            P.op("dve", lambda e: e.tensor_scalar(out=xn[:], in0=psS[:, 0:256], scalar1=DEC[:, 0:1], scalar2=None, op0=ALU.mult), r=[psS, DEC], w=[xn])
            yield
            P.op("dve", lambda e: e.scalar_tensor_tensor(out=xn[:], in0=psS[:, 256:512], scalar=DEC[:, 1:2], in1=xn[:], op0=ALU.mult, op1=ALU.add), r=[psS, DEC, xn], w=[xn])
            yield
            P.op("dve", lambda e: e.tensor_tensor(out=xn[:], in0=psO[:, 0:256], in1=xn[:], op=ALU.add), r=[psO, xn], w=[xn])
            prel(psS); prel(psO)
            yield
            yield from out_tail(k, t, h)

        def out_chunk_ml(k, sl, s, c, h):
            t = s * n + c
            tk = slice(t * 128, (t + 1) * 128)
            attm = ATT[k]; xn = XN[k]; Dk = DMS[k]; WT = WTS[k]
            _, _, _, _, dd = smalls(k)
            bigm = CST.ap[:, C_BIGF:C_BIGF + 256]
            while navail() < 1:
                yield
            psA = psh()
            for kc in range(2):
                P.op("pe", lambda e, kc=kc: e.matmul(psA[:, 0:128], lhsT=KT[:, kc, tk], rhs=QT[:, kc, tk], start=(kc == 0), stop=(kc == 1)), r=[KT, QT], w=[psA])
            mu2 = GMU.ap[:, :, t, h]
            P.op("dve", lambda e: e.tensor_tensor(out=Dk[:], in0=IDF.unsqueeze(1).to_broadcast([128, 2, 128]), in1=mu2.unsqueeze(2).to_broadcast([128, 2, 128]), op=ALU.mult), r=[CST, GMU], w=[Dk])
            yield
            P.op("pe", lambda e: e.matmul(psA[:, 128:384], lhsT=ONESF, rhs=Dk[:].rearrange("p a b -> p (a b)"), start=True, stop=False), r=[CST, Dk], w=[psA])
            P.op("pe", lambda e: e.matmul(psA[:, 128:384], lhsT=IDF, rhs=bigm, start=False, stop=True), r=[CST], w=[psA])
            yield
            for d in range(2):
                col = d * 4 + h
                P.op("act", lambda e, d=d, col=col: e.activation(out=WT[:], in_=psA[:, 128 + d * 128:256 + d * 128], func=AF.Exp, bias=GA[:, d, t, h:h + 1], scale=-1.0), r=[psA, GA], w=[WT])
                yield
                P.op("dve", lambda e, d=d: e.tensor_tensor(out=attm[:, d, :], in0=psA[:, 0:128], in1=WT[:], op=ALU.mult), r=[psA, WT], w=[attm])
                yield
            prel(psA)
            for d in range(2):
                col = d * 4 + h
                sa = SALL[sl][d][c]
                td = TOT[k][d]
                while navail() < 2:
                    yield
                psN = psh()
                P.op("pe", lambda e, d=d, psN=psN: e.matmul(psN[:, 0:257], lhsT=attm[:, d, :], rhs=VG[:, t, 0:257], start=True, stop=True), r=[attm, VG.part(t, NT)], w=[psN])
                psI = psh()
                for kc in range(2):
                    P.op("pe", lambda e, kc=kc, psI=psI, sa=sa: e.matmul(psI[:, 0:257], lhsT=QT[:, kc, tk], rhs=sa[:, kc, :], start=(kc == 0), stop=(kc == 1)), r=[QT, sa], w=[psI])
                yield
                P.op("dve", lambda e, psI=psI, td=td, col=col: e.tensor_scalar(out=td[:], in0=psI[:, 0:257], scalar1=GSP[:, d, t, h:h + 1], scalar2=None, op0=ALU.mult), r=[psI, GSP], w=[td])
                prel(psI)
                yield
                P.op("dve", lambda e, psN=psN, td=td: e.tensor_tensor(out=td[:], in0=psN[:, 0:257], in1=td[:], op=ALU.add), r=[psN, td], w=[td])
                prel(psN)
                yield
            den2 = TOTT[k][:, :, 256:257]
            P.op("dve", lambda e: e.scalar_tensor_tensor(out=dd[:, 0:2].unsqueeze(2), in0=den2, scalar=-1.0, in1=den2, op0=ALU.mult, op1=ALU.max), r=[TOTT[k]], w=[dd])
            yield
            P.op("dve", lambda e: e.tensor_tensor(out=dd[:, 0:2], in0=dd[:, 0:2], in1=GEM.ap[:, :, t, h], op=ALU.max), r=[dd, GEM], w=[dd])
            yield
            P.op("dve", lambda e: e.reciprocal(out=dd[:], in_=dd[:]), r=[dd], w=[dd])
            yield
            P.op("dve", lambda e: e.tensor_scalar(out=xn[:], in0=TOT[k][0][:, 0:256], scalar1=dd[:, 0:1], scalar2=None, op0=ALU.mult), r=[TOT[k][0], dd], w=[xn])
            yield
            P.op("dve", lambda e: e.scalar_tensor_tensor(out=xn[:], in0=TOT[k][1][:, 0:256], scalar=dd[:, 1:2], in1=xn[:], op0=ALU.mult, op1=ALU.add), r=[TOT[k][1], dd, xn], w=[xn])
            yield
            yield from out_tail(k, t, 4 + h)

        def mixer(h, kind):
            ret = kind == "ret"
            if ret:
                lgf = LG.ap[:, h:h + 1]; lgb = LG.ap[:, 4 + h:5 + h]
                vec = lambda k_: CST.ap[:, C_VEC + k_:C_VEC + k_ + 1]
                for col, (src, lg) in enumerate([(vec(0), lgf), (vec(1), lgb), (vec(2), lgf), (vec(3), lgb)]):
                    P.op("act", lambda e, col=col, src=src, lg=lg: e.activation(out=DEC[:, col:col + 1], in_=src, func=AF.Exp, scale=lg), r=[CST, LG], w=[DEC])
                P.op("act", lambda e: e.activation(out=DEC[:, 4:5], in_=lgf, func=AF.Exp, scale=128.0), r=[LG], w=[DEC])
                P.op("act", lambda e: e.activation(out=DEC[:, 5:6], in_=lgb, func=AF.Exp, scale=128.0), r=[LG], w=[DEC])
                P.op("act", lambda e: e.activation(out=RM[:], in_=cstm(C_D1), func=AF.Exp, scale=lgf), r=[CST, LG], w=[RM])
                P.op("dve", lambda e: e.tensor_tensor(out=RM[:], in0=RM[:], in1=cstm(C_U1), op=ALU.mult), r=[RM, CST], w=[RM])
                P.op("act", lambda e: e.activation(out=RMT[:], in_=cstm(C_D2), func=AF.Exp, scale=lgb), r=[CST, LG], w=[RMT])
                P.op("dve", lambda e: e.tensor_tensor(out=RMT[:], in0=RMT[:], in1=cstm(C_U2), op=ALU.mult), r=[RMT, CST], w=[RMT])
                P.op("dve", lambda e: e.tensor_tensor(out=RM[:], in0=RM[:], in1=RMT[:], op=ALU.add), r=[RM, RMT], w=[RM])
            gsrc = gn_ret if ret else gn_ml
            P.op("sp", lambda e: e.dma_start(out=GN[:], in_=gsrc[0:1, h * 256:(h + 1) * 256].broadcast_to([128, 256])), w=[GN], dma="gn")
            ocf = out_chunk_ret if ret else out_chunk_ml
            for s0 in range(0, nseq, NSL):
                run_rr([state_chain(sl, s0 + sl, d, h, kind) for sl in range(NSL) for d in range(2)])
                jobs = [(sl, s0 + sl, c) for sl in range(NSL) for c in range(n)]
                for j0 in range(0, len(jobs), 2):
                    run_rr([ocf(k, jobs[j0 + k][0], jobs[j0 + k][1], jobs[j0 + k][2], h) for k in range(min(2, len(jobs) - j0))])

        def gates_pre():
            for t in range(NT):
                ps = psn()
                for kc in range(KC):
                    P.op("pe", lambda e, ps=ps, kc=kc, t=t: e.matmul(ps[:, 0:16], lhsT=HT[:, kc, t * 128:(t + 1) * 128], rhs=WGT[:, kc, :], start=(kc == 0), stop=(kc == KC - 1)), r=[HT, WGT], w=[ps])
                P.op("dve", lambda e, ps=ps, t=t: e.tensor_tensor(out=GTS[:, t, :], in0=ps[:, 0:16], in1=BG[:], op=ALU.add), r=[ps, BG], w=[GTS])
            P.op("act", lambda e: e.activation(out=LFN[:], in_=GTS[:], func=AF.Exp, scale=-1.0), r=[GTS], w=[LFN])
            P.op("act", lambda e: e.activation(out=LFN[:], in_=LFN[:], func=AF.Ln, bias=1.0), r=[LFN], w=[LFN])
            for t in range(NT):
                ps = psn()
                P.op("pe", lambda e, ps=ps, t=t: e.matmul(ps[:, 0:4], lhsT=cstm(C_U1), rhs=LFN[:, t, 4:8], start=True, stop=True), r=[CST, LFN], w=[ps])
                P.op("pe", lambda e, ps=ps, t=t: e.matmul(ps[:, 4:8], lhsT=cstm(C_U2), rhs=LFN[:, t, 12:16], start=True, stop=True), r=[CST, LFN], w=[ps])
                P.op("pe", lambda e, ps=ps, t=t: e.matmul(ps[:, 8:12], lhsT=ONESF, rhs=LFN[:, t, 4:8], start=True, stop=True), r=[CST, LFN], w=[ps])
                P.op("pe", lambda e, ps=ps, t=t: e.matmul(ps[:, 12:16], lhsT=ONESF, rhs=LFN[:, t, 12:16], start=True, stop=True), r=[CST, LFN], w=[ps])
                P.op("dve", lambda e, ps=ps, t=t: e.tensor_copy(out=GNB[:, :, t, :], in_=ps[:, 0:8].rearrange("p (a b) -> p a b", a=2)), r=[ps], w=[GNB])
                P.op("dve", lambda e, ps=ps, t=t: e.tensor_copy(out=GBL[:, :, t, :], in_=ps[:, 8:16].rearrange("p (a b) -> p a b", a=2)), r=[ps], w=[GBL])
                P.op("dve", lambda e, t=t: e.tensor_tensor(out=GA[:, 0, t, :], in0=GTS[:, t, 0:4], in1=GNB[:, 0, t, :], op=ALU.add), r=[GTS, GNB], w=[GA])
                P.op("dve", lambda e, t=t: e.tensor_tensor(out=GA[:, 1, t, :], in0=GTS[:, t, 8:12], in1=GNB[:, 1, t, :], op=ALU.add), r=[GTS, GNB], w=[GA])
            GD = [Dm, sbt(XN[0].lo, [4, 128], F32)]
            GAm = [Am, sbt(TOTT[0].lo, [4, 128], F32)]
            GV8 = [v8, [sbt(SMALL[0].lo + j * 32, [8], F32) for j in range(4)]]
            MPV = [mprev, sbt(SMALL[1].lo, [8], F32)]
            MNW = [mnew, sbt(SMALL[1].lo + 32, [8], F32)]
            for s in range(nseq):
                for d in range(2):
                    if not sample:
                        P.op("dve", lambda e, d=d: e.memset(MPV[d][:], 0.0), w=[MPV[d]])
                    else:
                        P.op("sp", lambda e, d=d: e.dma_start(out=MPV[d][:], in_=st_m[0:1, :].broadcast_to([128, 8])), w=[MPV[d]], dma="mprev%d" % d)

                def gchain(d, s=s):
                    ds = slice(d * 4, d * 4 + 4)
                    order = range(n) if d == 0 else range(n - 1, -1, -1)
                    neg = cstm(C_NEGF if d == 0 else C_NEGB)
                    Dm_, Am_, v8_, mprev_, mnew_ = GD[d], GAm[d], GV8[d], MPV[d], MNW[d]
                    for c in order:
                        t = s * n + c
                        yield
                        P.op("dve", lambda e, t=t: e.tensor_tensor(out=Dm_[:], in0=IDF.unsqueeze(1).to_broadcast([128, 4, 128]), in1=GA[:, d, t, :].unsqueeze(2).to_broadcast([128, 4, 128]), op=ALU.mult), r=[CST, GA.part(d, 2)], w=[Dm_])
                        ps = psn()
                        P.op("pe", lambda e, ps=ps: e.matmul(ps[:], lhsT=ONESF, rhs=Dm_[:].rearrange("p a b -> p (a b)"), start=True, stop=True), r=[CST, Dm_], w=[ps])
                        P.op("dve", lambda e, ps=ps: e.tensor_tensor(out=Am_[:], in0=ps[:].rearrange("p (a b) -> p a b", a=4), in1=neg.unsqueeze(1).to_broadcast([128, 4, 128]), op=ALU.add), r=[ps, CST], w=[Am_])
                        P.op("dve", lambda e: e.tensor_reduce(out=v8_[0][:, 0:4], in_=Am_[:], axis=AX.X, op=ALU.max), r=[Am_], w=[v8_[0]])
                        P.op("dve", lambda e, ps=ps: e.tensor_reduce(out=v8_[1][:, 0:4], in_=ps[:].rearrange("p (a b) -> p a b", a=4), axis=AX.X, op=ALU.max), r=[ps], w=[v8_[1]])
                        yield
                        P.op("dve", lambda e, t=t: e.tensor_tensor(out=GMU[:, d, t, :], in0=v8_[0][:, 0:4], in1=mprev_[:, ds], op=ALU.max), r=[v8_[0], mprev_], w=[GMU.part(d, 2)])
                        P.op("dve", lambda e: e.tensor_tensor(out=mnew_[:, ds], in0=v8_[1][:, 0:4], in1=mprev_[:, ds], op=ALU.max), r=[v8_[1], mprev_], w=[mnew_])
                        P.op("dve", lambda e, t=t: e.tensor_tensor(out=v8_[2][:, 0:4], in0=mprev_[:, ds], in1=GMU[:, d, t, :], op=ALU.subtract), r=[mprev_, GMU.part(d, 2)], w=[v8_[2]])
                        P.op("act", lambda e, t=t: e.activation(out=GSP[:, d, t, :], in_=v8_[2][:, 0:4], func=AF.Exp), r=[v8_[2]], w=[GSP.part(d, 2)])
                        P.op("dve", lambda e, t=t: e.tensor_tensor(out=v8_[3][:, 0:4], in0=GA[:, d, t, :], in1=mnew_[:, ds], op=ALU.subtract), r=[GA.part(d, 2), mnew_], w=[v8_[3]])
                        P.op("act", lambda e, t=t: e.activation(out=GWK[:, d, t, :], in_=v8_[3][:, 0:4], func=AF.Exp), r=[v8_[3]], w=[GWK.part(d, 2)])
                        yield
                        P.op("dve", lambda e: e.tensor_tensor(out=v8_[2][:, 4:8], in0=mprev_[:, ds], in1=mnew_[:, ds], op=ALU.subtract), r=[mprev_, mnew_], w=[v8_[2]])
                        P.op("act", lambda e, t=t: e.activation(out=GSC[:, d, t, :], in_=v8_[2][:, 4:8], func=AF.Exp), r=[v8_[2]], w=[GSC.part(d, 2)])
                        P.op("dve", lambda e, t=t: e.tensor_tensor(out=v8_[3][:, 4:8], in0=GNB[:, d, t, :], in1=GMU[:, d, t, :], op=ALU.subtract), r=[GNB.part(d, 2), GMU.part(d, 2)], w=[v8_[3]])
                        P.op("act", lambda e, t=t: e.activation(out=GEM[:, d, t, :], in_=v8_[3][:, 4:8], func=AF.Exp), r=[v8_[3]], w=[GEM.part(d, 2)])
                        P.op("dve", lambda e, t=t: e.tensor_tensor(out=mprev_[:, ds], in0=mnew_[:, ds], in1=GBL[:, d, t, :], op=ALU.subtract), r=[mnew_, GBL.part(d, 2)], w=[mprev_])
                    yield

                gens = [gchain(0), gchain(1)]
                while gens:
                    nxt = []
                    for g in gens:
                        try:
                            next(g)
                            nxt.append(g)
                        except StopIteration:
                            pass
                    gens = nxt
                if not sample:
                    P.op("dve", lambda e: e.tensor_copy(out=MPV[0][:, 4:8], in_=MPV[1][:, 4:8]), r=[MPV[1]], w=[MPV[0]])
                    P.op("sp", lambda e, s=s: e.dma_start(out=o_m[s:s + 1, :], in_=MPV[0][0:1, :]), r=[MPV[0]], dma="om")

        def mod_blks(hh):
            return list(range(16 + hh * 4, 20 + hh * 4))

        for h in range(NH):
            wq = load_w(h * 256); proj_fm(wq, QT, "rq", 0)
            wk = load_w(1024 + h * 256); proj_fm(wk, KT, "rk", 0)
            wv = load_w(2048 + h * 256); wg = load_w(3072 + h * 256)
            if not sample:
                mod_dma(mod_blks(h))
            proj_tm(wv, wg, AF.Silu)
            make_ktm()
            mixer(h, "ret")
            if not sample:
                mod_compute(mod_blks(h))
        gates_pre()
        for h in range(NH):
            wq = load_w(4096 + h * 256); proj_fm(wq, QT, "mq", h * 256)
            wk = load_w(5120 + h * 256); proj_fm(wk, KT, "mk", 1024 + h * 256)
            wv = load_w(6144 + h * 256); wg = load_w(7168 + h * 256)
            if not sample:
                mod_dma(mod_blks(4 + h))
            proj_tm(wv, wg, AF.Sigmoid)
            make_ktm(1.0 / 16.0)
            mixer(h, "ml")
            if not sample:
                mod_compute(mod_blks(4 + h))

        MT = sbt(O_HT, [16, Town], BF16)
        for t in range(NTO):
            transpose_tile(MIXO[:, t, :], MIXO.part(t, NTO), MT, MT, t * 128)
        rb = Bump(O_PROJ, SZ_PROJ)
        r_g1 = rb([D], F32); r_l1g = rb([D], F32); r_l1b = rb([D], F32); r_sc2 = rb([D], F32); r_sh2 = rb([D], F32)
        load_modrow(r_g1, 2, mg); load_row(r_l1g, ln1_g); load_row(r_l1b, ln1_b)
        load_modrow(r_sh2, 3, mg); load_modrow(r_sc2, 4, mg, plus1=True)
        YA = [sbt((O_MIXO if t < 4 else O_MIX) + (t % 4) * 8192, [D], F32) for t in range(NTO)]
        hb = Bump(O_HT + 32768, 32768)
        xts = [hb([D], F32) for _ in range(2)]
        h2s = [hb([D], BF16) for _ in range(2)]
        wsl2 = [sbt(O_W + k * 8192, [16, 256], BF16) for k in range(2)]
        for cbk in range(8):
            ws = wsl2[cbk % 2]
            P.op("pool", lambda e, ws=ws, cbk=cbk: e.dma_start(out=ws[:], in_=w_out[:, cbk * 256:(cbk + 1) * 256].rearrange("(kc p) n -> p kc n", p=128)), w=[ws], dma="wsl%d" % (cbk % 2))
            for t in range(NTO):
                ps = psn()
                for kc in range(KC):
                    P.op("pe", lambda e, ps=ps, kc=kc, t=t, ws=ws: e.matmul(ps[:, 0:256], lhsT=MT[:, kc, t * 128:(t + 1) * 128], rhs=ws[:, kc, :], start=(kc == 0), stop=(kc == KC - 1)), r=[MT, ws], w=[ps])
                P.op("dve", lambda e, ps=ps, t=t, cbk=cbk: e.tensor_tensor(out=YA[t][:, cbk * 256:(cbk + 1) * 256], in0=ps[:, 0:256], in1=r_g1[:, cbk * 256:(cbk + 1) * 256], op=ALU.mult), r=[ps, r_g1], w=[YA[t]])
        smo = [sbt(O_MIX + 32768 + k * 256, [64], F32) for k in range(2)]

        def o_tile(t):
            def g(k):
                xt = xts[k]; h2 = h2s[k]; ya = YA[t]
                st, mv, rstd, nmr = small4(smo[k])
                P.op("sp", lambda e: e.dma_start(out=xt[:], in_=x_own[t * 128:(t + 1) * 128, :]), w=[xt], dma="xo%d" % k)
                yield
                P.op("dve", lambda e: e.scalar_tensor_tensor(out=ya[:], in0=xt[:], scalar=ALPHA, in1=ya[:], op0=ALU.mult, op1=ALU.add), r=[xt, ya], w=[ya])
                yield
                ln_stats(ya.ap, ya, st, mv, rstd, None, D)
                yield
                P.op("dve", lambda e: e.scalar_tensor_tensor(out=ya[:], in0=ya[:], scalar=mv[:, 0:1], in1=r_l1g[:], op0=ALU.subtract, op1=ALU.mult), r=[ya, mv, r_l1g], w=[ya])
                yield
                P.op("dve", lambda e: e.scalar_tensor_tensor(out=ya[:], in0=ya[:], scalar=rstd[:], in1=r_l1b[:], op0=ALU.mult, op1=ALU.add), r=[ya, rstd, r_l1b], w=[ya])
                yield
                P.op("sp", lambda e: e.dma_start(out=x1_d[x1off + t * 128: x1off + (t + 1) * 128, :], in_=ya[:]), r=[ya], w=[DR("x1", x1off // 128 + t, x1off // 128 + t + 1)], dma="x1o%d" % t)
                ln_stats(ya.ap, ya, st, mv, rstd, None, D)
                yield
                P.op("dve", lambda e: e.scalar_tensor_tensor(out=xt[:], in0=ya[:], scalar=mv[:, 0:1], in1=r_sc2[:], op0=ALU.subtract, op1=ALU.mult), r=[ya, mv, r_sc2], w=[xt])
                yield
                P.op("dve", lambda e: e.scalar_tensor_tensor(out=h2[:], in0=xt[:], scalar=rstd[:], in1=r_sh2[:], op0=ALU.mult, op1=ALU.add), r=[xt, rstd, r_sh2], w=[h2])
                yield
                yield from transpose_tile_g(h2.ap, h2, MT, MT, t * 128)
            return g
        run_pipe([o_tile(t) for t in range(NTO)], 2)

        H2T = MT
        achunks = []
        for k in range(21):
            achunks.append(sbt(O_PROJ + k * Town * 2, [Town], BF16))
        for k in range(16):
            achunks.append(sbt(O_MIXO + k * Town * 2, [Town], BF16))
        for k in range(6):
            achunks.append(sbt(O_HT + 32768 + k * Town * 2, [Town], BF16))
        wd2 = sbt(O_HT + 32768 + 6 * 2048, [NFF, 128], BF16)
        wd1 = sbt(O_W, [NFF, 128], BF16)
        fb = Bump(O_MIX, SZ_MIX)
        zt = [fb([512], F32) for _ in range(2)]
        z2 = [fb([512], F32) for _ in range(2)]
        wup = [sbt(O_W + k * 8192, [16, 256], BF16) for k in range(2)]
        nb_ = 512 // Lc
        v3 = lambda ap: ap.rearrange("p (a b) -> p a b", a=nb_)
        for c in range(NFF):
            ws = wup[c % 2]
            P.op("pool", lambda e, ws=ws, c=c: e.dma_start(out=ws[:, :, 0:128], in_=w_up[:, c * 128:(c + 1) * 128].rearrange("(kc p) n -> p kc n", p=128)), w=[ws], dma="wup%d" % (c % 2))
            P.op("pool", lambda e, ws=ws, c=c: e.dma_start(out=ws[:, :, 128:256], in_=w_up[:, DFF + c * 128:DFF + (c + 1) * 128].rearrange("(kc p) n -> p kc n", p=128)), w=[ws], dma="wup%d" % (c % 2))
            w0 = CFF.ap[:, c * 3:c * 3 + 1]; w1 = CFF.ap[:, c * 3 + 1:c * 3 + 2]; w2 = CFF.ap[:, c * 3 + 2:c * 3 + 3]
            for tb in range(Town // 512):
                pu = psn()
                for kc in range(KC):
                    P.op("pe", lambda e, pu=pu, kc=kc, tb=tb, ws=ws: e.matmul(pu[:], lhsT=ws[:, kc, 0:128], rhs=H2T[:, kc, tb * 512:(tb + 1) * 512], start=(kc == 0), stop=(kc == KC - 1)), r=[ws, H2T], w=[pu])
                pg = psn()
                for kc in range(KC):
                    P.op("pe", lambda e, pg=pg, kc=kc, tb=tb, ws=ws: e.matmul(pg[:], lhsT=ws[:, kc, 128:256], rhs=H2T[:, kc, tb * 512:(tb + 1) * 512], start=(kc == 0), stop=(kc == KC - 1)), r=[ws, H2T], w=[pg])
                z = zt[(c * (Town // 512) + tb) % 2]; zz = z2[(c * (Town // 512) + tb) % 2]
                P.op("dve", lambda e, pu=pu, z=z, w1=w1: e.tensor_scalar(out=z[:], in0=pu[:], scalar1=w1, scalar2=None, op0=ALU.mult), r=[pu, CFF], w=[z])
                P.op("dve", lambda e, pu=pu, z=z, w0=w0: e.scalar_tensor_tensor(out=v3(z[:])[:, :, 1:Lc], in0=v3(pu[:])[:, :, 0:Lc - 1], scalar=w0, in1=v3(z[:])[:, :, 1:Lc], op0=ALU.mult, op1=ALU.add), r=[pu, CFF, z], w=[z])
                P.op("dve", lambda e, pu=pu, z=z, w2=w2: e.scalar_tensor_tensor(out=v3(z[:])[:, :, 0:Lc - 1], in0=v3(pu[:])[:, :, 1:Lc], scalar=w2, in1=v3(z[:])[:, :, 0:Lc - 1], op0=ALU.mult, op1=ALU.add), r=[pu, CFF, z], w=[z])
                P.op("act", lambda e, z=z, zz=zz: e.activation(out=zz[:], in_=z[:], func=AF.Silu), r=[z], w=[zz])
                ac = achunks[c]
                P.op("dve", lambda e, pg=pg, zz=zz, ac=ac, tb=tb: e.tensor_tensor(out=ac[:, tb * 512:(tb + 1) * 512], in0=pg[:], in1=zz[:], op=ALU.mult), r=[pg, zz], w=[ac])
        fb = Bump(O_MIX, SZ_MIX)
        r_g2 = fb([D], F32); r_l2g = fb([D], F32); r_l2b = fb([D], F32)
        x1t = fb([D], F32); y2 = fb([D], F32)
        st, mv, rstd, nmr = ST_C, MV_C, RSTD_C, NMR_C
        load_modrow(r_g2, 5, mg); load_row(r_l2g, ln2_g); load_row(r_l2b, ln2_b)
        wds = [wd1, wd2]
        stg = [sbt(O_W + 11264 + k * 512, [128], F32) for k in range(8)]
        cnt = 0
        for cbk in range(16):
            ws = wds[cbk % 2]
            P.op("pool", lambda e, ws=ws, cbk=cbk: e.dma_start(out=ws[:], in_=w_down[:, cbk * 128:(cbk + 1) * 128].rearrange("(c p) n -> p c n", p=128)), w=[ws], dma="wd%d" % (cbk % 2))
            for t in range(NTO):
                ps = psn()
                for c in range(NFF):
                    P.op("pe", lambda e, ps=ps, c=c, t=t, ws=ws: e.matmul(ps[:, 0:128], lhsT=achunks[c][:, t * 128:(t + 1) * 128], rhs=ws[:, c, :], start=(c == 0), stop=(c == NFF - 1)), r=[achunks[c], ws], w=[ps])
                sg_ = stg[cnt % 8]
                P.op("dve", lambda e, ps=ps, cbk=cbk, sg_=sg_: e.tensor_tensor(out=sg_[:], in0=ps[:, 0:128], in1=r_g2[:, cbk * 128:(cbk + 1) * 128], op=ALU.mult), r=[ps, r_g2], w=[sg_])
                P.op("sp", lambda e, sg_=sg_, cbk=cbk, t=t: e.dma_start(out=y2_d[x1off + t * 128: x1off + (t + 1) * 128, cbk * 128:(cbk + 1) * 128], in_=sg_[:]), r=[sg_], w=[DR("y2", x1off // 128 + t, x1off // 128 + t + 1)], dma="stg%d" % (cnt % 8))
                cnt += 1
        for t in range(NTO):
            row = x1off // 128 + t
            P.op("sp", lambda e, t=t: e.dma_start(out=x1t[:], in_=x1_d[x1off + t * 128: x1off + (t + 1) * 128, :]), r=[DR("x1", row, row + 1)], w=[x1t], dma="x1t")
            P.op("sp", lambda e, t=t: e.dma_start(out=y2[:], in_=y2_d[x1off + t * 128: x1off + (t + 1) * 128, :]), r=[DR("y2", row, row + 1)], w=[y2], dma="y2t")
            P.op("dve", lambda e: e.scalar_tensor_tensor(out=y2[:], in0=x1t[:], scalar=ALPHA, in1=y2[:], op0=ALU.mult, op1=ALU.add), r=[x1t, y2], w=[y2])
            ln_stats(y2.ap, y2, st, mv, rstd, None, D)
            P.op("dve", lambda e: e.scalar_tensor_tensor(out=y2[:], in0=y2[:], scalar=mv[:, 0:1], in1=r_l2g[:], op0=ALU.subtract, op1=ALU.mult), r=[y2, mv, r_l2g], w=[y2])
            P.op("dve", lambda e: e.scalar_tensor_tensor(out=y2[:], in0=y2[:], scalar=rstd[:], in1=r_l2b[:], op0=ALU.mult, op1=ALU.add), r=[y2, rstd, r_l2b], w=[y2])
            P.op("sp", lambda e, t=t: e.dma_start(out=y_out[t * 128:(t + 1) * 128, :], in_=y2[:]), r=[y2], dma="yout")

    stage_mod(list(range(16)), O_PROJ, O_MIX)
    for gi in GROUPS:
        process_group(gi)
    P.emit()
    es.close()
    return nc


GROUPS = (0, 1)
_CACHE = {}


def kernel(x_prompt, x_sample, state_ret, state_mlstm_C, state_mlstm_n, state_mlstm_m, c, c_ctx,
           w_mod, b_mod, w_in, b_gate, conv_qk, ret_theta, gn_ret, gn_mlstm, w_out,
           ln1_g, ln1_b, w_up, conv_ff, w_down, ln2_g, ln2_b):
    f = lambda a: np.ascontiguousarray(np.asarray(a, dtype=np.float32))
    if "nc" not in _CACHE:
        _CACHE["nc"] = build_program()
    nc = _CACHE["nc"]
    cst = make_consts()
    xp_all = f(x_prompt).reshape(8, 1024, D)
    xs_all = f(x_sample)
    shared = dict(
        w_mod=f(w_mod)[0], b_mod=f(b_mod), w_in=f(w_in)[0], b_gate=f(b_gate),
        conv_qkT=f(f(conv_qk)[0].T.reshape(16, 128, 3).transpose(1, 0, 2).reshape(128, 48)),
        theta=f(ret_theta).reshape(1, 8), gn_ret=f(gn_ret).reshape(1, 1024), gn_ml=f(gn_mlstm).reshape(1, 1024),
        w_out=f(w_out)[0], ln1_g=f(ln1_g), ln1_b=f(ln1_b), w_up=f(w_up)[0],
        conv_ffT=f(f(conv_ff)[0].T.reshape(NFF, 128, 3).transpose(1, 0, 2).reshape(128, NFF * 3)),
        w_down=f(w_down)[0], ln2_g=f(ln2_g), ln2_b=f(ln2_b), cst=cst)
    in_maps = []
    for i in range(8):
        b, p = i // 4, i % 4
        fl = np.zeros((1, 4), np.float32); fl[0, p] = 1.0
        cT = np.stack([f(c_ctx).reshape(16, 128).T, f(c)[b].reshape(16, 128).T], axis=2).reshape(128, 32)
        m = dict(shared)
        m.update(xp=xp_all[i], xs=xs_all[b], xo=f(xs_all[b, p * 512:(p + 1) * 512]), flags=fl, cT=f(cT),
                 st_ret=f(state_ret)[b, 0], st_C=f(state_mlstm_C)[b, 0],
                 st_n=f(f(state_mlstm_n)[b, 0].reshape(2, 4, 2, 128).transpose(0, 1, 3, 2)),
                 st_m=f(state_mlstm_m)[b, 0].reshape(1, 8))
        in_maps.append(m)
    res = run_bass_kernel_spmd(nc, in_maps, core_ids=list(range(8)))
    R = res.results
    y_prompt = np.concatenate([R[i]["yp"] for i in range(8)], 0).reshape(32, 256, D)
    y_sample = np.stack([np.concatenate([R[b * 4 + p]["ys"] for p in range(4)], 0) for b in range(2)], 0)
    n_ret = np.concatenate([R[i]["o_ret"] for i in range(8)], 0)[:, None]
    n_C = np.concatenate([R[i]["o_C"] for i in range(8)], 0)[:, None]
    n_n = np.concatenate([R[i]["o_n"] for i in range(8)], 0).transpose(0, 1, 2, 4, 3).reshape(32, 2, 4, 256)[:, None]
    n_m = np.concatenate([R[i]["o_m"] for i in range(8)], 0).reshape(32, 2, 4)[:, None]
    return (y_prompt, y_sample, np.ascontiguousarray(n_ret), np.ascontiguousarray(n_C),
            np.ascontiguousarray(n_n), np.ascontiguousarray(n_m))
```
